# Optimizing a Trainium2 kernel written in Bass

```python
import jax, jax.numpy as jnp
from jax import lax
import numpy as np

D_MODEL = 1024
BATCH = 4
SEQ = 8192
DEPTH = 1

CHUNK = 64
HEAD_DIM = 64
CONV_GROUPS = 8
CONV_DIM = CONV_GROUPS * HEAD_DIM
RWKV_HEADS = 8
RWKV_DIM = RWKV_HEADS * HEAD_DIM
MIX_DIM = CONV_DIM + RWKV_DIM
CONV_K = 3
DECAY_RANK = 64
ICLR_RANK = 64
GATE_RANK = 160
D_FF = 2816
RWKV_COLS = 3 * RWKV_DIM + DECAY_RANK + ICLR_RANK + GATE_RANK
IN_COLS = 3 * CONV_DIM + RWKV_COLS
RMS_EPS = 1e-6
GN_EPS = HEAD_DIM * 1e-5

kernel_name = 'hybrid_shortconv_rwkv7_convffn_block'


def _rms_norm(z, g):
    zf = z.astype(jnp.float32)
    zf = zf * lax.rsqrt(jnp.mean(zf * zf, axis=-1, keepdims=True) + RMS_EPS)
    return (zf * g.astype(jnp.float32)).astype(z.dtype)


def _causal_dwconv(z, w):
    c = z.shape[-1]
    return lax.conv_general_dilated(
        z, w[:, None, :].astype(z.dtype), window_strides=(1,),
        padding=[(CONV_K - 1, 0)], dimension_numbers=('NWC', 'WIO', 'NWC'),
        feature_group_count=c)


def _token_shift(z):
    return jnp.pad(z[:, :-1], ((0, 0), (1, 0), (0, 0)))


def _wkv7(r, decay, k, v, kk, a):
    b, t, h, n = r.shape
    n_chunks = t // CHUNK

    def to_chunks(z):
        return z.reshape(b, n_chunks, CHUNK, h, n).transpose(1, 2, 0, 3, 4)

    def step(S, inp):
        r_t, w_t, k_t, v_t, kk_t, a_t = inp
        sa = jnp.einsum('bhvk,bhk->bhv', S, -kk_t)
        S = (S * w_t[:, :, None, :] + sa[..., None] * (kk_t * a_t)[:, :, None, :]
             + v_t[..., None] * k_t[:, :, None, :])
        return S, jnp.einsum('bhvk,bhk->bhv', S, r_t)

    def chunk_step(S, inp):
        return lax.scan(step, S, inp)

    S0 = jnp.zeros((b, h, n, n), jnp.float32)
    xs = tuple(to_chunks(z) for z in (r, decay, k, v, kk, a))
    _, y = lax.scan(chunk_step, S0, xs)
    return y.transpose(2, 0, 1, 3, 4).reshape(b, t, h, n)


def _head_group_norm(y, wgt, bias):
    mu = jnp.mean(y, axis=-1, keepdims=True)
    var = jnp.mean(jnp.square(y - mu), axis=-1, keepdims=True)
    yn = (y - mu) * lax.rsqrt(var + GN_EPS)
    b, t, h, n = y.shape
    return yn.reshape(b, t, h * n) * wgt.astype(jnp.float32) + bias.astype(jnp.float32)


def _token_mixer(xn, w_in, conv_a_w, shift_mu, w0, w2, a0, a2, g2, k_k, k_a, r_k,
                 lnx_w, lnx_b, w_out):
    b, t, _ = xn.shape
    p = jnp.einsum('btd,de->bte', xn, w_in)
    gB, gC, hA = jnp.split(p[..., :3 * CONV_DIM], 3, axis=-1)
    y_a = gC * _causal_dwconv(gB * hA, conv_a_w)
    q = p[..., 3 * CONV_DIM:]
    q = q + (_token_shift(q) - q) * shift_mu
    splits = np.cumsum([RWKV_DIM, RWKV_DIM, RWKV_DIM, DECAY_RANK, ICLR_RANK]).tolist()
    r, k, v, wd, ad, gd = jnp.split(q, splits, axis=-1)
    wdf = wd.astype(jnp.float32)
    w_log = -jax.nn.softplus(-(w0.astype(jnp.float32)
                               + jnp.tanh(wdf) @ w2.astype(jnp.float32))) - 0.5
    decay = jnp.exp(-jnp.exp(w_log))
    a = jax.nn.sigmoid(a0 + ad @ a2)
    g = jax.nn.sigmoid(gd) @ g2
    heads = lambda z: z.astype(jnp.float32).reshape(b, t, RWKV_HEADS, HEAD_DIM)
    pheads = lambda z: z.astype(jnp.float32).reshape(RWKV_HEADS, HEAD_DIM)
    r4, k4, v4, a4, d4 = heads(r), heads(k), heads(v), heads(a), heads(decay)
    kk = k4 * pheads(k_k)
    kk = kk / jnp.maximum(jnp.linalg.norm(kk, axis=-1, keepdims=True), 1e-12)
    k4 = k4 * (1.0 + (a4 - 1.0) * pheads(k_a))
    y = _wkv7(r4, d4, k4, v4, kk, a4)
    y_b = _head_group_norm(y, lnx_w, lnx_b)
    bonus = jnp.sum(r4 * k4 * r_k.astype(jnp.float32), axis=-1, keepdims=True) * v4
    y_b = (y_b + bonus.reshape(b, t, RWKV_DIM)).astype(xn.dtype) * g
    return jnp.einsum('bte,ed->btd', jnp.concatenate([y_a, y_b], axis=-1), w_out)


def _channel_mixer(hn, w_up, ffn_conv_w, ffn_conv_b, w_down):
    f = jnp.einsum('btd,df->btf', hn, w_up)
    f = _causal_dwconv(f, ffn_conv_w) + ffn_conv_b
    gate, up = jnp.split(f, 2, axis=-1)
    return jnp.einsum('btf,fd->btd', jax.nn.gelu(gate) * up, w_down)


def setup_inputs(seed: int = 0) -> dict:
    key = jax.random.key(seed)
    ks = iter(jax.random.split(key, 32))
    nrm = lambda shape, s: jax.random.normal(next(ks), shape, jnp.float32) * s
    L = DEPTH
    return {
        'x': jax.random.normal(next(ks), (BATCH, SEQ, D_MODEL), jnp.float32),
        'pre_mix_g': 1.0 + nrm((L, D_MODEL), 0.02),
        'w_in': nrm((L, D_MODEL, IN_COLS), D_MODEL ** -0.5),
        'conv_a_w': nrm((L, CONV_K, CONV_DIM), 0.5),
        'shift_mu': jax.random.uniform(next(ks), (L, RWKV_COLS), jnp.float32),
        'w0': jax.random.uniform(next(ks), (L, RWKV_DIM), jnp.float32, -6.0, 0.0),
        'w2': nrm((L, DECAY_RANK, RWKV_DIM), 0.1),
        'a0': nrm((L, RWKV_DIM), 0.1),
        'a2': nrm((L, ICLR_RANK, RWKV_DIM), ICLR_RANK ** -0.5),
        'g2': nrm((L, GATE_RANK, RWKV_DIM), GATE_RANK ** -0.5),
        'k_k': 0.85 + nrm((L, RWKV_DIM), 0.02),
        'k_a': 1.0 + nrm((L, RWKV_DIM), 0.02),
        'r_k': nrm((L, RWKV_HEADS, HEAD_DIM), 0.1),
        'lnx_w': 1.0 + nrm((L, RWKV_DIM), 0.02),
        'lnx_b': nrm((L, RWKV_DIM), 0.01),
        'w_out': nrm((L, MIX_DIM, D_MODEL), MIX_DIM ** -0.5),
        'post_mix_g': 1.0 + nrm((L, D_MODEL), 0.02),
        'pre_ffn_g': 1.0 + nrm((L, D_MODEL), 0.02),
        'w_up': nrm((L, D_MODEL, 2 * D_FF), D_MODEL ** -0.5),
        'ffn_conv_w': nrm((L, CONV_K, 2 * D_FF), 0.5),
        'ffn_conv_b': nrm((L, 2 * D_FF), 0.01),
        'w_down': nrm((L, D_FF, D_MODEL), D_FF ** -0.5),
        'post_ffn_g': 1.0 + nrm((L, D_MODEL), 0.02),
    }


def reference(x, pre_mix_g, w_in, conv_a_w, shift_mu, w0, w2, a0, a2, g2, k_k, k_a, r_k,
              lnx_w, lnx_b, w_out, post_mix_g, pre_ffn_g, w_up, ffn_conv_w, ffn_conv_b,
              w_down, post_ffn_g):
    h = x
    for l in range(DEPTH):
        mix = _token_mixer(_rms_norm(h, pre_mix_g[l]), w_in[l], conv_a_w[l], shift_mu[l],
                           w0[l], w2[l], a0[l], a2[l], g2[l], k_k[l], k_a[l], r_k[l],
                           lnx_w[l], lnx_b[l], w_out[l])
        h = h + _rms_norm(mix, post_mix_g[l])
        ffn = _channel_mixer(_rms_norm(h, pre_ffn_g[l]), w_up[l], ffn_conv_w[l],
                             ffn_conv_b[l], w_down[l])
        h = h + _rms_norm(ffn, post_ffn_g[l])
    return h
```

```python
import contextlib
import numpy as np
import concourse.bass as bass
import concourse.mybir as mybir
from concourse.bass_utils import run_bass_kernel_spmd

F32 = mybir.dt.float32
BF16 = mybir.dt.bfloat16
AF = mybir.ActivationFunctionType
ALU = mybir.AluOpType
AX = mybir.AxisListType

D = 1024
W = 256
NBLK = 2
CH = 64
NCH = W // CH
INC = 3360
DFF = 2816
NPAIR = 22
RMS_EPS = 1e-6
GN_EPS = 64 * 1e-5
EPOCH = 12000
DEBUG_SUB = 99
DEBUG_STOP = 99

O_PMG, O_PFG, O_CAW, O_MU, O_W0, O_A0, O_KK, O_KA, O_RK, O_LW, O_LB, O_FCW, O_FCB, O_HM = (
    0, 8, 16, 28, 43, 47, 51, 55, 59, 63, 67, 71, 203, 247)
NCC = 248
M_ID, M_SU, M_IU, M_SL, M_BO, M_F, M_RST = 0, 128, 256, 384, 512, 640, 704
NKM = 704 + 256


class Key:
    __slots__ = ("name", "writer", "readers", "excl")

    def __init__(self, name, excl=False):
        self.name = name
        self.writer = None
        self.readers = []
        self.excl = excl


def PKey(name):
    return Key(name, excl=True)


class Sched:
    ENGS = ("pe", "act", "dve", "pool", "sp")

    def __init__(self, nc, sem_stack, prefix):
        self.nc = nc
        self.sem_stack = sem_stack
        self.prefix = prefix
        self.ops = {e: [] for e in self.ENGS}
        self.count = {e: 0 for e in self.ENGS}
        self.sems = {}
        self.waited = {e: {} for e in self.ENGS}
        self.dma_counts = {}
        self.last_tok = {e: None for e in self.ENGS}

    def _eng_sem(self, eng, idx):
        sid = f"{self.prefix}s_{eng}_{idx // EPOCH}"
        self.sems.setdefault(sid, None)
        return sid, (idx % EPOCH) + 1

    def new_dma_sem(self, name):
        sid = f"{self.prefix}d_{name}"
        assert sid not in self.sems, sid
        self.sems[sid] = None
        self.dma_counts[sid] = 0
        return sid

    def _need_waits(self, eng, tokens):
        w = self.waited[eng]
        best = {}
        for t in tokens:
            if t is None:
                continue
            sid, val, _ = t
            if w.get(sid, 0) >= val:
                continue
            if best.get(sid, 0) < val:
                best[sid] = val
        for sid, val in best.items():
            w[sid] = val
        return list(best.items())

    def op(self, eng, fn, reads=(), writes=(), dma_sem=None):
        toks = []
        raw = set()
        for k in reads:
            toks.append(k.writer)
            if k.writer is not None:
                raw.add(k.writer)
            if k.excl:
                toks.extend(r for r in k.readers if r[2] != eng)
        for k in writes:
            toks.append(k.writer)
            toks.extend(k.readers)
        waits = self._need_waits(eng, toks)
        if dma_sem is None:
            idx = self.count[eng]
            self.count[eng] += 1
            sid, val = self._eng_sem(eng, idx)
            tok = (sid, val, eng)
            inc = (sid, 1)
            self.last_tok[eng] = tok
        else:
            self.dma_counts[dma_sem] += 16
            tok = (dma_sem, self.dma_counts[dma_sem], "dma")
            inc = (dma_sem, 16)
        self.ops[eng].append((fn, waits, inc))
        for k in reads:
            k.readers.append(tok)
        for k in writes:
            k.writer = tok
            k.readers = []
        return tok

    def barrier(self, extra_keys=()):
        toks = [t for t in self.last_tok.values() if t is not None]
        for k in extra_keys:
            toks.append(k.writer)
            toks.extend(k.readers)
        for eng in self.ENGS:
            waits = self._need_waits(eng, [t for t in toks if t is not None and t[2] != eng])
            if waits:
                self.ops[eng].append((None, waits, None))

    def final_wait(self, eng, keys):
        toks = []
        for k in keys:
            toks.append(k.writer)
            toks.extend(k.readers)
        waits = self._need_waits(eng, toks)
        self.ops[eng].append((None, waits, None))

    def emit(self):
        nc = self.nc
        with contextlib.ExitStack() as st:
            handles = {sid: self.sem_stack.enter_context(nc.semaphore(sid)) for sid in self.sems}
            block = st.enter_context(nc.Block())

            def run(engobj, lst):
                for fn, waits, inc in lst:
                    for sid, val in waits:
                        engobj.wait_ge(handles[sid], val)
                    if fn is not None:
                        fn(engobj).then_inc(handles[inc[0]], inc[1])

            @block.tensor
            def _(e):
                run(e, self.ops["pe"])

            @block.scalar
            def _(e):
                run(e, self.ops["act"])

            @block.vector
            def _(e):
                run(e, self.ops["dve"])

            @block.gpsimd
            def _(e):
                run(e, self.ops["pool"])

            @block.sync
            def _(e):
                run(e, self.ops["sp"])


class Rot:
    def __init__(self, tiles, excl=False):
        self.tiles = tiles
        self.keys = [Key(f"rot{i}", excl) for i in range(len(tiles))]
        self.i = 0

    def next(self):
        j = self.i % len(self.tiles)
        self.i += 1
        return self.tiles[j], self.keys[j]


def _rms_transpose(S, src, ksrc, dstT, kdst, tcol, scr, ident, kid, ptr, extra_scale=None, kextra=None):
    junk, kjunk = scr["junk"].next()
    st, kst = scr["stat"].next()
    xs, kxs = scr["xs"].next()
    pt, kpt = ptr.next()
    S.op("act", lambda e: e.activation(out=junk[:], in_=src, func=AF.Square, accum_out=st[:, 0:1]),
         reads=[ksrc], writes=[kjunk, kst])
    S.op("act", lambda e: e.activation(out=st[:, 1:2], in_=st[:, 0:1], func=AF.Sqrt, scale=1.0 / D, bias=RMS_EPS),
         reads=[kst], writes=[kst])
    S.op("dve", lambda e: e.reciprocal(out=st[:, 2:3], in_=st[:, 1:2]), reads=[kst], writes=[kst])
    rs = st[:, 2:3]
    if extra_scale is not None:
        S.op("dve", lambda e: e.tensor_tensor(out=st[:, 3:4], in0=st[:, 2:3], in1=extra_scale, op=ALU.mult),
             reads=[kst, kextra], writes=[kst])
        rs = st[:, 3:4]
    S.op("pool", lambda e: e.tensor_scalar(out=xs[:], in0=src, scalar1=rs, scalar2=None, op0=ALU.mult),
         reads=[ksrc, kst], writes=[kxs])
    for kc in range(8):
        S.op("pe", lambda e, kc=kc: e.transpose(out=pt[:, kc, :], in_=xs[:, kc * 128:(kc + 1) * 128], identity=ident[:]),
             reads=[kxs, kid], writes=[kpt])
    S.op("act", lambda e: e.activation(out=dstT[:, :, tcol:tcol + 128], in_=pt[:], func=AF.Copy),
         reads=[kpt], writes=[kdst])


def _post_norm_residual(S, pd_pairs, res, kres, grow, kgrow, scr):
    st, kst = scr["stat"].next()
    for hf, (pd, kpd) in enumerate(pd_pairs):
        junk, kjunk = scr["junk"].next()
        S.op("act", lambda e, pd=pd, hf=hf, junk=junk: e.activation(out=junk[:, hf * 512:(hf + 1) * 512], in_=pd[:], func=AF.Square,
                                                                  accum_out=st[:, hf:hf + 1]),
             reads=[kpd], writes=[kjunk, kst])
    S.op("dve", lambda e: e.tensor_tensor(out=st[:, 2:3], in0=st[:, 0:1], in1=st[:, 1:2], op=ALU.add), reads=[kst], writes=[kst])
    S.op("act", lambda e: e.activation(out=st[:, 3:4], in_=st[:, 2:3], func=AF.Sqrt, scale=1.0 / D, bias=RMS_EPS),
         reads=[kst], writes=[kst])
    S.op("dve", lambda e: e.reciprocal(out=st[:, 4:5], in_=st[:, 3:4]), reads=[kst], writes=[kst])
    for hf, (pd, kpd) in enumerate(pd_pairs):
        tmp, ktmp = scr["tmp512"].next()
        S.op("dve", lambda e, pd=pd, hf=hf, tmp=tmp: e.scalar_tensor_tensor(
            out=tmp[:], in0=pd[:], scalar=st[:, 4:5], in1=grow[:, hf * 512:(hf + 1) * 512], op0=ALU.mult, op1=ALU.mult),
            reads=[kpd, kst, kgrow], writes=[ktmp])
        S.op("pool", lambda e, hf=hf, tmp=tmp: e.tensor_tensor(out=res[:, hf * 512:(hf + 1) * 512], in0=res[:, hf * 512:(hf + 1) * 512],
                                                               in1=tmp[:], op=ALU.add),
             reads=[ktmp, kres], writes=[kres])


def phase2_ffn(nc, S, st, io, n_main, shared):
    sb = lambda n, s, d: st.enter_context(nc.sbuf_tensor(n, s, d))
    ps = lambda n, s, d: st.enter_context(nc.psum_tensor(n, s, d))
    cc, kcc = shared["cc"], shared["kcc"]
    ident, kid = shared["ident"], shared["kid"]
    hscr, khscr = io["hscr"], io["khscr"]
    out = io["out"]

    wup = sb("wup", [128, 8, DFF * 2], BF16)
    wdn = sb("wdn", [128, NPAIR, D], BF16)
    kwup = [Key(f"wup{k}") for k in range(8)]
    kwdn = Key("wdn")
    grow = sb("grow2", [128, D], F32)
    kgrow = Key("grow2")
    fh = sb("fh", [128, NPAIR, 2, 2], F32)
    kfh = [Key(f"fh{i}") for i in range(NPAIR)]
    hts = [sb(f"ht{i}", [128, NBLK, D], F32) for i in range(2)]
    khts = [[Key(f"ht{i}_{b}") for b in range(NBLK)] for i in range(2)]
    hnTs = [sb(f"hnT{i}", [128, 8, W], BF16) for i in range(2)]
    khnT = [Key(f"hnT{i}") for i in range(2)]
    act = sb("actb", [128, NPAIR, W], BF16)
    kact = [Key(f"act{i}") for i in range(NPAIR)]
    scr = {
        "junk": Rot([sb(f"junk{i}", [128, D], BF16) for i in range(2)]),
        "stat": Rot([sb(f"stat{i}", [128, 8], F32) for i in range(4)]),
        "xs": Rot([sb(f"xs{i}", [128, D], BF16) for i in range(2)]),
        "tmp512": Rot([sb(f"tmp512_{i}", [128, 512], F32) for i in range(2)]),
    }
    fbuf = Rot([sb(f"fbuf{i}", [128, 2, W + 2], F32) for i in range(2)])
    cg = Rot([sb(f"cg{i}", [128, W], F32) for i in range(2)])
    cu = Rot([sb(f"cu{i}", [128, W], F32) for i in range(2)])
    t1 = Rot([sb(f"t1_{i}", [128, W], F32) for i in range(2)])
    t2 = Rot([sb(f"t2_{i}", [128, W], F32) for i in range(2)])
    sg = Rot([sb(f"sg{i}", [128, W], F32) for i in range(2)])
    WQ = DFF // 4
    wst = Rot([sb(f"wst{i}", [128, WQ], F32) for i in range(2)])
    ptr = Rot([ps("ptr2", [128, 8, 128], BF16)], excl=True)
    pf = Rot([ps(f"pf{i}", [128, 2, W], F32) for i in range(3)], excl=True)
    pd = [[ps(f"pd{b}{h}", [128, 512], F32) for h in range(2)] for b in range(NBLK)]
    kpd = [[PKey(f"pd{b}{h}") for h in range(2)] for b in range(NBLK)]
    dsem = {n: S.new_dma_sem("p2_" + n) for n in ("wst0", "wst1", "wdn", "grow", "h0", "h1", "hw", "out0", "out1")}

    S.op("sp", lambda e: e.dma_start(out=grow[:], in_=io["post_ffn_g"].partition_broadcast(128)), writes=[kgrow], dma_sem=dsem["grow"])
    for kc in range(8):
        for q in range(8):
            j = kc * 8 + q
            wt, kwt = wst.next()
            S.op("sp", lambda e, wt=wt, kc=kc, q=q: e.dma_start(out=wt[:], in_=io["w_up"][kc * 128:(kc + 1) * 128, q * WQ:(q + 1) * WQ]),
                 writes=[kwt], dma_sem=dsem[f"wst{j % 2}"])
            eng = ("act", "dve", "pool")[j % 3]
            if eng == "act":
                S.op("act", lambda e, wt=wt, kc=kc, q=q: e.activation(out=wup[:, kc, q * WQ:(q + 1) * WQ], in_=wt[:], func=AF.Copy,
                                                                       scale=cc[:, O_PFG + kc:O_PFG + kc + 1]),
                     reads=[kwt, kcc], writes=[kwup[kc]])
            else:
                S.op(eng, lambda e, wt=wt, kc=kc, q=q: e.tensor_scalar(out=wup[:, kc, q * WQ:(q + 1) * WQ], in0=wt[:],
                                                                        scalar1=cc[:, O_PFG + kc:O_PFG + kc + 1], scalar2=None, op0=ALU.mult),
                     reads=[kwt, kcc], writes=[kwup[kc]])
    S.op("pool", lambda e: e.dma_start(out=wdn[:], in_=io["w_down"].rearrange("(i p) d -> p i d", p=128)), writes=[kwdn], dma_sem=dsem["wdn"])

    hw = hts[1]
    S.op("sp", lambda e: e.dma_start(out=hw[:, 0, :], in_=hscr[W - 128:W, :]), reads=[khscr[0]], writes=[khts[1][0]], dma_sem=dsem["hw"])
    _rms_transpose(S, hw[:, 0, :], khts[1][0], hnTs[1], khnT[1], 0, scr, ident, kid, ptr,
                   extra_scale=cc[:, O_HM:O_HM + 1], kextra=kcc)
    pfh, kpfh = pf.next()
    pfh_v = pfh[:].rearrange("p a w -> p (a w)")
    for ch in range(2 * NPAIR):
        i, hf = ch % NPAIR, ch // NPAIR
        col = (i * 2 + hf) * 2
        for kc in range(8):
            S.op("pe", lambda e, ch=ch, kc=kc, col=col: e.matmul(pfh_v[:, col:col + 2], lhsT=wup[:, kc, ch * 128:(ch + 1) * 128],
                                                                  rhs=hnTs[1][:, kc, 126:128], start=(kc == 0), stop=(kc == 7)),
                 reads=[kwup[kc], khnT[1]], writes=[kpfh])
    S.op("act", lambda e: e.activation(out=fh[:].rearrange("p i a b -> p (i a b)"), in_=pfh_v[:, 0:NPAIR * 4], func=AF.Copy),
         reads=[kpfh], writes=kfh)

    def load(t):
        b = t % 2
        S.op("sp", lambda e: e.dma_start(out=hts[b][:], in_=hscr[W * (1 + t):W * (2 + t), :].rearrange("(b p) d -> p b d", p=128)),
             reads=[khscr[1 + t]], writes=khts[b], dma_sem=dsem[f"h{b}"])

    def prologue(t):
        b = t % 2
        for blk in range(NBLK):
            _rms_transpose(S, hts[b][:, blk, :], khts[b][blk], hnTs[b], khnT[b], blk * 128, scr, ident, kid, ptr)

    state = {}

    def up_mm(t, i):
        b = t % 2
        p, kp = pf.next()
        state[(t, i)] = (p, kp)
        for hf in range(2):
            ch = hf * NPAIR + i
            for kc in range(8):
                S.op("pe", lambda e, p=p, hf=hf, ch=ch, kc=kc: e.matmul(p[:, hf, :], lhsT=wup[:, kc, ch * 128:(ch + 1) * 128],
                                                                         rhs=hnTs[b][:, kc, :], start=(kc == 0), stop=(kc == 7)),
                     reads=[kwup[kc], khnT[b]], writes=[kp])

    def elem(t, i):
        p, kp = state.pop((t, i))
        fb, kfb = fbuf.next()
        S.op("pool", lambda e: e.tensor_copy(out=fb[:, :, 0:2], in_=fh[:, i, :, :]), reads=[kfh[i]], writes=[kfb])
        S.op("act", lambda e: e.activation(out=fb[:, :, 2:W + 2], in_=p[:], func=AF.Copy), reads=[kp], writes=[kfb])
        S.op("pool", lambda e: e.tensor_copy(out=fh[:, i, :, :], in_=fb[:, :, W:W + 2]), reads=[kfb], writes=[kfh[i]])
        outs = []
        for hf, rot in ((0, cg), (1, cu)):
            ch = hf * NPAIR + i
            c, kc_ = rot.next()
            wof = O_FCW + ch * 3
            S.op("act", lambda e, c=c, hf=hf, wof=wof, ch=ch: e.activation(out=c[:], in_=fb[:, hf, 2:W + 2], func=AF.Identity,
                                                                          scale=cc[:, wof + 2:wof + 3], bias=cc[:, O_FCB + ch:O_FCB + ch + 1]),
                 reads=[kfb, kcc], writes=[kc_])
            S.op("dve", lambda e, c=c, hf=hf, wof=wof: e.scalar_tensor_tensor(out=c[:], in0=fb[:, hf, 1:W + 1], scalar=cc[:, wof + 1:wof + 2],
                                                                             in1=c[:], op0=ALU.mult, op1=ALU.add),
                 reads=[kfb, kcc, kc_], writes=[kc_])
            S.op("dve", lambda e, c=c, hf=hf, wof=wof: e.scalar_tensor_tensor(out=c[:], in0=fb[:, hf, 0:W], scalar=cc[:, wof:wof + 1],
                                                                             in1=c[:], op0=ALU.mult, op1=ALU.add),
                 reads=[kfb, kcc, kc_], writes=[kc_])
            outs.append((c, kc_))
        (g_, kg), (u_, ku) = outs
        a1, ka1 = t1.next()
        a2, ka2 = t2.next()
        s_, ks = sg.next()
        S.op("act", lambda e: e.activation(out=a1[:], in_=g_[:], func=AF.Square), reads=[kg], writes=[ka1])
        S.op("pool", lambda e: e.tensor_scalar(out=a1[:], in0=a1[:], scalar1=0.044715, scalar2=1.0, op0=ALU.mult, op1=ALU.add),
             reads=[ka1], writes=[ka1])
        S.op("pool", lambda e: e.tensor_tensor(out=a2[:], in0=a1[:], in1=g_[:], op=ALU.mult), reads=[ka1, kg], writes=[ka2])
        S.op("act", lambda e: e.activation(out=s_[:], in_=a2[:], func=AF.Sigmoid, scale=1.5957691216), reads=[ka2], writes=[ks])
        S.op("dve", lambda e: e.tensor_tensor(out=a2[:], in0=g_[:], in1=u_[:], op=ALU.mult), reads=[kg, ku, ks], writes=[ka2])
        S.op("dve", lambda e: e.tensor_tensor(out=act[:, i, :], in0=a2[:], in1=s_[:], op=ALU.mult), reads=[ka2, ks], writes=[kact[i]])

    def down_mm(t, i):
        for blk in range(NBLK):
            for hf in range(2):
                S.op("pe", lambda e, blk=blk, hf=hf: e.matmul(pd[blk][hf][:], lhsT=act[:, i, blk * 128:(blk + 1) * 128],
                                                              rhs=wdn[:, i, hf * 512:(hf + 1) * 512], start=(i == 0), stop=(i == NPAIR - 1)),
                     reads=[kact[i], kwdn], writes=[kpd[blk][hf]])

    def epilogue(t):
        b = t % 2
        for blk in range(NBLK):
            _post_norm_residual(S, [(pd[blk][0], kpd[blk][0]), (pd[blk][1], kpd[blk][1])], hts[b][:, blk, :], khts[b][blk], grow, kgrow, scr)
        S.op("sp", lambda e: e.dma_start(out=out[W * t:W * (t + 1), :].rearrange("(b p) d -> p b d", p=128), in_=hts[b][:]),
             reads=khts[b], dma_sem=dsem[f"out{b}"])

    load(0)
    if n_main > 1:
        load(1)
    prologue(0)
    for t in range(n_main):
        for i in range(NPAIR + 2):
            if i < NPAIR:
                up_mm(t, i)
            if 0 <= i - 1 < NPAIR:
                elem(t, i - 1)
            if 0 <= i - 2 < NPAIR:
                down_mm(t, i - 2)
            if i == 14 and t + 1 < n_main:
                prologue(t + 1)
        epilogue(t)
        if t + 2 < n_main:
            load(t + 2)
    S.final_wait("sp", [k for ks_ in khts for k in ks_])


def build(n_pre, n_main, mode="full"):
    nc = bass.Bass("TRN2", target_bir_lowering=False)
    TT = (n_pre + 1 + n_main) * W
    io = {}
    di = lambda n, s: nc.dram_tensor(n, s, F32, kind="ExternalInput").ap()
    io["cc"] = di("cc", [128, NCC])
    io["km"] = di("km", [128, NKM])
    io["post_ffn_g"] = di("post_ffn_g", [D])
    io["w_up"] = di("w_up", [D, 2 * DFF])
    io["w_down"] = di("w_down", [DFF, D])
    if mode == "ffn":
        io["hscr"] = di("hscr", [(1 + n_main) * W, D])
    else:
        io["xin"] = di("xin", [TT, D])
        io["post_mix_g"] = di("post_mix_g", [D])
        io["w_in"] = di("w_in", [D, INC])
        io["w_out"] = di("w_out", [D, D])
        io["w2"] = di("w2", [64, 512])
        io["a2"] = di("a2", [64, 512])
        io["g2"] = di("g2", [160, 512])
        io["hscr"] = nc.dram_tensor("hscr", [(1 + n_main) * W, D], F32, kind="Internal").ap()
    io["khscr"] = [Key(f"hscr{i}") for i in range(1 + n_main)]
    io["out"] = nc.dram_tensor("out", [n_main * W, D], F32, kind="ExternalOutput").ap()

    with contextlib.ExitStack() as sem_stack, contextlib.ExitStack() as st0:
        cc = st0.enter_context(nc.sbuf_tensor("cc_sb", [128, NCC], F32))
        ident = st0.enter_context(nc.sbuf_tensor("ident", [128, 128], BF16))

        def shared_loads(S):
            kcc, kid = Key("cc"), Key("ident")
            d0 = S.new_dma_sem("cc")
            d1 = S.new_dma_sem("ident")
            S.op("sp", lambda e: e.dma_start(out=cc[:], in_=io["cc"][:, :]), writes=[kcc], dma_sem=d0)
            S.op("pool", lambda e: e.dma_start(out=ident[:], in_=io["km"][:, M_ID:M_ID + 128]), writes=[kid], dma_sem=d1)
            return {"cc": cc, "kcc": kcc, "ident": ident, "kid": kid}

        if mode != "ffn":
            S1 = Sched(nc, sem_stack, "a")
            shared = shared_loads(S1)
            with contextlib.ExitStack() as st1:
                phase1_mixer(nc, S1, st1, io, n_pre, n_main, shared)
                S1.final_wait("sp", io["khscr"])
                S1.emit()
            S2 = Sched(nc, sem_stack, "b")
            shared = {"cc": cc, "kcc": Key("cc2"), "ident": ident, "kid": Key("ident2")}
            io["khscr"] = [Key(f"hscr2_{i}") for i in range(1 + n_main)]
        else:
            S2 = Sched(nc, sem_stack, "b")
            shared = shared_loads(S2)
        with contextlib.ExitStack() as st2:
            phase2_ffn(nc, S2, st2, io, n_main, shared)
            S2.emit()
    return nc


def phase1_mixer(nc, S, st, io, n_pre, n_main, shared):
    sb = lambda n, s, d: st.enter_context(nc.sbuf_tensor(n, s, d))
    ps = lambda n, s, d: st.enter_context(nc.psum_tensor(n, s, d))
    cc, kcc = shared["cc"], shared["kcc"]
    ident, kid = shared["ident"], shared["kid"]
    xin, hscr, khscr = io["xin"], io["hscr"], io["khscr"]
    n_tiles = n_pre + 1 + n_main
    C05 = 0.6065306597126334

    win = sb("win", [128, 8, INC], BF16)
    kwin = [Key(f"win{k}") for k in range(8)]
    wout = sb("wout", [128, 8, D], BF16)
    kwout = Key("wout")
    w2b = sb("w2b", [128, 512], BF16)
    a2b = sb("a2b", [128, 512], BF16)
    g2b0 = sb("g2b0", [128, 512], BF16)
    g2b1 = sb("g2b1", [128, 512], BF16)
    wg1 = sb("wg1", [128, 8, 128], BF16)
    kwg1 = Key("wg1")
    ksmallw = Key("smallw")
    msu4 = sb("msu4", [128, 512], BF16)
    msl = sb("msl", [128, 128], BF16)
    bones = sb("bones", [128, 128], BF16)
    fst = sb("fst", [128, 64], BF16)
    rst = sb("rst", [128, W], F32)
    kconst = Key("p1const")
    grow = sb("grow1", [128, D], F32)
    kgrow = Key("grow1")
    dc = sb("dc", [128, 20], F32)
    kdc = Key("dc")
    dsem = {n: S.new_dma_sem("p1_" + n) for n in ("wst0", "wst1", "wout", "small", "const", "grow", "x0", "x1", "h0", "h1")}
    WQ = INC // 8
    wst = Rot([sb(f"wst1_{i}", [128, WQ], F32) for i in range(2)])

    S.op("sp", lambda e: e.dma_start(out=grow[:], in_=io["post_mix_g"].partition_broadcast(128)), writes=[kgrow], dma_sem=dsem["grow"])
    S.op("sp", lambda e: e.dma_start(out=rst[:], in_=io["km"][:, M_RST:M_RST + W]), writes=[kconst], dma_sem=dsem["const"])
    uniq = [0]

    def pool_dma(fn, key):
        uniq[0] += 1
        S.op("pool", fn, writes=[key], dma_sem=S.new_dma_sem(f"p1u{uniq[0]}"))

    kmsl, kbones, kfst = Key("msl"), Key("bones"), Key("fst")
    kmsu4 = [Key(f"msu4_{q}") for q in range(4)]
    kw2b, ka2b, kg2b0, kg2b1 = Key("w2b"), Key("a2b"), Key("g2b0"), Key("g2b1")
    for dst, c0, n_, k_ in ((msl, M_SL, 128, kmsl), (bones, M_BO, 128, kbones), (fst, M_F, 64, kfst)):
        pool_dma(lambda e, dst=dst, c0=c0, n_=n_: e.dma_start(out=dst[:], in_=io["km"][:, c0:c0 + n_]), k_)
    for q, c0 in enumerate((M_SU, M_IU, M_SU, M_IU)):
        pool_dma(lambda e, q=q, c0=c0: e.dma_start(out=msu4[:, q * 128:(q + 1) * 128], in_=io["km"][:, c0:c0 + 128]), kmsu4[q])
    S.op("pool", lambda e: e.memset(w2b[:], 0.0), writes=[kw2b])
    S.op("pool", lambda e: e.memset(a2b[:], 0.0), writes=[ka2b])
    S.op("pool", lambda e: e.memset(g2b1[:], 0.0), writes=[kg2b1])
    S.op("pool", lambda e: e.memset(wg1[:], 0.0), writes=[kwg1])
    pool_dma(lambda e: e.dma_start(out=w2b[0:64, :], in_=io["w2"][:, :]), kw2b)
    pool_dma(lambda e: e.dma_start(out=a2b[64:128, :], in_=io["a2"][:, :]), ka2b)
    pool_dma(lambda e: e.dma_start(out=g2b0[:], in_=io["g2"][0:128, :]), kg2b0)
    pool_dma(lambda e: e.dma_start(out=g2b1[0:32, :], in_=io["g2"][128:160, :]), kg2b1)
    pool_dma(lambda e: e.dma_start(out=wout[:], in_=io["w_out"].rearrange("(k p) d -> p k d", p=128)), kwout)
    S.op("dve", lambda e: e.tensor_scalar(out=dc[:, 0:15], in0=cc[:, O_MU:O_MU + 15], scalar1=-1.0, scalar2=1.0, op0=ALU.mult, op1=ALU.add),
         reads=[kcc], writes=[kdc])
    S.op("dve", lambda e: e.tensor_scalar(out=dc[:, 15:19], in0=cc[:, O_KA:O_KA + 4], scalar1=-1.0, scalar2=1.0, op0=ALU.mult, op1=ALU.add),
         reads=[kcc], writes=[kdc])
    for kc in range(8):
        for q in range(8):
            j = kc * 8 + q
            wt, kwt = wst.next()
            S.op("sp", lambda e, wt=wt, kc=kc, q=q: e.dma_start(out=wt[:], in_=io["w_in"][kc * 128:(kc + 1) * 128, q * WQ:(q + 1) * WQ]),
                 writes=[kwt], dma_sem=dsem[f"wst{j % 2}"])
            if j % 2 == 0:
                S.op("act", lambda e, wt=wt, kc=kc, q=q: e.activation(out=win[:, kc, q * WQ:(q + 1) * WQ], in_=wt[:], func=AF.Copy,
                                                                       scale=cc[:, O_PMG + kc:O_PMG + kc + 1]),
                     reads=[kwt, kcc], writes=[kwin[kc]])
            else:
                S.op("dve", lambda e, wt=wt, kc=kc, q=q: e.tensor_scalar(out=win[:, kc, q * WQ:(q + 1) * WQ], in0=wt[:],
                                                                          scalar1=cc[:, O_PMG + kc:O_PMG + kc + 1], scalar2=None, op0=ALU.mult),
                     reads=[kwt, kcc], writes=[kwin[kc]])

    S.op("dve", lambda e: e.tensor_copy(out=wg1[:, :, 0:32], in_=win[:, :, INC - 32:INC]), reads=kwin + [kwg1], writes=[kwg1])

    xts = [sb(f"xt{i}", [128, NBLK, D], F32) for i in range(2)]
    kxts = [[Key(f"xt{i}_{b}") for b in range(NBLK)] for i in range(2)]
    xnT = sb("xnT", [128, 8, W], BF16)
    kxnT = Key("xnT")
    scr = {
        "junk": Rot([sb(f"junk1_{i}", [128, D], BF16) for i in range(2)]),
        "stat": Rot([sb(f"stat1_{i}", [128, 8], F32) for i in range(4)]),
        "xs": Rot([sb(f"xs1_{i}", [128, D], BF16) for i in range(2)]),
        "tmp512": Rot([sb(f"tmp512a_{i}", [128, 512], F32) for i in range(2)]),
    }
    tf = Rot([sb(f"tf{i}", [128, W], F32) for i in range(10)])
    tb = Rot([sb(f"tb{i}", [128, W], BF16) for i in range(4)])
    qbuf = Rot([sb(f"qbuf{i}", [128, W + 1], F32) for i in range(2)])
    ubuf = Rot([sb(f"ubuf{i}", [128, W + 2], F32) for i in range(2)])
    qh = sb("qh", [128, 15], F32)
    kqh = [Key(f"qh{j}") for j in range(15)]
    uh = sb("uh", [128, 4, 2], F32)
    kuh = [Key(f"uh{c}") for c in range(4)]
    rkv = {n: Rot([sb(f"{n}c{i}", [128, W], F32) for i in range(2)]) for n in ("r", "k", "v")}
    wa = sb("wa", [128, W], F32); kwa = Key("wa")
    g0t = sb("g0t", [128, W], F32); kg0 = Key("g0t")
    g1t = sb("g1t", [128, W], F32); kg1 = Key("g1t")
    twad = sb("twad", [128, W], BF16); ktwad = Key("twad")
    sgd0 = sb("sgd0", [128, W], BF16); sgd1 = sb("sgd1", [128, W], BF16); ksgd = Key("sgd")
    sig = sb("sig", [128, 4, W], F32); ksig = [Key(f"sig{c}") for c in range(4)]
    a4 = sb("a4", [128, 4, W], F32); ka4 = [Key(f"a4{c}") for c in range(4)]
    bonus = sb("bonus", [128, 4, W], F32); kbonus = [Key(f"bonus{c}") for c in range(4)]
    ARbd = sb("ARbd", [128, 4, NCH, 2, 128], BF16); kAbd = [Key(f"Abd{c}") for c in range(4)]; kRbd = [Key(f"Rbd{c}") for c in range(4)]
    Bbd = sb("Bbd", [128, 4, NCH, 128], BF16); kBbd = [Key(f"Bbd{c}") for c in range(4)]
    Kbd = sb("Kbd", [128, 4, NCH, 128], BF16); kKbd = [Key(f"Kbd{c}") for c in range(4)]
    Vbd = sb("Vbd", [128, 4, NCH, 128], BF16); kVbd = [Key(f"Vbd{c}") for c in range(4)]
    PC = sb("PC", [128, 4, NCH], F32); kPC = [Key(f"PC{c}") for c in range(4)]
    M1 = sb("M1", [128, 4, 512], BF16); kM1 = Key("M1")
    NT0 = sb("NT0", [128, 4, 128], BF16); kNT0 = Key("NT0")
    M2 = sb("M2", [128, 4, 320], BF16); kM2 = Key("M2")
    NN = Rot([sb(f"NN{i}", [128, 4, 256], BF16) for i in range(2)])
    Tb = Rot([sb(f"Tb{i}", [128, 4, 128], BF16) for i in range(2)])
    Zb = sb("Zb", [128, 4, 64], BF16); kZb = Key("Zb")
    Ub = sb("Ub", [128, 4, 64], BF16); kUb = Key("Ub")
    S32 = sb("S32", [128, 4, 64], F32); kS32 = Key("S32")
    Sbf = sb("Sbf", [128, 4, 64], BF16); kSbf = Key("Sbf")
    tmpS = sb("tmpS", [128, 4, 64], F32); ktmpS = Key("tmpS")
    ysq = sb("ysq", [128, 4, 64], F32); kysq = Key("ysq")
    ycen = sb("ycen", [128, 4, 64], F32); kycen = Key("ycen")
    gst = Rot([sb(f"gst{i}", [128, 8, 4], F32) for i in range(2)])
    ynbd = sb("ynbd", [128, 4, 128], BF16); kynbd = Key("ynbd")
    ynf = sb("ynf", [128, 4, W], F32); kynf = [Key(f"ynf{c}") for c in range(4)]
    yT = sb("yT", [128, 8, W], BF16); kyT = [Key(f"yT{c}") for c in range(8)]

    ptr = Rot([ps("ptr1", [128, 8, 128], BF16)], excl=True)
    pp = [ps(f"pp{i}", [128, 2, W], F32) for i in range(2)]
    kpp = [PKey(f"pp{i}") for i in range(2)]
    psm = ps("psm", [128, 2, W], F32); kpsm = PKey("psm")
    PA = ps("PA", [128, 1024], F32); kPA = PKey("PA")
    PB = ps("PB", [128, 1024], F32); kPB0 = PKey("PB0"); kPB1 = PKey("PB1")

    for t_, k_ in ((qh, kqh), (uh, kuh)):
        S.op("pool", lambda e, t_=t_: e.memset(t_[:], 0.0), writes=k_)
    S.op("pool", lambda e: e.memset(S32[:], 0.0), writes=[kS32])
    S.op("pool", lambda e: e.memset(Sbf[:], 0.0), writes=[kSbf])
    S.op("pool", lambda e: e.memset(ARbd[:], 0.0), writes=kAbd + kRbd)
    S.op("pool", lambda e: e.memset(Bbd[:], 0.0), writes=kBbd)
    S.op("pool", lambda e: e.memset(Kbd[:], 0.0), writes=kKbd)
    S.op("pool", lambda e: e.memset(Vbd[:], 0.0), writes=kVbd)
    S.op("pool", lambda e: e.memset(ynbd[:], 0.0), writes=[kynbd])
    S.op("pool", lambda e: e.memset(g1t[:], 0.0), writes=[kg1])

    slot_i = [0]

    def proj(col0, ncols, wsrc=None, kw=None):
        j = slot_i[0] % 4
        slot_i[0] += 1
        t_, half, key = pp[j // 2], j % 2, kpp[j // 2]
        for kc in range(8):
            lhsT = win[:, kc, col0:col0 + ncols] if wsrc is None else wsrc[:, kc, :]
            S.op("pe", lambda e, kc=kc, lhsT=lhsT: e.matmul(t_[0:ncols, half, :], lhsT=lhsT, rhs=xnT[:, kc, :],
                                                            start=(kc == 0), stop=(kc == 7)),
                 reads=[(kwin if kw is None else kw)[kc], kxnT], writes=[key])
        return t_[0:ncols, half, :], key

    def shift_lerp(p_ap, kp, jj, dst_ap, kdst, np_=128):
        qb, kqb = qbuf.next()
        S.op("pool", lambda e: e.tensor_copy(out=qb[0:np_, 0:1], in_=qh[0:np_, jj:jj + 1]), reads=[kqh[jj]], writes=[kqb])
        S.op("act", lambda e: e.activation(out=qb[0:np_, 1:W + 1], in_=p_ap, func=AF.Copy), reads=[kp], writes=[kqb])
        S.op("pool", lambda e: e.tensor_copy(out=qh[0:np_, jj:jj + 1], in_=qb[0:np_, W:W + 1]), reads=[kqb], writes=[kqh[jj]])
        tmp, ktmp = tf.next()
        S.op("pool", lambda e: e.tensor_scalar(out=tmp[0:np_, :], in0=qb[0:np_, 1:W + 1], scalar1=dc[0:np_, jj:jj + 1], scalar2=None, op0=ALU.mult),
             reads=[kqb, kdc], writes=[ktmp])
        S.op("dve", lambda e: e.scalar_tensor_tensor(out=dst_ap, in0=qb[0:np_, 0:W], scalar=cc[0:np_, O_MU + jj:O_MU + jj + 1], in1=tmp[0:np_, :],
                                                     op0=ALU.mult, op1=ALU.add),
             reads=[kqb, kcc, ktmp], writes=[kdst])

    def bd_view(t4, c, h):
        return t4[h * 64:(h + 1) * 64, c, :, h * 64:(h + 1) * 64]

    def v3(t2, h):
        return t2[h * 64:(h + 1) * 64, :].rearrange("p (n t) -> p n t", n=NCH)

    def load_x(t):
        b = t % 2
        S.op("sp", lambda e: e.dma_start(out=xts[b][:], in_=xin[W * t:W * (t + 1), :].rearrange("(b p) d -> p b d", p=128)),
             writes=kxts[b], dma_sem=dsem[f"x{b}"])

    load_x(0)
    if n_tiles > 1:
        load_x(1)

    def do_tile(t):
        full = t >= n_pre
        if DEBUG_STOP < 1:
            return
        b = t % 2
        xt = xts[b]
        for blk in range(NBLK):
            _rms_transpose(S, xt[:, blk, :], kxts[b][blk], xnT, kxnT, blk * 128, scr, ident, kid, ptr)

        if DEBUG_STOP < 2:
            return
        if full:
            def b1(c):
                pB, kpB = proj(c * 128, 128)
                pH, kpH = proj(1024 + c * 128, 128)
                hAs, khAs = tf.next()
                S.op("act", lambda e, hAs=hAs, pH=pH: e.activation(out=hAs[:], in_=pH, func=AF.Copy), reads=[kpH], writes=[khAs])
                ub, kub = ubuf.next()
                S.op("pool", lambda e, ub=ub, c=c: e.tensor_copy(out=ub[:, 0:2], in_=uh[:, c, :]), reads=[kuh[c]], writes=[kub])
                S.op("dve", lambda e, ub=ub, pB=pB, hAs=hAs: e.tensor_tensor(out=ub[:, 2:W + 2], in0=pB, in1=hAs[:], op=ALU.mult),
                     reads=[kpB, khAs], writes=[kub])
                S.op("pool", lambda e, ub=ub, c=c: e.tensor_copy(out=uh[:, c, :], in_=ub[:, W:W + 2]), reads=[kub], writes=[kuh[c]])
                ta, kta = tf.next()
                wof = O_CAW + c * 3
                S.op("act", lambda e, ta=ta, ub=ub, wof=wof: e.activation(out=ta[:], in_=ub[:, 2:W + 2], func=AF.Copy, scale=cc[:, wof + 2:wof + 3]),
                     reads=[kub, kcc], writes=[kta])
                S.op("dve", lambda e, ta=ta, ub=ub, wof=wof: e.scalar_tensor_tensor(out=ta[:], in0=ub[:, 1:W + 1], scalar=cc[:, wof + 1:wof + 2], in1=ta[:],
                                                                                  op0=ALU.mult, op1=ALU.add), reads=[kub, kcc, kta], writes=[kta])
                S.op("dve", lambda e, ta=ta, ub=ub, wof=wof: e.scalar_tensor_tensor(out=ta[:], in0=ub[:, 0:W], scalar=cc[:, wof:wof + 1], in1=ta[:],
                                                                                  op0=ALU.mult, op1=ALU.add), reads=[kub, kcc, kta], writes=[kta])
                pC, kpC = proj(512 + c * 128, 128)
                S.op("dve", lambda e, ta=ta, pC=pC, c=c: e.tensor_tensor(out=yT[:, c, :], in0=pC, in1=ta[:], op=ALU.mult),
                     reads=[kpC, kta], writes=[kyT[c]])
            for c_ in range(4):
                b1(c_)

        if DEBUG_STOP < 3:
            return
        QC = 1536
        p_, kp_ = proj(QC + 12 * 128, 128)
        shift_lerp(p_, kp_, 12, wa[:], kwa)
        S.op("act", lambda e: e.activation(out=twad[0:64, :], in_=wa[0:64, :], func=AF.Tanh), reads=[kwa], writes=[ktwad])
        S.op("dve", lambda e: e.tensor_copy(out=twad[64:128, :], in_=wa[64:128, :]), reads=[kwa], writes=[ktwad])
        if full:
            p_, kp_ = proj(QC + 13 * 128, 128)
            shift_lerp(p_, kp_, 13, g0t[:], kg0)
            p_, kp_ = proj(None, 128, wsrc=wg1, kw=[kwg1] * 8)
            shift_lerp(p_[0:32, :], kp_, 14, g1t[0:32, :], kg1, np_=32)
            S.op("act", lambda e: e.activation(out=sgd0[:], in_=g0t[:], func=AF.Sigmoid), reads=[kg0], writes=[ksgd])
            S.op("act", lambda e: e.activation(out=sgd1[:], in_=g1t[:], func=AF.Sigmoid), reads=[kg1], writes=[ksgd])
        for c in range(4):
            S.op("pe", lambda e, c=c: e.matmul(psm[:, 0, :], lhsT=w2b[:, c * 128:(c + 1) * 128], rhs=twad[:], start=True, stop=True),
                 reads=[kw2b, ktwad], writes=[kpsm])
            S.op("pe", lambda e, c=c: e.matmul(psm[:, 1, :], lhsT=a2b[:, c * 128:(c + 1) * 128], rhs=twad[:], start=True, stop=True),
                 reads=[ka2b, ktwad], writes=[kpsm])
            S.op("act", lambda e, c=c: e.activation(out=sig[:, c, :], in_=psm[:, 0, :], func=AF.Sigmoid, bias=cc[:, O_W0 + c:O_W0 + c + 1]),
                 reads=[kpsm, kcc], writes=[ksig[c]])
            S.op("act", lambda e, c=c: e.activation(out=a4[:, c, :], in_=psm[:, 1, :], func=AF.Sigmoid, bias=cc[:, O_A0 + c:O_A0 + c + 1]),
                 reads=[kpsm, kcc], writes=[ka4[c]])

        if DEBUG_STOP < 4:
            return
        def b4(c):
            kt_, kkt = rkv["k"].next()
            vt_, kvt = rkv["v"].next()
            p_, kp_ = proj(QC + (4 + c) * 128, 128)
            shift_lerp(p_, kp_, 4 + c, kt_[:], kkt)
            p_, kp_ = proj(QC + (8 + c) * 128, 128)
            shift_lerp(p_, kp_, 8 + c, vt_[:], kvt)
            if full:
                rt_, krt = rkv["r"].next()
                p_, kp_ = proj(QC + c * 128, 128)
                shift_lerp(p_, kp_, c, rt_[:], krt)
            kkr, kkkr = tf.next()
            S.op("pool", lambda e, kkr=kkr, kt_=kt_, c=c: e.tensor_scalar(out=kkr[:], in0=kt_[:], scalar1=cc[:, O_KK + c:O_KK + c + 1], scalar2=None, op0=ALU.mult),
                 reads=[kkt, kcc], writes=[kkkr])
            sq, ksq = tb.next()
            S.op("act", lambda e, sq=sq, kkr=kkr: e.activation(out=sq[:], in_=kkr[:], func=AF.Square), reads=[kkkr], writes=[ksq])
            S.op("pe", lambda e, sq=sq: e.matmul(psm[:, 0, :], lhsT=bones[:], rhs=sq[:], start=True, stop=True), reads=[kbones, ksq], writes=[kpsm])
            nrm, knrm = tf.next()
            S.op("act", lambda e, nrm=nrm: e.activation(out=nrm[:], in_=psm[:, 0, :], func=AF.Sqrt, bias=1e-24), reads=[kpsm], writes=[knrm])
            S.op("dve", lambda e, nrm=nrm: e.reciprocal(out=nrm[:], in_=nrm[:]), reads=[knrm], writes=[knrm])
            kk, kkk = tf.next()
            S.op("dve", lambda e, kk=kk, kkr=kkr, nrm=nrm: e.tensor_tensor(out=kk[:], in0=kkr[:], in1=nrm[:], op=ALU.mult), reads=[kkkr, knrm], writes=[kkk])
            cs, kcs = tf.next()
            S.op("dve", lambda e, cs=cs, c=c: e.tensor_tensor_scan(out=cs[:], data0=rst[:], data1=sig[:, c, :], initial=0.0, op0=ALU.mult, op1=ALU.add),
                 reads=[kconst, ksig[c]], writes=[kcs])
            E1, kE1 = tf.next()
            E2, kE2 = tf.next()
            E3, kE3 = tf.next()
            dd, kdd = tf.next()
            S.op("act", lambda e, E1=E1, cs=cs: e.activation(out=E1[:], in_=cs[:], func=AF.Exp, scale=-C05), reads=[kcs], writes=[kE1])
            S.op("act", lambda e, E2=E2, cs=cs: e.activation(out=E2[:], in_=cs[:], func=AF.Exp, scale=C05), reads=[kcs], writes=[kE2])
            S.op("pool", lambda e, dd=dd, cs=cs, c=c: e.tensor_tensor(out=dd[:], in0=cs[:], in1=sig[:, c, :], op=ALU.subtract), reads=[kcs, ksig[c]], writes=[kdd])
            S.op("act", lambda e, E3=E3, dd=dd: e.activation(out=E3[:], in_=dd[:], func=AF.Exp, scale=-C05), reads=[kdd], writes=[kE3])
            S.op("pool", lambda e, E1=E1, c=c: e.tensor_copy(out=PC[:, c, :], in_=E1[:].rearrange("p (n t) -> p n t", n=NCH)[:, :, CH - 1]),
                 reads=[kE1], writes=[kPC[c]])
            mm, kmm = tf.next()
            S.op("pool", lambda e, mm=mm, c=c: e.tensor_scalar(out=mm[:], in0=a4[:, c, :], scalar1=cc[:, O_KA + c:O_KA + c + 1], scalar2=dc[:, 15 + c:16 + c],
                                                               op0=ALU.mult, op1=ALU.add), reads=[ka4[c], kcc, kdc], writes=[kmm])
            kp, kkp = tf.next()
            S.op("dve", lambda e, kp=kp, kt_=kt_, mm=mm: e.tensor_tensor(out=kp[:], in0=kt_[:], in1=mm[:], op=ALU.mult), reads=[kkt, kmm], writes=[kkp])
            akk, kakk = tf.next()
            S.op("pool", lambda e, akk=akk, kk=kk, c=c: e.tensor_tensor(out=akk[:], in0=a4[:, c, :], in1=kk[:], op=ALU.mult), reads=[ka4[c], kkk], writes=[kakk])
            for h in range(2):
                S.op("dve", lambda e, h=h, kk=kk, E3=E3, c=c: e.scalar_tensor_tensor(out=ARbd[h * 64:(h + 1) * 64, c, :, 0, h * 64:(h + 1) * 64], in0=v3(kk, h), scalar=-1.0,
                                                                                   in1=v3(E3, h), op0=ALU.mult, op1=ALU.mult),
                     reads=[kkk, kE3], writes=[kAbd[c]])
                S.op("dve", lambda e, h=h, akk=akk, E2=E2, c=c: e.tensor_tensor(out=bd_view(Bbd, c, h), in0=v3(akk, h), in1=v3(E2, h), op=ALU.mult),
                     reads=[kakk, kE2], writes=[kBbd[c]])
                S.op("pool", lambda e, h=h, kp=kp, E2=E2, c=c: e.tensor_tensor(out=bd_view(Kbd, c, h), in0=v3(kp, h), in1=v3(E2, h), op=ALU.mult),
                     reads=[kkp, kE2], writes=[kKbd[c]])
                S.op("act", lambda e, h=h, vt_=vt_, c=c: e.activation(out=bd_view(Vbd, c, h), in_=v3(vt_, h), func=AF.Copy), reads=[kvt], writes=[kVbd[c]])
                if full:
                    S.op("pool", lambda e, h=h, rt_=rt_, E1=E1, c=c: e.tensor_tensor(out=ARbd[h * 64:(h + 1) * 64, c, :, 1, h * 64:(h + 1) * 64], in0=v3(rt_, h), in1=v3(E1, h),
                                                                                   op=ALU.mult), reads=[krt, kE1], writes=[kRbd[c]])
            if full:
                rk, krk = tf.next()
                S.op("pool", lambda e, rk=rk, rt_=rt_, kp=kp: e.tensor_tensor(out=rk[:], in0=rt_[:], in1=kp[:], op=ALU.mult), reads=[krt, kkp], writes=[krk])
                rkb, krkb = tb.next()
                S.op("dve", lambda e, rkb=rkb, rk=rk, c=c: e.tensor_scalar(out=rkb[:], in0=rk[:], scalar1=cc[:, O_RK + c:O_RK + c + 1], scalar2=None, op0=ALU.mult),
                     reads=[krk, kcc], writes=[krkb])
                S.op("pe", lambda e, rkb=rkb: e.matmul(psm[:, 1, :], lhsT=bones[:], rhs=rkb[:], start=True, stop=True), reads=[kbones, krkb], writes=[kpsm])
                S.op("dve", lambda e, vt_=vt_, c=c: e.tensor_tensor(out=bonus[:, c, :], in0=psm[:, 1, :], in1=vt_[:], op=ALU.mult), reads=[kpsm, kvt], writes=[kbonus[c]])

        for c_ in range(4):
            b4(c_)
        if DEBUG_STOP < 5:
            return

        PA3 = PA[:].rearrange("p (a w) -> p a w", a=2)
        PB3 = PB[:].rearrange("p (a w) -> p a w", a=2)
        PAd = PA[:].rearrange("p (a w) -> p a w", a=4)
        PBt = PB[:, 0:512].rearrange("p (a w) -> p a w", a=4)
        PQ0 = PB[:, 512:768].rearrange("p (a w) -> p a w", a=4)
        PQ1 = PB[:, 768:1024].rearrange("p (a w) -> p a w", a=4)
        def chunk(n):
            for pi in range(2):
                for i in range(2):
                    c = pi * 2 + i
                    rhsAR = ARbd[:, c, n, :, :].rearrange("p a w -> p (a w)")
                    S.op("pe", lambda e, i=i, c=c, rhsAR=rhsAR: e.matmul(PA3[:, i, 0:256], lhsT=Bbd[:, c, n, :], rhs=rhsAR, start=True, stop=True),
                         reads=[kBbd[c], kAbd[c], kRbd[c]], writes=[kPA])
                    S.op("pe", lambda e, i=i, c=c, rhsAR=rhsAR: e.matmul(PA3[:, i, 256:512], lhsT=Kbd[:, c, n, :], rhs=rhsAR, start=True, stop=True),
                         reads=[kKbd[c], kAbd[c], kRbd[c]], writes=[kPA])
                    S.op("pe", lambda e, i=i, c=c: e.matmul(PB3[:, i, 0:128], lhsT=ARbd[:, c, n, 0, :], rhs=Bbd[:, c, n, :], start=True, stop=True),
                         reads=[kAbd[c], kBbd[c]], writes=[kPB0, kPB1])
                    S.op("pe", lambda e, i=i, c=c: e.matmul(PB3[:, i, 128:256], lhsT=Bbd[:, c, n, :], rhs=ident[:], start=True, stop=True),
                         reads=[kBbd[c], kid], writes=[kPB0, kPB1])
                    S.op("pe", lambda e, i=i, c=c: e.matmul(PB3[:, i, 256:384], lhsT=Kbd[:, c, n, :], rhs=ident[:], start=True, stop=True),
                         reads=[kKbd[c], kid], writes=[kPB0, kPB1])
                    S.op("pe", lambda e, i=i, c=c: e.matmul(PB3[:, i, 384:448], lhsT=Vbd[:, c, n, :], rhs=fst[:], start=True, stop=True),
                         reads=[kVbd[c], kfst], writes=[kPB0, kPB1])
                if DEBUG_STOP == 5 and DEBUG_SUB < 1:
                    continue
                S.op("dve", lambda e, pi=pi: e.tensor_tensor(out=M1[:, 2 * pi:2 * pi + 2, :], in0=PA3, in1=msu4[:].unsqueeze(1).to_broadcast([128, 2, 512]), op=ALU.mult),
                     reads=[kPA] + kmsu4, writes=[kM1])
                if DEBUG_STOP == 5 and DEBUG_SUB < 2:
                    continue
                S.op("dve", lambda e, pi=pi: e.tensor_tensor(out=NT0[:, 2 * pi:2 * pi + 2, :], in0=PB3[:, :, 0:128], in1=msl[:].unsqueeze(1).to_broadcast([128, 2, 128]), op=ALU.mult),
                     reads=[kPB0, kPB1, kmsl], writes=[kNT0])
                if DEBUG_STOP == 5 and DEBUG_SUB < 3:
                    continue
                S.op("act", lambda e, pi=pi: e.activation(out=M2[:, 2 * pi:2 * pi + 2, :], in_=PB3[:, :, 128:448], func=AF.Copy), reads=[kPB0, kPB1], writes=[kM2])
            if DEBUG_STOP < 6:
                return
            Tcur, kTcur = Tb.next()
            S.op("pool", lambda e, Tcur=Tcur: e.tensor_tensor(out=Tcur[:], in0=M1[:, :, 0:128], in1=ident[:].unsqueeze(1).to_broadcast([128, 4, 128]), op=ALU.add),
                 reads=[kM1, kid], writes=[kTcur])
            Nprev = lambda c: M1[:, c, 0:128]
            NTprev = lambda c: NT0[:, c, :]
            kprev = [kM1, kNT0]
            for j in range(1, 6):
                NNj, kNNj = NN.next()
                for c in range(4):
                    if j < 5:
                        S.op("pe", lambda e, c=c, Nprev=Nprev, NTprev=NTprev: e.matmul(PAd[:, c, 0:128], lhsT=NTprev(c), rhs=Nprev(c), start=True, stop=True),
                             reads=kprev, writes=[kPA])
                    S.op("pe", lambda e, c=c, Nprev=Nprev, NTprev=NTprev: e.matmul(PAd[:, c, 128:256], lhsT=Nprev(c), rhs=NTprev(c), start=True, stop=True),
                         reads=kprev, writes=[kPA])
                if j < 5:
                    S.op("act", lambda e, NNj=NNj: e.activation(out=NNj[:], in_=PAd, func=AF.Copy), reads=[kPA], writes=[kNNj])
                else:
                    S.op("act", lambda e, NNj=NNj: e.activation(out=NNj[:, :, 128:256], in_=PAd[:, :, 128:256], func=AF.Copy), reads=[kPA], writes=[kNNj])
                for c in range(4):
                    S.op("pe", lambda e, c=c, NNj=NNj, Tcur=Tcur: e.matmul(PBt[:, c, :], lhsT=NNj[:, c, 128:256], rhs=Tcur[:, c, :], start=True, stop=True),
                         reads=[kNNj, kTcur], writes=[kPB0])
                Tnew, kTnew = Tb.next()
                S.op("dve", lambda e, Tnew=Tnew, Tcur=Tcur: e.tensor_tensor(out=Tnew[:], in0=PBt, in1=Tcur[:], op=ALU.add), reads=[kPB0, kTcur], writes=[kTnew])
                Tcur, kTcur = Tnew, kTnew
                Nprev = (lambda NNj: (lambda c: NNj[:, c, 0:128]))(NNj)
                NTprev = (lambda NNj: (lambda c: NNj[:, c, 128:256]))(NNj)
                kprev = [kNNj]
            if DEBUG_STOP < 7:
                return
            for c in range(4):
                S.op("pe", lambda e, c=c: e.matmul(PQ0[:, c, :], lhsT=ARbd[:, c, n, 0, :], rhs=Sbf[:, c, :], start=True, stop=False),
                     reads=[kAbd[c], kSbf], writes=[kPB1])
                S.op("pe", lambda e, c=c: e.matmul(PQ0[:, c, :], lhsT=M1[:, c, 256:384], rhs=M2[:, c, 256:320], start=False, stop=True),
                     reads=[kM1, kM2], writes=[kPB1])
            S.op("act", lambda e: e.activation(out=Zb[:], in_=PQ0, func=AF.Copy), reads=[kPB1], writes=[kZb])
            for c in range(4):
                S.op("pe", lambda e, c=c, Tcur=Tcur: e.matmul(PQ1[:, c, :], lhsT=Tcur[:, c, :], rhs=Zb[:, c, :], start=True, stop=True),
                     reads=[kTcur, kZb], writes=[kPB1])
            S.op("act", lambda e: e.activation(out=Ub[:], in_=PQ1, func=AF.Copy), reads=[kPB1], writes=[kUb])
            for c in range(4):
                S.op("pe", lambda e, c=c: e.matmul(PQ0[:, c, :], lhsT=M2[:, c, 0:128], rhs=Ub[:, c, :], start=True, stop=False),
                     reads=[kM2, kUb], writes=[kPB1])
                S.op("pe", lambda e, c=c: e.matmul(PQ0[:, c, :], lhsT=M2[:, c, 128:256], rhs=M2[:, c, 256:320], start=False, stop=True),
                     reads=[kM2], writes=[kPB1])
            if full:
                for c in range(4):
                    S.op("pe", lambda e, c=c: e.matmul(PQ1[:, c, :], lhsT=ARbd[:, c, n, 1, :], rhs=Sbf[:, c, :], start=True, stop=False),
                         reads=[kRbd[c], kSbf], writes=[kPB1])
                    S.op("pe", lambda e, c=c: e.matmul(PQ1[:, c, :], lhsT=M1[:, c, 128:256], rhs=Ub[:, c, :], start=False, stop=False),
                         reads=[kM1, kUb], writes=[kPB1])
                    S.op("pe", lambda e, c=c: e.matmul(PQ1[:, c, :], lhsT=M1[:, c, 384:512], rhs=M2[:, c, 256:320], start=False, stop=True),
                         reads=[kM1, kM2], writes=[kPB1])
            pcb = PC[:, :, n:n + 1].to_broadcast([128, 4, 64])
            S.op("dve", lambda e: e.tensor_tensor(out=tmpS[:], in0=PQ0, in1=S32[:], op=ALU.add), reads=[kPB1, kS32], writes=[ktmpS])
            S.op("dve", lambda e, pcb=pcb: e.tensor_tensor(out=Sbf[:], in0=tmpS[:], in1=pcb, op=ALU.mult), reads=[ktmpS] + kPC, writes=[kSbf])
            S.op("pool", lambda e, pcb=pcb: e.tensor_tensor(out=S32[:], in0=tmpS[:], in1=pcb, op=ALU.mult), reads=[ktmpS] + kPC, writes=[kS32])
            if full:
                g_, kg_ = gst.next()
                S.op("dve", lambda e, g_=g_: e.tensor_reduce(out=g_[:, 0, :], in_=PQ1, axis=AX.X, op=ALU.add), reads=[kPB1], writes=[kg_])
                S.op("act", lambda e: e.activation(out=ysq[:], in_=PQ1, func=AF.Square), reads=[kPB1], writes=[kysq])
                S.op("dve", lambda e, g_=g_: e.tensor_reduce(out=g_[:, 1, :], in_=ysq[:], axis=AX.X, op=ALU.add), reads=[kysq], writes=[kg_])
                S.op("dve", lambda e, g_=g_: e.tensor_scalar(out=g_[:, 2, :], in0=g_[:, 0, :], scalar1=1.0 / 64, scalar2=None, op0=ALU.mult), reads=[kg_], writes=[kg_])
                S.op("dve", lambda e, g_=g_: e.tensor_tensor(out=g_[:, 3, :], in0=g_[:, 2, :], in1=g_[:, 2, :], op=ALU.mult), reads=[kg_], writes=[kg_])
                S.op("dve", lambda e, g_=g_: e.scalar_tensor_tensor(out=g_[:, 4, :], in0=g_[:, 1, :], scalar=1.0 / 64, in1=g_[:, 3, :], op0=ALU.mult, op1=ALU.subtract),
                     reads=[kg_], writes=[kg_])
                S.op("act", lambda e, g_=g_: e.activation(out=g_[:, 5, :], in_=g_[:, 4, :], func=AF.Sqrt, bias=GN_EPS), reads=[kg_], writes=[kg_])
                S.op("dve", lambda e, g_=g_: e.reciprocal(out=g_[:, 6, :], in_=g_[:, 5, :]), reads=[kg_], writes=[kg_])
                S.op("dve", lambda e, g_=g_: e.tensor_tensor(out=ycen[:], in0=PQ1, in1=g_[:, 2, :].unsqueeze(2).to_broadcast([128, 4, 64]), op=ALU.subtract),
                     reads=[kPB1, kg_], writes=[kycen])
                for h in range(2):
                    hs = slice(h * 64, (h + 1) * 64)
                    S.op("dve", lambda e, hs=hs, g_=g_: e.tensor_tensor(out=ynbd[hs, :, hs], in0=ycen[hs, :, :], in1=g_[hs, 6, :].unsqueeze(2).to_broadcast([64, 4, 64]),
                                                                       op=ALU.mult), reads=[kycen, kg_], writes=[kynbd])
                for c in range(4):
                    S.op("pe", lambda e, c=c: e.matmul(PQ0[:, c, :], lhsT=ynbd[:, c, :], rhs=fst[:], start=True, stop=True), reads=[kynbd, kfst], writes=[kPB1])
                S.op("act", lambda e: e.activation(out=ynf[:, :, n * CH:(n + 1) * CH], in_=PQ0, func=AF.Copy), reads=[kPB1], writes=kynf)

        for n_ in range(NCH):
            chunk(n_)
        if DEBUG_STOP < 8:
            return

        if not full:
            if t + 2 < n_tiles:
                load_x(t + 2)
            return
        for c in range(4):
            S.op("pe", lambda e, c=c: e.matmul(psm[:, 0, :], lhsT=g2b0[:, c * 128:(c + 1) * 128], rhs=sgd0[:], start=True, stop=False), reads=[kg2b0, ksgd], writes=[kpsm])
            S.op("pe", lambda e, c=c: e.matmul(psm[:, 0, :], lhsT=g2b1[:, c * 128:(c + 1) * 128], rhs=sgd1[:], start=False, stop=True), reads=[kg2b1, ksgd], writes=[kpsm])
            y1, ky1 = tf.next()
            S.op("dve", lambda e, c=c, y1=y1: e.scalar_tensor_tensor(out=y1[:], in0=ynf[:, c, :], scalar=cc[:, O_LW + c:O_LW + c + 1], in1=bonus[:, c, :], op0=ALU.mult, op1=ALU.add),
                 reads=[kynf[c], kcc, kbonus[c]], writes=[ky1])
            S.op("dve", lambda e, c=c, y1=y1: e.scalar_tensor_tensor(out=yT[:, 4 + c, :], in0=y1[:], scalar=cc[:, O_LB + c:O_LB + c + 1], in1=psm[:, 0, :], op0=ALU.add, op1=ALU.mult),
                 reads=[ky1, kcc, kpsm], writes=[kyT[4 + c]])
        if DEBUG_STOP < 9:
            return
        for blk in range(NBLK):
            for hf in range(2):
                pflat = pp[hf][:].rearrange("p a w -> p (a w)")
                for e_ in range(8):
                    S.op("pe", lambda e, e_=e_, hf=hf, pflat=pflat, blk=blk: e.matmul(pflat, lhsT=yT[:, e_, blk * 128:(blk + 1) * 128], rhs=wout[:, e_, hf * 512:(hf + 1) * 512],
                                                                                      start=(e_ == 0), stop=(e_ == 7)), reads=[kyT[e_], kwout], writes=[kpp[hf]])
            class _V:
                def __init__(self, ap): self.ap = ap
                def __getitem__(self, k): return self.ap
            _post_norm_residual(S, [(_V(pp[0][:].rearrange("p a w -> p (a w)")), kpp[0]), (_V(pp[1][:].rearrange("p a w -> p (a w)")), kpp[1])],
                                xt[:, blk, :], kxts[b][blk], grow, kgrow, scr)
        ht_i = t - n_pre
        S.op("sp", lambda e, xt=xt, ht_i=ht_i: e.dma_start(out=hscr[W * ht_i:W * (ht_i + 1), :].rearrange("(b p) d -> p b d", p=128), in_=xt[:]),
             reads=kxts[b], writes=[khscr[ht_i]], dma_sem=dsem[f"h{b}"])
        if t + 2 < n_tiles:
            load_x(t + 2)

    for t_i in range(n_tiles):
        do_tile(t_i)


def _host_consts():
    km = np.zeros((128, NKM), np.float32)
    km[:, M_ID:M_ID + 128] = np.eye(128, dtype=np.float32)
    idx = np.arange(128)
    same = (idx[:, None] // 64) == (idx[None, :] // 64)
    s, t = idx[:, None] % 64, idx[None, :] % 64
    km[:, M_SU:M_SU + 128] = (same & (s < t)).astype(np.float32)
    km[:, M_IU:M_IU + 128] = (same & (s <= t)).astype(np.float32)
    km[:, M_SL:M_SL + 128] = (same & (s > t)).astype(np.float32)
    km[:, M_BO:M_BO + 128] = same.astype(np.float32)
    km[:, M_F:M_F + 64] = (idx[:, None] % 64 == np.arange(64)[None, :]).astype(np.float32)
    rst = np.ones((128, 256), np.float32)
    rst[:, ::64] = 0.0
    km[:, M_RST:M_RST + 256] = rst
    return km


def _pack_cc(inp, hmask):
    cc = np.zeros((128, NCC), np.float32)
    col = lambda v, n: np.ascontiguousarray(np.asarray(v, np.float32).reshape(n, 128).T)
    cc[:, O_PMG:O_PMG + 8] = col(inp["pre_mix_g"][0], 8)
    cc[:, O_PFG:O_PFG + 8] = col(inp["pre_ffn_g"][0], 8)
    caw = np.asarray(inp["conv_a_w"][0], np.float32)
    cc[:, O_CAW:O_CAW + 12] = caw.T.reshape(4, 128, 3).transpose(1, 0, 2).reshape(128, 12)
    mu = np.zeros(1920, np.float32)
    mu[:1824] = np.asarray(inp["shift_mu"][0], np.float32)
    cc[:, O_MU:O_MU + 15] = col(mu, 15)
    for off, name in ((O_W0, "w0"), (O_A0, "a0"), (O_KK, "k_k"), (O_KA, "k_a"), (O_LW, "lnx_w"), (O_LB, "lnx_b")):
        cc[:, off:off + 4] = col(inp[name][0], 4)
    cc[:, O_RK:O_RK + 4] = col(np.asarray(inp["r_k"][0], np.float32).reshape(512), 4)
    fcw = np.asarray(inp["ffn_conv_w"][0], np.float32)
    cc[:, O_FCW:O_FCW + 132] = fcw.T.reshape(44, 128, 3).transpose(1, 0, 2).reshape(128, 132)
    cc[:, O_FCB:O_FCB + 44] = col(inp["ffn_conv_b"][0], 44)
    cc[:, O_HM] = hmask
    return cc


_NC_CACHE = {}


def kernel(**inputs):
    n_pre, n_main = 15, 16
    x = np.asarray(inputs["x"], np.float32)
    B, T, _ = x.shape
    half = T // 2
    if "full" not in _NC_CACHE:
        _NC_CACHE["full"] = build(n_pre, n_main, "full")
    nc = _NC_CACHE["full"]
    km = _host_consts()
    f = lambda n: np.ascontiguousarray(np.asarray(inputs[n], np.float32)[0])
    in_maps = []
    for c in range(8):
        b, h = c // 2, c % 2
        xin = np.zeros((T, D), np.float32)
        if h == 0:
            xin[half:] = x[b, :half]
        else:
            xin[:] = x[b]
        in_maps.append({
            "xin": xin, "cc": _pack_cc(inputs, float(h)), "km": km,
            "post_mix_g": f("post_mix_g"), "post_ffn_g": f("post_ffn_g"),
            "w_in": f("w_in"), "w_out": f("w_out"), "w_up": f("w_up"), "w_down": f("w_down"),
            "w2": f("w2"), "a2": f("a2"), "g2": f("g2"),
        })
    res = run_bass_kernel_spmd(nc, in_maps, core_ids=list(range(8)))
    out = np.zeros((B, T, D), np.float32)
    for c in range(8):
        b, h = c // 2, c % 2
        out[b, h * half:(h + 1) * half] = res.results[c]["out"]
    return out
```

```python
import contextlib
import numpy as np
import concourse.bass as bass
import concourse.mybir as mybir
from concourse.bass_utils import run_bass_kernel_spmd

F32 = mybir.dt.float32
BF16 = mybir.dt.bfloat16
AF = mybir.ActivationFunctionType
ALU = mybir.AluOpType
AX = mybir.AxisListType

D = 1024
W = 256
NBLK = 2
CH = 64
NCH = W // CH
INC = 3360
DFF = 2816
NPAIR = 22
RMS_EPS = 1e-6
GN_EPS = 64 * 1e-5
EPOCH = 12000
DEBUG_SUB = 99
DEBUG_STOP = 99

O_PMG, O_PFG, O_CAW, O_MU, O_W0, O_A0, O_KK, O_KA, O_RK, O_LW, O_LB, O_FCW, O_FCB, O_HM = (
    0, 8, 16, 28, 43, 47, 51, 55, 59, 63, 67, 71, 203, 247)
NCC = 248
M_ID, M_SU, M_IU, M_SL, M_BO, M_F, M_RST = 0, 128, 256, 384, 512, 640, 704
NKM = 704 + 256


class Key:
    __slots__ = ("name", "writer", "readers", "excl")

    def __init__(self, name, excl=False):
        self.name = name
        self.writer = None
        self.readers = []
        self.excl = excl


def PKey(name):
    return Key(name, excl=True)


class Sched:
    ENGS = ("pe", "act", "dve", "pool", "sp")

    def __init__(self, nc, sem_stack, prefix):
        self.nc = nc
        self.sem_stack = sem_stack
        self.prefix = prefix
        self.ops = {e: [] for e in self.ENGS}
        self.count = {e: 0 for e in self.ENGS}
        self.sems = {}
        self.waited = {e: {} for e in self.ENGS}
        self.dma_counts = {}
        self.last_tok = {e: None for e in self.ENGS}

    def _eng_sem(self, eng, idx):
        sid = f"{self.prefix}s_{eng}_{idx // EPOCH}"
        self.sems.setdefault(sid, None)
        return sid, (idx % EPOCH) + 1

    def new_dma_sem(self, name):
        sid = f"{self.prefix}d_{name}"
        assert sid not in self.sems, sid
        self.sems[sid] = None
        self.dma_counts[sid] = 0
        return sid

    def _need_waits(self, eng, tokens):
        w = self.waited[eng]
        best = {}
        for t in tokens:
            if t is None:
                continue
            sid, val, _ = t
            if w.get(sid, 0) >= val:
                continue
            if best.get(sid, 0) < val:
                best[sid] = val
        for sid, val in best.items():
            w[sid] = val
        return list(best.items())

    def op(self, eng, fn, reads=(), writes=(), dma_sem=None):
        toks = []
        raw = set()
        for k in reads:
            toks.append(k.writer)
            if k.writer is not None:
                raw.add(k.writer)
            if k.excl:
                toks.extend(r for r in k.readers if r[2] != eng)
        for k in writes:
            toks.append(k.writer)
            toks.extend(k.readers)
        waits = self._need_waits(eng, toks)
        if dma_sem is None:
            idx = self.count[eng]
            self.count[eng] += 1
            sid, val = self._eng_sem(eng, idx)
            tok = (sid, val, eng)
            inc = (sid, 1)
            self.last_tok[eng] = tok
        else:
            self.dma_counts[dma_sem] += 16
            tok = (dma_sem, self.dma_counts[dma_sem], "dma")
            inc = (dma_sem, 16)
        self.ops[eng].append((fn, waits, inc))
        for k in reads:
            k.readers.append(tok)
        for k in writes:
            k.writer = tok
            k.readers = []
        return tok

    def barrier(self, extra_keys=()):
        toks = [t for t in self.last_tok.values() if t is not None]
        for k in extra_keys:
            toks.append(k.writer)
            toks.extend(k.readers)
        for eng in self.ENGS:
            waits = self._need_waits(eng, [t for t in toks if t is not None and t[2] != eng])
            if waits:
                self.ops[eng].append((None, waits, None))

    def final_wait(self, eng, keys):
        toks = []
        for k in keys:
            toks.append(k.writer)
            toks.extend(k.readers)
        waits = self._need_waits(eng, toks)
        self.ops[eng].append((None, waits, None))

    def emit(self):
        nc = self.nc
        with contextlib.ExitStack() as st:
            handles = {sid: self.sem_stack.enter_context(nc.semaphore(sid)) for sid in self.sems}
            block = st.enter_context(nc.Block())

            def run(engobj, lst):
                for fn, waits, inc in lst:
                    for sid, val in waits:
                        engobj.wait_ge(handles[sid], val)
                    if fn is not None:
                        fn(engobj).then_inc(handles[inc[0]], inc[1])

            @block.tensor
            def _(e):
                run(e, self.ops["pe"])

            @block.scalar
            def _(e):
                run(e, self.ops["act"])

            @block.vector
            def _(e):
                run(e, self.ops["dve"])

            @block.gpsimd
            def _(e):
                run(e, self.ops["pool"])

            @block.sync
            def _(e):
                run(e, self.ops["sp"])


class Rot:
    def __init__(self, tiles, excl=False):
        self.tiles = tiles
        self.keys = [Key(f"rot{i}", excl) for i in range(len(tiles))]
        self.i = 0

    def next(self):
        j = self.i % len(self.tiles)
        self.i += 1
        return self.tiles[j], self.keys[j]


def _rms_transpose(S, src, ksrc, dstT, kdst, tcol, scr, ident, kid, ptr, extra_scale=None, kextra=None):
    junk, kjunk = scr["junk"].next()
    st, kst = scr["stat"].next()
    xs, kxs = scr["xs"].next()
    pt, kpt = ptr.next()
    S.op("act", lambda e: e.activation(out=junk[:], in_=src, func=AF.Square, accum_out=st[:, 0:1]),
         reads=[ksrc], writes=[kjunk, kst])
    S.op("act", lambda e: e.activation(out=st[:, 1:2], in_=st[:, 0:1], func=AF.Sqrt, scale=1.0 / D, bias=RMS_EPS),
         reads=[kst], writes=[kst])
    S.op("dve", lambda e: e.reciprocal(out=st[:, 2:3], in_=st[:, 1:2]), reads=[kst], writes=[kst])
    rs = st[:, 2:3]
    if extra_scale is not None:
        S.op("dve", lambda e: e.tensor_tensor(out=st[:, 3:4], in0=st[:, 2:3], in1=extra_scale, op=ALU.mult),
             reads=[kst, kextra], writes=[kst])
        rs = st[:, 3:4]
    S.op("pool", lambda e: e.tensor_scalar(out=xs[:], in0=src, scalar1=rs, scalar2=1.0, op0=ALU.mult, op1=ALU.mult),
         reads=[ksrc, kst], writes=[kxs])
    for kc in range(8):
        S.op("pe", lambda e, kc=kc: e.transpose(out=pt[:, kc, :], in_=xs[:, kc * 128:(kc + 1) * 128], identity=ident[:]),
             reads=[kxs, kid], writes=[kpt])
    S.op("act", lambda e: e.activation(out=dstT[:, :, tcol:tcol + 128], in_=pt[:], func=AF.Copy),
         reads=[kpt], writes=[kdst])


def _post_norm_residual(S, pd_pairs, res, kres, grow, kgrow, scr):
    st, kst = scr["stat"].next()
    for hf, (pd, kpd) in enumerate(pd_pairs):
        junk, kjunk = scr["junk"].next()
        S.op("act", lambda e, pd=pd, hf=hf, junk=junk: e.activation(out=junk[:, hf * 512:(hf + 1) * 512], in_=pd[:], func=AF.Square,
                                                                  accum_out=st[:, hf:hf + 1]),
             reads=[kpd], writes=[kjunk, kst])
    S.op("dve", lambda e: e.tensor_tensor(out=st[:, 2:3], in0=st[:, 0:1], in1=st[:, 1:2], op=ALU.add), reads=[kst], writes=[kst])
    S.op("act", lambda e: e.activation(out=st[:, 3:4], in_=st[:, 2:3], func=AF.Sqrt, scale=1.0 / D, bias=RMS_EPS),
         reads=[kst], writes=[kst])
    S.op("dve", lambda e: e.reciprocal(out=st[:, 4:5], in_=st[:, 3:4]), reads=[kst], writes=[kst])
    for hf, (pd, kpd) in enumerate(pd_pairs):
        tmp, ktmp = scr["tmp512"].next()
        S.op("dve", lambda e, pd=pd, hf=hf, tmp=tmp: e.scalar_tensor_tensor(
            out=tmp[:], in0=pd[:], scalar=st[:, 4:5], in1=grow[:, hf * 512:(hf + 1) * 512], op0=ALU.mult, op1=ALU.mult),
            reads=[kpd, kst, kgrow], writes=[ktmp])
        S.op("pool", lambda e, hf=hf, tmp=tmp: e.tensor_tensor(out=res[:, hf * 512:(hf + 1) * 512], in0=res[:, hf * 512:(hf + 1) * 512],
                                                               in1=tmp[:], op=ALU.add),
             reads=[ktmp, kres], writes=[kres])


def phase2_ffn(nc, S, st, io, n_main, shared):
    sb = lambda n, s, d: st.enter_context(nc.sbuf_tensor(n, s, d))
    ps = lambda n, s, d: st.enter_context(nc.psum_tensor(n, s, d))
    cc, kcc = shared["cc"], shared["kcc"]
    ident, kid = shared["ident"], shared["kid"]
    hscr, khscr = io["hscr"], io["khscr"]
    out = io["out"]

    wup = sb("wup", [128, 8, DFF * 2], BF16)
    wdn = sb("wdn", [128, NPAIR, D], BF16)
    kwup = [Key(f"wup{k}") for k in range(8)]
    kwdn = Key("wdn")
    grow = sb("grow2", [128, D], F32)
    kgrow = Key("grow2")
    fh = sb("fh", [128, NPAIR, 2, 2], F32)
    kfh = [Key(f"fh{i}") for i in range(NPAIR)]
    hts = [sb(f"ht{i}", [128, NBLK, D], F32) for i in range(2)]
    khts = [[Key(f"ht{i}_{b}") for b in range(NBLK)] for i in range(2)]
    hnTs = [sb(f"hnT{i}", [128, 8, W], BF16) for i in range(2)]
    khnT = [Key(f"hnT{i}") for i in range(2)]
    act = sb("actb", [128, NPAIR, W], BF16)
    kact = [Key(f"act{i}") for i in range(NPAIR)]
    scr = {
        "junk": Rot([sb(f"junk{i}", [128, D], BF16) for i in range(2)]),
        "stat": Rot([sb(f"stat{i}", [128, 8], F32) for i in range(4)]),
        "xs": Rot([sb(f"xs{i}", [128, D], BF16) for i in range(2)]),
        "tmp512": Rot([sb(f"tmp512_{i}", [128, 512], F32) for i in range(2)]),
    }
    fbuf = Rot([sb(f"fbuf{i}", [128, 2, W + 2], F32) for i in range(2)])
    cg = Rot([sb(f"cg{i}", [128, W], F32) for i in range(2)])
    cu = Rot([sb(f"cu{i}", [128, W], F32) for i in range(2)])
    t1 = Rot([sb(f"t1_{i}", [128, W], F32) for i in range(2)])
    t2 = Rot([sb(f"t2_{i}", [128, W], F32) for i in range(2)])
    sg = Rot([sb(f"sg{i}", [128, W], F32) for i in range(2)])
    WQ = DFF // 4
    wst = Rot([sb(f"wst{i}", [128, WQ], F32) for i in range(2)])
    ptr = Rot([ps("ptr2", [128, 8, 128], BF16)], excl=True)
    pf = Rot([ps(f"pf{i}", [128, 2, W], F32) for i in range(3)], excl=True)
    pd = [[ps(f"pd{b}{h}", [128, 512], F32) for h in range(2)] for b in range(NBLK)]
    kpd = [[PKey(f"pd{b}{h}") for h in range(2)] for b in range(NBLK)]
    dsem = {n: S.new_dma_sem("p2_" + n) for n in ("wst0", "wst1", "wdn", "grow", "h0", "h1", "hw", "out0", "out1")}

    S.op("sp", lambda e: e.dma_start(out=grow[:], in_=io["post_ffn_g"].partition_broadcast(128)), writes=[kgrow], dma_sem=dsem["grow"])
    for kc in range(8):
        for q in range(8):
            j = kc * 8 + q
            wt, kwt = wst.next()
            S.op("sp", lambda e, wt=wt, kc=kc, q=q: e.dma_start(out=wt[:], in_=io["w_up"][kc * 128:(kc + 1) * 128, q * WQ:(q + 1) * WQ]),
                 writes=[kwt], dma_sem=dsem[f"wst{j % 2}"])
            eng = ("act", "dve", "pool")[j % 3]
            if eng == "act":
                S.op("act", lambda e, wt=wt, kc=kc, q=q: e.activation(out=wup[:, kc, q * WQ:(q + 1) * WQ], in_=wt[:], func=AF.Copy,
                                                                       scale=cc[:, O_PFG + kc:O_PFG + kc + 1]),
                     reads=[kwt, kcc], writes=[kwup[kc]])
            else:
                S.op(eng, lambda e, wt=wt, kc=kc, q=q: e.tensor_scalar(out=wup[:, kc, q * WQ:(q + 1) * WQ], in0=wt[:],
                                                                        scalar1=cc[:, O_PFG + kc:O_PFG + kc + 1], scalar2=1.0, op0=ALU.mult, op1=ALU.mult),
                     reads=[kwt, kcc], writes=[kwup[kc]])
    S.op("pool", lambda e: e.dma_start(out=wdn[:], in_=io["w_down"].rearrange("(i p) d -> p i d", p=128)), writes=[kwdn], dma_sem=dsem["wdn"])

    hw = hts[1]
    S.op("sp", lambda e: e.dma_start(out=hw[:, 0, :], in_=hscr[W - 128:W, :]), reads=[khscr[0]], writes=[khts[1][0]], dma_sem=dsem["hw"])
    _rms_transpose(S, hw[:, 0, :], khts[1][0], hnTs[1], khnT[1], 0, scr, ident, kid, ptr,
                   extra_scale=cc[:, O_HM:O_HM + 1], kextra=kcc)
    pfh, kpfh = pf.next()
    pfh_v = pfh[:].rearrange("p a w -> p (a w)")
    for ch in range(2 * NPAIR):
        i, hf = ch % NPAIR, ch // NPAIR
        col = (i * 2 + hf) * 2
        for kc in range(8):
            S.op("pe", lambda e, ch=ch, kc=kc, col=col: e.matmul(pfh_v[:, col:col + 2], lhsT=wup[:, kc, ch * 128:(ch + 1) * 128],
                                                                  rhs=hnTs[1][:, kc, 126:128], start=(kc == 0), stop=(kc == 7)),
                 reads=[kwup[kc], khnT[1]], writes=[kpfh])
    S.op("act", lambda e: e.activation(out=fh[:].rearrange("p i a b -> p (i a b)"), in_=pfh_v[:, 0:NPAIR * 4], func=AF.Copy),
         reads=[kpfh], writes=kfh)

    def load(t):
        b = t % 2
        S.op("sp", lambda e: e.dma_start(out=hts[b][:], in_=hscr[W * (1 + t):W * (2 + t), :].rearrange("(b p) d -> p b d", p=128)),
             reads=[khscr[1 + t]], writes=khts[b], dma_sem=dsem[f"h{b}"])

    def prologue(t):
        b = t % 2
        for blk in range(NBLK):
            _rms_transpose(S, hts[b][:, blk, :], khts[b][blk], hnTs[b], khnT[b], blk * 128, scr, ident, kid, ptr)

    state = {}

    def up_mm(t, i):
        b = t % 2
        p, kp = pf.next()
        state[(t, i)] = (p, kp)
        for hf in range(2):
            ch = hf * NPAIR + i
            for kc in range(8):
                S.op("pe", lambda e, p=p, hf=hf, ch=ch, kc=kc: e.matmul(p[:, hf, :], lhsT=wup[:, kc, ch * 128:(ch + 1) * 128],
                                                                         rhs=hnTs[b][:, kc, :], start=(kc == 0), stop=(kc == 7)),
                     reads=[kwup[kc], khnT[b]], writes=[kp])

    def elem(t, i):
        p, kp = state.pop((t, i))
        fb, kfb = fbuf.next()
        S.op("pool", lambda e: e.tensor_copy(out=fb[:, :, 0:2], in_=fh[:, i, :, :]), reads=[kfh[i]], writes=[kfb])
        S.op("act", lambda e: e.activation(out=fb[:, :, 2:W + 2], in_=p[:], func=AF.Copy), reads=[kp], writes=[kfb])
        S.op("pool", lambda e: e.tensor_copy(out=fh[:, i, :, :], in_=fb[:, :, W:W + 2]), reads=[kfb], writes=[kfh[i]])
        outs = []
        for hf, rot in ((0, cg), (1, cu)):
            ch = hf * NPAIR + i
            c, kc_ = rot.next()
            wof = O_FCW + ch * 3
            S.op("act", lambda e, c=c, hf=hf, wof=wof, ch=ch: e.activation(out=c[:], in_=fb[:, hf, 2:W + 2], func=AF.Identity,
                                                                          scale=cc[:, wof + 2:wof + 3], bias=cc[:, O_FCB + ch:O_FCB + ch + 1]),
                 reads=[kfb, kcc], writes=[kc_])
            S.op("dve", lambda e, c=c, hf=hf, wof=wof: e.scalar_tensor_tensor(out=c[:], in0=fb[:, hf, 1:W + 1], scalar=cc[:, wof + 1:wof + 2],
                                                                             in1=c[:], op0=ALU.mult, op1=ALU.add),
                 reads=[kfb, kcc, kc_], writes=[kc_])
            S.op("dve", lambda e, c=c, hf=hf, wof=wof: e.scalar_tensor_tensor(out=c[:], in0=fb[:, hf, 0:W], scalar=cc[:, wof:wof + 1],
                                                                             in1=c[:], op0=ALU.mult, op1=ALU.add),
                 reads=[kfb, kcc, kc_], writes=[kc_])
            outs.append((c, kc_))
        (g_, kg), (u_, ku) = outs
        a1, ka1 = t1.next()
        a2, ka2 = t2.next()
        s_, ks = sg.next()
        S.op("act", lambda e: e.activation(out=a1[:], in_=g_[:], func=AF.Square), reads=[kg], writes=[ka1])
        S.op("pool", lambda e: e.tensor_scalar(out=a1[:], in0=a1[:], scalar1=0.044715, scalar2=1.0, op0=ALU.mult, op1=ALU.add),
             reads=[ka1], writes=[ka1])
        S.op("pool", lambda e: e.tensor_tensor(out=a2[:], in0=a1[:], in1=g_[:], op=ALU.mult), reads=[ka1, kg], writes=[ka2])
        S.op("act", lambda e: e.activation(out=s_[:], in_=a2[:], func=AF.Sigmoid, scale=1.5957691216), reads=[ka2], writes=[ks])
        S.op("dve", lambda e: e.tensor_tensor(out=a2[:], in0=g_[:], in1=u_[:], op=ALU.mult), reads=[kg, ku, ks], writes=[ka2])
        S.op("dve", lambda e: e.tensor_tensor(out=act[:, i, :], in0=a2[:], in1=s_[:], op=ALU.mult), reads=[ka2, ks], writes=[kact[i]])

    def down_mm(t, i):
        for blk in range(NBLK):
            for hf in range(2):
                S.op("pe", lambda e, blk=blk, hf=hf: e.matmul(pd[blk][hf][:], lhsT=act[:, i, blk * 128:(blk + 1) * 128],
                                                              rhs=wdn[:, i, hf * 512:(hf + 1) * 512], start=(i == 0), stop=(i == NPAIR - 1)),
                     reads=[kact[i], kwdn], writes=[kpd[blk][hf]])

    def epilogue(t):
        b = t % 2
        for blk in range(NBLK):
            _post_norm_residual(S, [(pd[blk][0], kpd[blk][0]), (pd[blk][1], kpd[blk][1])], hts[b][:, blk, :], khts[b][blk], grow, kgrow, scr)
        S.op("sp", lambda e: e.dma_start(out=out[W * t:W * (t + 1), :].rearrange("(b p) d -> p b d", p=128), in_=hts[b][:]),
             reads=khts[b], dma_sem=dsem[f"out{b}"])

    load(0)
    if n_main > 1:
        load(1)
    prologue(0)
    for t in range(n_main):
        for i in range(NPAIR + 2):
            if i < NPAIR:
                up_mm(t, i)
            if 0 <= i - 1 < NPAIR:
                elem(t, i - 1)
            if 0 <= i - 2 < NPAIR:
                down_mm(t, i - 2)
            if i == 14 and t + 1 < n_main:
                prologue(t + 1)
        epilogue(t)
        if t + 2 < n_main:
            load(t + 2)
    S.final_wait("sp", [k for ks_ in khts for k in ks_])


def build(n_pre, n_main, mode="full"):
    nc = bass.Bass("TRN2", target_bir_lowering=False)
    TT = (n_pre + 1 + n_main) * W
    io = {}
    di = lambda n, s: nc.dram_tensor(n, s, F32, kind="ExternalInput").ap()
    io["cc"] = di("cc", [128, NCC])
    io["km"] = di("km", [128, NKM])
    io["post_ffn_g"] = di("post_ffn_g", [D])
    io["w_up"] = di("w_up", [D, 2 * DFF])
    io["w_down"] = di("w_down", [DFF, D])
    if mode == "ffn":
        io["hscr"] = di("hscr", [(1 + n_main) * W, D])
    else:
        io["xin"] = di("xin", [TT, D])
        io["post_mix_g"] = di("post_mix_g", [D])
        io["w_in"] = di("w_in", [D, INC])
        io["w_out"] = di("w_out", [D, D])
        io["w2"] = di("w2", [64, 512])
        io["a2"] = di("a2", [64, 512])
        io["g2"] = di("g2", [160, 512])
        io["hscr"] = nc.dram_tensor("hscr", [(1 + n_main) * W, D], F32, kind="Internal").ap()
    io["khscr"] = [Key(f"hscr{i}") for i in range(1 + n_main)]
    io["out"] = nc.dram_tensor("out", [n_main * W, D], F32, kind="ExternalOutput").ap()

    with contextlib.ExitStack() as sem_stack, contextlib.ExitStack() as st0:
        cc = st0.enter_context(nc.sbuf_tensor("cc_sb", [128, NCC], F32))
        ident = st0.enter_context(nc.sbuf_tensor("ident", [128, 128], BF16))

        def shared_loads(S):
            kcc, kid = Key("cc"), Key("ident")
            d0 = S.new_dma_sem("cc")
            d1 = S.new_dma_sem("ident")
            S.op("sp", lambda e: e.dma_start(out=cc[:], in_=io["cc"][:, :]), writes=[kcc], dma_sem=d0)
            S.op("pool", lambda e: e.dma_start(out=ident[:], in_=io["km"][:, M_ID:M_ID + 128]), writes=[kid], dma_sem=d1)
            return {"cc": cc, "kcc": kcc, "ident": ident, "kid": kid}

        if mode != "ffn":
            S1 = Sched(nc, sem_stack, "a")
            shared = shared_loads(S1)
            with contextlib.ExitStack() as st1:
                phase1_mixer(nc, S1, st1, io, n_pre, n_main, shared)
                S1.final_wait("sp", io["khscr"])
                S1.emit()
            S2 = Sched(nc, sem_stack, "b")
            shared = {"cc": cc, "kcc": Key("cc2"), "ident": ident, "kid": Key("ident2")}
            io["khscr"] = [Key(f"hscr2_{i}") for i in range(1 + n_main)]
        else:
            S2 = Sched(nc, sem_stack, "b")
            shared = shared_loads(S2)
        with contextlib.ExitStack() as st2:
            phase2_ffn(nc, S2, st2, io, n_main, shared)
            S2.emit()
    return nc


def phase1_mixer(nc, S, st, io, n_pre, n_main, shared):
    sb = lambda n, s, d: st.enter_context(nc.sbuf_tensor(n, s, d))
    ps = lambda n, s, d: st.enter_context(nc.psum_tensor(n, s, d))
    cc, kcc = shared["cc"], shared["kcc"]
    ident, kid = shared["ident"], shared["kid"]
    xin, hscr, khscr = io["xin"], io["hscr"], io["khscr"]
    n_tiles = n_pre + 1 + n_main
    C05 = 0.6065306597126334

    win = sb("win", [128, 8, INC], BF16)
    kwin = [Key(f"win{k}") for k in range(8)]
    wout = sb("wout", [128, 8, D], BF16)
    kwout = Key("wout")
    w2b = sb("w2b", [128, 512], BF16)
    a2b = sb("a2b", [128, 512], BF16)
    g2b0 = sb("g2b0", [128, 512], BF16)
    g2b1 = sb("g2b1", [128, 512], BF16)
    wg1 = sb("wg1", [128, 8, 128], BF16)
    kwg1 = Key("wg1")
    ksmallw = Key("smallw")
    msu4 = sb("msu4", [128, 512], BF16)
    msl = sb("msl", [128, 128], BF16)
    bones = sb("bones", [128, 128], BF16)
    fst = sb("fst", [128, 64], BF16)
    rst = sb("rst", [128, W], F32)
    kconst = Key("p1const")
    grow = sb("grow1", [128, D], F32)
    kgrow = Key("grow1")
    dc = sb("dc", [128, 20], F32)
    kdc = Key("dc")
    dsem = {n: S.new_dma_sem("p1_" + n) for n in ("wst0", "wst1", "wout", "small", "const", "grow", "x0", "x1", "h0", "h1")}
    WQ = INC // 8
    wst = Rot([sb(f"wst1_{i}", [128, WQ], F32) for i in range(2)])

    S.op("sp", lambda e: e.dma_start(out=grow[:], in_=io["post_mix_g"].partition_broadcast(128)), writes=[kgrow], dma_sem=dsem["grow"])
    S.op("sp", lambda e: e.dma_start(out=rst[:], in_=io["km"][:, M_RST:M_RST + W]), writes=[kconst], dma_sem=dsem["const"])
    uniq = [0]

    def pool_dma(fn, key):
        uniq[0] += 1
        S.op("pool", fn, writes=[key], dma_sem=S.new_dma_sem(f"p1u{uniq[0]}"))

    kmsl, kbones, kfst = Key("msl"), Key("bones"), Key("fst")
    kmsu4 = [Key(f"msu4_{q}") for q in range(4)]
    kw2b, ka2b, kg2b0, kg2b1 = Key("w2b"), Key("a2b"), Key("g2b0"), Key("g2b1")
    for dst, c0, n_, k_ in ((msl, M_SL, 128, kmsl), (bones, M_BO, 128, kbones), (fst, M_F, 64, kfst)):
        pool_dma(lambda e, dst=dst, c0=c0, n_=n_: e.dma_start(out=dst[:], in_=io["km"][:, c0:c0 + n_]), k_)
    for q, c0 in enumerate((M_SU, M_IU, M_SU, M_IU)):
        pool_dma(lambda e, q=q, c0=c0: e.dma_start(out=msu4[:, q * 128:(q + 1) * 128], in_=io["km"][:, c0:c0 + 128]), kmsu4[q])
    S.op("pool", lambda e: e.memset(w2b[:], 0.0), writes=[kw2b])
    S.op("pool", lambda e: e.memset(a2b[:], 0.0), writes=[ka2b])
    S.op("pool", lambda e: e.memset(g2b1[:], 0.0), writes=[kg2b1])
    S.op("pool", lambda e: e.memset(wg1[:], 0.0), writes=[kwg1])
    pool_dma(lambda e: e.dma_start(out=w2b[0:64, :], in_=io["w2"][:, :]), kw2b)
    pool_dma(lambda e: e.dma_start(out=a2b[64:128, :], in_=io["a2"][:, :]), ka2b)
    pool_dma(lambda e: e.dma_start(out=g2b0[:], in_=io["g2"][0:128, :]), kg2b0)
    pool_dma(lambda e: e.dma_start(out=g2b1[0:32, :], in_=io["g2"][128:160, :]), kg2b1)
    pool_dma(lambda e: e.dma_start(out=wout[:], in_=io["w_out"].rearrange("(k p) d -> p k d", p=128)), kwout)
    S.op("dve", lambda e: e.tensor_scalar(out=dc[:, 0:15], in0=cc[:, O_MU:O_MU + 15], scalar1=-1.0, scalar2=1.0, op0=ALU.mult, op1=ALU.add),
         reads=[kcc], writes=[kdc])
    S.op("dve", lambda e: e.tensor_scalar(out=dc[:, 15:19], in0=cc[:, O_KA:O_KA + 4], scalar1=-1.0, scalar2=1.0, op0=ALU.mult, op1=ALU.add),
         reads=[kcc], writes=[kdc])
    for kc in range(8):
        for q in range(8):
            j = kc * 8 + q
            wt, kwt = wst.next()
            S.op("sp", lambda e, wt=wt, kc=kc, q=q: e.dma_start(out=wt[:], in_=io["w_in"][kc * 128:(kc + 1) * 128, q * WQ:(q + 1) * WQ]),
                 writes=[kwt], dma_sem=dsem[f"wst{j % 2}"])
            if j % 2 == 0:
                S.op("act", lambda e, wt=wt, kc=kc, q=q: e.activation(out=win[:, kc, q * WQ:(q + 1) * WQ], in_=wt[:], func=AF.Copy,
                                                                       scale=cc[:, O_PMG + kc:O_PMG + kc + 1]),
                     reads=[kwt, kcc], writes=[kwin[kc]])
            else:
                S.op("dve", lambda e, wt=wt, kc=kc, q=q: e.tensor_scalar(out=win[:, kc, q * WQ:(q + 1) * WQ], in0=wt[:],
                                                                          scalar1=cc[:, O_PMG + kc:O_PMG + kc + 1], scalar2=1.0, op0=ALU.mult, op1=ALU.mult),
                     reads=[kwt, kcc], writes=[kwin[kc]])

    S.op("dve", lambda e: e.tensor_copy(out=wg1[:, :, 0:32], in_=win[:, :, INC - 32:INC]), reads=kwin + [kwg1], writes=[kwg1])

    xts = [sb(f"xt{i}", [128, NBLK, D], F32) for i in range(2)]
    kxts = [[Key(f"xt{i}_{b}") for b in range(NBLK)] for i in range(2)]
    xnT = sb("xnT", [128, 8, W], BF16)
    kxnT = Key("xnT")
    scr = {
        "junk": Rot([sb(f"junk1_{i}", [128, D], BF16) for i in range(2)]),
        "stat": Rot([sb(f"stat1_{i}", [128, 8], F32) for i in range(4)]),
        "xs": Rot([sb(f"xs1_{i}", [128, D], BF16) for i in range(2)]),
        "tmp512": Rot([sb(f"tmp512a_{i}", [128, 512], F32) for i in range(2)]),
    }
    tf = Rot([sb(f"tf{i}", [128, W], F32) for i in range(10)])
    tb = Rot([sb(f"tb{i}", [128, W], BF16) for i in range(4)])
    qbuf = Rot([sb(f"qbuf{i}", [128, W + 1], F32) for i in range(2)])
    ubuf = Rot([sb(f"ubuf{i}", [128, W + 2], F32) for i in range(2)])
    qh = sb("qh", [128, 15], F32)
    kqh = [Key(f"qh{j}") for j in range(15)]
    uh = sb("uh", [128, 4, 2], F32)
    kuh = [Key(f"uh{c}") for c in range(4)]
    rkv = {n: Rot([sb(f"{n}c{i}", [128, W], F32) for i in range(2)]) for n in ("r", "k", "v")}
    wa = sb("wa", [128, W], F32); kwa = Key("wa")
    g0t = sb("g0t", [128, W], F32); kg0 = Key("g0t")
    g1t = sb("g1t", [128, W], F32); kg1 = Key("g1t")
    twad = sb("twad", [128, W], BF16); ktwad = Key("twad")
    sgd0 = sb("sgd0", [128, W], BF16); sgd1 = sb("sgd1", [128, W], BF16); ksgd = Key("sgd")
    sig = sb("sig", [128, 4, W], F32); ksig = [Key(f"sig{c}") for c in range(4)]
    a4 = sb("a4", [128, 4, W], F32); ka4 = [Key(f"a4{c}") for c in range(4)]
    bonus = sb("bonus", [128, 4, W], F32); kbonus = [Key(f"bonus{c}") for c in range(4)]
    ARbd = sb("ARbd", [128, 4, NCH, 2, 128], BF16); kAbd = [Key(f"Abd{c}") for c in range(4)]; kRbd = [Key(f"Rbd{c}") for c in range(4)]
    Bbd = sb("Bbd", [128, 4, NCH, 128], BF16); kBbd = [Key(f"Bbd{c}") for c in range(4)]
    Kbd = sb("Kbd", [128, 4, NCH, 128], BF16); kKbd = [Key(f"Kbd{c}") for c in range(4)]
    Vbd = sb("Vbd", [128, 4, NCH, 128], BF16); kVbd = [Key(f"Vbd{c}") for c in range(4)]
    PC = sb("PC", [128, 4, NCH], F32); kPC = [Key(f"PC{c}") for c in range(4)]
    M1 = sb("M1", [128, 4, 512], BF16); kM1 = Key("M1")
    NT0 = sb("NT0", [128, 4, 128], BF16); kNT0 = Key("NT0")
    M2 = sb("M2", [128, 4, 320], BF16); kM2 = Key("M2")
    NN = Rot([sb(f"NN{i}", [128, 4, 256], BF16) for i in range(2)])
    Tb = Rot([sb(f"Tb{i}", [128, 4, 128], BF16) for i in range(2)])
    Zb = sb("Zb", [128, 4, 64], BF16); kZb = Key("Zb")
    Ub = sb("Ub", [128, 4, 64], BF16); kUb = Key("Ub")
    S32 = sb("S32", [128, 4, 64], F32); kS32 = Key("S32")
    Sbf = sb("Sbf", [128, 4, 64], BF16); kSbf = Key("Sbf")
    tmpS = sb("tmpS", [128, 4, 64], F32); ktmpS = Key("tmpS")
    ysq = sb("ysq", [128, 4, 64], F32); kysq = Key("ysq")
    ycen = sb("ycen", [128, 4, 64], F32); kycen = Key("ycen")
    gst = Rot([sb(f"gst{i}", [128, 8, 4], F32) for i in range(2)])
    ynbd = sb("ynbd", [128, 4, 128], BF16); kynbd = Key("ynbd")
    ynf = sb("ynf", [128, 4, W], F32); kynf = [Key(f"ynf{c}") for c in range(4)]
    yT = sb("yT", [128, 8, W], BF16); kyT = [Key(f"yT{c}") for c in range(8)]

    ptr = Rot([ps("ptr1", [128, 8, 128], BF16)], excl=True)
    pp = [ps(f"pp{i}", [128, 2, W], F32) for i in range(2)]
    kpp = [PKey(f"pp{i}") for i in range(2)]
    psm = ps("psm", [128, 2, W], F32); kpsm = PKey("psm")
    PA = ps("PA", [128, 1024], F32); kPA = PKey("PA")
    PB = ps("PB", [128, 1024], F32); kPB0 = PKey("PB0"); kPB1 = PKey("PB1")

    for t_, k_ in ((qh, kqh), (uh, kuh)):
        S.op("pool", lambda e, t_=t_: e.memset(t_[:], 0.0), writes=k_)
    S.op("pool", lambda e: e.memset(S32[:], 0.0), writes=[kS32])
    S.op("pool", lambda e: e.memset(Sbf[:], 0.0), writes=[kSbf])
    S.op("pool", lambda e: e.memset(ARbd[:], 0.0), writes=kAbd + kRbd)
    S.op("pool", lambda e: e.memset(Bbd[:], 0.0), writes=kBbd)
    S.op("pool", lambda e: e.memset(Kbd[:], 0.0), writes=kKbd)
    S.op("pool", lambda e: e.memset(Vbd[:], 0.0), writes=kVbd)
    S.op("pool", lambda e: e.memset(ynbd[:], 0.0), writes=[kynbd])
    S.op("pool", lambda e: e.memset(g1t[:], 0.0), writes=[kg1])

    slot_i = [0]

    def proj(col0, ncols, wsrc=None, kw=None):
        j = slot_i[0] % 4
        slot_i[0] += 1
        t_, half, key = pp[j // 2], j % 2, kpp[j // 2]
        for kc in range(8):
            lhsT = win[:, kc, col0:col0 + ncols] if wsrc is None else wsrc[:, kc, :]
            S.op("pe", lambda e, kc=kc, lhsT=lhsT: e.matmul(t_[0:ncols, half, :], lhsT=lhsT, rhs=xnT[:, kc, :],
                                                            start=(kc == 0), stop=(kc == 7)),
                 reads=[(kwin if kw is None else kw)[kc], kxnT], writes=[key])
        return t_[0:ncols, half, :], key

    def shift_lerp(p_ap, kp, jj, dst_ap, kdst, np_=128):
        qb, kqb = qbuf.next()
        S.op("pool", lambda e: e.tensor_copy(out=qb[0:np_, 0:1], in_=qh[0:np_, jj:jj + 1]), reads=[kqh[jj]], writes=[kqb])
        S.op("act", lambda e: e.activation(out=qb[0:np_, 1:W + 1], in_=p_ap, func=AF.Copy), reads=[kp], writes=[kqb])
        S.op("pool", lambda e: e.tensor_copy(out=qh[0:np_, jj:jj + 1], in_=qb[0:np_, W:W + 1]), reads=[kqb], writes=[kqh[jj]])
        tmp, ktmp = tf.next()
        S.op("pool", lambda e: e.tensor_scalar(out=tmp[0:np_, :], in0=qb[0:np_, 1:W + 1], scalar1=dc[0:np_, jj:jj + 1], scalar2=1.0, op0=ALU.mult, op1=ALU.mult),
             reads=[kqb, kdc], writes=[ktmp])
        S.op("dve", lambda e: e.scalar_tensor_tensor(out=dst_ap, in0=qb[0:np_, 0:W], scalar=cc[0:np_, O_MU + jj:O_MU + jj + 1], in1=tmp[0:np_, :],
                                                     op0=ALU.mult, op1=ALU.add),
             reads=[kqb, kcc, ktmp], writes=[kdst])

    def bd_view(t4, c, h):
        return t4[h * 64:(h + 1) * 64, c, :, h * 64:(h + 1) * 64]

    def v3(t2, h):
        return t2[h * 64:(h + 1) * 64, :].rearrange("p (n t) -> p n t", n=NCH)

    def load_x(t):
        b = t % 2
        S.op("sp", lambda e: e.dma_start(out=xts[b][:], in_=xin[W * t:W * (t + 1), :].rearrange("(b p) d -> p b d", p=128)),
             writes=kxts[b], dma_sem=dsem[f"x{b}"])

    load_x(0)
    if n_tiles > 1:
        load_x(1)

    def do_tile(t):
        full = t >= n_pre
        if DEBUG_STOP < 1:
            return
        b = t % 2
        xt = xts[b]
        for blk in range(NBLK):
            _rms_transpose(S, xt[:, blk, :], kxts[b][blk], xnT, kxnT, blk * 128, scr, ident, kid, ptr)

        if DEBUG_STOP < 2:
            return
        if full:
            def b1(c):
                pB, kpB = proj(c * 128, 128)
                pH, kpH = proj(1024 + c * 128, 128)
                hAs, khAs = tf.next()
                S.op("act", lambda e, hAs=hAs, pH=pH: e.activation(out=hAs[:], in_=pH, func=AF.Copy), reads=[kpH], writes=[khAs])
                ub, kub = ubuf.next()
                S.op("pool", lambda e, ub=ub, c=c: e.tensor_copy(out=ub[:, 0:2], in_=uh[:, c, :]), reads=[kuh[c]], writes=[kub])
                S.op("dve", lambda e, ub=ub, pB=pB, hAs=hAs: e.tensor_tensor(out=ub[:, 2:W + 2], in0=pB, in1=hAs[:], op=ALU.mult),
                     reads=[kpB, khAs], writes=[kub])
                S.op("pool", lambda e, ub=ub, c=c: e.tensor_copy(out=uh[:, c, :], in_=ub[:, W:W + 2]), reads=[kub], writes=[kuh[c]])
                ta, kta = tf.next()
                wof = O_CAW + c * 3
                S.op("act", lambda e, ta=ta, ub=ub, wof=wof: e.activation(out=ta[:], in_=ub[:, 2:W + 2], func=AF.Copy, scale=cc[:, wof + 2:wof + 3]),
                     reads=[kub, kcc], writes=[kta])
                S.op("dve", lambda e, ta=ta, ub=ub, wof=wof: e.scalar_tensor_tensor(out=ta[:], in0=ub[:, 1:W + 1], scalar=cc[:, wof + 1:wof + 2], in1=ta[:],
                                                                                  op0=ALU.mult, op1=ALU.add), reads=[kub, kcc, kta], writes=[kta])
                S.op("dve", lambda e, ta=ta, ub=ub, wof=wof: e.scalar_tensor_tensor(out=ta[:], in0=ub[:, 0:W], scalar=cc[:, wof:wof + 1], in1=ta[:],
                                                                                  op0=ALU.mult, op1=ALU.add), reads=[kub, kcc, kta], writes=[kta])
                pC, kpC = proj(512 + c * 128, 128)
                S.op("dve", lambda e, ta=ta, pC=pC, c=c: e.tensor_tensor(out=yT[:, c, :], in0=pC, in1=ta[:], op=ALU.mult),
                     reads=[kpC, kta], writes=[kyT[c]])
            for c_ in range(4):
                b1(c_)

        if DEBUG_STOP < 3:
            return
        QC = 1536
        p_, kp_ = proj(QC + 12 * 128, 128)
        shift_lerp(p_, kp_, 12, wa[:], kwa)
        S.op("act", lambda e: e.activation(out=twad[0:64, :], in_=wa[0:64, :], func=AF.Tanh), reads=[kwa], writes=[ktwad])
        S.op("dve", lambda e: e.tensor_copy(out=twad[64:128, :], in_=wa[64:128, :]), reads=[kwa], writes=[ktwad])
        if full:
            p_, kp_ = proj(QC + 13 * 128, 128)
            shift_lerp(p_, kp_, 13, g0t[:], kg0)
            p_, kp_ = proj(None, 128, wsrc=wg1, kw=[kwg1] * 8)
            shift_lerp(p_[0:32, :], kp_, 14, g1t[0:32, :], kg1, np_=32)
            S.op("act", lambda e: e.activation(out=sgd0[:], in_=g0t[:], func=AF.Sigmoid), reads=[kg0], writes=[ksgd])
            S.op("act", lambda e: e.activation(out=sgd1[:], in_=g1t[:], func=AF.Sigmoid), reads=[kg1], writes=[ksgd])
        for c in range(4):
            S.op("pe", lambda e, c=c: e.matmul(psm[:, 0, :], lhsT=w2b[:, c * 128:(c + 1) * 128], rhs=twad[:], start=True, stop=True),
                 reads=[kw2b, ktwad], writes=[kpsm])
            S.op("pe", lambda e, c=c: e.matmul(psm[:, 1, :], lhsT=a2b[:, c * 128:(c + 1) * 128], rhs=twad[:], start=True, stop=True),
                 reads=[ka2b, ktwad], writes=[kpsm])
            S.op("act", lambda e, c=c: e.activation(out=sig[:, c, :], in_=psm[:, 0, :], func=AF.Sigmoid, bias=cc[:, O_W0 + c:O_W0 + c + 1]),
                 reads=[kpsm, kcc], writes=[ksig[c]])
            S.op("act", lambda e, c=c: e.activation(out=a4[:, c, :], in_=psm[:, 1, :], func=AF.Sigmoid, bias=cc[:, O_A0 + c:O_A0 + c + 1]),
                 reads=[kpsm, kcc], writes=[ka4[c]])

        if DEBUG_STOP < 4:
            return
        def b4(c):
            kt_, kkt = rkv["k"].next()
            vt_, kvt = rkv["v"].next()
            p_, kp_ = proj(QC + (4 + c) * 128, 128)
            shift_lerp(p_, kp_, 4 + c, kt_[:], kkt)
            p_, kp_ = proj(QC + (8 + c) * 128, 128)
            shift_lerp(p_, kp_, 8 + c, vt_[:], kvt)
            if full:
                rt_, krt = rkv["r"].next()
                p_, kp_ = proj(QC + c * 128, 128)
                shift_lerp(p_, kp_, c, rt_[:], krt)
            kkr, kkkr = tf.next()
            S.op("pool", lambda e, kkr=kkr, kt_=kt_, c=c: e.tensor_scalar(out=kkr[:], in0=kt_[:], scalar1=cc[:, O_KK + c:O_KK + c + 1], scalar2=1.0, op0=ALU.mult, op1=ALU.mult),
                 reads=[kkt, kcc], writes=[kkkr])
            sq, ksq = tb.next()
            S.op("act", lambda e, sq=sq, kkr=kkr: e.activation(out=sq[:], in_=kkr[:], func=AF.Square), reads=[kkkr], writes=[ksq])
            S.op("pe", lambda e, sq=sq: e.matmul(psm[:, 0, :], lhsT=bones[:], rhs=sq[:], start=True, stop=True), reads=[kbones, ksq], writes=[kpsm])
            nrm, knrm = tf.next()
            S.op("act", lambda e, nrm=nrm: e.activation(out=nrm[:], in_=psm[:, 0, :], func=AF.Sqrt, bias=1e-24), reads=[kpsm], writes=[knrm])
            S.op("dve", lambda e, nrm=nrm: e.reciprocal(out=nrm[:], in_=nrm[:]), reads=[knrm], writes=[knrm])
            kk, kkk = tf.next()
            S.op("dve", lambda e, kk=kk, kkr=kkr, nrm=nrm: e.tensor_tensor(out=kk[:], in0=kkr[:], in1=nrm[:], op=ALU.mult), reads=[kkkr, knrm], writes=[kkk])
            cs, kcs = tf.next()
            S.op("dve", lambda e, cs=cs, c=c: e.tensor_tensor_scan(out=cs[:], data0=rst[:], data1=sig[:, c, :], initial=0.0, op0=ALU.mult, op1=ALU.add),
                 reads=[kconst, ksig[c]], writes=[kcs])
            E1, kE1 = tf.next()
            E2, kE2 = tf.next()
            E3, kE3 = tf.next()
            dd, kdd = tf.next()
            S.op("act", lambda e, E1=E1, cs=cs: e.activation(out=E1[:], in_=cs[:], func=AF.Exp, scale=-C05), reads=[kcs], writes=[kE1])
            S.op("act", lambda e, E2=E2, cs=cs: e.activation(out=E2[:], in_=cs[:], func=AF.Exp, scale=C05), reads=[kcs], writes=[kE2])
            S.op("pool", lambda e, dd=dd, cs=cs, c=c: e.tensor_tensor(out=dd[:], in0=cs[:], in1=sig[:, c, :], op=ALU.subtract), reads=[kcs, ksig[c]], writes=[kdd])
            S.op("act", lambda e, E3=E3, dd=dd: e.activation(out=E3[:], in_=dd[:], func=AF.Exp, scale=-C05), reads=[kdd], writes=[kE3])
            S.op("pool", lambda e, E1=E1, c=c: e.tensor_copy(out=PC[:, c, :], in_=E1[:].rearrange("p (n t) -> p n t", n=NCH)[:, :, CH - 1]),
                 reads=[kE1], writes=[kPC[c]])
            mm, kmm = tf.next()
            S.op("pool", lambda e, mm=mm, c=c: e.tensor_scalar(out=mm[:], in0=a4[:, c, :], scalar1=cc[:, O_KA + c:O_KA + c + 1], scalar2=dc[:, 15 + c:16 + c],
                                                               op0=ALU.mult, op1=ALU.add), reads=[ka4[c], kcc, kdc], writes=[kmm])
            kp, kkp = tf.next()
            S.op("dve", lambda e, kp=kp, kt_=kt_, mm=mm: e.tensor_tensor(out=kp[:], in0=kt_[:], in1=mm[:], op=ALU.mult), reads=[kkt, kmm], writes=[kkp])
            akk, kakk = tf.next()
            S.op("pool", lambda e, akk=akk, kk=kk, c=c: e.tensor_tensor(out=akk[:], in0=a4[:, c, :], in1=kk[:], op=ALU.mult), reads=[ka4[c], kkk], writes=[kakk])
            for h in range(2):
                S.op("dve", lambda e, h=h, kk=kk, E3=E3, c=c: e.scalar_tensor_tensor(out=ARbd[h * 64:(h + 1) * 64, c, :, 0, h * 64:(h + 1) * 64], in0=v3(kk, h), scalar=-1.0,
                                                                                   in1=v3(E3, h), op0=ALU.mult, op1=ALU.mult),
                     reads=[kkk, kE3], writes=[kAbd[c]])
                S.op("dve", lambda e, h=h, akk=akk, E2=E2, c=c: e.tensor_tensor(out=bd_view(Bbd, c, h), in0=v3(akk, h), in1=v3(E2, h), op=ALU.mult),
                     reads=[kakk, kE2], writes=[kBbd[c]])
                S.op("pool", lambda e, h=h, kp=kp, E2=E2, c=c: e.tensor_tensor(out=bd_view(Kbd, c, h), in0=v3(kp, h), in1=v3(E2, h), op=ALU.mult),
                     reads=[kkp, kE2], writes=[kKbd[c]])
                S.op("act", lambda e, h=h, vt_=vt_, c=c: e.activation(out=bd_view(Vbd, c, h), in_=v3(vt_, h), func=AF.Copy), reads=[kvt], writes=[kVbd[c]])
                if full:
                    S.op("pool", lambda e, h=h, rt_=rt_, E1=E1, c=c: e.tensor_tensor(out=ARbd[h * 64:(h + 1) * 64, c, :, 1, h * 64:(h + 1) * 64], in0=v3(rt_, h), in1=v3(E1, h),
                                                                                   op=ALU.mult), reads=[krt, kE1], writes=[kRbd[c]])
            if full:
                rk, krk = tf.next()
                S.op("pool", lambda e, rk=rk, rt_=rt_, kp=kp: e.tensor_tensor(out=rk[:], in0=rt_[:], in1=kp[:], op=ALU.mult), reads=[krt, kkp], writes=[krk])
                rkb, krkb = tb.next()
                S.op("dve", lambda e, rkb=rkb, rk=rk, c=c: e.tensor_scalar(out=rkb[:], in0=rk[:], scalar1=cc[:, O_RK + c:O_RK + c + 1], scalar2=1.0, op0=ALU.mult, op1=ALU.mult),
                     reads=[krk, kcc], writes=[krkb])
                S.op("pe", lambda e, rkb=rkb: e.matmul(psm[:, 1, :], lhsT=bones[:], rhs=rkb[:], start=True, stop=True), reads=[kbones, krkb], writes=[kpsm])
                S.op("dve", lambda e, vt_=vt_, c=c: e.tensor_tensor(out=bonus[:, c, :], in0=psm[:, 1, :], in1=vt_[:], op=ALU.mult), reads=[kpsm, kvt], writes=[kbonus[c]])

        for c_ in range(4):
            b4(c_)
        if DEBUG_STOP < 5:
            return

        PA3 = PA[:].rearrange("p (a w) -> p a w", a=2)
        PB3 = PB[:].rearrange("p (a w) -> p a w", a=2)
        PAd = PA[:].rearrange("p (a w) -> p a w", a=4)
        PBt = PB[:, 0:512].rearrange("p (a w) -> p a w", a=4)
        PQ0 = PB[:, 512:768].rearrange("p (a w) -> p a w", a=4)
        PQ1 = PB[:, 768:1024].rearrange("p (a w) -> p a w", a=4)
        def chunk(n):
            for pi in range(2):
                for i in range(2):
                    c = pi * 2 + i
                    rhsAR = ARbd[:, c, n, :, :].rearrange("p a w -> p (a w)")
                    S.op("pe", lambda e, i=i, c=c, rhsAR=rhsAR: e.matmul(PA3[:, i, 0:256], lhsT=Bbd[:, c, n, :], rhs=rhsAR, start=True, stop=True),
                         reads=[kBbd[c], kAbd[c], kRbd[c]], writes=[kPA])
                    S.op("pe", lambda e, i=i, c=c, rhsAR=rhsAR: e.matmul(PA3[:, i, 256:512], lhsT=Kbd[:, c, n, :], rhs=rhsAR, start=True, stop=True),
                         reads=[kKbd[c], kAbd[c], kRbd[c]], writes=[kPA])
                    S.op("pe", lambda e, i=i, c=c: e.matmul(PB3[:, i, 0:128], lhsT=ARbd[:, c, n, 0, :], rhs=Bbd[:, c, n, :], start=True, stop=True),
                         reads=[kAbd[c], kBbd[c]], writes=[kPB0, kPB1])
                    S.op("pe", lambda e, i=i, c=c: e.matmul(PB3[:, i, 128:256], lhsT=Bbd[:, c, n, :], rhs=ident[:], start=True, stop=True),
                         reads=[kBbd[c], kid], writes=[kPB0, kPB1])
                    S.op("pe", lambda e, i=i, c=c: e.matmul(PB3[:, i, 256:384], lhsT=Kbd[:, c, n, :], rhs=ident[:], start=True, stop=True),
                         reads=[kKbd[c], kid], writes=[kPB0, kPB1])
                    S.op("pe", lambda e, i=i, c=c: e.matmul(PB3[:, i, 384:448], lhsT=Vbd[:, c, n, :], rhs=fst[:], start=True, stop=True),
                         reads=[kVbd[c], kfst], writes=[kPB0, kPB1])
                if DEBUG_STOP == 5 and DEBUG_SUB < 1:
                    continue
                S.op("dve", lambda e, pi=pi: e.tensor_tensor(out=M1[:, 2 * pi:2 * pi + 2, :], in0=PA3, in1=msu4[:].unsqueeze(1).to_broadcast([128, 2, 512]), op=ALU.mult),
                     reads=[kPA] + kmsu4, writes=[kM1])
                if DEBUG_STOP == 5 and DEBUG_SUB < 2:
                    continue
                S.op("dve", lambda e, pi=pi: e.tensor_tensor(out=NT0[:, 2 * pi:2 * pi + 2, :], in0=PB3[:, :, 0:128], in1=msl[:].unsqueeze(1).to_broadcast([128, 2, 128]), op=ALU.mult),
                     reads=[kPB0, kPB1, kmsl], writes=[kNT0])
                if DEBUG_STOP == 5 and DEBUG_SUB < 3:
                    continue
                S.op("act", lambda e, pi=pi: e.activation(out=M2[:, 2 * pi:2 * pi + 2, :], in_=PB3[:, :, 128:448], func=AF.Copy), reads=[kPB0, kPB1], writes=[kM2])
            if DEBUG_STOP < 6:
                return
            Tcur, kTcur = Tb.next()
            S.op("pool", lambda e, Tcur=Tcur: e.tensor_tensor(out=Tcur[:], in0=M1[:, :, 0:128], in1=ident[:].unsqueeze(1).to_broadcast([128, 4, 128]), op=ALU.add),
                 reads=[kM1, kid], writes=[kTcur])
            Nprev = lambda c: M1[:, c, 0:128]
            NTprev = lambda c: NT0[:, c, :]
            kprev = [kM1, kNT0]
            for j in range(1, 6):
                NNj, kNNj = NN.next()
                for c in range(4):
                    if j < 5:
                        S.op("pe", lambda e, c=c, Nprev=Nprev, NTprev=NTprev: e.matmul(PAd[:, c, 0:128], lhsT=NTprev(c), rhs=Nprev(c), start=True, stop=True),
                             reads=kprev, writes=[kPA])
                    S.op("pe", lambda e, c=c, Nprev=Nprev, NTprev=NTprev: e.matmul(PAd[:, c, 128:256], lhsT=Nprev(c), rhs=NTprev(c), start=True, stop=True),
                         reads=kprev, writes=[kPA])
                if j < 5:
                    S.op("act", lambda e, NNj=NNj: e.activation(out=NNj[:], in_=PAd, func=AF.Copy), reads=[kPA], writes=[kNNj])
                else:
                    S.op("act", lambda e, NNj=NNj: e.activation(out=NNj[:, :, 128:256], in_=PAd[:, :, 128:256], func=AF.Copy), reads=[kPA], writes=[kNNj])
                for c in range(4):
                    S.op("pe", lambda e, c=c, NNj=NNj, Tcur=Tcur: e.matmul(PBt[:, c, :], lhsT=NNj[:, c, 128:256], rhs=Tcur[:, c, :], start=True, stop=True),
                         reads=[kNNj, kTcur], writes=[kPB0])
                Tnew, kTnew = Tb.next()
                S.op("dve", lambda e, Tnew=Tnew, Tcur=Tcur: e.tensor_tensor(out=Tnew[:], in0=PBt, in1=Tcur[:], op=ALU.add), reads=[kPB0, kTcur], writes=[kTnew])
                Tcur, kTcur = Tnew, kTnew
                Nprev = (lambda NNj: (lambda c: NNj[:, c, 0:128]))(NNj)
                NTprev = (lambda NNj: (lambda c: NNj[:, c, 128:256]))(NNj)
                kprev = [kNNj]
            if DEBUG_STOP < 7:
                return
            for c in range(4):
                S.op("pe", lambda e, c=c: e.matmul(PQ0[:, c, :], lhsT=ARbd[:, c, n, 0, :], rhs=Sbf[:, c, :], start=True, stop=False),
                     reads=[kAbd[c], kSbf], writes=[kPB1])
                S.op("pe", lambda e, c=c: e.matmul(PQ0[:, c, :], lhsT=M1[:, c, 256:384], rhs=M2[:, c, 256:320], start=False, stop=True),
                     reads=[kM1, kM2], writes=[kPB1])
            S.op("act", lambda e: e.activation(out=Zb[:], in_=PQ0, func=AF.Copy), reads=[kPB1], writes=[kZb])
            for c in range(4):
                S.op("pe", lambda e, c=c, Tcur=Tcur: e.matmul(PQ1[:, c, :], lhsT=Tcur[:, c, :], rhs=Zb[:, c, :], start=True, stop=True),
                     reads=[kTcur, kZb], writes=[kPB1])
            S.op("act", lambda e: e.activation(out=Ub[:], in_=PQ1, func=AF.Copy), reads=[kPB1], writes=[kUb])
            for c in range(4):
                S.op("pe", lambda e, c=c: e.matmul(PQ0[:, c, :], lhsT=M2[:, c, 0:128], rhs=Ub[:, c, :], start=True, stop=False),
                     reads=[kM2, kUb], writes=[kPB1])
                S.op("pe", lambda e, c=c: e.matmul(PQ0[:, c, :], lhsT=M2[:, c, 128:256], rhs=M2[:, c, 256:320], start=False, stop=True),
                     reads=[kM2], writes=[kPB1])
            if full:
                for c in range(4):
                    S.op("pe", lambda e, c=c: e.matmul(PQ1[:, c, :], lhsT=ARbd[:, c, n, 1, :], rhs=Sbf[:, c, :], start=True, stop=False),
                         reads=[kRbd[c], kSbf], writes=[kPB1])
                    S.op("pe", lambda e, c=c: e.matmul(PQ1[:, c, :], lhsT=M1[:, c, 128:256], rhs=Ub[:, c, :], start=False, stop=False),
                         reads=[kM1, kUb], writes=[kPB1])
                    S.op("pe", lambda e, c=c: e.matmul(PQ1[:, c, :], lhsT=M1[:, c, 384:512], rhs=M2[:, c, 256:320], start=False, stop=True),
                         reads=[kM1, kM2], writes=[kPB1])
            pcb = PC[:, :, n:n + 1].to_broadcast([128, 4, 64])
            S.op("dve", lambda e: e.tensor_tensor(out=tmpS[:], in0=PQ0, in1=S32[:], op=ALU.add), reads=[kPB1, kS32], writes=[ktmpS])
            S.op("dve", lambda e, pcb=pcb: e.tensor_tensor(out=Sbf[:], in0=tmpS[:], in1=pcb, op=ALU.mult), reads=[ktmpS] + kPC, writes=[kSbf])
            S.op("pool", lambda e, pcb=pcb: e.tensor_tensor(out=S32[:], in0=tmpS[:], in1=pcb, op=ALU.mult), reads=[ktmpS] + kPC, writes=[kS32])
            if full:
                g_, kg_ = gst.next()
                S.op("dve", lambda e, g_=g_: e.tensor_reduce(out=g_[:, 0, :], in_=PQ1, axis=AX.X, op=ALU.add), reads=[kPB1], writes=[kg_])
                S.op("act", lambda e: e.activation(out=ysq[:], in_=PQ1, func=AF.Square), reads=[kPB1], writes=[kysq])
                S.op("dve", lambda e, g_=g_: e.tensor_reduce(out=g_[:, 1, :], in_=ysq[:], axis=AX.X, op=ALU.add), reads=[kysq], writes=[kg_])
                S.op("dve", lambda e, g_=g_: e.tensor_scalar(out=g_[:, 2, :], in0=g_[:, 0, :], scalar1=1.0 / 64, scalar2=1.0, op0=ALU.mult, op1=ALU.mult), reads=[kg_], writes=[kg_])
                S.op("dve", lambda e, g_=g_: e.tensor_tensor(out=g_[:, 3, :], in0=g_[:, 2, :], in1=g_[:, 2, :], op=ALU.mult), reads=[kg_], writes=[kg_])
                S.op("dve", lambda e, g_=g_: e.scalar_tensor_tensor(out=g_[:, 4, :], in0=g_[:, 1, :], scalar=1.0 / 64, in1=g_[:, 3, :], op0=ALU.mult, op1=ALU.subtract),
                     reads=[kg_], writes=[kg_])
                S.op("act", lambda e, g_=g_: e.activation(out=g_[:, 5, :], in_=g_[:, 4, :], func=AF.Sqrt, bias=GN_EPS), reads=[kg_], writes=[kg_])
                S.op("dve", lambda e, g_=g_: e.reciprocal(out=g_[:, 6, :], in_=g_[:, 5, :]), reads=[kg_], writes=[kg_])
                S.op("dve", lambda e, g_=g_: e.tensor_tensor(out=ycen[:], in0=PQ1, in1=g_[:, 2, :].unsqueeze(2).to_broadcast([128, 4, 64]), op=ALU.subtract),
                     reads=[kPB1, kg_], writes=[kycen])
                for h in range(2):
                    hs = slice(h * 64, (h + 1) * 64)
                    S.op("dve", lambda e, hs=hs, g_=g_: e.tensor_tensor(out=ynbd[hs, :, hs], in0=ycen[hs, :, :], in1=g_[hs, 6, :].unsqueeze(2).to_broadcast([64, 4, 64]),
                                                                       op=ALU.mult), reads=[kycen, kg_], writes=[kynbd])
                for c in range(4):
                    S.op("pe", lambda e, c=c: e.matmul(PQ0[:, c, :], lhsT=ynbd[:, c, :], rhs=fst[:], start=True, stop=True), reads=[kynbd, kfst], writes=[kPB1])
                S.op("act", lambda e: e.activation(out=ynf[:, :, n * CH:(n + 1) * CH], in_=PQ0, func=AF.Copy), reads=[kPB1], writes=kynf)

        for n_ in range(NCH):
            chunk(n_)
        if DEBUG_STOP < 8:
            return

        if not full:
            if t + 2 < n_tiles:
                load_x(t + 2)
            return
        for c in range(4):
            S.op("pe", lambda e, c=c: e.matmul(psm[:, 0, :], lhsT=g2b0[:, c * 128:(c + 1) * 128], rhs=sgd0[:], start=True, stop=False), reads=[kg2b0, ksgd], writes=[kpsm])
            S.op("pe", lambda e, c=c: e.matmul(psm[:, 0, :], lhsT=g2b1[:, c * 128:(c + 1) * 128], rhs=sgd1[:], start=False, stop=True), reads=[kg2b1, ksgd], writes=[kpsm])
            y1, ky1 = tf.next()
            S.op("dve", lambda e, c=c, y1=y1: e.scalar_tensor_tensor(out=y1[:], in0=ynf[:, c, :], scalar=cc[:, O_LW + c:O_LW + c + 1], in1=bonus[:, c, :], op0=ALU.mult, op1=ALU.add),
                 reads=[kynf[c], kcc, kbonus[c]], writes=[ky1])
            S.op("dve", lambda e, c=c, y1=y1: e.scalar_tensor_tensor(out=yT[:, 4 + c, :], in0=y1[:], scalar=cc[:, O_LB + c:O_LB + c + 1], in1=psm[:, 0, :], op0=ALU.add, op1=ALU.mult),
                 reads=[ky1, kcc, kpsm], writes=[kyT[4 + c]])
        if DEBUG_STOP < 9:
            return
        for blk in range(NBLK):
            for hf in range(2):
                pflat = pp[hf][:].rearrange("p a w -> p (a w)")
                for e_ in range(8):
                    S.op("pe", lambda e, e_=e_, hf=hf, pflat=pflat, blk=blk: e.matmul(pflat, lhsT=yT[:, e_, blk * 128:(blk + 1) * 128], rhs=wout[:, e_, hf * 512:(hf + 1) * 512],
                                                                                      start=(e_ == 0), stop=(e_ == 7)), reads=[kyT[e_], kwout], writes=[kpp[hf]])
            class _V:
                def __init__(self, ap): self.ap = ap
                def __getitem__(self, k): return self.ap
            _post_norm_residual(S, [(_V(pp[0][:].rearrange("p a w -> p (a w)")), kpp[0]), (_V(pp[1][:].rearrange("p a w -> p (a w)")), kpp[1])],
                                xt[:, blk, :], kxts[b][blk], grow, kgrow, scr)
        ht_i = t - n_pre
        S.op("sp", lambda e, xt=xt, ht_i=ht_i: e.dma_start(out=hscr[W * ht_i:W * (ht_i + 1), :].rearrange("(b p) d -> p b d", p=128), in_=xt[:]),
             reads=kxts[b], writes=[khscr[ht_i]], dma_sem=dsem[f"h{b}"])
        if t + 2 < n_tiles:
            load_x(t + 2)

    for t_i in range(n_tiles):
        do_tile(t_i)


def _host_consts():
    km = np.zeros((128, NKM), np.float32)
    km[:, M_ID:M_ID + 128] = np.eye(128, dtype=np.float32)
    idx = np.arange(128)
    same = (idx[:, None] // 64) == (idx[None, :] // 64)
    s, t = idx[:, None] % 64, idx[None, :] % 64
    km[:, M_SU:M_SU + 128] = (same & (s < t)).astype(np.float32)
    km[:, M_IU:M_IU + 128] = (same & (s <= t)).astype(np.float32)
    km[:, M_SL:M_SL + 128] = (same & (s > t)).astype(np.float32)
    km[:, M_BO:M_BO + 128] = same.astype(np.float32)
    km[:, M_F:M_F + 64] = (idx[:, None] % 64 == np.arange(64)[None, :]).astype(np.float32)
    rst = np.ones((128, 256), np.float32)
    rst[:, ::64] = 0.0
    km[:, M_RST:M_RST + 256] = rst
    return km


def _pack_cc(inp, hmask):
    cc = np.zeros((128, NCC), np.float32)
    col = lambda v, n: np.ascontiguousarray(np.asarray(v, np.float32).reshape(n, 128).T)
    cc[:, O_PMG:O_PMG + 8] = col(inp["pre_mix_g"][0], 8)
    cc[:, O_PFG:O_PFG + 8] = col(inp["pre_ffn_g"][0], 8)
    caw = np.asarray(inp["conv_a_w"][0], np.float32)
    cc[:, O_CAW:O_CAW + 12] = caw.T.reshape(4, 128, 3).transpose(1, 0, 2).reshape(128, 12)
    mu = np.zeros(1920, np.float32)
    mu[:1824] = np.asarray(inp["shift_mu"][0], np.float32)
    cc[:, O_MU:O_MU + 15] = col(mu, 15)
    for off, name in ((O_W0, "w0"), (O_A0, "a0"), (O_KK, "k_k"), (O_KA, "k_a"), (O_LW, "lnx_w"), (O_LB, "lnx_b")):
        cc[:, off:off + 4] = col(inp[name][0], 4)
    cc[:, O_RK:O_RK + 4] = col(np.asarray(inp["r_k"][0], np.float32).reshape(512), 4)
    fcw = np.asarray(inp["ffn_conv_w"][0], np.float32)
    cc[:, O_FCW:O_FCW + 132] = fcw.T.reshape(44, 128, 3).transpose(1, 0, 2).reshape(128, 132)
    cc[:, O_FCB:O_FCB + 44] = col(inp["ffn_conv_b"][0], 44)
    cc[:, O_HM] = hmask
    return cc


_NC_CACHE = {}


def kernel(**inputs):
    n_pre, n_main = 15, 16
    x = np.asarray(inputs["x"], np.float32)
    B, T, _ = x.shape
    half = T // 2
    if "full" not in _NC_CACHE:
        _NC_CACHE["full"] = build(n_pre, n_main, "full")
    nc = _NC_CACHE["full"]
    km = _host_consts()
    f = lambda n: np.ascontiguousarray(np.asarray(inputs[n], np.float32)[0])
    in_maps = []
    for c in range(8):
        b, h = c // 2, c % 2
        xin = np.zeros((T, D), np.float32)
        if h == 0:
            xin[half:] = x[b, :half]
        else:
            xin[:] = x[b]
        in_maps.append({
            "xin": xin, "cc": _pack_cc(inputs, float(h)), "km": km,
            "post_mix_g": f("post_mix_g"), "post_ffn_g": f("post_ffn_g"),
            "w_in": f("w_in"), "w_out": f("w_out"), "w_up": f("w_up"), "w_down": f("w_down"),
            "w2": f("w2"), "a2": f("a2"), "g2": f("g2"),
        })
    res = run_bass_kernel_spmd(nc, in_maps, core_ids=list(range(8)))
    out = np.zeros((B, T, D), np.float32)
    for c in range(8):
        b, h = c // 2, c % 2
        out[b, h * half:(h + 1) * half] = res.results[c]["out"]
    return out
```

```python
import contextlib
import numpy as np
import concourse.bass as bass
import concourse.mybir as mybir
from concourse.bass_utils import run_bass_kernel_spmd

F32 = mybir.dt.float32
BF16 = mybir.dt.bfloat16
AF = mybir.ActivationFunctionType
ALU = mybir.AluOpType
AX = mybir.AxisListType

D = 1024
W = 256
NBLK = 2
CH = 64
NCH = W // CH
INC = 3360
DFF = 2816
NPAIR = 22
RMS_EPS = 1e-6
GN_EPS = 64 * 1e-5
EPOCH = 12000
DEBUG_SUB = 99
DEBUG_STOP = 99

O_PMG, O_PFG, O_CAW, O_MU, O_W0, O_A0, O_KK, O_KA, O_RK, O_LW, O_LB, O_FCW, O_FCB, O_HM = (
    0, 8, 16, 28, 43, 47, 51, 55, 59, 63, 67, 71, 203, 247)
NCC = 248
M_ID, M_SU, M_IU, M_SL, M_BO, M_F, M_RST = 0, 128, 256, 384, 512, 640, 704
NKM = 704 + 256


class Key:
    __slots__ = ("name", "writer", "readers", "excl")

    def __init__(self, name, excl=False):
        self.name = name
        self.writer = None
        self.readers = []
        self.excl = excl


def PKey(name):
    return Key(name, excl=True)


class Sched:
    ENGS = ("pe", "act", "dve", "pool", "sp")

    def __init__(self, nc, sem_stack, prefix):
        self.nc = nc
        self.sem_stack = sem_stack
        self.prefix = prefix
        self.ops = {e: [] for e in self.ENGS}
        self.count = {e: 0 for e in self.ENGS}
        self.sems = {}
        self.waited = {e: {} for e in self.ENGS}
        self.dma_counts = {}
        self.last_tok = {e: None for e in self.ENGS}

    def _eng_sem(self, eng, idx):
        sid = f"{self.prefix}s_{eng}_{idx // EPOCH}"
        self.sems.setdefault(sid, None)
        return sid, (idx % EPOCH) + 1

    def new_dma_sem(self, name):
        sid = f"{self.prefix}d_{name}"
        assert sid not in self.sems, sid
        self.sems[sid] = None
        self.dma_counts[sid] = 0
        return sid

    def _need_waits(self, eng, tokens):
        w = self.waited[eng]
        best = {}
        for t in tokens:
            if t is None:
                continue
            sid, val, _ = t
            if w.get(sid, 0) >= val:
                continue
            if best.get(sid, 0) < val:
                best[sid] = val
        for sid, val in best.items():
            w[sid] = val
        return list(best.items())

    def op(self, eng, fn, reads=(), writes=(), dma_sem=None):
        toks = []
        raw = set()
        for k in reads:
            toks.append(k.writer)
            if k.writer is not None:
                raw.add(k.writer)
            if k.excl:
                toks.extend(r for r in k.readers if r[2] != eng)
        for k in writes:
            toks.append(k.writer)
            toks.extend(k.readers)
        if eng == "pe":
            toks = [t for t in toks if t is not None and t[2] != "pe"]
        waits = self._need_waits(eng, toks)
        if dma_sem is None:
            idx = self.count[eng]
            self.count[eng] += 1
            sid, val = self._eng_sem(eng, idx)
            tok = (sid, val, eng)
            inc = (sid, 1)
            self.last_tok[eng] = tok
        else:
            self.dma_counts[dma_sem] += 16
            tok = (dma_sem, self.dma_counts[dma_sem], "dma")
            inc = (dma_sem, 16)
        self.ops[eng].append((fn, waits, inc))
        for k in reads:
            k.readers.append(tok)
        for k in writes:
            k.writer = tok
            k.readers = []
        return tok

    def barrier(self, extra_keys=()):
        toks = [t for t in self.last_tok.values() if t is not None]
        for k in extra_keys:
            toks.append(k.writer)
            toks.extend(k.readers)
        for eng in self.ENGS:
            waits = self._need_waits(eng, [t for t in toks if t is not None and t[2] != eng])
            if waits:
                self.ops[eng].append((None, waits, None))

    def final_wait(self, eng, keys):
        toks = []
        for k in keys:
            toks.append(k.writer)
            toks.extend(k.readers)
        waits = self._need_waits(eng, toks)
        self.ops[eng].append((None, waits, None))

    def emit(self):
        nc = self.nc
        with contextlib.ExitStack() as st:
            handles = {sid: self.sem_stack.enter_context(nc.semaphore(sid)) for sid in self.sems}
            block = st.enter_context(nc.Block())

            def run(engobj, lst):
                for fn, waits, inc in lst:
                    for sid, val in waits:
                        engobj.wait_ge(handles[sid], val)
                    if fn is not None:
                        fn(engobj).then_inc(handles[inc[0]], inc[1])

            @block.tensor
            def _(e):
                run(e, self.ops["pe"])

            @block.scalar
            def _(e):
                run(e, self.ops["act"])

            @block.vector
            def _(e):
                run(e, self.ops["dve"])

            @block.gpsimd
            def _(e):
                run(e, self.ops["pool"])

            @block.sync
            def _(e):
                run(e, self.ops["sp"])


class Rot:
    def __init__(self, tiles, excl=False):
        self.tiles = tiles
        self.keys = [Key(f"rot{i}", excl) for i in range(len(tiles))]
        self.i = 0

    def next(self):
        j = self.i % len(self.tiles)
        self.i += 1
        return self.tiles[j], self.keys[j]


def _rms_transpose(S, src, ksrc, dstT, kdst, tcol, scr, ident, kid, ptr, extra_scale=None, kextra=None):
    st, kst = scr["stat"].next()
    xs, kxs = scr["xs"].next()
    pt, kpt = ptr.next()
    S.op("act", lambda e: e.activation(out=xs[:], in_=src, func=AF.Square, accum_out=st[:, 0:1]),
         reads=[ksrc], writes=[kxs, kst])
    S.op("act", lambda e: e.activation(out=st[:, 1:2], in_=st[:, 0:1], func=AF.Sqrt, scale=1.0 / D, bias=RMS_EPS),
         reads=[kst], writes=[kst])
    S.op("dve", lambda e: e.reciprocal(out=st[:, 2:3], in_=st[:, 1:2]), reads=[kst], writes=[kst])
    rs = st[:, 2:3]
    if extra_scale is not None:
        S.op("dve", lambda e: e.tensor_tensor(out=st[:, 3:4], in0=st[:, 2:3], in1=extra_scale, op=ALU.mult),
             reads=[kst, kextra], writes=[kst])
        rs = st[:, 3:4]
    S.op("pool", lambda e: e.tensor_scalar(out=xs[:], in0=src, scalar1=rs, scalar2=1.0, op0=ALU.mult, op1=ALU.mult),
         reads=[ksrc, kst], writes=[kxs])
    for kc in range(8):
        S.op("pe", lambda e, kc=kc: e.transpose(out=pt[:, kc, :], in_=xs[:, kc * 128:(kc + 1) * 128], identity=ident[:]),
             reads=[kxs, kid], writes=[kpt])
    S.op("act", lambda e: e.activation(out=dstT[:, :, tcol:tcol + 128], in_=pt[:], func=AF.Copy),
         reads=[kpt], writes=[kdst])


def _post_norm_residual(S, pd_pairs, res, kres, grow, kgrow, scr):
    st, kst = scr["stat"].next()
    tmps = [scr["tmp512"].next() for _ in range(2)]
    for hf, (pd, kpd) in enumerate(pd_pairs):
        junk, kjunk = tmps[hf]
        S.op("act", lambda e, pd=pd, hf=hf, junk=junk: e.activation(out=junk[:], in_=pd[:], func=AF.Square,
                                                                  accum_out=st[:, hf:hf + 1]),
             reads=[kpd], writes=[kjunk, kst])
    S.op("dve", lambda e: e.tensor_tensor(out=st[:, 2:3], in0=st[:, 0:1], in1=st[:, 1:2], op=ALU.add), reads=[kst], writes=[kst])
    S.op("act", lambda e: e.activation(out=st[:, 3:4], in_=st[:, 2:3], func=AF.Sqrt, scale=1.0 / D, bias=RMS_EPS),
         reads=[kst], writes=[kst])
    S.op("dve", lambda e: e.reciprocal(out=st[:, 4:5], in_=st[:, 3:4]), reads=[kst], writes=[kst])
    for hf, (pd, kpd) in enumerate(pd_pairs):
        tmp, ktmp = tmps[hf]
        S.op("dve", lambda e, pd=pd, hf=hf, tmp=tmp: e.scalar_tensor_tensor(
            out=tmp[:], in0=pd[:], scalar=st[:, 4:5], in1=grow[:, hf * 512:(hf + 1) * 512], op0=ALU.mult, op1=ALU.mult),
            reads=[kpd, kst, kgrow], writes=[ktmp])
        S.op("pool", lambda e, hf=hf, tmp=tmp: e.tensor_tensor(out=res[:, hf * 512:(hf + 1) * 512], in0=res[:, hf * 512:(hf + 1) * 512],
                                                               in1=tmp[:], op=ALU.add),
             reads=[ktmp, kres], writes=[kres])


def phase2_ffn(nc, S, st, io, n_main, shared):
    sb = lambda n, s, d: st.enter_context(nc.sbuf_tensor(n, s, d))
    ps = lambda n, s, d: st.enter_context(nc.psum_tensor(n, s, d))
    cc, kcc = shared["cc"], shared["kcc"]
    ident, kid = shared["ident"], shared["kid"]
    hscr, khscr = io["hscr"], io["khscr"]
    out = io["out"]

    wup = sb("wup", [128, 8, DFF * 2], BF16)
    wdn = sb("wdn", [128, NPAIR, D], BF16)
    kwup = [Key(f"wup{k}") for k in range(8)]
    kwdn = Key("wdn")
    grow = sb("grow2", [128, D], F32)
    kgrow = Key("grow2")
    fh = sb("fh", [128, NPAIR, 2, 2], F32)
    kfh = [Key(f"fh{i}") for i in range(NPAIR)]
    hts = [sb(f"ht{i}", [128, NBLK, D], F32) for i in range(2)]
    khts = [[Key(f"ht{i}_{b}") for b in range(NBLK)] for i in range(2)]
    hnTs = [sb(f"hnT{i}", [128, 8, W], BF16) for i in range(2)]
    khnT = [Key(f"hnT{i}") for i in range(2)]
    act = sb("actb", [128, NPAIR, W], BF16)
    kact = [Key(f"act{i}") for i in range(NPAIR)]
    scr = {
        "stat": Rot([sb(f"stat{i}", [128, 8], F32) for i in range(4)]),
        "xs": Rot([sb(f"xs{i}", [128, D], BF16) for i in range(2)]),
        "tmp512": Rot([sb(f"tmp512_{i}", [128, 512], F32) for i in range(2)]),
    }
    fbuf = Rot([sb(f"fbuf{i}", [128, 2, W + 2], F32) for i in range(2)])
    cg = Rot([sb(f"cg{i}", [128, W], F32) for i in range(2)])
    cu = Rot([sb(f"cu{i}", [128, W], F32) for i in range(2)])
    t1 = Rot([sb(f"t1_{i}", [128, W], F32) for i in range(2)])
    t2 = Rot([sb(f"t2_{i}", [128, W], F32) for i in range(2)])
    sg = Rot([sb(f"sg{i}", [128, W], F32) for i in range(2)])
    WQ = DFF // 4
    wst = Rot([sb(f"wst{i}", [128, WQ], F32) for i in range(2)])
    ptr = Rot([ps("ptr2", [128, 8, 128], BF16)], excl=True)
    pf = Rot([ps(f"pf{i}", [128, 2, W], F32) for i in range(3)], excl=True)
    pd = [[ps(f"pd{b}{h}", [128, 512], F32) for h in range(2)] for b in range(NBLK)]
    kpd = [[PKey(f"pd{b}{h}") for h in range(2)] for b in range(NBLK)]
    dsem = {n: S.new_dma_sem("p2_" + n) for n in ("wst0", "wst1", "wdn", "grow", "h0", "h1", "hw", "out0", "out1")}

    S.op("sp", lambda e: e.dma_start(out=grow[:], in_=io["post_ffn_g"].partition_broadcast(128)), writes=[kgrow], dma_sem=dsem["grow"])
    for kc in range(8):
        for q in range(8):
            j = kc * 8 + q
            wt, kwt = wst.next()
            S.op("sp", lambda e, wt=wt, kc=kc, q=q: e.dma_start(out=wt[:], in_=io["w_up"][kc * 128:(kc + 1) * 128, q * WQ:(q + 1) * WQ]),
                 writes=[kwt], dma_sem=dsem[f"wst{j % 2}"])
            eng = ("act", "dve", "pool")[j % 3]
            if eng == "act":
                S.op("act", lambda e, wt=wt, kc=kc, q=q: e.activation(out=wup[:, kc, q * WQ:(q + 1) * WQ], in_=wt[:], func=AF.Copy,
                                                                       scale=cc[:, O_PFG + kc:O_PFG + kc + 1]),
                     reads=[kwt, kcc], writes=[kwup[kc]])
            else:
                S.op(eng, lambda e, wt=wt, kc=kc, q=q: e.tensor_scalar(out=wup[:, kc, q * WQ:(q + 1) * WQ], in0=wt[:],
                                                                        scalar1=cc[:, O_PFG + kc:O_PFG + kc + 1], scalar2=1.0, op0=ALU.mult, op1=ALU.mult),
                     reads=[kwt, kcc], writes=[kwup[kc]])
    S.op("pool", lambda e: e.dma_start(out=wdn[:], in_=io["w_down"].rearrange("(i p) d -> p i d", p=128)), writes=[kwdn], dma_sem=dsem["wdn"])

    hw = hts[1]
    S.op("sp", lambda e: e.dma_start(out=hw[:, 0, :], in_=hscr[W - 128:W, :]), reads=[khscr[0]], writes=[khts[1][0]], dma_sem=dsem["hw"])
    _rms_transpose(S, hw[:, 0, :], khts[1][0], hnTs[1], khnT[1], 0, scr, ident, kid, ptr,
                   extra_scale=cc[:, O_HM:O_HM + 1], kextra=kcc)
    pfh, kpfh = pf.next()
    pfh_v = pfh[:].rearrange("p a w -> p (a w)")
    for ch in range(2 * NPAIR):
        i, hf = ch % NPAIR, ch // NPAIR
        col = (i * 2 + hf) * 2
        for kc in range(8):
            S.op("pe", lambda e, ch=ch, kc=kc, col=col: e.matmul(pfh_v[:, col:col + 2], lhsT=wup[:, kc, ch * 128:(ch + 1) * 128],
                                                                  rhs=hnTs[1][:, kc, 126:128], start=(kc == 0), stop=(kc == 7)),
                 reads=[kwup[kc], khnT[1]], writes=[kpfh])
    S.op("act", lambda e: e.activation(out=fh[:].rearrange("p i a b -> p (i a b)"), in_=pfh_v[:, 0:NPAIR * 4], func=AF.Copy),
         reads=[kpfh], writes=kfh)

    def load(t):
        b = t % 2
        S.op("sp", lambda e: e.dma_start(out=hts[b][:], in_=hscr[W * (1 + t):W * (2 + t), :].rearrange("(b p) d -> p b d", p=128)),
             reads=[khscr[1 + t]], writes=khts[b], dma_sem=dsem[f"h{b}"])

    def prologue(t):
        b = t % 2
        for blk in range(NBLK):
            _rms_transpose(S, hts[b][:, blk, :], khts[b][blk], hnTs[b], khnT[b], blk * 128, scr, ident, kid, ptr)

    state = {}

    def up_mm(t, i):
        b = t % 2
        p, kp = pf.next()
        state[(t, i)] = (p, kp)
        for hf in range(2):
            ch = hf * NPAIR + i
            for kc in range(8):
                S.op("pe", lambda e, p=p, hf=hf, ch=ch, kc=kc: e.matmul(p[:, hf, :], lhsT=wup[:, kc, ch * 128:(ch + 1) * 128],
                                                                         rhs=hnTs[b][:, kc, :], start=(kc == 0), stop=(kc == 7)),
                     reads=[kwup[kc], khnT[b]], writes=[kp])

    def elem(t, i):
        p, kp = state.pop((t, i))
        fb, kfb = fbuf.next()
        S.op("pool", lambda e: e.tensor_copy(out=fb[:, :, 0:2], in_=fh[:, i, :, :]), reads=[kfh[i]], writes=[kfb])
        S.op("act", lambda e: e.activation(out=fb[:, :, 2:W + 2], in_=p[:], func=AF.Copy), reads=[kp], writes=[kfb])
        S.op("pool", lambda e: e.tensor_copy(out=fh[:, i, :, :], in_=fb[:, :, W:W + 2]), reads=[kfb], writes=[kfh[i]])
        outs = []
        for hf, rot in ((0, cg), (1, cu)):
            ch = hf * NPAIR + i
            c, kc_ = rot.next()
            wof = O_FCW + ch * 3
            S.op("act", lambda e, c=c, hf=hf, wof=wof, ch=ch: e.activation(out=c[:], in_=fb[:, hf, 2:W + 2], func=AF.Identity,
                                                                          scale=cc[:, wof + 2:wof + 3], bias=cc[:, O_FCB + ch:O_FCB + ch + 1]),
                 reads=[kfb, kcc], writes=[kc_])
            S.op("dve", lambda e, c=c, hf=hf, wof=wof: e.scalar_tensor_tensor(out=c[:], in0=fb[:, hf, 1:W + 1], scalar=cc[:, wof + 1:wof + 2],
                                                                             in1=c[:], op0=ALU.mult, op1=ALU.add),
                 reads=[kfb, kcc, kc_], writes=[kc_])
            S.op("dve", lambda e, c=c, hf=hf, wof=wof: e.scalar_tensor_tensor(out=c[:], in0=fb[:, hf, 0:W], scalar=cc[:, wof:wof + 1],
                                                                             in1=c[:], op0=ALU.mult, op1=ALU.add),
                 reads=[kfb, kcc, kc_], writes=[kc_])
            outs.append((c, kc_))
        (g_, kg), (u_, ku) = outs
        a1, ka1 = t1.next()
        a2, ka2 = t2.next()
        s_, ks = sg.next()
        S.op("act", lambda e: e.activation(out=a1[:], in_=g_[:], func=AF.Square), reads=[kg], writes=[ka1])
        S.op("pool", lambda e: e.tensor_scalar(out=a1[:], in0=a1[:], scalar1=0.044715, scalar2=1.0, op0=ALU.mult, op1=ALU.add),
             reads=[ka1], writes=[ka1])
        S.op("pool", lambda e: e.tensor_tensor(out=a2[:], in0=a1[:], in1=g_[:], op=ALU.mult), reads=[ka1, kg], writes=[ka2])
        S.op("act", lambda e: e.activation(out=s_[:], in_=a2[:], func=AF.Sigmoid, scale=1.5957691216), reads=[ka2], writes=[ks])
        S.op("dve", lambda e: e.tensor_tensor(out=a2[:], in0=g_[:], in1=u_[:], op=ALU.mult), reads=[kg, ku, ks], writes=[ka2])
        S.op("dve", lambda e: e.tensor_tensor(out=act[:, i, :], in0=a2[:], in1=s_[:], op=ALU.mult), reads=[ka2, ks], writes=[kact[i]])

    def down_mm(t, i):
        for blk in range(NBLK):
            for hf in range(2):
                S.op("pe", lambda e, blk=blk, hf=hf: e.matmul(pd[blk][hf][:], lhsT=act[:, i, blk * 128:(blk + 1) * 128],
                                                              rhs=wdn[:, i, hf * 512:(hf + 1) * 512], start=(i == 0), stop=(i == NPAIR - 1)),
                     reads=[kact[i], kwdn], writes=[kpd[blk][hf]])

    def epilogue(t):
        b = t % 2
        for blk in range(NBLK):
            _post_norm_residual(S, [(pd[blk][0], kpd[blk][0]), (pd[blk][1], kpd[blk][1])], hts[b][:, blk, :], khts[b][blk], grow, kgrow, scr)
        S.op("sp", lambda e: e.dma_start(out=out[W * t:W * (t + 1), :].rearrange("(b p) d -> p b d", p=128), in_=hts[b][:]),
             reads=khts[b], dma_sem=dsem[f"out{b}"])

    load(0)
    if n_main > 1:
        load(1)
    prologue(0)
    for t in range(n_main):
        for i in range(NPAIR + 2):
            if i < NPAIR:
                up_mm(t, i)
            if 0 <= i - 1 < NPAIR:
                elem(t, i - 1)
            if 0 <= i - 2 < NPAIR:
                down_mm(t, i - 2)
            if i == 14 and t + 1 < n_main:
                prologue(t + 1)
        epilogue(t)
        if t + 2 < n_main:
            load(t + 2)
    S.final_wait("sp", [k for ks_ in khts for k in ks_])


def build(n_pre, n_main, mode="full"):
    nc = bass.Bass("TRN2", target_bir_lowering=False)
    TT = (n_pre + 1 + n_main) * W
    io = {}
    di = lambda n, s: nc.dram_tensor(n, s, F32, kind="ExternalInput").ap()
    io["cc"] = di("cc", [128, NCC])
    io["km"] = di("km", [128, NKM])
    io["post_ffn_g"] = di("post_ffn_g", [D])
    io["w_up"] = di("w_up", [D, 2 * DFF])
    io["w_down"] = di("w_down", [DFF, D])
    if mode == "ffn":
        io["hscr"] = di("hscr", [(1 + n_main) * W, D])
    else:
        io["xin"] = di("xin", [TT, D])
        io["post_mix_g"] = di("post_mix_g", [D])
        io["w_in"] = di("w_in", [D, INC])
        io["w_out"] = di("w_out", [D, D])
        io["w2"] = di("w2", [64, 512])
        io["a2"] = di("a2", [64, 512])
        io["g2"] = di("g2", [160, 512])
        io["hscr"] = nc.dram_tensor("hscr", [(1 + n_main) * W, D], F32, kind="Internal").ap()
    io["khscr"] = [Key(f"hscr{i}") for i in range(1 + n_main)]
    io["out"] = nc.dram_tensor("out", [n_main * W, D], F32, kind="ExternalOutput").ap()

    with contextlib.ExitStack() as sem_stack, contextlib.ExitStack() as st0:
        cc = st0.enter_context(nc.sbuf_tensor("cc_sb", [128, NCC], F32))
        ident = st0.enter_context(nc.sbuf_tensor("ident", [128, 128], BF16))

        def shared_loads(S):
            kcc, kid = Key("cc"), Key("ident")
            d0 = S.new_dma_sem("cc")
            d1 = S.new_dma_sem("ident")
            S.op("sp", lambda e: e.dma_start(out=cc[:], in_=io["cc"][:, :]), writes=[kcc], dma_sem=d0)
            S.op("pool", lambda e: e.dma_start(out=ident[:], in_=io["km"][:, M_ID:M_ID + 128]), writes=[kid], dma_sem=d1)
            return {"cc": cc, "kcc": kcc, "ident": ident, "kid": kid}

        if mode != "ffn":
            S1 = Sched(nc, sem_stack, "a")
            shared = shared_loads(S1)
            with contextlib.ExitStack() as st1:
                phase1_mixer(nc, S1, st1, io, n_pre, n_main, shared)
                S1.final_wait("sp", io["khscr"])
                S1.emit()
            S2 = Sched(nc, sem_stack, "b")
            shared = {"cc": cc, "kcc": Key("cc2"), "ident": ident, "kid": Key("ident2")}
            io["khscr"] = [Key(f"hscr2_{i}") for i in range(1 + n_main)]
        else:
            S2 = Sched(nc, sem_stack, "b")
            shared = shared_loads(S2)
        with contextlib.ExitStack() as st2:
            phase2_ffn(nc, S2, st2, io, n_main, shared)
            S2.emit()
    return nc


def phase1_mixer(nc, S, st, io, n_pre, n_main, shared):
    sb = lambda n, s, d: st.enter_context(nc.sbuf_tensor(n, s, d))
    ps = lambda n, s, d: st.enter_context(nc.psum_tensor(n, s, d))
    cc, kcc = shared["cc"], shared["kcc"]
    ident, kid = shared["ident"], shared["kid"]
    xin, hscr, khscr = io["xin"], io["hscr"], io["khscr"]
    n_tiles = n_pre + 1 + n_main
    C05 = 0.6065306597126334

    win = sb("win", [128, 8, INC], BF16)
    kwin = [Key(f"win{k}") for k in range(8)]
    wout = sb("wout", [128, 8, D], BF16)
    kwout = Key("wout")
    w2b = sb("w2b", [128, 512], BF16)
    a2b = sb("a2b", [128, 512], BF16)
    g2b0 = sb("g2b0", [128, 512], BF16)
    g2b1 = sb("g2b1", [128, 512], BF16)
    wg1 = sb("wg1", [128, 8, 128], BF16)
    kwg1 = Key("wg1")
    ksmallw = Key("smallw")
    msu4 = sb("msu4", [128, 512], BF16)
    msl = sb("msl", [128, 128], BF16)
    bones = sb("bones", [128, 128], BF16)
    fst = sb("fst", [128, 64], BF16)
    rst = sb("rst", [128, W], F32)
    kconst = Key("p1const")
    grow = sb("grow1", [128, D], F32)
    kgrow = Key("grow1")
    dc = sb("dc", [128, 20], F32)
    kdc = Key("dc")
    dsem = {n: S.new_dma_sem("p1_" + n) for n in ("wst0", "wst1", "wst2", "wst3", "const", "grow", "x0", "x1", "h0", "h1")}

    S.op("sp", lambda e: e.dma_start(out=grow[:], in_=io["post_mix_g"].partition_broadcast(128)), writes=[kgrow], dma_sem=dsem["grow"])
    S.op("sp", lambda e: e.dma_start(out=rst[:], in_=io["km"][:, M_RST:M_RST + W]), writes=[kconst], dma_sem=dsem["const"])
    uniq = [0]

    def pool_dma(fn, key):
        uniq[0] += 1
        S.op("pool", fn, writes=[key], dma_sem=S.new_dma_sem(f"p1u{uniq[0]}"))

    kmsl, kbones, kfst = Key("msl"), Key("bones"), Key("fst")
    kmsu4 = [Key(f"msu4_{q}") for q in range(4)]
    kw2b, ka2b, kg2b0, kg2b1 = Key("w2b"), Key("a2b"), Key("g2b0"), Key("g2b1")
    for dst, c0, n_, k_ in ((msl, M_SL, 128, kmsl), (bones, M_BO, 128, kbones), (fst, M_F, 64, kfst)):
        pool_dma(lambda e, dst=dst, c0=c0, n_=n_: e.dma_start(out=dst[:], in_=io["km"][:, c0:c0 + n_]), k_)
    for q, c0 in enumerate((M_SU, M_IU, M_SU, M_IU)):
        pool_dma(lambda e, q=q, c0=c0: e.dma_start(out=msu4[:, q * 128:(q + 1) * 128], in_=io["km"][:, c0:c0 + 128]), kmsu4[q])
    S.op("pool", lambda e: e.memset(w2b[:], 0.0), writes=[kw2b])
    S.op("pool", lambda e: e.memset(a2b[:], 0.0), writes=[ka2b])
    S.op("pool", lambda e: e.memset(g2b1[:], 0.0), writes=[kg2b1])
    S.op("pool", lambda e: e.memset(wg1[:], 0.0), writes=[kwg1])
    pool_dma(lambda e: e.dma_start(out=w2b[0:64, :], in_=io["w2"][:, :]), kw2b)
    pool_dma(lambda e: e.dma_start(out=a2b[64:128, :], in_=io["a2"][:, :]), ka2b)
    pool_dma(lambda e: e.dma_start(out=g2b0[:], in_=io["g2"][0:128, :]), kg2b0)
    pool_dma(lambda e: e.dma_start(out=g2b1[0:32, :], in_=io["g2"][128:160, :]), kg2b1)
    pool_dma(lambda e: e.dma_start(out=wout[:], in_=io["w_out"].rearrange("(k p) d -> p k d", p=128)), kwout)
    S.op("dve", lambda e: e.tensor_scalar(out=dc[:, 0:15], in0=cc[:, O_MU:O_MU + 15], scalar1=-1.0, scalar2=1.0, op0=ALU.mult, op1=ALU.add),
         reads=[kcc], writes=[kdc])
    S.op("dve", lambda e: e.tensor_scalar(out=dc[:, 15:19], in0=cc[:, O_KA:O_KA + 4], scalar1=-1.0, scalar2=1.0, op0=ALU.mult, op1=ALU.add),
         reads=[kcc], writes=[kdc])
    xts = [sb(f"xt{i}", [128, NBLK, D], F32) for i in range(2)]
    kxts = [[Key(f"xt{i}_{b}") for b in range(NBLK)] for i in range(2)]
    xnT = sb("xnT", [128, 8, W], BF16)
    kxnT = Key("xnT")
    scr = {
        "stat": Rot([sb(f"stat1_{i}", [128, 8], F32) for i in range(4)]),
        "xs": Rot([sb(f"xs1_{i}", [128, D], BF16) for i in range(2)]),
        "tmp512": Rot([sb(f"tmp512a_{i}", [128, 512], F32) for i in range(2)]),
    }
    tf = Rot([sb(f"tf{i}", [128, W], F32) for i in range(9)])
    tb = Rot([sb(f"tb{i}", [128, W], BF16) for i in range(4)])
    qbuf = Rot([sb(f"qbuf{i}", [128, W + 1], F32) for i in range(2)])
    ubuf = Rot([sb(f"ubuf{i}", [128, W + 2], F32) for i in range(2)])
    qh = sb("qh", [128, 15], F32)
    kqh = [Key(f"qh{j}") for j in range(15)]
    uh = sb("uh", [128, 4, 2], F32)
    kuh = [Key(f"uh{c}") for c in range(4)]
    rkv = {n: Rot([sb(f"{n}c{i}", [128, W], F32) for i in range(2)]) for n in ("r", "k", "v")}
    wa = sb("wa", [128, W], F32); kwa = Key("wa")
    g0t = sb("g0t", [128, W], F32); kg0 = Key("g0t")
    g1t = sb("g1t", [128, W], F32); kg1 = Key("g1t")
    twad = sb("twad", [128, W], BF16); ktwad = Key("twad")
    sgd0 = sb("sgd0", [128, W], BF16); sgd1 = sb("sgd1", [128, W], BF16); ksgd = Key("sgd")
    sig = sb("sig", [128, 4, W], F32); ksig = [Key(f"sig{c}") for c in range(4)]
    a4 = sb("a4", [128, 4, W], F32); ka4 = [Key(f"a4{c}") for c in range(4)]
    bonus = sb("bonus", [128, 4, W], F32); kbonus = [Key(f"bonus{c}") for c in range(4)]
    ARbd = sb("ARbd", [128, 4, NCH, 2, 128], BF16); kAbd = [Key(f"Abd{c}") for c in range(4)]; kRbd = [Key(f"Rbd{c}") for c in range(4)]
    Bbd = sb("Bbd", [128, 4, NCH, 128], BF16); kBbd = [Key(f"Bbd{c}") for c in range(4)]
    Kbd = sb("Kbd", [128, 4, NCH, 128], BF16); kKbd = [Key(f"Kbd{c}") for c in range(4)]
    Vbd = sb("Vbd", [128, 4, NCH, 128], BF16); kVbd = [Key(f"Vbd{c}") for c in range(4)]
    PC = sb("PC", [128, 4, NCH], F32); kPC = [Key(f"PC{c}") for c in range(4)]
    M1s = [(sb(f"M1_{i}", [128, 4, 512], BF16), Key(f"M1_{i}")) for i in range(2)]
    NT0s = [(sb(f"NT0_{i}", [128, 4, 128], BF16), Key(f"NT0_{i}")) for i in range(2)]
    M2s = [(sb(f"M2_{i}", [128, 4, 320], BF16), Key(f"M2_{i}")) for i in range(2)]
    NNs = [[(sb(f"NN{i}{j}", [128, 4, 256], BF16), Key(f"NN{i}{j}")) for j in range(2)] for i in range(2)]
    Tbs = [[(sb(f"Tb{i}{j}", [128, 4, 128], BF16), Key(f"Tb{i}{j}")) for j in range(2)] for i in range(2)]
    Zb = sb("Zb", [128, 4, 64], BF16); kZb = Key("Zb")
    Ub = sb("Ub", [128, 4, 64], BF16); kUb = Key("Ub")
    S32 = sb("S32", [128, 4, 64], F32); kS32 = Key("S32")
    Sbf = sb("Sbf", [128, 4, 64], BF16); kSbf = Key("Sbf")
    gst = Rot([sb(f"gst{i}", [128, 8, 4], F32) for i in range(2)])
    ynbd = sb("ynbd", [128, 4, 128], BF16); kynbd = Key("ynbd")
    ynf = sb("ynf", [128, 4, W], F32); kynf = [Key(f"ynf{c}") for c in range(4)]
    yT = sb("yT", [128, 8, W], BF16); kyT = [Key(f"yT{c}") for c in range(8)]

    WQ = INC // 4
    stg = [(sig, ksig), (a4, ka4), (bonus, kbonus), (ynf, kynf)]
    for kc in range(8):
        for q in range(4):
            j = kc * 4 + q
            buf, kbuf = stg[j % 4]
            wt = buf[:].rearrange("p c w -> p (c w)")[:, 0:WQ]
            S.op("sp", lambda e, wt=wt, kc=kc, q=q: e.dma_start(out=wt, in_=io["w_in"][kc * 128:(kc + 1) * 128, q * WQ:(q + 1) * WQ]),
                 writes=kbuf, dma_sem=dsem[f"wst{j % 4}"])
            if j % 2 == 0:
                S.op("act", lambda e, wt=wt, kc=kc, q=q: e.activation(out=win[:, kc, q * WQ:(q + 1) * WQ], in_=wt, func=AF.Copy,
                                                                       scale=cc[:, O_PMG + kc:O_PMG + kc + 1]),
                     reads=kbuf + [kcc], writes=[kwin[kc]])
            else:
                S.op("dve", lambda e, wt=wt, kc=kc, q=q: e.tensor_scalar(out=win[:, kc, q * WQ:(q + 1) * WQ], in0=wt,
                                                                          scalar1=cc[:, O_PMG + kc:O_PMG + kc + 1], scalar2=1.0, op0=ALU.mult, op1=ALU.mult),
                     reads=kbuf + [kcc], writes=[kwin[kc]])
    S.op("dve", lambda e: e.tensor_copy(out=wg1[:, :, 0:32], in_=win[:, :, INC - 32:INC]), reads=kwin + [kwg1], writes=[kwg1])

    ptr = Rot([ps("ptr1", [128, 8, 128], BF16)], excl=True)
    pp = [ps(f"pp{i}", [128, 2, W], F32) for i in range(2)]
    kpp = [PKey(f"pp{i}") for i in range(2)]
    psm = ps("psm", [128, 2, W], F32); kpsm = PKey("psm")
    PA = ps("PA", [128, 1024], F32); kPA = PKey("PA")
    PB = ps("PB", [128, 1024], F32); kPB0 = PKey("PB0"); kPB1 = PKey("PB1")

    for t_, k_ in ((qh, kqh), (uh, kuh)):
        S.op("pool", lambda e, t_=t_: e.memset(t_[:], 0.0), writes=k_)
    S.op("pool", lambda e: e.memset(S32[:], 0.0), writes=[kS32])
    S.op("pool", lambda e: e.memset(Sbf[:], 0.0), writes=[kSbf])
    S.op("pool", lambda e: e.memset(ARbd[:], 0.0), writes=kAbd + kRbd)
    S.op("pool", lambda e: e.memset(Bbd[:], 0.0), writes=kBbd)
    S.op("pool", lambda e: e.memset(Kbd[:], 0.0), writes=kKbd)
    S.op("pool", lambda e: e.memset(Vbd[:], 0.0), writes=kVbd)
    S.op("pool", lambda e: e.memset(ynbd[:], 0.0), writes=[kynbd])
    S.op("pool", lambda e: e.memset(g1t[:], 0.0), writes=[kg1])

    slot_i = [0]

    def proj(col0, ncols, wsrc=None, kw=None):
        j = slot_i[0] % 4
        slot_i[0] += 1
        t_, half, key = pp[j // 2], j % 2, kpp[j // 2]
        for kc in range(8):
            lhsT = win[:, kc, col0:col0 + ncols] if wsrc is None else wsrc[:, kc, :]
            S.op("pe", lambda e, kc=kc, lhsT=lhsT: e.matmul(t_[0:ncols, half, :], lhsT=lhsT, rhs=xnT[:, kc, :],
                                                            start=(kc == 0), stop=(kc == 7)),
                 reads=[(kwin if kw is None else kw)[kc], kxnT], writes=[key])
        return t_[0:ncols, half, :], key

    def shift_lerp(p_ap, kp, jj, dst_ap, kdst, np_=128):
        qb, kqb = qbuf.next()
        S.op("pool", lambda e: e.tensor_copy(out=qb[0:np_, 0:1], in_=qh[0:np_, jj:jj + 1]), reads=[kqh[jj]], writes=[kqb])
        S.op("act", lambda e: e.activation(out=qb[0:np_, 1:W + 1], in_=p_ap, func=AF.Copy), reads=[kp], writes=[kqb])
        S.op("pool", lambda e: e.tensor_copy(out=qh[0:np_, jj:jj + 1], in_=qb[0:np_, W:W + 1]), reads=[kqb], writes=[kqh[jj]])
        tmp, ktmp = tf.next()
        S.op("pool", lambda e: e.tensor_scalar(out=tmp[0:np_, :], in0=qb[0:np_, 1:W + 1], scalar1=dc[0:np_, jj:jj + 1], scalar2=1.0, op0=ALU.mult, op1=ALU.mult),
             reads=[kqb, kdc], writes=[ktmp])
        S.op("dve", lambda e: e.scalar_tensor_tensor(out=dst_ap, in0=qb[0:np_, 0:W], scalar=cc[0:np_, O_MU + jj:O_MU + jj + 1], in1=tmp[0:np_, :],
                                                     op0=ALU.mult, op1=ALU.add),
             reads=[kqb, kcc, ktmp], writes=[kdst])

    def bd_view(t4, c, h):
        return t4[h * 64:(h + 1) * 64, c, :, h * 64:(h + 1) * 64]

    def v3(t2, h):
        return t2[h * 64:(h + 1) * 64, :].rearrange("p (n t) -> p n t", n=NCH)

    def load_x(t):
        b = t % 2
        S.op("sp", lambda e: e.dma_start(out=xts[b][:], in_=xin[W * t:W * (t + 1), :].rearrange("(b p) d -> p b d", p=128)),
             writes=kxts[b], dma_sem=dsem[f"x{b}"])

    load_x(0)
    if n_tiles > 1:
        load_x(1)

    def do_tile(t):
        full = t >= n_pre
        if DEBUG_STOP < 1:
            return
        b = t % 2
        xt = xts[b]
        for blk in range(NBLK):
            _rms_transpose(S, xt[:, blk, :], kxts[b][blk], xnT, kxnT, blk * 128, scr, ident, kid, ptr)

        if DEBUG_STOP < 2:
            return
        if full:
            def b1(c):
                pB, kpB = proj(c * 128, 128)
                pH, kpH = proj(1024 + c * 128, 128)
                hAs, khAs = tf.next()
                S.op("act", lambda e, hAs=hAs, pH=pH: e.activation(out=hAs[:], in_=pH, func=AF.Copy), reads=[kpH], writes=[khAs])
                ub, kub = ubuf.next()
                S.op("pool", lambda e, ub=ub, c=c: e.tensor_copy(out=ub[:, 0:2], in_=uh[:, c, :]), reads=[kuh[c]], writes=[kub])
                S.op("dve", lambda e, ub=ub, pB=pB, hAs=hAs: e.tensor_tensor(out=ub[:, 2:W + 2], in0=pB, in1=hAs[:], op=ALU.mult),
                     reads=[kpB, khAs], writes=[kub])
                S.op("pool", lambda e, ub=ub, c=c: e.tensor_copy(out=uh[:, c, :], in_=ub[:, W:W + 2]), reads=[kub], writes=[kuh[c]])
                ta, kta = tf.next()
                wof = O_CAW + c * 3
                S.op("act", lambda e, ta=ta, ub=ub, wof=wof: e.activation(out=ta[:], in_=ub[:, 2:W + 2], func=AF.Copy, scale=cc[:, wof + 2:wof + 3]),
                     reads=[kub, kcc], writes=[kta])
                S.op("dve", lambda e, ta=ta, ub=ub, wof=wof: e.scalar_tensor_tensor(out=ta[:], in0=ub[:, 1:W + 1], scalar=cc[:, wof + 1:wof + 2], in1=ta[:],
                                                                                  op0=ALU.mult, op1=ALU.add), reads=[kub, kcc, kta], writes=[kta])
                S.op("dve", lambda e, ta=ta, ub=ub, wof=wof: e.scalar_tensor_tensor(out=ta[:], in0=ub[:, 0:W], scalar=cc[:, wof:wof + 1], in1=ta[:],
                                                                                  op0=ALU.mult, op1=ALU.add), reads=[kub, kcc, kta], writes=[kta])
                pC, kpC = proj(512 + c * 128, 128)
                S.op("dve", lambda e, ta=ta, pC=pC, c=c: e.tensor_tensor(out=yT[:, c, :], in0=pC, in1=ta[:], op=ALU.mult),
                     reads=[kpC, kta], writes=[kyT[c]])
            for c_ in range(4):
                b1(c_)

        if DEBUG_STOP < 3:
            return
        QC = 1536
        p_, kp_ = proj(QC + 12 * 128, 128)
        shift_lerp(p_, kp_, 12, wa[:], kwa)
        S.op("act", lambda e: e.activation(out=twad[0:64, :], in_=wa[0:64, :], func=AF.Tanh), reads=[kwa], writes=[ktwad])
        S.op("dve", lambda e: e.tensor_copy(out=twad[64:128, :], in_=wa[64:128, :]), reads=[kwa], writes=[ktwad])
        if full:
            p_, kp_ = proj(QC + 13 * 128, 128)
            shift_lerp(p_, kp_, 13, g0t[:], kg0)
            p_, kp_ = proj(None, 128, wsrc=wg1, kw=[kwg1] * 8)
            shift_lerp(p_[0:32, :], kp_, 14, g1t[0:32, :], kg1, np_=32)
            S.op("act", lambda e: e.activation(out=sgd0[:], in_=g0t[:], func=AF.Sigmoid), reads=[kg0], writes=[ksgd])
            S.op("act", lambda e: e.activation(out=sgd1[:], in_=g1t[:], func=AF.Sigmoid), reads=[kg1], writes=[ksgd])
        for c in range(4):
            S.op("pe", lambda e, c=c: e.matmul(psm[:, 0, :], lhsT=w2b[:, c * 128:(c + 1) * 128], rhs=twad[:], start=True, stop=True),
                 reads=[kw2b, ktwad], writes=[kpsm])
            S.op("pe", lambda e, c=c: e.matmul(psm[:, 1, :], lhsT=a2b[:, c * 128:(c + 1) * 128], rhs=twad[:], start=True, stop=True),
                 reads=[ka2b, ktwad], writes=[kpsm])
            S.op("act", lambda e, c=c: e.activation(out=sig[:, c, :], in_=psm[:, 0, :], func=AF.Sigmoid, bias=cc[:, O_W0 + c:O_W0 + c + 1]),
                 reads=[kpsm, kcc], writes=[ksig[c]])
            S.op("act", lambda e, c=c: e.activation(out=a4[:, c, :], in_=psm[:, 1, :], func=AF.Sigmoid, bias=cc[:, O_A0 + c:O_A0 + c + 1]),
                 reads=[kpsm, kcc], writes=[ka4[c]])

        if DEBUG_STOP < 4:
            return
        def b4(c):
            kt_, kkt = rkv["k"].next()
            vt_, kvt = rkv["v"].next()
            p_, kp_ = proj(QC + (4 + c) * 128, 128)
            shift_lerp(p_, kp_, 4 + c, kt_[:], kkt)
            p_, kp_ = proj(QC + (8 + c) * 128, 128)
            shift_lerp(p_, kp_, 8 + c, vt_[:], kvt)
            if full:
                rt_, krt = rkv["r"].next()
                p_, kp_ = proj(QC + c * 128, 128)
                shift_lerp(p_, kp_, c, rt_[:], krt)
            kkr, kkkr = tf.next()
            S.op("pool", lambda e, kkr=kkr, kt_=kt_, c=c: e.tensor_scalar(out=kkr[:], in0=kt_[:], scalar1=cc[:, O_KK + c:O_KK + c + 1], scalar2=1.0, op0=ALU.mult, op1=ALU.mult),
                 reads=[kkt, kcc], writes=[kkkr])
            sq, ksq = tb.next()
            S.op("act", lambda e, sq=sq, kkr=kkr: e.activation(out=sq[:], in_=kkr[:], func=AF.Square), reads=[kkkr], writes=[ksq])
            S.op("pe", lambda e, sq=sq: e.matmul(psm[:, 0, :], lhsT=bones[:], rhs=sq[:], start=True, stop=True), reads=[kbones, ksq], writes=[kpsm])
            nrm, knrm = tf.next()
            S.op("act", lambda e, nrm=nrm: e.activation(out=nrm[:], in_=psm[:, 0, :], func=AF.Sqrt, bias=1e-24), reads=[kpsm], writes=[knrm])
            S.op("dve", lambda e, nrm=nrm: e.reciprocal(out=nrm[:], in_=nrm[:]), reads=[knrm], writes=[knrm])
            kk, kkk = tf.next()
            S.op("dve", lambda e, kk=kk, kkr=kkr, nrm=nrm: e.tensor_tensor(out=kk[:], in0=kkr[:], in1=nrm[:], op=ALU.mult), reads=[kkkr, knrm], writes=[kkk])
            cs, kcs = tf.next()
            S.op("dve", lambda e, cs=cs, c=c: e.tensor_tensor_scan(out=cs[:], data0=rst[:], data1=sig[:, c, :], initial=0.0, op0=ALU.mult, op1=ALU.add),
                 reads=[kconst, ksig[c]], writes=[kcs])
            E1, kE1 = tf.next()
            E2, kE2 = tf.next()
            E3, kE3 = tf.next()
            dd, kdd = tf.next()
            S.op("act", lambda e, E1=E1, cs=cs: e.activation(out=E1[:], in_=cs[:], func=AF.Exp, scale=-C05), reads=[kcs], writes=[kE1])
            S.op("act", lambda e, E2=E2, cs=cs: e.activation(out=E2[:], in_=cs[:], func=AF.Exp, scale=C05), reads=[kcs], writes=[kE2])
            S.op("pool", lambda e, dd=dd, cs=cs, c=c: e.tensor_tensor(out=dd[:], in0=cs[:], in1=sig[:, c, :], op=ALU.subtract), reads=[kcs, ksig[c]], writes=[kdd])
            S.op("act", lambda e, E3=E3, dd=dd: e.activation(out=E3[:], in_=dd[:], func=AF.Exp, scale=-C05), reads=[kdd], writes=[kE3])
            S.op("pool", lambda e, E1=E1, c=c: e.tensor_copy(out=PC[:, c, :], in_=E1[:].rearrange("p (n t) -> p n t", n=NCH)[:, :, CH - 1]),
                 reads=[kE1], writes=[kPC[c]])
            mm, kmm = tf.next()
            S.op("pool", lambda e, mm=mm, c=c: e.tensor_scalar(out=mm[:], in0=a4[:, c, :], scalar1=cc[:, O_KA + c:O_KA + c + 1], scalar2=dc[:, 15 + c:16 + c],
                                                               op0=ALU.mult, op1=ALU.add), reads=[ka4[c], kcc, kdc], writes=[kmm])
            kp, kkp = tf.next()
            S.op("dve", lambda e, kp=kp, kt_=kt_, mm=mm: e.tensor_tensor(out=kp[:], in0=kt_[:], in1=mm[:], op=ALU.mult), reads=[kkt, kmm], writes=[kkp])
            akk, kakk = tf.next()
            S.op("pool", lambda e, akk=akk, kk=kk, c=c: e.tensor_tensor(out=akk[:], in0=a4[:, c, :], in1=kk[:], op=ALU.mult), reads=[ka4[c], kkk], writes=[kakk])
            for h in range(2):
                S.op("dve", lambda e, h=h, kk=kk, E3=E3, c=c: e.scalar_tensor_tensor(out=ARbd[h * 64:(h + 1) * 64, c, :, 0, h * 64:(h + 1) * 64], in0=v3(kk, h), scalar=-1.0,
                                                                                   in1=v3(E3, h), op0=ALU.mult, op1=ALU.mult),
                     reads=[kkk, kE3], writes=[kAbd[c]])
                S.op("dve", lambda e, h=h, akk=akk, E2=E2, c=c: e.tensor_tensor(out=bd_view(Bbd, c, h), in0=v3(akk, h), in1=v3(E2, h), op=ALU.mult),
                     reads=[kakk, kE2], writes=[kBbd[c]])
                S.op("pool", lambda e, h=h, kp=kp, E2=E2, c=c: e.tensor_tensor(out=bd_view(Kbd, c, h), in0=v3(kp, h), in1=v3(E2, h), op=ALU.mult),
                     reads=[kkp, kE2], writes=[kKbd[c]])
                S.op("act", lambda e, h=h, vt_=vt_, c=c: e.activation(out=bd_view(Vbd, c, h), in_=v3(vt_, h), func=AF.Copy), reads=[kvt], writes=[kVbd[c]])
                if full:
                    S.op("pool", lambda e, h=h, rt_=rt_, E1=E1, c=c: e.tensor_tensor(out=ARbd[h * 64:(h + 1) * 64, c, :, 1, h * 64:(h + 1) * 64], in0=v3(rt_, h), in1=v3(E1, h),
                                                                                   op=ALU.mult), reads=[krt, kE1], writes=[kRbd[c]])
            if full:
                rk, krk = tf.next()
                S.op("pool", lambda e, rk=rk, rt_=rt_, kp=kp: e.tensor_tensor(out=rk[:], in0=rt_[:], in1=kp[:], op=ALU.mult), reads=[krt, kkp], writes=[krk])
                rkb, krkb = tb.next()
                S.op("dve", lambda e, rkb=rkb, rk=rk, c=c: e.tensor_scalar(out=rkb[:], in0=rk[:], scalar1=cc[:, O_RK + c:O_RK + c + 1], scalar2=1.0, op0=ALU.mult, op1=ALU.mult),
                     reads=[krk, kcc], writes=[krkb])
                S.op("pe", lambda e, rkb=rkb: e.matmul(psm[:, 1, :], lhsT=bones[:], rhs=rkb[:], start=True, stop=True), reads=[kbones, krkb], writes=[kpsm])
                S.op("dve", lambda e, vt_=vt_, c=c: e.tensor_tensor(out=bonus[:, c, :], in0=psm[:, 1, :], in1=vt_[:], op=ALU.mult), reads=[kpsm, kvt], writes=[kbonus[c]])

        for c_ in range(4):
            b4(c_)
        if DEBUG_STOP < 5:
            return

        PA3 = PA[:].rearrange("p (a w) -> p a w", a=2)
        PAd = PA[:].rearrange("p (a w) -> p a w", a=4)
        PBt = PB[:, 0:512].rearrange("p (a w) -> p a w", a=4)
        PQ0 = PB[:, 512:768].rearrange("p (a w) -> p a w", a=4)
        PQ1 = PB[:, 768:1024].rearrange("p (a w) -> p a w", a=4)
        Tfinal = {}

        def gen_AD(n, s_):
            M1, kM1 = M1s[s_]
            NT0, kNT0 = NT0s[s_]
            M2, kM2 = M2s[s_]
            for pi in range(2):
                for i in range(2):
                    c = pi * 2 + i
                    rhsAR = ARbd[:, c, n, :, :].rearrange("p a w -> p (a w)")
                    S.op("pe", lambda e, i=i, c=c, rhsAR=rhsAR: e.matmul(PA3[:, i, 0:256], lhsT=Bbd[:, c, n, :], rhs=rhsAR, start=True, stop=True),
                         reads=[kBbd[c], kAbd[c], kRbd[c]], writes=[kPA])
                    S.op("pe", lambda e, i=i, c=c, rhsAR=rhsAR: e.matmul(PA3[:, i, 256:512], lhsT=Kbd[:, c, n, :], rhs=rhsAR, start=True, stop=True),
                         reads=[kKbd[c], kAbd[c], kRbd[c]], writes=[kPA])
                S.op("dve", lambda e, pi=pi: e.tensor_tensor(out=M1[:, 2 * pi:2 * pi + 2, :], in0=PA3, in1=msu4[:].unsqueeze(1).to_broadcast([128, 2, 512]), op=ALU.mult),
                     reads=[kPA] + kmsu4, writes=[kM1])
                yield
                for i in range(2):
                    c = pi * 2 + i
                    S.op("pe", lambda e, i=i, c=c: e.matmul(PA3[:, i, 0:128], lhsT=ARbd[:, c, n, 0, :], rhs=Bbd[:, c, n, :], start=True, stop=True),
                         reads=[kAbd[c], kBbd[c]], writes=[kPA])
                    S.op("pe", lambda e, i=i, c=c: e.matmul(PA3[:, i, 128:256], lhsT=Bbd[:, c, n, :], rhs=ident[:], start=True, stop=True),
                         reads=[kBbd[c], kid], writes=[kPA])
                    S.op("pe", lambda e, i=i, c=c: e.matmul(PA3[:, i, 256:384], lhsT=Kbd[:, c, n, :], rhs=ident[:], start=True, stop=True),
                         reads=[kKbd[c], kid], writes=[kPA])
                    S.op("pe", lambda e, i=i, c=c: e.matmul(PA3[:, i, 384:448], lhsT=Vbd[:, c, n, :], rhs=fst[:], start=True, stop=True),
                         reads=[kVbd[c], kfst], writes=[kPA])
                S.op("dve", lambda e, pi=pi: e.tensor_tensor(out=NT0[:, 2 * pi:2 * pi + 2, :], in0=PA3[:, :, 0:128], in1=msl[:].unsqueeze(1).to_broadcast([128, 2, 128]), op=ALU.mult),
                     reads=[kPA, kmsl], writes=[kNT0])
                S.op("act", lambda e, pi=pi: e.activation(out=M2[:, 2 * pi:2 * pi + 2, :], in_=PA3[:, :, 128:448], func=AF.Copy), reads=[kPA], writes=[kM2])
                yield
            Tcur, kTcur = Tbs[s_][0]
            S.op("pool", lambda e, Tcur=Tcur: e.tensor_tensor(out=Tcur[:], in0=M1[:, :, 0:128], in1=ident[:].unsqueeze(1).to_broadcast([128, 4, 128]), op=ALU.add),
                 reads=[kM1, kid], writes=[kTcur])
            Nprev = lambda c: M1[:, c, 0:128]
            NTprev = lambda c: NT0[:, c, :]
            kprev = [kM1, kNT0]
            for j in range(1, 6):
                NNj, kNNj = NNs[s_][j % 2]
                for c in range(4):
                    if j < 5:
                        S.op("pe", lambda e, c=c, Nprev=Nprev, NTprev=NTprev: e.matmul(PAd[:, c, 0:128], lhsT=NTprev(c), rhs=Nprev(c), start=True, stop=True),
                             reads=kprev, writes=[kPA])
                    S.op("pe", lambda e, c=c, Nprev=Nprev, NTprev=NTprev: e.matmul(PAd[:, c, 128:256], lhsT=Nprev(c), rhs=NTprev(c), start=True, stop=True),
                         reads=kprev, writes=[kPA])
                if j < 5:
                    S.op("act", lambda e, NNj=NNj: e.activation(out=NNj[:], in_=PAd, func=AF.Copy), reads=[kPA], writes=[kNNj])
                else:
                    S.op("act", lambda e, NNj=NNj: e.activation(out=NNj[:, :, 128:256], in_=PAd[:, :, 128:256], func=AF.Copy), reads=[kPA], writes=[kNNj])
                yield
                for c in range(4):
                    S.op("pe", lambda e, c=c, NNj=NNj, Tcur=Tcur: e.matmul(PBt[:, c, :], lhsT=NNj[:, c, 128:256], rhs=Tcur[:, c, :], start=True, stop=True),
                         reads=[kNNj, kTcur], writes=[kPB0])
                Tnew, kTnew = Tbs[s_][j % 2]
                S.op("dve", lambda e, Tnew=Tnew, Tcur=Tcur: e.tensor_tensor(out=Tnew[:], in0=PBt, in1=Tcur[:], op=ALU.add), reads=[kPB0, kTcur], writes=[kTnew])
                Tcur, kTcur = Tnew, kTnew
                Nprev = (lambda NNj: (lambda c: NNj[:, c, 0:128]))(NNj)
                NTprev = (lambda NNj: (lambda c: NNj[:, c, 128:256]))(NNj)
                kprev = [kNNj]
                yield
            Tfinal[n] = (Tcur, kTcur)

        def gen_SQ(n, s_):
            M1, kM1 = M1s[s_]
            M2, kM2 = M2s[s_]
            Tcur, kTcur = Tfinal[n]
            for c in range(4):
                S.op("pe", lambda e, c=c: e.matmul(PQ0[:, c, :], lhsT=ARbd[:, c, n, 0, :], rhs=Sbf[:, c, :], start=True, stop=False),
                     reads=[kAbd[c], kSbf], writes=[kPB1])
                S.op("pe", lambda e, c=c: e.matmul(PQ0[:, c, :], lhsT=M1[:, c, 256:384], rhs=M2[:, c, 256:320], start=False, stop=True),
                     reads=[kM1, kM2], writes=[kPB1])
            S.op("act", lambda e: e.activation(out=Zb[:], in_=PQ0, func=AF.Copy), reads=[kPB1], writes=[kZb])
            yield
            for c in range(4):
                S.op("pe", lambda e, c=c: e.matmul(PQ1[:, c, :], lhsT=Tcur[:, c, :], rhs=Zb[:, c, :], start=True, stop=True),
                     reads=[kTcur, kZb], writes=[kPB1])
            S.op("act", lambda e: e.activation(out=Ub[:], in_=PQ1, func=AF.Copy), reads=[kPB1], writes=[kUb])
            yield
            for c in range(4):
                S.op("pe", lambda e, c=c: e.matmul(PQ0[:, c, :], lhsT=M2[:, c, 0:128], rhs=Ub[:, c, :], start=True, stop=False),
                     reads=[kM2, kUb], writes=[kPB1])
                S.op("pe", lambda e, c=c: e.matmul(PQ0[:, c, :], lhsT=M2[:, c, 128:256], rhs=M2[:, c, 256:320], start=False, stop=True),
                     reads=[kM2], writes=[kPB1])
            if full:
                for c in range(4):
                    S.op("pe", lambda e, c=c: e.matmul(PQ1[:, c, :], lhsT=ARbd[:, c, n, 1, :], rhs=Sbf[:, c, :], start=True, stop=False),
                         reads=[kRbd[c], kSbf], writes=[kPB1])
                    S.op("pe", lambda e, c=c: e.matmul(PQ1[:, c, :], lhsT=M1[:, c, 128:256], rhs=Ub[:, c, :], start=False, stop=False),
                         reads=[kM1, kUb], writes=[kPB1])
                    S.op("pe", lambda e, c=c: e.matmul(PQ1[:, c, :], lhsT=M1[:, c, 384:512], rhs=M2[:, c, 256:320], start=False, stop=True),
                         reads=[kM1, kM2], writes=[kPB1])
            pcb = PC[:, :, n:n + 1].to_broadcast([128, 4, 64])
            tS_, ktS = tf.next()
            tmpS = tS_[:].rearrange("p (c v) -> p c v", c=4)
            S.op("dve", lambda e: e.tensor_tensor(out=tmpS, in0=PQ0, in1=S32[:], op=ALU.add), reads=[kPB1, kS32], writes=[ktS])
            S.op("dve", lambda e: e.tensor_tensor(out=Sbf[:], in0=tmpS, in1=pcb, op=ALU.mult), reads=[ktS] + kPC, writes=[kSbf])
            S.op("pool", lambda e: e.tensor_tensor(out=S32[:], in0=tmpS, in1=pcb, op=ALU.mult), reads=[ktS] + kPC, writes=[kS32])
            yield
            if full:
                g_, kg_ = gst.next()
                ys_, kysq = tf.next()
                ysq = ys_[:].rearrange("p (c v) -> p c v", c=4)
                yc_, kycen = tf.next()
                ycen = yc_[:].rearrange("p (c v) -> p c v", c=4)
                S.op("dve", lambda e: e.tensor_reduce(out=g_[:, 0, :], in_=PQ1, axis=AX.X, op=ALU.add), reads=[kPB1], writes=[kg_])
                S.op("act", lambda e: e.activation(out=ysq, in_=PQ1, func=AF.Square), reads=[kPB1], writes=[kysq])
                S.op("dve", lambda e: e.tensor_reduce(out=g_[:, 1, :], in_=ysq, axis=AX.X, op=ALU.add), reads=[kysq], writes=[kg_])
                S.op("dve", lambda e: e.tensor_scalar(out=g_[:, 2, :], in0=g_[:, 0, :], scalar1=1.0 / 64, scalar2=1.0, op0=ALU.mult, op1=ALU.mult), reads=[kg_], writes=[kg_])
                S.op("dve", lambda e: e.tensor_tensor(out=g_[:, 3, :], in0=g_[:, 2, :], in1=g_[:, 2, :], op=ALU.mult), reads=[kg_], writes=[kg_])
                S.op("dve", lambda e: e.scalar_tensor_tensor(out=g_[:, 4, :], in0=g_[:, 1, :], scalar=1.0 / 64, in1=g_[:, 3, :], op0=ALU.mult, op1=ALU.subtract),
                     reads=[kg_], writes=[kg_])
                S.op("act", lambda e: e.activation(out=g_[:, 5, :], in_=g_[:, 4, :], func=AF.Sqrt, bias=GN_EPS), reads=[kg_], writes=[kg_])
                S.op("dve", lambda e: e.reciprocal(out=g_[:, 6, :], in_=g_[:, 5, :]), reads=[kg_], writes=[kg_])
                S.op("dve", lambda e: e.tensor_tensor(out=ycen, in0=PQ1, in1=g_[:, 2, :].unsqueeze(2).to_broadcast([128, 4, 64]), op=ALU.subtract),
                     reads=[kPB1, kg_], writes=[kycen])
                yield
                for h in range(2):
                    hs = slice(h * 64, (h + 1) * 64)
                    S.op("dve", lambda e, hs=hs: e.tensor_tensor(out=ynbd[hs, :, hs], in0=ycen[hs, :, :], in1=g_[hs, 6, :].unsqueeze(2).to_broadcast([64, 4, 64]),
                                                                  op=ALU.mult), reads=[kycen, kg_], writes=[kynbd])
                for c in range(4):
                    S.op("pe", lambda e, c=c: e.matmul(PQ0[:, c, :], lhsT=ynbd[:, c, :], rhs=fst[:], start=True, stop=True), reads=[kynbd, kfst], writes=[kPB1])
                S.op("act", lambda e: e.activation(out=ynf[:, :, n * CH:(n + 1) * CH], in_=PQ0, func=AF.Copy), reads=[kPB1], writes=kynf)
                yield

        def drain(g):
            for _ in g:
                pass

        def interleave(ga, gb, ra=2):
            a_live, b_live = ga is not None, gb is not None
            while a_live or b_live:
                for _ in range(ra):
                    if a_live:
                        try:
                            next(ga)
                        except StopIteration:
                            a_live = False
                if b_live:
                    try:
                        next(gb)
                    except StopIteration:
                        b_live = False

        drain(gen_AD(0, 0))
        for n_ in range(NCH):
            gd = gen_AD(n_ + 1, (n_ + 1) % 2) if n_ + 1 < NCH else None
            interleave(gd, gen_SQ(n_, n_ % 2))
        if DEBUG_STOP < 8:
            return

        if not full:
            if t + 2 < n_tiles:
                load_x(t + 2)
            return
        for c in range(4):
            S.op("pe", lambda e, c=c: e.matmul(psm[:, 0, :], lhsT=g2b0[:, c * 128:(c + 1) * 128], rhs=sgd0[:], start=True, stop=False), reads=[kg2b0, ksgd], writes=[kpsm])
            S.op("pe", lambda e, c=c: e.matmul(psm[:, 0, :], lhsT=g2b1[:, c * 128:(c + 1) * 128], rhs=sgd1[:], start=False, stop=True), reads=[kg2b1, ksgd], writes=[kpsm])
            y1, ky1 = tf.next()
            S.op("dve", lambda e, c=c, y1=y1: e.scalar_tensor_tensor(out=y1[:], in0=ynf[:, c, :], scalar=cc[:, O_LW + c:O_LW + c + 1], in1=bonus[:, c, :], op0=ALU.mult, op1=ALU.add),
                 reads=[kynf[c], kcc, kbonus[c]], writes=[ky1])
            S.op("dve", lambda e, c=c, y1=y1: e.scalar_tensor_tensor(out=yT[:, 4 + c, :], in0=y1[:], scalar=cc[:, O_LB + c:O_LB + c + 1], in1=psm[:, 0, :], op0=ALU.add, op1=ALU.mult),
                 reads=[ky1, kcc, kpsm], writes=[kyT[4 + c]])
        if DEBUG_STOP < 9:
            return
        for blk in range(NBLK):
            for hf in range(2):
                pflat = pp[hf][:].rearrange("p a w -> p (a w)")
                for e_ in range(8):
                    S.op("pe", lambda e, e_=e_, hf=hf, pflat=pflat, blk=blk: e.matmul(pflat, lhsT=yT[:, e_, blk * 128:(blk + 1) * 128], rhs=wout[:, e_, hf * 512:(hf + 1) * 512],
                                                                                      start=(e_ == 0), stop=(e_ == 7)), reads=[kyT[e_], kwout], writes=[kpp[hf]])
            class _V:
                def __init__(self, ap): self.ap = ap
                def __getitem__(self, k): return self.ap
            _post_norm_residual(S, [(_V(pp[0][:].rearrange("p a w -> p (a w)")), kpp[0]), (_V(pp[1][:].rearrange("p a w -> p (a w)")), kpp[1])],
                                xt[:, blk, :], kxts[b][blk], grow, kgrow, scr)
        ht_i = t - n_pre
        S.op("sp", lambda e, xt=xt, ht_i=ht_i: e.dma_start(out=hscr[W * ht_i:W * (ht_i + 1), :].rearrange("(b p) d -> p b d", p=128), in_=xt[:]),
             reads=kxts[b], writes=[khscr[ht_i]], dma_sem=dsem[f"h{b}"])
        if t + 2 < n_tiles:
            load_x(t + 2)

    for t_i in range(n_tiles):
        do_tile(t_i)


def _host_consts():
    km = np.zeros((128, NKM), np.float32)
    km[:, M_ID:M_ID + 128] = np.eye(128, dtype=np.float32)
    idx = np.arange(128)
    same = (idx[:, None] // 64) == (idx[None, :] // 64)
    s, t = idx[:, None] % 64, idx[None, :] % 64
    km[:, M_SU:M_SU + 128] = (same & (s < t)).astype(np.float32)
    km[:, M_IU:M_IU + 128] = (same & (s <= t)).astype(np.float32)
    km[:, M_SL:M_SL + 128] = (same & (s > t)).astype(np.float32)
    km[:, M_BO:M_BO + 128] = same.astype(np.float32)
    km[:, M_F:M_F + 64] = (idx[:, None] % 64 == np.arange(64)[None, :]).astype(np.float32)
    rst = np.ones((128, 256), np.float32)
    rst[:, ::64] = 0.0
    km[:, M_RST:M_RST + 256] = rst
    return km


def _pack_cc(inp, hmask):
    cc = np.zeros((128, NCC), np.float32)
    col = lambda v, n: np.ascontiguousarray(np.asarray(v, np.float32).reshape(n, 128).T)
    cc[:, O_PMG:O_PMG + 8] = col(inp["pre_mix_g"][0], 8)
    cc[:, O_PFG:O_PFG + 8] = col(inp["pre_ffn_g"][0], 8)
    caw = np.asarray(inp["conv_a_w"][0], np.float32)
    cc[:, O_CAW:O_CAW + 12] = caw.T.reshape(4, 128, 3).transpose(1, 0, 2).reshape(128, 12)
    mu = np.zeros(1920, np.float32)
    mu[:1824] = np.asarray(inp["shift_mu"][0], np.float32)
    cc[:, O_MU:O_MU + 15] = col(mu, 15)
    for off, name in ((O_W0, "w0"), (O_A0, "a0"), (O_KK, "k_k"), (O_KA, "k_a"), (O_LW, "lnx_w"), (O_LB, "lnx_b")):
        cc[:, off:off + 4] = col(inp[name][0], 4)
    cc[:, O_RK:O_RK + 4] = col(np.asarray(inp["r_k"][0], np.float32).reshape(512), 4)
    fcw = np.asarray(inp["ffn_conv_w"][0], np.float32)
    cc[:, O_FCW:O_FCW + 132] = fcw.T.reshape(44, 128, 3).transpose(1, 0, 2).reshape(128, 132)
    cc[:, O_FCB:O_FCB + 44] = col(inp["ffn_conv_b"][0], 44)
    cc[:, O_HM] = hmask
    return cc


_NC_CACHE = {}


def kernel(**inputs):
    n_pre, n_main = 15, 16
    x = np.asarray(inputs["x"], np.float32)
    B, T, _ = x.shape
    half = T // 2
    if "full" not in _NC_CACHE:
        _NC_CACHE["full"] = build(n_pre, n_main, "full")
    nc = _NC_CACHE["full"]
    km = _host_consts()
    f = lambda n: np.ascontiguousarray(np.asarray(inputs[n], np.float32)[0])
    in_maps = []
    for c in range(8):
        b, h = c // 2, c % 2
        xin = np.zeros((T, D), np.float32)
        if h == 0:
            xin[half:] = x[b, :half]
        else:
            xin[:] = x[b]
        in_maps.append({
            "xin": xin, "cc": _pack_cc(inputs, float(h)), "km": km,
            "post_mix_g": f("post_mix_g"), "post_ffn_g": f("post_ffn_g"),
            "w_in": f("w_in"), "w_out": f("w_out"), "w_up": f("w_up"), "w_down": f("w_down"),
            "w2": f("w2"), "a2": f("a2"), "g2": f("g2"),
        })
    res = run_bass_kernel_spmd(nc, in_maps, core_ids=list(range(8)))
    out = np.zeros((B, T, D), np.float32)
    for c in range(8):
        b, h = c // 2, c % 2
        out[b, h * half:(h + 1) * half] = res.results[c]["out"]
    return out
```

```python
import contextlib
import numpy as np
import concourse.bass as bass
import concourse.mybir as mybir
from concourse.bass_utils import run_bass_kernel_spmd

F32 = mybir.dt.float32
BF16 = mybir.dt.bfloat16
AF = mybir.ActivationFunctionType
ALU = mybir.AluOpType
AX = mybir.AxisListType

D = 1024
W = 256
NBLK = 2
CH = 64
NCH = W // CH
INC = 3360
DFF = 2816
NPAIR = 22
RMS_EPS = 1e-6
GN_EPS = 64 * 1e-5
EPOCH = 12000
DEBUG_SUB = 99
DEBUG_STOP = 99

O_PMG, O_PFG, O_CAW, O_MU, O_W0, O_A0, O_KK, O_KA, O_RK, O_LW, O_LB, O_FCW, O_FCB, O_HM = (
    0, 8, 16, 28, 43, 47, 51, 55, 59, 63, 67, 71, 203, 247)
NCC = 248
M_ID, M_SU, M_IU, M_SL, M_BO, M_F, M_RST = 0, 128, 256, 384, 512, 640, 704
NKM = 704 + 256


class Key:
    __slots__ = ("name", "writer", "readers", "excl")

    def __init__(self, name, excl=False):
        self.name = name
        self.writer = None
        self.readers = []
        self.excl = excl


def PKey(name):
    return Key(name, excl=True)


class Sched:
    ENGS = ("pe", "act", "dve", "pool", "sp")

    def __init__(self, nc, sem_stack, prefix):
        self.nc = nc
        self.sem_stack = sem_stack
        self.prefix = prefix
        self.ops = {e: [] for e in self.ENGS}
        self.count = {e: 0 for e in self.ENGS}
        self.sems = {}
        self.waited = {e: {} for e in self.ENGS}
        self.dma_counts = {}
        self.last_tok = {e: None for e in self.ENGS}

    def _eng_sem(self, eng, idx):
        sid = f"{self.prefix}s_{eng}_{idx // EPOCH}"
        self.sems.setdefault(sid, None)
        return sid, (idx % EPOCH) + 1

    def new_dma_sem(self, name):
        sid = f"{self.prefix}d_{name}"
        assert sid not in self.sems, sid
        self.sems[sid] = None
        self.dma_counts[sid] = 0
        return sid

    def _need_waits(self, eng, tokens):
        w = self.waited[eng]
        best = {}
        for t in tokens:
            if t is None:
                continue
            sid, val, _ = t
            if w.get(sid, 0) >= val:
                continue
            if best.get(sid, 0) < val:
                best[sid] = val
        for sid, val in best.items():
            w[sid] = val
        return list(best.items())

    def op(self, eng, fn, reads=(), writes=(), dma_sem=None):
        toks = []
        raw = set()
        for k in reads:
            toks.append(k.writer)
            if k.writer is not None:
                raw.add(k.writer)
            if k.excl:
                toks.extend(r for r in k.readers if r[2] != eng)
        for k in writes:
            toks.append(k.writer)
            toks.extend(k.readers)
        if eng == "pe":
            toks = [t for t in toks if t is not None and t[2] != "pe"]
        waits = self._need_waits(eng, toks)
        if dma_sem is None:
            idx = self.count[eng]
            self.count[eng] += 1
            sid, val = self._eng_sem(eng, idx)
            tok = (sid, val, eng)
            inc = (sid, 1)
            self.last_tok[eng] = tok
        else:
            self.dma_counts[dma_sem] += 16
            tok = (dma_sem, self.dma_counts[dma_sem], "dma")
            inc = (dma_sem, 16)
        self.ops[eng].append((fn, waits, inc))
        for k in reads:
            k.readers.append(tok)
        for k in writes:
            k.writer = tok
            k.readers = []
        return tok

    def barrier(self, extra_keys=()):
        toks = [t for t in self.last_tok.values() if t is not None]
        for k in extra_keys:
            toks.append(k.writer)
            toks.extend(k.readers)
        for eng in self.ENGS:
            waits = self._need_waits(eng, [t for t in toks if t is not None and t[2] != eng])
            if waits:
                self.ops[eng].append((None, waits, None))

    def final_wait(self, eng, keys):
        toks = []
        for k in keys:
            toks.append(k.writer)
            toks.extend(k.readers)
        waits = self._need_waits(eng, toks)
        self.ops[eng].append((None, waits, None))

    def emit(self):
        nc = self.nc
        with contextlib.ExitStack() as st:
            handles = {sid: self.sem_stack.enter_context(nc.semaphore(sid)) for sid in self.sems}
            block = st.enter_context(nc.Block())

            def run(engobj, lst):
                for fn, waits, inc in lst:
                    for sid, val in waits:
                        engobj.wait_ge(handles[sid], val)
                    if fn is not None:
                        fn(engobj).then_inc(handles[inc[0]], inc[1])

            @block.tensor
            def _(e):
                run(e, self.ops["pe"])

            @block.scalar
            def _(e):
                run(e, self.ops["act"])

            @block.vector
            def _(e):
                run(e, self.ops["dve"])

            @block.gpsimd
            def _(e):
                run(e, self.ops["pool"])

            @block.sync
            def _(e):
                run(e, self.ops["sp"])


class Rot:
    def __init__(self, tiles, excl=False):
        self.tiles = tiles
        self.keys = [Key(f"rot{i}", excl) for i in range(len(tiles))]
        self.i = 0

    def next(self):
        j = self.i % len(self.tiles)
        self.i += 1
        return self.tiles[j], self.keys[j]


def _rms_transpose(S, src, ksrc, dstT, kdst, tcol, scr, ident, kid, ptr, extra_scale=None, kextra=None):
    st, kst = scr["stat"].next()
    xs, kxs = scr["xs"].next()
    pt, kpt = ptr.next()
    S.op("act", lambda e: e.activation(out=xs[:], in_=src, func=AF.Square, accum_out=st[:, 0:1]),
         reads=[ksrc], writes=[kxs, kst])
    S.op("act", lambda e: e.activation(out=st[:, 1:2], in_=st[:, 0:1], func=AF.Sqrt, scale=1.0 / D, bias=RMS_EPS),
         reads=[kst], writes=[kst])
    S.op("dve", lambda e: e.reciprocal(out=st[:, 2:3], in_=st[:, 1:2]), reads=[kst], writes=[kst])
    rs = st[:, 2:3]
    if extra_scale is not None:
        S.op("dve", lambda e: e.tensor_tensor(out=st[:, 3:4], in0=st[:, 2:3], in1=extra_scale, op=ALU.mult),
             reads=[kst, kextra], writes=[kst])
        rs = st[:, 3:4]
    S.op("pool", lambda e: e.tensor_scalar(out=xs[:], in0=src, scalar1=rs, scalar2=1.0, op0=ALU.mult, op1=ALU.mult),
         reads=[ksrc, kst], writes=[kxs])
    for kc in range(8):
        S.op("pe", lambda e, kc=kc: e.transpose(out=pt[:, kc, :], in_=xs[:, kc * 128:(kc + 1) * 128], identity=ident[:]),
             reads=[kxs, kid], writes=[kpt])
    S.op("act", lambda e: e.activation(out=dstT[:, :, tcol:tcol + 128], in_=pt[:], func=AF.Copy),
         reads=[kpt], writes=[kdst])


def _post_norm_residual(S, pd_pairs, res, kres, grow, kgrow, scr):
    st, kst = scr["stat"].next()
    tmps = [scr["tmp512"].next() for _ in range(2)]
    for hf, (pd, kpd) in enumerate(pd_pairs):
        junk, kjunk = tmps[hf]
        S.op("act", lambda e, pd=pd, hf=hf, junk=junk: e.activation(out=junk[:], in_=pd[:], func=AF.Square,
                                                                  accum_out=st[:, hf:hf + 1]),
             reads=[kpd], writes=[kjunk, kst])
    S.op("dve", lambda e: e.tensor_tensor(out=st[:, 2:3], in0=st[:, 0:1], in1=st[:, 1:2], op=ALU.add), reads=[kst], writes=[kst])
    S.op("act", lambda e: e.activation(out=st[:, 3:4], in_=st[:, 2:3], func=AF.Sqrt, scale=1.0 / D, bias=RMS_EPS),
         reads=[kst], writes=[kst])
    S.op("dve", lambda e: e.reciprocal(out=st[:, 4:5], in_=st[:, 3:4]), reads=[kst], writes=[kst])
    for hf, (pd, kpd) in enumerate(pd_pairs):
        tmp, ktmp = tmps[hf]
        S.op("dve", lambda e, pd=pd, hf=hf, tmp=tmp: e.scalar_tensor_tensor(
            out=tmp[:], in0=pd[:], scalar=st[:, 4:5], in1=grow[:, hf * 512:(hf + 1) * 512], op0=ALU.mult, op1=ALU.mult),
            reads=[kpd, kst, kgrow], writes=[ktmp])
        S.op("pool", lambda e, hf=hf, tmp=tmp: e.tensor_tensor(out=res[:, hf * 512:(hf + 1) * 512], in0=res[:, hf * 512:(hf + 1) * 512],
                                                               in1=tmp[:], op=ALU.add),
             reads=[ktmp, kres], writes=[kres])


def phase2_ffn(nc, S, st, io, n_main, shared):
    sb = lambda n, s, d: st.enter_context(nc.sbuf_tensor(n, s, d))
    ps = lambda n, s, d: st.enter_context(nc.psum_tensor(n, s, d))
    cc, kcc = shared["cc"], shared["kcc"]
    ident, kid = shared["ident"], shared["kid"]
    hscr, khscr = io["hscr"], io["khscr"]
    out = io["out"]

    wup = sb("wup", [128, 8, DFF * 2], BF16)
    wdn = sb("wdn", [128, NPAIR, D], BF16)
    kwup = [Key(f"wup{k}") for k in range(8)]
    kwdn = Key("wdn")
    grow = sb("grow2", [128, D], F32)
    kgrow = Key("grow2")
    fh = sb("fh", [128, NPAIR, 2, 2], F32)
    kfh = [Key(f"fh{i}") for i in range(NPAIR)]
    hts = [sb(f"ht{i}", [128, NBLK, D], F32) for i in range(2)]
    khts = [[Key(f"ht{i}_{b}") for b in range(NBLK)] for i in range(2)]
    hnTs = [sb(f"hnT{i}", [128, 8, W], BF16) for i in range(2)]
    khnT = [Key(f"hnT{i}") for i in range(2)]
    act = sb("actb", [128, NPAIR, W], BF16)
    kact = [Key(f"act{i}") for i in range(NPAIR)]
    scr = {
        "stat": Rot([sb(f"stat{i}", [128, 8], F32) for i in range(4)]),
        "xs": Rot([sb(f"xs{i}", [128, D], BF16) for i in range(2)]),
        "tmp512": Rot([sb(f"tmp512_{i}", [128, 512], F32) for i in range(2)]),
    }
    fbuf = Rot([sb(f"fbuf{i}", [128, 2, W + 2], F32) for i in range(4)])
    cg = Rot([sb(f"cg{i}", [128, W], F32) for i in range(4)])
    cu = Rot([sb(f"cu{i}", [128, W], F32) for i in range(4)])
    t1 = Rot([sb(f"t1_{i}", [128, W], F32) for i in range(3)])
    t2 = Rot([sb(f"t2_{i}", [128, W], F32) for i in range(3)])
    sg = Rot([sb(f"sg{i}", [128, W], F32) for i in range(3)])
    WQ = DFF // 4
    ptr = Rot([ps("ptr2", [128, 8, 128], BF16)], excl=True)
    pf = Rot([ps(f"pf{i}", [128, 2, W], F32) for i in range(3)], excl=True)
    pd = [[ps(f"pd{b}{h}", [128, 512], F32) for h in range(2)] for b in range(NBLK)]
    kpd = [[PKey(f"pd{b}{h}") for h in range(2)] for b in range(NBLK)]
    dsem = {n: S.new_dma_sem("p2_" + n) for n in ("wst0", "wst1", "wst2", "wst3", "wdn", "grow", "h0", "h1", "hw", "out0", "out1")}

    S.op("sp", lambda e: e.dma_start(out=grow[:], in_=io["post_ffn_g"].partition_broadcast(128)), writes=[kgrow], dma_sem=dsem["grow"])
    actf = act[:].rearrange("p i w -> p (i w)").bitcast(F32)
    kstg = [Key(f"stg{q}") for q in range(4)]
    for kc in range(8):
        for q in range(8):
            j = kc * 8 + q
            r_ = j % 4
            wt, kwt = actf[:, r_ * WQ:(r_ + 1) * WQ], kstg[r_]
            S.op("sp", lambda e, wt=wt, kc=kc, q=q: e.dma_start(out=wt, in_=io["w_up"][kc * 128:(kc + 1) * 128, q * WQ:(q + 1) * WQ]),
                 writes=[kwt], dma_sem=dsem[f"wst{r_}"])
            if j % 2 == 0:
                S.op("act", lambda e, wt=wt, kc=kc, q=q: e.activation(out=wup[:, kc, q * WQ:(q + 1) * WQ], in_=wt, func=AF.Copy,
                                                                       scale=cc[:, O_PFG + kc:O_PFG + kc + 1]),
                     reads=[kwt, kcc], writes=[kwup[kc]])
            else:
                S.op("dve", lambda e, wt=wt, kc=kc, q=q: e.tensor_scalar(out=wup[:, kc, q * WQ:(q + 1) * WQ], in0=wt,
                                                                          scalar1=cc[:, O_PFG + kc:O_PFG + kc + 1], scalar2=1.0, op0=ALU.mult, op1=ALU.mult),
                     reads=[kwt, kcc], writes=[kwup[kc]])
    S.op("pool", lambda e: e.memset(act[:, :, 0:1], 0.0), writes=kstg + kact)
    S.op("pool", lambda e: e.dma_start(out=wdn[:], in_=io["w_down"].rearrange("(i p) d -> p i d", p=128)), writes=[kwdn], dma_sem=dsem["wdn"])

    hw = hts[1]
    S.op("sp", lambda e: e.dma_start(out=hw[:, 0, :], in_=hscr[W - 128:W, :]), reads=[khscr[0]], writes=[khts[1][0]], dma_sem=dsem["hw"])
    _rms_transpose(S, hw[:, 0, :], khts[1][0], hnTs[1], khnT[1], 0, scr, ident, kid, ptr,
                   extra_scale=cc[:, O_HM:O_HM + 1], kextra=kcc)
    pfh, kpfh = pf.next()
    pfh_v = pfh[:].rearrange("p a w -> p (a w)")
    for ch in range(2 * NPAIR):
        i, hf = ch % NPAIR, ch // NPAIR
        col = (i * 2 + hf) * 2
        for kc in range(8):
            S.op("pe", lambda e, ch=ch, kc=kc, col=col: e.matmul(pfh_v[:, col:col + 2], lhsT=wup[:, kc, ch * 128:(ch + 1) * 128],
                                                                  rhs=hnTs[1][:, kc, 126:128], start=(kc == 0), stop=(kc == 7)),
                 reads=[kwup[kc], khnT[1]], writes=[kpfh])
    S.op("act", lambda e: e.activation(out=fh[:].rearrange("p i a b -> p (i a b)"), in_=pfh_v[:, 0:NPAIR * 4], func=AF.Copy),
         reads=[kpfh], writes=kfh)

    def load(t):
        b = t % 2
        S.op("sp", lambda e: e.dma_start(out=hts[b][:], in_=hscr[W * (1 + t):W * (2 + t), :].rearrange("(b p) d -> p b d", p=128)),
             reads=[khscr[1 + t]], writes=khts[b], dma_sem=dsem[f"h{b}"])

    def prologue(t):
        b = t % 2
        for blk in range(NBLK):
            _rms_transpose(S, hts[b][:, blk, :], khts[b][blk], hnTs[b], khnT[b], blk * 128, scr, ident, kid, ptr)

    state = {}

    def up_mm(t, i):
        b = t % 2
        p, kp = pf.next()
        state[(t, i)] = (p, kp)
        for hf in range(2):
            ch = hf * NPAIR + i
            for kc in range(8):
                S.op("pe", lambda e, p=p, hf=hf, ch=ch, kc=kc: e.matmul(p[:, hf, :], lhsT=wup[:, kc, ch * 128:(ch + 1) * 128],
                                                                         rhs=hnTs[b][:, kc, :], start=(kc == 0), stop=(kc == 7)),
                     reads=[kwup[kc], khnT[b]], writes=[kp])

    def elem(t, i):
        p, kp = state.pop((t, i))
        fb, kfb = fbuf.next()
        S.op("pool", lambda e: e.tensor_copy(out=fb[:, :, 0:2], in_=fh[:, i, :, :]), reads=[kfh[i]], writes=[kfb])
        S.op("act", lambda e: e.activation(out=fb[:, :, 2:W + 2], in_=p[:], func=AF.Copy), reads=[kp], writes=[kfb])
        yield
        S.op("pool", lambda e: e.tensor_copy(out=fh[:, i, :, :], in_=fb[:, :, W:W + 2]), reads=[kfb], writes=[kfh[i]])
        outs = []
        for hf, rot in ((0, cg), (1, cu)):
            ch = hf * NPAIR + i
            c, kc_ = rot.next()
            wof = O_FCW + ch * 3
            S.op("act", lambda e, c=c, hf=hf, wof=wof, ch=ch: e.activation(out=c[:], in_=fb[:, hf, 2:W + 2], func=AF.Identity,
                                                                          scale=cc[:, wof + 2:wof + 3], bias=cc[:, O_FCB + ch:O_FCB + ch + 1]),
                 reads=[kfb, kcc], writes=[kc_])
            outs.append((c, kc_, hf, wof))
        yield
        for c, kc_, hf, wof in outs:
            S.op("dve", lambda e, c=c, hf=hf, wof=wof: e.scalar_tensor_tensor(out=c[:], in0=fb[:, hf, 1:W + 1], scalar=cc[:, wof + 1:wof + 2],
                                                                             in1=c[:], op0=ALU.mult, op1=ALU.add),
                 reads=[kfb, kcc, kc_], writes=[kc_])
        yield
        for c, kc_, hf, wof in outs:
            S.op("dve", lambda e, c=c, hf=hf, wof=wof: e.scalar_tensor_tensor(out=c[:], in0=fb[:, hf, 0:W], scalar=cc[:, wof:wof + 1],
                                                                             in1=c[:], op0=ALU.mult, op1=ALU.add),
                 reads=[kfb, kcc, kc_], writes=[kc_])
        outs = [(c, kc_) for c, kc_, hf, wof in outs]
        yield
        (g_, kg), (u_, ku) = outs
        a1, ka1 = t1.next()
        a2, ka2 = t2.next()
        s_, ks = sg.next()
        S.op("pool", lambda e: e.tensor_tensor(out=a1[:], in0=g_[:], in1=g_[:], op=ALU.mult), reads=[kg], writes=[ka1])
        yield
        S.op("pool", lambda e: e.tensor_scalar(out=a1[:], in0=a1[:], scalar1=0.044715, scalar2=1.0, op0=ALU.mult, op1=ALU.add),
             reads=[ka1], writes=[ka1])
        S.op("pool", lambda e: e.tensor_tensor(out=a2[:], in0=a1[:], in1=g_[:], op=ALU.mult), reads=[ka1, kg], writes=[ka2])
        S.op("dve", lambda e: e.tensor_tensor(out=a1[:], in0=g_[:], in1=u_[:], op=ALU.mult), reads=[kg, ku, ka2], writes=[ka1])
        yield
        S.op("act", lambda e: e.activation(out=s_[:], in_=a2[:], func=AF.Sigmoid, scale=1.5957691216), reads=[ka2], writes=[ks])
        yield
        S.op("dve", lambda e: e.tensor_tensor(out=act[:, i, :], in0=a1[:], in1=s_[:], op=ALU.mult), reads=[ka1, ks], writes=[kact[i]])

    def down_mm(t, i):
        for blk in range(NBLK):
            for hf in range(2):
                S.op("pe", lambda e, blk=blk, hf=hf: e.matmul(pd[blk][hf][:], lhsT=act[:, i, blk * 128:(blk + 1) * 128],
                                                              rhs=wdn[:, i, hf * 512:(hf + 1) * 512], start=(i == 0), stop=(i == NPAIR - 1)),
                     reads=[kact[i], kwdn], writes=[kpd[blk][hf]])

    def epilogue(t):
        b = t % 2
        for blk in range(NBLK):
            _post_norm_residual(S, [(pd[blk][0], kpd[blk][0]), (pd[blk][1], kpd[blk][1])], hts[b][:, blk, :], khts[b][blk], grow, kgrow, scr)
        S.op("sp", lambda e: e.dma_start(out=out[W * t:W * (t + 1), :].rearrange("(b p) d -> p b d", p=128), in_=hts[b][:]),
             reads=khts[b], dma_sem=dsem[f"out{b}"])

    load(0)
    if n_main > 1:
        load(1)
    prologue(0)
    TSTEP = 2
    for t in range(n_main):
        active = []
        i_next, tick, pro_done = 0, 0, False
        while i_next < NPAIR or active:
            if i_next < NPAIR and tick % TSTEP == 0:
                up_mm(t, i_next)
                active.append((i_next, elem(t, i_next)))
                i_next += 1
                if i_next == 15 and t + 1 < n_main and not pro_done:
                    prologue(t + 1)
                    pro_done = True
            for item in list(active):
                i_, g_ = item
                try:
                    next(g_)
                except StopIteration:
                    active.remove(item)
                    down_mm(t, i_)
            tick += 1
        epilogue(t)
        if t + 2 < n_main:
            load(t + 2)
    S.final_wait("sp", [k for ks_ in khts for k in ks_])


def build(n_pre, n_main, mode="full"):
    nc = bass.Bass("TRN2", target_bir_lowering=False)
    TT = (n_pre + 1 + n_main) * W
    io = {}
    di = lambda n, s: nc.dram_tensor(n, s, F32, kind="ExternalInput").ap()
    io["cc"] = di("cc", [128, NCC])
    io["km"] = di("km", [128, NKM])
    io["post_ffn_g"] = di("post_ffn_g", [D])
    io["w_up"] = di("w_up", [D, 2 * DFF])
    io["w_down"] = di("w_down", [DFF, D])
    if mode == "ffn":
        io["hscr"] = di("hscr", [(1 + n_main) * W, D])
    else:
        io["xin"] = di("xin", [TT, D])
        io["post_mix_g"] = di("post_mix_g", [D])
        io["w_in"] = di("w_in", [D, INC])
        io["w_out"] = di("w_out", [D, D])
        io["w2"] = di("w2", [64, 512])
        io["a2"] = di("a2", [64, 512])
        io["g2"] = di("g2", [160, 512])
        io["hscr"] = nc.dram_tensor("hscr", [(1 + n_main) * W, D], F32, kind="Internal").ap()
    io["khscr"] = [Key(f"hscr{i}") for i in range(1 + n_main)]
    io["out"] = nc.dram_tensor("out", [n_main * W, D], F32, kind="ExternalOutput").ap()

    with contextlib.ExitStack() as sem_stack, contextlib.ExitStack() as st0:
        cc = st0.enter_context(nc.sbuf_tensor("cc_sb", [128, NCC], F32))
        ident = st0.enter_context(nc.sbuf_tensor("ident", [128, 128], BF16))

        def shared_loads(S):
            kcc, kid = Key("cc"), Key("ident")
            d0 = S.new_dma_sem("cc")
            d1 = S.new_dma_sem("ident")
            S.op("sp", lambda e: e.dma_start(out=cc[:], in_=io["cc"][:, :]), writes=[kcc], dma_sem=d0)
            S.op("pool", lambda e: e.dma_start(out=ident[:], in_=io["km"][:, M_ID:M_ID + 128]), writes=[kid], dma_sem=d1)
            return {"cc": cc, "kcc": kcc, "ident": ident, "kid": kid}

        if mode != "ffn":
            S1 = Sched(nc, sem_stack, "a")
            shared = shared_loads(S1)
            with contextlib.ExitStack() as st1:
                phase1_mixer(nc, S1, st1, io, n_pre, n_main, shared)
                S1.final_wait("sp", io["khscr"])
                S1.emit()
            S2 = Sched(nc, sem_stack, "b")
            shared = {"cc": cc, "kcc": Key("cc2"), "ident": ident, "kid": Key("ident2")}
            io["khscr"] = [Key(f"hscr2_{i}") for i in range(1 + n_main)]
        else:
            S2 = Sched(nc, sem_stack, "b")
            shared = shared_loads(S2)
        with contextlib.ExitStack() as st2:
            phase2_ffn(nc, S2, st2, io, n_main, shared)
            S2.emit()
    return nc


def phase1_mixer(nc, S, st, io, n_pre, n_main, shared):
    sb = lambda n, s, d: st.enter_context(nc.sbuf_tensor(n, s, d))
    ps = lambda n, s, d: st.enter_context(nc.psum_tensor(n, s, d))
    cc, kcc = shared["cc"], shared["kcc"]
    ident, kid = shared["ident"], shared["kid"]
    xin, hscr, khscr = io["xin"], io["hscr"], io["khscr"]
    n_tiles = n_pre + 1 + n_main
    C05 = 0.6065306597126334

    win = sb("win", [128, 8, INC], BF16)
    kwin = [Key(f"win{k}") for k in range(8)]
    wout = sb("wout", [128, 8, D], BF16)
    kwout = Key("wout")
    w2b = sb("w2b", [128, 512], BF16)
    a2b = sb("a2b", [128, 512], BF16)
    g2b0 = sb("g2b0", [128, 512], BF16)
    g2b1 = sb("g2b1", [128, 512], BF16)
    wg1 = sb("wg1", [128, 8, 128], BF16)
    kwg1 = Key("wg1")
    ksmallw = Key("smallw")
    msu4 = sb("msu4", [128, 512], BF16)
    msl = sb("msl", [128, 128], BF16)
    bones = sb("bones", [128, 128], BF16)
    fst = sb("fst", [128, 64], BF16)
    rst = sb("rst", [128, W], F32)
    kconst = Key("p1const")
    grow = sb("grow1", [128, D], F32)
    kgrow = Key("grow1")
    dc = sb("dc", [128, 20], F32)
    kdc = Key("dc")
    dsem = {n: S.new_dma_sem("p1_" + n) for n in ("wst0", "wst1", "wst2", "wst3", "const", "grow", "x0", "x1", "h0", "h1")}

    S.op("sp", lambda e: e.dma_start(out=grow[:], in_=io["post_mix_g"].partition_broadcast(128)), writes=[kgrow], dma_sem=dsem["grow"])
    S.op("sp", lambda e: e.dma_start(out=rst[:], in_=io["km"][:, M_RST:M_RST + W]), writes=[kconst], dma_sem=dsem["const"])
    uniq = [0]

    def pool_dma(fn, key):
        uniq[0] += 1
        S.op("pool", fn, writes=[key], dma_sem=S.new_dma_sem(f"p1u{uniq[0]}"))

    kmsl, kbones, kfst = Key("msl"), Key("bones"), Key("fst")
    kmsu4 = [Key(f"msu4_{q}") for q in range(4)]
    kw2b, ka2b, kg2b0, kg2b1 = Key("w2b"), Key("a2b"), Key("g2b0"), Key("g2b1")
    for dst, c0, n_, k_ in ((msl, M_SL, 128, kmsl), (bones, M_BO, 128, kbones), (fst, M_F, 64, kfst)):
        pool_dma(lambda e, dst=dst, c0=c0, n_=n_: e.dma_start(out=dst[:], in_=io["km"][:, c0:c0 + n_]), k_)
    for q, c0 in enumerate((M_SU, M_IU, M_SU, M_IU)):
        pool_dma(lambda e, q=q, c0=c0: e.dma_start(out=msu4[:, q * 128:(q + 1) * 128], in_=io["km"][:, c0:c0 + 128]), kmsu4[q])
    S.op("pool", lambda e: e.memset(w2b[:], 0.0), writes=[kw2b])
    S.op("pool", lambda e: e.memset(a2b[:], 0.0), writes=[ka2b])
    S.op("pool", lambda e: e.memset(g2b1[:], 0.0), writes=[kg2b1])
    S.op("pool", lambda e: e.memset(wg1[:], 0.0), writes=[kwg1])
    pool_dma(lambda e: e.dma_start(out=w2b[0:64, :], in_=io["w2"][:, :]), kw2b)
    pool_dma(lambda e: e.dma_start(out=a2b[64:128, :], in_=io["a2"][:, :]), ka2b)
    pool_dma(lambda e: e.dma_start(out=g2b0[:], in_=io["g2"][0:128, :]), kg2b0)
    pool_dma(lambda e: e.dma_start(out=g2b1[0:32, :], in_=io["g2"][128:160, :]), kg2b1)
    pool_dma(lambda e: e.dma_start(out=wout[:], in_=io["w_out"].rearrange("(k p) d -> p k d", p=128)), kwout)
    S.op("dve", lambda e: e.tensor_scalar(out=dc[:, 0:15], in0=cc[:, O_MU:O_MU + 15], scalar1=-1.0, scalar2=1.0, op0=ALU.mult, op1=ALU.add),
         reads=[kcc], writes=[kdc])
    S.op("dve", lambda e: e.tensor_scalar(out=dc[:, 15:19], in0=cc[:, O_KA:O_KA + 4], scalar1=-1.0, scalar2=1.0, op0=ALU.mult, op1=ALU.add),
         reads=[kcc], writes=[kdc])
    xts = [sb(f"xt{i}", [128, NBLK, D], F32) for i in range(2)]
    kxts = [[Key(f"xt{i}_{b}") for b in range(NBLK)] for i in range(2)]
    xnT = sb("xnT", [128, 8, W], BF16)
    kxnT = Key("xnT")
    scr = {
        "stat": Rot([sb(f"stat1_{i}", [128, 8], F32) for i in range(4)]),
        "xs": Rot([sb(f"xs1_{i}", [128, D], BF16) for i in range(2)]),
        "tmp512": Rot([sb(f"tmp512a_{i}", [128, 512], F32) for i in range(2)]),
    }
    tf = Rot([sb(f"tf{i}", [128, W], F32) for i in range(9)])
    tb = Rot([sb(f"tb{i}", [128, W], BF16) for i in range(4)])
    qbuf = Rot([sb(f"qbuf{i}", [128, W + 1], F32) for i in range(2)])
    ubuf = Rot([sb(f"ubuf{i}", [128, W + 2], F32) for i in range(2)])
    qh = sb("qh", [128, 15], F32)
    kqh = [Key(f"qh{j}") for j in range(15)]
    uh = sb("uh", [128, 4, 2], F32)
    kuh = [Key(f"uh{c}") for c in range(4)]
    rkv = {n: Rot([sb(f"{n}c{i}", [128, W], F32) for i in range(2)]) for n in ("r", "k", "v")}
    wa = sb("wa", [128, W], F32); kwa = Key("wa")
    g0t = sb("g0t", [128, W], F32); kg0 = Key("g0t")
    g1t = sb("g1t", [128, W], F32); kg1 = Key("g1t")
    twad = sb("twad", [128, W], BF16); ktwad = Key("twad")
    sgd0 = sb("sgd0", [128, W], BF16); sgd1 = sb("sgd1", [128, W], BF16); ksgd = Key("sgd")
    sig = sb("sig", [128, 4, W], F32); ksig = [Key(f"sig{c}") for c in range(4)]
    a4 = sb("a4", [128, 4, W], F32); ka4 = [Key(f"a4{c}") for c in range(4)]
    bonus = sb("bonus", [128, 4, W], F32); kbonus = [Key(f"bonus{c}") for c in range(4)]
    ARbd = sb("ARbd", [128, 4, NCH, 2, 128], BF16); kAbd = [Key(f"Abd{c}") for c in range(4)]; kRbd = [Key(f"Rbd{c}") for c in range(4)]
    Bbd = sb("Bbd", [128, 4, NCH, 128], BF16); kBbd = [Key(f"Bbd{c}") for c in range(4)]
    Kbd = sb("Kbd", [128, 4, NCH, 128], BF16); kKbd = [Key(f"Kbd{c}") for c in range(4)]
    Vbd = sb("Vbd", [128, 4, NCH, 128], BF16); kVbd = [Key(f"Vbd{c}") for c in range(4)]
    PC = sb("PC", [128, 4, NCH], F32); kPC = [Key(f"PC{c}") for c in range(4)]
    M1s = [(sb(f"M1_{i}", [128, 4, 512], BF16), Key(f"M1_{i}")) for i in range(2)]
    NT0s = [(sb(f"NT0_{i}", [128, 4, 128], BF16), Key(f"NT0_{i}")) for i in range(2)]
    M2s = [(sb(f"M2_{i}", [128, 4, 320], BF16), Key(f"M2_{i}")) for i in range(2)]
    NNs = [[(sb(f"NN{i}{j}", [128, 4, 256], BF16), Key(f"NN{i}{j}")) for j in range(2)] for i in range(2)]
    Tbs = [[(sb(f"Tb{i}{j}", [128, 4, 128], BF16), Key(f"Tb{i}{j}")) for j in range(2)] for i in range(2)]
    Zb = sb("Zb", [128, 4, 64], BF16); kZb = Key("Zb")
    Ub = sb("Ub", [128, 4, 64], BF16); kUb = Key("Ub")
    S32 = sb("S32", [128, 4, 64], F32); kS32 = Key("S32")
    Sbf = sb("Sbf", [128, 4, 64], BF16); kSbf = Key("Sbf")
    gst = Rot([sb(f"gst{i}", [128, 8, 4], F32) for i in range(2)])
    ynbd = sb("ynbd", [128, 4, 128], BF16); kynbd = Key("ynbd")
    ynf = sb("ynf", [128, 4, W], F32); kynf = [Key(f"ynf{c}") for c in range(4)]
    yT = sb("yT", [128, 8, W], BF16); kyT = [Key(f"yT{c}") for c in range(8)]

    WQ = INC // 4
    stg = [(sig, ksig), (a4, ka4), (bonus, kbonus), (ynf, kynf)]
    for kc in range(8):
        for q in range(4):
            j = kc * 4 + q
            buf, kbuf = stg[j % 4]
            wt = buf[:].rearrange("p c w -> p (c w)")[:, 0:WQ]
            S.op("sp", lambda e, wt=wt, kc=kc, q=q: e.dma_start(out=wt, in_=io["w_in"][kc * 128:(kc + 1) * 128, q * WQ:(q + 1) * WQ]),
                 writes=kbuf, dma_sem=dsem[f"wst{j % 4}"])
            if j % 2 == 0:
                S.op("act", lambda e, wt=wt, kc=kc, q=q: e.activation(out=win[:, kc, q * WQ:(q + 1) * WQ], in_=wt, func=AF.Copy,
                                                                       scale=cc[:, O_PMG + kc:O_PMG + kc + 1]),
                     reads=kbuf + [kcc], writes=[kwin[kc]])
            else:
                S.op("dve", lambda e, wt=wt, kc=kc, q=q: e.tensor_scalar(out=win[:, kc, q * WQ:(q + 1) * WQ], in0=wt,
                                                                          scalar1=cc[:, O_PMG + kc:O_PMG + kc + 1], scalar2=1.0, op0=ALU.mult, op1=ALU.mult),
                     reads=kbuf + [kcc], writes=[kwin[kc]])
    S.op("dve", lambda e: e.tensor_copy(out=wg1[:, :, 0:32], in_=win[:, :, INC - 32:INC]), reads=kwin + [kwg1], writes=[kwg1])

    ptr = Rot([ps("ptr1", [128, 8, 128], BF16)], excl=True)
    pp = [ps(f"pp{i}", [128, 2, W], F32) for i in range(2)]
    kpp = [PKey(f"pp{i}") for i in range(2)]
    psm = ps("psm", [128, 2, W], F32); kpsm = PKey("psm")
    PA = ps("PA", [128, 1024], F32); kPA = PKey("PA")
    PB = ps("PB", [128, 1024], F32); kPB0 = PKey("PB0"); kPB1 = PKey("PB1")

    for t_, k_ in ((qh, kqh), (uh, kuh)):
        S.op("pool", lambda e, t_=t_: e.memset(t_[:], 0.0), writes=k_)
    S.op("pool", lambda e: e.memset(S32[:], 0.0), writes=[kS32])
    S.op("pool", lambda e: e.memset(Sbf[:], 0.0), writes=[kSbf])
    S.op("pool", lambda e: e.memset(ARbd[:], 0.0), writes=kAbd + kRbd)
    S.op("pool", lambda e: e.memset(Bbd[:], 0.0), writes=kBbd)
    S.op("pool", lambda e: e.memset(Kbd[:], 0.0), writes=kKbd)
    S.op("pool", lambda e: e.memset(Vbd[:], 0.0), writes=kVbd)
    S.op("pool", lambda e: e.memset(ynbd[:], 0.0), writes=[kynbd])
    S.op("pool", lambda e: e.memset(g1t[:], 0.0), writes=[kg1])

    slot_i = [0]

    def proj(col0, ncols, wsrc=None, kw=None):
        j = slot_i[0] % 4
        slot_i[0] += 1
        t_, half, key = pp[j // 2], j % 2, kpp[j // 2]
        for kc in range(8):
            lhsT = win[:, kc, col0:col0 + ncols] if wsrc is None else wsrc[:, kc, :]
            S.op("pe", lambda e, kc=kc, lhsT=lhsT: e.matmul(t_[0:ncols, half, :], lhsT=lhsT, rhs=xnT[:, kc, :],
                                                            start=(kc == 0), stop=(kc == 7)),
                 reads=[(kwin if kw is None else kw)[kc], kxnT], writes=[key])
        return t_[0:ncols, half, :], key

    def shift_lerp(p_ap, kp, jj, dst_ap, kdst, np_=128):
        qb, kqb = qbuf.next()
        S.op("pool", lambda e: e.tensor_copy(out=qb[0:np_, 0:1], in_=qh[0:np_, jj:jj + 1]), reads=[kqh[jj]], writes=[kqb])
        S.op("act", lambda e: e.activation(out=qb[0:np_, 1:W + 1], in_=p_ap, func=AF.Copy), reads=[kp], writes=[kqb])
        S.op("pool", lambda e: e.tensor_copy(out=qh[0:np_, jj:jj + 1], in_=qb[0:np_, W:W + 1]), reads=[kqb], writes=[kqh[jj]])
        tmp, ktmp = tf.next()
        S.op("pool", lambda e: e.tensor_scalar(out=tmp[0:np_, :], in0=qb[0:np_, 1:W + 1], scalar1=dc[0:np_, jj:jj + 1], scalar2=1.0, op0=ALU.mult, op1=ALU.mult),
             reads=[kqb, kdc], writes=[ktmp])
        S.op("dve", lambda e: e.scalar_tensor_tensor(out=dst_ap, in0=qb[0:np_, 0:W], scalar=cc[0:np_, O_MU + jj:O_MU + jj + 1], in1=tmp[0:np_, :],
                                                     op0=ALU.mult, op1=ALU.add),
             reads=[kqb, kcc, ktmp], writes=[kdst])

    def bd_view(t4, c, h):
        return t4[h * 64:(h + 1) * 64, c, :, h * 64:(h + 1) * 64]

    def v3(t2, h):
        return t2[h * 64:(h + 1) * 64, :].rearrange("p (n t) -> p n t", n=NCH)

    def load_x(t):
        b = t % 2
        S.op("sp", lambda e: e.dma_start(out=xts[b][:], in_=xin[W * t:W * (t + 1), :].rearrange("(b p) d -> p b d", p=128)),
             writes=kxts[b], dma_sem=dsem[f"x{b}"])

    load_x(0)
    if n_tiles > 1:
        load_x(1)

    def do_tile(t):
        full = t >= n_pre
        if DEBUG_STOP < 1:
            return
        b = t % 2
        xt = xts[b]
        for blk in range(NBLK):
            _rms_transpose(S, xt[:, blk, :], kxts[b][blk], xnT, kxnT, blk * 128, scr, ident, kid, ptr)

        if DEBUG_STOP < 2:
            return
        if full:
            def b1(c):
                pB, kpB = proj(c * 128, 128)
                pH, kpH = proj(1024 + c * 128, 128)
                hAs, khAs = tf.next()
                S.op("act", lambda e, hAs=hAs, pH=pH: e.activation(out=hAs[:], in_=pH, func=AF.Copy), reads=[kpH], writes=[khAs])
                ub, kub = ubuf.next()
                S.op("pool", lambda e, ub=ub, c=c: e.tensor_copy(out=ub[:, 0:2], in_=uh[:, c, :]), reads=[kuh[c]], writes=[kub])
                S.op("dve", lambda e, ub=ub, pB=pB, hAs=hAs: e.tensor_tensor(out=ub[:, 2:W + 2], in0=pB, in1=hAs[:], op=ALU.mult),
                     reads=[kpB, khAs], writes=[kub])
                S.op("pool", lambda e, ub=ub, c=c: e.tensor_copy(out=uh[:, c, :], in_=ub[:, W:W + 2]), reads=[kub], writes=[kuh[c]])
                ta, kta = tf.next()
                wof = O_CAW + c * 3
                S.op("act", lambda e, ta=ta, ub=ub, wof=wof: e.activation(out=ta[:], in_=ub[:, 2:W + 2], func=AF.Copy, scale=cc[:, wof + 2:wof + 3]),
                     reads=[kub, kcc], writes=[kta])
                S.op("dve", lambda e, ta=ta, ub=ub, wof=wof: e.scalar_tensor_tensor(out=ta[:], in0=ub[:, 1:W + 1], scalar=cc[:, wof + 1:wof + 2], in1=ta[:],
                                                                                  op0=ALU.mult, op1=ALU.add), reads=[kub, kcc, kta], writes=[kta])
                S.op("dve", lambda e, ta=ta, ub=ub, wof=wof: e.scalar_tensor_tensor(out=ta[:], in0=ub[:, 0:W], scalar=cc[:, wof:wof + 1], in1=ta[:],
                                                                                  op0=ALU.mult, op1=ALU.add), reads=[kub, kcc, kta], writes=[kta])
                pC, kpC = proj(512 + c * 128, 128)
                S.op("dve", lambda e, ta=ta, pC=pC, c=c: e.tensor_tensor(out=yT[:, c, :], in0=pC, in1=ta[:], op=ALU.mult),
                     reads=[kpC, kta], writes=[kyT[c]])
            for c_ in range(4):
                b1(c_)

        if DEBUG_STOP < 3:
            return
        QC = 1536
        p_, kp_ = proj(QC + 12 * 128, 128)
        shift_lerp(p_, kp_, 12, wa[:], kwa)
        S.op("act", lambda e: e.activation(out=twad[0:64, :], in_=wa[0:64, :], func=AF.Tanh), reads=[kwa], writes=[ktwad])
        S.op("dve", lambda e: e.tensor_copy(out=twad[64:128, :], in_=wa[64:128, :]), reads=[kwa], writes=[ktwad])
        if full:
            p_, kp_ = proj(QC + 13 * 128, 128)
            shift_lerp(p_, kp_, 13, g0t[:], kg0)
            p_, kp_ = proj(None, 128, wsrc=wg1, kw=[kwg1] * 8)
            shift_lerp(p_[0:32, :], kp_, 14, g1t[0:32, :], kg1, np_=32)
            S.op("act", lambda e: e.activation(out=sgd0[:], in_=g0t[:], func=AF.Sigmoid), reads=[kg0], writes=[ksgd])
            S.op("act", lambda e: e.activation(out=sgd1[:], in_=g1t[:], func=AF.Sigmoid), reads=[kg1], writes=[ksgd])
        for c in range(4):
            S.op("pe", lambda e, c=c: e.matmul(psm[:, 0, :], lhsT=w2b[:, c * 128:(c + 1) * 128], rhs=twad[:], start=True, stop=True),
                 reads=[kw2b, ktwad], writes=[kpsm])
            S.op("pe", lambda e, c=c: e.matmul(psm[:, 1, :], lhsT=a2b[:, c * 128:(c + 1) * 128], rhs=twad[:], start=True, stop=True),
                 reads=[ka2b, ktwad], writes=[kpsm])
            S.op("act", lambda e, c=c: e.activation(out=sig[:, c, :], in_=psm[:, 0, :], func=AF.Sigmoid, bias=cc[:, O_W0 + c:O_W0 + c + 1]),
                 reads=[kpsm, kcc], writes=[ksig[c]])
            S.op("act", lambda e, c=c: e.activation(out=a4[:, c, :], in_=psm[:, 1, :], func=AF.Sigmoid, bias=cc[:, O_A0 + c:O_A0 + c + 1]),
                 reads=[kpsm, kcc], writes=[ka4[c]])

        if DEBUG_STOP < 4:
            return
        def b4(c):
            kt_, kkt = rkv["k"].next()
            vt_, kvt = rkv["v"].next()
            p_, kp_ = proj(QC + (4 + c) * 128, 128)
            shift_lerp(p_, kp_, 4 + c, kt_[:], kkt)
            p_, kp_ = proj(QC + (8 + c) * 128, 128)
            shift_lerp(p_, kp_, 8 + c, vt_[:], kvt)
            if full:
                rt_, krt = rkv["r"].next()
                p_, kp_ = proj(QC + c * 128, 128)
                shift_lerp(p_, kp_, c, rt_[:], krt)
            kkr, kkkr = tf.next()
            S.op("pool", lambda e, kkr=kkr, kt_=kt_, c=c: e.tensor_scalar(out=kkr[:], in0=kt_[:], scalar1=cc[:, O_KK + c:O_KK + c + 1], scalar2=1.0, op0=ALU.mult, op1=ALU.mult),
                 reads=[kkt, kcc], writes=[kkkr])
            sq, ksq = tb.next()
            S.op("act", lambda e, sq=sq, kkr=kkr: e.activation(out=sq[:], in_=kkr[:], func=AF.Square), reads=[kkkr], writes=[ksq])
            S.op("pe", lambda e, sq=sq: e.matmul(psm[:, 0, :], lhsT=bones[:], rhs=sq[:], start=True, stop=True), reads=[kbones, ksq], writes=[kpsm])
            nrm, knrm = tf.next()
            S.op("act", lambda e, nrm=nrm: e.activation(out=nrm[:], in_=psm[:, 0, :], func=AF.Sqrt, bias=1e-24), reads=[kpsm], writes=[knrm])
            S.op("dve", lambda e, nrm=nrm: e.reciprocal(out=nrm[:], in_=nrm[:]), reads=[knrm], writes=[knrm])
            kk, kkk = tf.next()
            S.op("dve", lambda e, kk=kk, kkr=kkr, nrm=nrm: e.tensor_tensor(out=kk[:], in0=kkr[:], in1=nrm[:], op=ALU.mult), reads=[kkkr, knrm], writes=[kkk])
            cs, kcs = tf.next()
            S.op("dve", lambda e, cs=cs, c=c: e.tensor_tensor_scan(out=cs[:], data0=rst[:], data1=sig[:, c, :], initial=0.0, op0=ALU.mult, op1=ALU.add),
                 reads=[kconst, ksig[c]], writes=[kcs])
            E1, kE1 = tf.next()
            E2, kE2 = tf.next()
            E3, kE3 = tf.next()
            dd, kdd = tf.next()
            S.op("act", lambda e, E1=E1, cs=cs: e.activation(out=E1[:], in_=cs[:], func=AF.Exp, scale=-C05), reads=[kcs], writes=[kE1])
            S.op("act", lambda e, E2=E2, cs=cs: e.activation(out=E2[:], in_=cs[:], func=AF.Exp, scale=C05), reads=[kcs], writes=[kE2])
            S.op("pool", lambda e, dd=dd, cs=cs, c=c: e.tensor_tensor(out=dd[:], in0=cs[:], in1=sig[:, c, :], op=ALU.subtract), reads=[kcs, ksig[c]], writes=[kdd])
            S.op("act", lambda e, E3=E3, dd=dd: e.activation(out=E3[:], in_=dd[:], func=AF.Exp, scale=-C05), reads=[kdd], writes=[kE3])
            S.op("pool", lambda e, E1=E1, c=c: e.tensor_copy(out=PC[:, c, :], in_=E1[:].rearrange("p (n t) -> p n t", n=NCH)[:, :, CH - 1]),
                 reads=[kE1], writes=[kPC[c]])
            mm, kmm = tf.next()
            S.op("pool", lambda e, mm=mm, c=c: e.tensor_scalar(out=mm[:], in0=a4[:, c, :], scalar1=cc[:, O_KA + c:O_KA + c + 1], scalar2=dc[:, 15 + c:16 + c],
                                                               op0=ALU.mult, op1=ALU.add), reads=[ka4[c], kcc, kdc], writes=[kmm])
            kp, kkp = tf.next()
            S.op("dve", lambda e, kp=kp, kt_=kt_, mm=mm: e.tensor_tensor(out=kp[:], in0=kt_[:], in1=mm[:], op=ALU.mult), reads=[kkt, kmm], writes=[kkp])
            akk, kakk = tf.next()
            S.op("pool", lambda e, akk=akk, kk=kk, c=c: e.tensor_tensor(out=akk[:], in0=a4[:, c, :], in1=kk[:], op=ALU.mult), reads=[ka4[c], kkk], writes=[kakk])
            for h in range(2):
                S.op("dve", lambda e, h=h, kk=kk, E3=E3, c=c: e.scalar_tensor_tensor(out=ARbd[h * 64:(h + 1) * 64, c, :, 0, h * 64:(h + 1) * 64], in0=v3(kk, h), scalar=-1.0,
                                                                                   in1=v3(E3, h), op0=ALU.mult, op1=ALU.mult),
                     reads=[kkk, kE3], writes=[kAbd[c]])
                S.op("dve", lambda e, h=h, akk=akk, E2=E2, c=c: e.tensor_tensor(out=bd_view(Bbd, c, h), in0=v3(akk, h), in1=v3(E2, h), op=ALU.mult),
                     reads=[kakk, kE2], writes=[kBbd[c]])
                S.op("pool", lambda e, h=h, kp=kp, E2=E2, c=c: e.tensor_tensor(out=bd_view(Kbd, c, h), in0=v3(kp, h), in1=v3(E2, h), op=ALU.mult),
                     reads=[kkp, kE2], writes=[kKbd[c]])
                S.op("act", lambda e, h=h, vt_=vt_, c=c: e.activation(out=bd_view(Vbd, c, h), in_=v3(vt_, h), func=AF.Copy), reads=[kvt], writes=[kVbd[c]])
                if full:
                    S.op("pool", lambda e, h=h, rt_=rt_, E1=E1, c=c: e.tensor_tensor(out=ARbd[h * 64:(h + 1) * 64, c, :, 1, h * 64:(h + 1) * 64], in0=v3(rt_, h), in1=v3(E1, h),
                                                                                   op=ALU.mult), reads=[krt, kE1], writes=[kRbd[c]])
            if full:
                rk, krk = tf.next()
                S.op("pool", lambda e, rk=rk, rt_=rt_, kp=kp: e.tensor_tensor(out=rk[:], in0=rt_[:], in1=kp[:], op=ALU.mult), reads=[krt, kkp], writes=[krk])
                rkb, krkb = tb.next()
                S.op("dve", lambda e, rkb=rkb, rk=rk, c=c: e.tensor_scalar(out=rkb[:], in0=rk[:], scalar1=cc[:, O_RK + c:O_RK + c + 1], scalar2=1.0, op0=ALU.mult, op1=ALU.mult),
                     reads=[krk, kcc], writes=[krkb])
                S.op("pe", lambda e, rkb=rkb: e.matmul(psm[:, 1, :], lhsT=bones[:], rhs=rkb[:], start=True, stop=True), reads=[kbones, krkb], writes=[kpsm])
                S.op("dve", lambda e, vt_=vt_, c=c: e.tensor_tensor(out=bonus[:, c, :], in0=psm[:, 1, :], in1=vt_[:], op=ALU.mult), reads=[kpsm, kvt], writes=[kbonus[c]])

        for c_ in range(4):
            b4(c_)
        if DEBUG_STOP < 5:
            return

        PA3 = PA[:].rearrange("p (a w) -> p a w", a=2)
        PAd = PA[:].rearrange("p (a w) -> p a w", a=4)
        PBt = PB[:, 0:512].rearrange("p (a w) -> p a w", a=4)
        PQ0 = PB[:, 512:768].rearrange("p (a w) -> p a w", a=4)
        PQ1 = PB[:, 768:1024].rearrange("p (a w) -> p a w", a=4)
        Tfinal = {}

        def gen_AD(n, s_):
            M1, kM1 = M1s[s_]
            NT0, kNT0 = NT0s[s_]
            M2, kM2 = M2s[s_]
            for pi in range(2):
                for i in range(2):
                    c = pi * 2 + i
                    rhsAR = ARbd[:, c, n, :, :].rearrange("p a w -> p (a w)")
                    S.op("pe", lambda e, i=i, c=c, rhsAR=rhsAR: e.matmul(PA3[:, i, 0:256], lhsT=Bbd[:, c, n, :], rhs=rhsAR, start=True, stop=True),
                         reads=[kBbd[c], kAbd[c], kRbd[c]], writes=[kPA])
                    S.op("pe", lambda e, i=i, c=c, rhsAR=rhsAR: e.matmul(PA3[:, i, 256:512], lhsT=Kbd[:, c, n, :], rhs=rhsAR, start=True, stop=True),
                         reads=[kKbd[c], kAbd[c], kRbd[c]], writes=[kPA])
                S.op("dve", lambda e, pi=pi: e.tensor_tensor(out=M1[:, 2 * pi:2 * pi + 2, :], in0=PA3, in1=msu4[:].unsqueeze(1).to_broadcast([128, 2, 512]), op=ALU.mult),
                     reads=[kPA] + kmsu4, writes=[kM1])
                yield
                for i in range(2):
                    c = pi * 2 + i
                    S.op("pe", lambda e, i=i, c=c: e.matmul(PA3[:, i, 0:128], lhsT=ARbd[:, c, n, 0, :], rhs=Bbd[:, c, n, :], start=True, stop=True),
                         reads=[kAbd[c], kBbd[c]], writes=[kPA])
                    S.op("pe", lambda e, i=i, c=c: e.matmul(PA3[:, i, 128:256], lhsT=Bbd[:, c, n, :], rhs=ident[:], start=True, stop=True),
                         reads=[kBbd[c], kid], writes=[kPA])
                    S.op("pe", lambda e, i=i, c=c: e.matmul(PA3[:, i, 256:384], lhsT=Kbd[:, c, n, :], rhs=ident[:], start=True, stop=True),
                         reads=[kKbd[c], kid], writes=[kPA])
                    S.op("pe", lambda e, i=i, c=c: e.matmul(PA3[:, i, 384:448], lhsT=Vbd[:, c, n, :], rhs=fst[:], start=True, stop=True),
                         reads=[kVbd[c], kfst], writes=[kPA])
                S.op("dve", lambda e, pi=pi: e.tensor_tensor(out=NT0[:, 2 * pi:2 * pi + 2, :], in0=PA3[:, :, 0:128], in1=msl[:].unsqueeze(1).to_broadcast([128, 2, 128]), op=ALU.mult),
                     reads=[kPA, kmsl], writes=[kNT0])
                S.op("act", lambda e, pi=pi: e.activation(out=M2[:, 2 * pi:2 * pi + 2, :], in_=PA3[:, :, 128:448], func=AF.Copy), reads=[kPA], writes=[kM2])
                yield
            Tcur, kTcur = Tbs[s_][0]
            S.op("pool", lambda e, Tcur=Tcur: e.tensor_tensor(out=Tcur[:], in0=M1[:, :, 0:128], in1=ident[:].unsqueeze(1).to_broadcast([128, 4, 128]), op=ALU.add),
                 reads=[kM1, kid], writes=[kTcur])
            Nprev = lambda c: M1[:, c, 0:128]
            NTprev = lambda c: NT0[:, c, :]
            kprev = [kM1, kNT0]
            for j in range(1, 6):
                NNj, kNNj = NNs[s_][j % 2]
                for c in range(4):
                    if j < 5:
                        S.op("pe", lambda e, c=c, Nprev=Nprev, NTprev=NTprev: e.matmul(PAd[:, c, 0:128], lhsT=NTprev(c), rhs=Nprev(c), start=True, stop=True),
                             reads=kprev, writes=[kPA])
                    S.op("pe", lambda e, c=c, Nprev=Nprev, NTprev=NTprev: e.matmul(PAd[:, c, 128:256], lhsT=Nprev(c), rhs=NTprev(c), start=True, stop=True),
                         reads=kprev, writes=[kPA])
                if j < 5:
                    S.op("act", lambda e, NNj=NNj: e.activation(out=NNj[:], in_=PAd, func=AF.Copy), reads=[kPA], writes=[kNNj])
                else:
                    S.op("act", lambda e, NNj=NNj: e.activation(out=NNj[:, :, 128:256], in_=PAd[:, :, 128:256], func=AF.Copy), reads=[kPA], writes=[kNNj])
                yield
                for c in range(4):
                    S.op("pe", lambda e, c=c, NNj=NNj, Tcur=Tcur: e.matmul(PBt[:, c, :], lhsT=NNj[:, c, 128:256], rhs=Tcur[:, c, :], start=True, stop=True),
                         reads=[kNNj, kTcur], writes=[kPB0])
                Tnew, kTnew = Tbs[s_][j % 2]
                S.op("dve", lambda e, Tnew=Tnew, Tcur=Tcur: e.tensor_tensor(out=Tnew[:], in0=PBt, in1=Tcur[:], op=ALU.add), reads=[kPB0, kTcur], writes=[kTnew])
                Tcur, kTcur = Tnew, kTnew
                Nprev = (lambda NNj: (lambda c: NNj[:, c, 0:128]))(NNj)
                NTprev = (lambda NNj: (lambda c: NNj[:, c, 128:256]))(NNj)
                kprev = [kNNj]
                yield
            Tfinal[n] = (Tcur, kTcur)

        def gen_SQ(n, s_):
            M1, kM1 = M1s[s_]
            M2, kM2 = M2s[s_]
            Tcur, kTcur = Tfinal[n]
            for c in range(4):
                S.op("pe", lambda e, c=c: e.matmul(PQ0[:, c, :], lhsT=ARbd[:, c, n, 0, :], rhs=Sbf[:, c, :], start=True, stop=False),
                     reads=[kAbd[c], kSbf], writes=[kPB1])
                S.op("pe", lambda e, c=c: e.matmul(PQ0[:, c, :], lhsT=M1[:, c, 256:384], rhs=M2[:, c, 256:320], start=False, stop=True),
                     reads=[kM1, kM2], writes=[kPB1])
            S.op("act", lambda e: e.activation(out=Zb[:], in_=PQ0, func=AF.Copy), reads=[kPB1], writes=[kZb])
            yield
            for c in range(4):
                S.op("pe", lambda e, c=c: e.matmul(PQ1[:, c, :], lhsT=Tcur[:, c, :], rhs=Zb[:, c, :], start=True, stop=True),
                     reads=[kTcur, kZb], writes=[kPB1])
            S.op("act", lambda e: e.activation(out=Ub[:], in_=PQ1, func=AF.Copy), reads=[kPB1], writes=[kUb])
            yield
            for c in range(4):
                S.op("pe", lambda e, c=c: e.matmul(PQ0[:, c, :], lhsT=M2[:, c, 0:128], rhs=Ub[:, c, :], start=True, stop=False),
                     reads=[kM2, kUb], writes=[kPB1])
                S.op("pe", lambda e, c=c: e.matmul(PQ0[:, c, :], lhsT=M2[:, c, 128:256], rhs=M2[:, c, 256:320], start=False, stop=True),
                     reads=[kM2], writes=[kPB1])
            if full:
                for c in range(4):
                    S.op("pe", lambda e, c=c: e.matmul(PQ1[:, c, :], lhsT=ARbd[:, c, n, 1, :], rhs=Sbf[:, c, :], start=True, stop=False),
                         reads=[kRbd[c], kSbf], writes=[kPB1])
                    S.op("pe", lambda e, c=c: e.matmul(PQ1[:, c, :], lhsT=M1[:, c, 128:256], rhs=Ub[:, c, :], start=False, stop=False),
                         reads=[kM1, kUb], writes=[kPB1])
                    S.op("pe", lambda e, c=c: e.matmul(PQ1[:, c, :], lhsT=M1[:, c, 384:512], rhs=M2[:, c, 256:320], start=False, stop=True),
                         reads=[kM1, kM2], writes=[kPB1])
            pcb = PC[:, :, n:n + 1].to_broadcast([128, 4, 64])
            tS_, ktS = tf.next()
            tmpS = tS_[:].rearrange("p (c v) -> p c v", c=4)
            S.op("dve", lambda e: e.tensor_tensor(out=tmpS, in0=PQ0, in1=S32[:], op=ALU.add), reads=[kPB1, kS32], writes=[ktS])
            S.op("dve", lambda e: e.tensor_tensor(out=Sbf[:], in0=tmpS, in1=pcb, op=ALU.mult), reads=[ktS] + kPC, writes=[kSbf])
            S.op("pool", lambda e: e.tensor_tensor(out=S32[:], in0=tmpS, in1=pcb, op=ALU.mult), reads=[ktS] + kPC, writes=[kS32])
            yield
            if full:
                g_, kg_ = gst.next()
                ys_, kysq = tf.next()
                ysq = ys_[:].rearrange("p (c v) -> p c v", c=4)
                yc_, kycen = tf.next()
                ycen = yc_[:].rearrange("p (c v) -> p c v", c=4)
                S.op("dve", lambda e: e.tensor_reduce(out=g_[:, 0, :], in_=PQ1, axis=AX.X, op=ALU.add), reads=[kPB1], writes=[kg_])
                S.op("act", lambda e: e.activation(out=ysq, in_=PQ1, func=AF.Square), reads=[kPB1], writes=[kysq])
                S.op("dve", lambda e: e.tensor_reduce(out=g_[:, 1, :], in_=ysq, axis=AX.X, op=ALU.add), reads=[kysq], writes=[kg_])
                S.op("dve", lambda e: e.tensor_scalar(out=g_[:, 2, :], in0=g_[:, 0, :], scalar1=1.0 / 64, scalar2=1.0, op0=ALU.mult, op1=ALU.mult), reads=[kg_], writes=[kg_])
                S.op("dve", lambda e: e.tensor_tensor(out=g_[:, 3, :], in0=g_[:, 2, :], in1=g_[:, 2, :], op=ALU.mult), reads=[kg_], writes=[kg_])
                S.op("dve", lambda e: e.scalar_tensor_tensor(out=g_[:, 4, :], in0=g_[:, 1, :], scalar=1.0 / 64, in1=g_[:, 3, :], op0=ALU.mult, op1=ALU.subtract),
                     reads=[kg_], writes=[kg_])
                S.op("act", lambda e: e.activation(out=g_[:, 5, :], in_=g_[:, 4, :], func=AF.Sqrt, bias=GN_EPS), reads=[kg_], writes=[kg_])
                S.op("dve", lambda e: e.reciprocal(out=g_[:, 6, :], in_=g_[:, 5, :]), reads=[kg_], writes=[kg_])
                S.op("dve", lambda e: e.tensor_tensor(out=ycen, in0=PQ1, in1=g_[:, 2, :].unsqueeze(2).to_broadcast([128, 4, 64]), op=ALU.subtract),
                     reads=[kPB1, kg_], writes=[kycen])
                yield
                for h in range(2):
                    hs = slice(h * 64, (h + 1) * 64)
                    S.op("dve", lambda e, hs=hs: e.tensor_tensor(out=ynbd[hs, :, hs], in0=ycen[hs, :, :], in1=g_[hs, 6, :].unsqueeze(2).to_broadcast([64, 4, 64]),
                                                                  op=ALU.mult), reads=[kycen, kg_], writes=[kynbd])
                for c in range(4):
                    S.op("pe", lambda e, c=c: e.matmul(PQ0[:, c, :], lhsT=ynbd[:, c, :], rhs=fst[:], start=True, stop=True), reads=[kynbd, kfst], writes=[kPB1])
                S.op("act", lambda e: e.activation(out=ynf[:, :, n * CH:(n + 1) * CH], in_=PQ0, func=AF.Copy), reads=[kPB1], writes=kynf)
                yield

        def drain(g):
            for _ in g:
                pass

        def interleave(ga, gb, ra=2):
            a_live, b_live = ga is not None, gb is not None
            while a_live or b_live:
                for _ in range(ra):
                    if a_live:
                        try:
                            next(ga)
                        except StopIteration:
                            a_live = False
                if b_live:
                    try:
                        next(gb)
                    except StopIteration:
                        b_live = False

        drain(gen_AD(0, 0))
        for n_ in range(NCH):
            gd = gen_AD(n_ + 1, (n_ + 1) % 2) if n_ + 1 < NCH else None
            interleave(gd, gen_SQ(n_, n_ % 2))
        if DEBUG_STOP < 8:
            return

        if not full:
            if t + 2 < n_tiles:
                load_x(t + 2)
            return
        for c in range(4):
            S.op("pe", lambda e, c=c: e.matmul(psm[:, 0, :], lhsT=g2b0[:, c * 128:(c + 1) * 128], rhs=sgd0[:], start=True, stop=False), reads=[kg2b0, ksgd], writes=[kpsm])
            S.op("pe", lambda e, c=c: e.matmul(psm[:, 0, :], lhsT=g2b1[:, c * 128:(c + 1) * 128], rhs=sgd1[:], start=False, stop=True), reads=[kg2b1, ksgd], writes=[kpsm])
            y1, ky1 = tf.next()
            S.op("dve", lambda e, c=c, y1=y1: e.scalar_tensor_tensor(out=y1[:], in0=ynf[:, c, :], scalar=cc[:, O_LW + c:O_LW + c + 1], in1=bonus[:, c, :], op0=ALU.mult, op1=ALU.add),
                 reads=[kynf[c], kcc, kbonus[c]], writes=[ky1])
            S.op("dve", lambda e, c=c, y1=y1: e.scalar_tensor_tensor(out=yT[:, 4 + c, :], in0=y1[:], scalar=cc[:, O_LB + c:O_LB + c + 1], in1=psm[:, 0, :], op0=ALU.add, op1=ALU.mult),
                 reads=[ky1, kcc, kpsm], writes=[kyT[4 + c]])
        if DEBUG_STOP < 9:
            return
        for blk in range(NBLK):
            for hf in range(2):
                pflat = pp[hf][:].rearrange("p a w -> p (a w)")
                for e_ in range(8):
                    S.op("pe", lambda e, e_=e_, hf=hf, pflat=pflat, blk=blk: e.matmul(pflat, lhsT=yT[:, e_, blk * 128:(blk + 1) * 128], rhs=wout[:, e_, hf * 512:(hf + 1) * 512],
                                                                                      start=(e_ == 0), stop=(e_ == 7)), reads=[kyT[e_], kwout], writes=[kpp[hf]])
            class _V:
                def __init__(self, ap): self.ap = ap
                def __getitem__(self, k): return self.ap
            _post_norm_residual(S, [(_V(pp[0][:].rearrange("p a w -> p (a w)")), kpp[0]), (_V(pp[1][:].rearrange("p a w -> p (a w)")), kpp[1])],
                                xt[:, blk, :], kxts[b][blk], grow, kgrow, scr)
        ht_i = t - n_pre
        S.op("sp", lambda e, xt=xt, ht_i=ht_i: e.dma_start(out=hscr[W * ht_i:W * (ht_i + 1), :].rearrange("(b p) d -> p b d", p=128), in_=xt[:]),
             reads=kxts[b], writes=[khscr[ht_i]], dma_sem=dsem[f"h{b}"])
        if t + 2 < n_tiles:
            load_x(t + 2)

    for t_i in range(n_tiles):
        do_tile(t_i)


def _host_consts():
    km = np.zeros((128, NKM), np.float32)
    km[:, M_ID:M_ID + 128] = np.eye(128, dtype=np.float32)
    idx = np.arange(128)
    same = (idx[:, None] // 64) == (idx[None, :] // 64)
    s, t = idx[:, None] % 64, idx[None, :] % 64
    km[:, M_SU:M_SU + 128] = (same & (s < t)).astype(np.float32)
    km[:, M_IU:M_IU + 128] = (same & (s <= t)).astype(np.float32)
    km[:, M_SL:M_SL + 128] = (same & (s > t)).astype(np.float32)
    km[:, M_BO:M_BO + 128] = same.astype(np.float32)
    km[:, M_F:M_F + 64] = (idx[:, None] % 64 == np.arange(64)[None, :]).astype(np.float32)
    rst = np.ones((128, 256), np.float32)
    rst[:, ::64] = 0.0
    km[:, M_RST:M_RST + 256] = rst
    return km


def _pack_cc(inp, hmask):
    cc = np.zeros((128, NCC), np.float32)
    col = lambda v, n: np.ascontiguousarray(np.asarray(v, np.float32).reshape(n, 128).T)
    cc[:, O_PMG:O_PMG + 8] = col(inp["pre_mix_g"][0], 8)
    cc[:, O_PFG:O_PFG + 8] = col(inp["pre_ffn_g"][0], 8)
    caw = np.asarray(inp["conv_a_w"][0], np.float32)
    cc[:, O_CAW:O_CAW + 12] = caw.T.reshape(4, 128, 3).transpose(1, 0, 2).reshape(128, 12)
    mu = np.zeros(1920, np.float32)
    mu[:1824] = np.asarray(inp["shift_mu"][0], np.float32)
    cc[:, O_MU:O_MU + 15] = col(mu, 15)
    for off, name in ((O_W0, "w0"), (O_A0, "a0"), (O_KK, "k_k"), (O_KA, "k_a"), (O_LW, "lnx_w"), (O_LB, "lnx_b")):
        cc[:, off:off + 4] = col(inp[name][0], 4)
    cc[:, O_RK:O_RK + 4] = col(np.asarray(inp["r_k"][0], np.float32).reshape(512), 4)
    fcw = np.asarray(inp["ffn_conv_w"][0], np.float32)
    cc[:, O_FCW:O_FCW + 132] = fcw.T.reshape(44, 128, 3).transpose(1, 0, 2).reshape(128, 132)
    cc[:, O_FCB:O_FCB + 44] = col(inp["ffn_conv_b"][0], 44)
    cc[:, O_HM] = hmask
    return cc


_NC_CACHE = {}


def kernel(**inputs):
    n_pre, n_main = 15, 16
    x = np.asarray(inputs["x"], np.float32)
    B, T, _ = x.shape
    half = T // 2
    if "full" not in _NC_CACHE:
        _NC_CACHE["full"] = build(n_pre, n_main, "full")
    nc = _NC_CACHE["full"]
    km = _host_consts()
    f = lambda n: np.ascontiguousarray(np.asarray(inputs[n], np.float32)[0])
    in_maps = []
    for c in range(8):
        b, h = c // 2, c % 2
        xin = np.zeros((T, D), np.float32)
        if h == 0:
            xin[half:] = x[b, :half]
        else:
            xin[:] = x[b]
        in_maps.append({
            "xin": xin, "cc": _pack_cc(inputs, float(h)), "km": km,
            "post_mix_g": f("post_mix_g"), "post_ffn_g": f("post_ffn_g"),
            "w_in": f("w_in"), "w_out": f("w_out"), "w_up": f("w_up"), "w_down": f("w_down"),
            "w2": f("w2"), "a2": f("a2"), "g2": f("g2"),
        })
    res = run_bass_kernel_spmd(nc, in_maps, core_ids=list(range(8)))
    out = np.zeros((B, T, D), np.float32)
    for c in range(8):
        b, h = c // 2, c % 2
        out[b, h * half:(h + 1) * half] = res.results[c]["out"]
    return out
```

```python
import contextlib
import numpy as np
import concourse.bass as bass
import concourse.mybir as mybir
from concourse.bass_utils import run_bass_kernel_spmd

F32 = mybir.dt.float32
BF16 = mybir.dt.bfloat16
AF = mybir.ActivationFunctionType
ALU = mybir.AluOpType
AX = mybir.AxisListType

D = 1024
W = 256
NBLK = 2
CH = 64
NCH = W // CH
INC = 3360
DFF = 2816
NPAIR = 22
QC = 1536
RMS_EPS = 1e-6
GN_EPS = 64 * 1e-5
EPOCH = 12000
DEBUG_SUB = 99
DEBUG_STOP = 99

O_PMG, O_PFG, O_CAW, O_MU, O_W0, O_A0, O_KK, O_KA, O_RK, O_LW, O_LB, O_FCW, O_FCB, O_HM = (
    0, 8, 16, 28, 43, 47, 51, 55, 59, 63, 67, 71, 203, 247)
NCC = 248
M_ID, M_SU, M_IU, M_SL, M_BO, M_F, M_RST = 0, 128, 256, 384, 512, 640, 704
NKM = 704 + 256


class Key:
    __slots__ = ("name", "writer", "readers", "excl")

    def __init__(self, name, excl=False):
        self.name = name
        self.writer = None
        self.readers = []
        self.excl = excl


def PKey(name):
    return Key(name, excl=True)


class Sched:
    ENGS = ("pe", "act", "dve", "pool", "sp")

    def __init__(self, nc, sem_stack, prefix):
        self.nc = nc
        self.sem_stack = sem_stack
        self.prefix = prefix
        self.ops = {e: [] for e in self.ENGS}
        self.count = {e: 0 for e in self.ENGS}
        self.sems = {}
        self.waited = {e: {} for e in self.ENGS}
        self.dma_counts = {}
        self.last_tok = {e: None for e in self.ENGS}

    def _eng_sem(self, eng, idx):
        sid = f"{self.prefix}s_{eng}_{idx // EPOCH}"
        self.sems.setdefault(sid, None)
        return sid, (idx % EPOCH) + 1

    def new_dma_sem(self, name):
        sid = f"{self.prefix}d_{name}"
        assert sid not in self.sems, sid
        self.sems[sid] = None
        self.dma_counts[sid] = 0
        return sid

    def _need_waits(self, eng, tokens):
        w = self.waited[eng]
        best = {}
        for t in tokens:
            if t is None:
                continue
            sid, val, _ = t
            if w.get(sid, 0) >= val:
                continue
            if best.get(sid, 0) < val:
                best[sid] = val
        for sid, val in best.items():
            w[sid] = val
        return list(best.items())

    def op(self, eng, fn, reads=(), writes=(), dma_sem=None):
        toks = []
        raw = set()
        for k in reads:
            toks.append(k.writer)
            if k.writer is not None:
                raw.add(k.writer)
            if k.excl:
                toks.extend(r for r in k.readers if r[2] != eng)
        for k in writes:
            toks.append(k.writer)
            toks.extend(k.readers)
        if eng == "pe":
            toks = [t for t in toks if t is not None and t[2] != "pe"]
        waits = self._need_waits(eng, toks)
        if dma_sem is None:
            idx = self.count[eng]
            self.count[eng] += 1
            sid, val = self._eng_sem(eng, idx)
            tok = (sid, val, eng)
            inc = (sid, 1)
            self.last_tok[eng] = tok
        else:
            self.dma_counts[dma_sem] += 16
            tok = (dma_sem, self.dma_counts[dma_sem], "dma")
            inc = (dma_sem, 16)
        self.ops[eng].append((fn, waits, inc))
        for k in reads:
            k.readers.append(tok)
        for k in writes:
            k.writer = tok
            k.readers = []
        return tok

    def barrier(self, extra_keys=()):
        toks = [t for t in self.last_tok.values() if t is not None]
        for k in extra_keys:
            toks.append(k.writer)
            toks.extend(k.readers)
        for eng in self.ENGS:
            waits = self._need_waits(eng, [t for t in toks if t is not None and t[2] != eng])
            if waits:
                self.ops[eng].append((None, waits, None))

    def final_wait(self, eng, keys):
        toks = []
        for k in keys:
            toks.append(k.writer)
            toks.extend(k.readers)
        waits = self._need_waits(eng, toks)
        self.ops[eng].append((None, waits, None))

    def emit(self):
        nc = self.nc
        with contextlib.ExitStack() as st:
            handles = {sid: self.sem_stack.enter_context(nc.semaphore(sid)) for sid in self.sems}
            block = st.enter_context(nc.Block())

            def run(engobj, lst):
                for fn, waits, inc in lst:
                    for sid, val in waits:
                        engobj.wait_ge(handles[sid], val)
                    if fn is not None:
                        fn(engobj).then_inc(handles[inc[0]], inc[1])

            @block.tensor
            def _(e):
                run(e, self.ops["pe"])

            @block.scalar
            def _(e):
                run(e, self.ops["act"])

            @block.vector
            def _(e):
                run(e, self.ops["dve"])

            @block.gpsimd
            def _(e):
                run(e, self.ops["pool"])

            @block.sync
            def _(e):
                run(e, self.ops["sp"])


class View:
    def __init__(self, ap):
        self.ap = ap

    def __getitem__(self, k):
        return self.ap


class View3:
    def __init__(self, x):
        self.x = x

    def __getitem__(self, k):
        return self.x[k[0], k[1], 128:256]


class Rot:
    def __init__(self, tiles, excl=False, keys=None):
        self.tiles = tiles
        self.keys = keys if keys is not None else [Key(f"rot{i}", excl) for i in range(len(tiles))]
        self.i = 0

    def next(self):
        j = self.i % len(self.tiles)
        self.i += 1
        return self.tiles[j], self.keys[j]


def _rms_transpose(S, src, ksrc, dstT, kdst, tcol, scr, ident, kid, ptr, extra_scale=None, kextra=None):
    st, kst = scr["stat"].next()
    xs, kxs = scr["xs"].next()
    pt, kpt = ptr.next()
    S.op("act", lambda e: e.activation(out=xs[:], in_=src, func=AF.Square, accum_out=st[:, 0:1]),
         reads=[ksrc], writes=[kxs, kst])
    S.op("act", lambda e: e.activation(out=st[:, 1:2], in_=st[:, 0:1], func=AF.Sqrt, scale=1.0 / D, bias=RMS_EPS),
         reads=[kst], writes=[kst])
    S.op("dve", lambda e: e.reciprocal(out=st[:, 2:3], in_=st[:, 1:2]), reads=[kst], writes=[kst])
    rs = st[:, 2:3]
    if extra_scale is not None:
        S.op("dve", lambda e: e.tensor_tensor(out=st[:, 3:4], in0=st[:, 2:3], in1=extra_scale, op=ALU.mult),
             reads=[kst, kextra], writes=[kst])
        rs = st[:, 3:4]
    S.op("pool", lambda e: e.tensor_scalar(out=xs[:], in0=src, scalar1=rs, scalar2=1.0, op0=ALU.mult, op1=ALU.mult),
         reads=[ksrc, kst], writes=[kxs])
    for kc in range(8):
        S.op("pe", lambda e, kc=kc: e.transpose(out=pt[:, kc, :], in_=xs[:, kc * 128:(kc + 1) * 128], identity=ident[:]),
             reads=[kxs, kid], writes=[kpt])
    S.op("act", lambda e: e.activation(out=dstT[:, :, tcol:tcol + 128], in_=pt[:], func=AF.Copy),
         reads=[kpt], writes=[kdst])


def _post_norm_residual(S, pd_pairs, res, kres, grow, kgrow, scr):
    st, kst = scr["stat"].next()
    tmps = [scr["tmp512"].next() for _ in range(2)]
    for hf, (pd, kpd) in enumerate(pd_pairs):
        junk, kjunk = tmps[hf]
        S.op("act", lambda e, pd=pd, hf=hf, junk=junk: e.activation(out=junk[:], in_=pd[:], func=AF.Square,
                                                                  accum_out=st[:, hf:hf + 1]),
             reads=[kpd], writes=[kjunk, kst])
    S.op("dve", lambda e: e.tensor_tensor(out=st[:, 2:3], in0=st[:, 0:1], in1=st[:, 1:2], op=ALU.add), reads=[kst], writes=[kst])
    S.op("act", lambda e: e.activation(out=st[:, 3:4], in_=st[:, 2:3], func=AF.Sqrt, scale=1.0 / D, bias=RMS_EPS),
         reads=[kst], writes=[kst])
    S.op("dve", lambda e: e.reciprocal(out=st[:, 4:5], in_=st[:, 3:4]), reads=[kst], writes=[kst])
    for hf, (pd, kpd) in enumerate(pd_pairs):
        tmp, ktmp = tmps[hf]
        S.op("dve", lambda e, pd=pd, hf=hf, tmp=tmp: e.scalar_tensor_tensor(
            out=tmp[:], in0=pd[:], scalar=st[:, 4:5], in1=grow[:, hf * 512:(hf + 1) * 512], op0=ALU.mult, op1=ALU.mult),
            reads=[kpd, kst, kgrow], writes=[ktmp])
        S.op("pool", lambda e, hf=hf, tmp=tmp: e.tensor_tensor(out=res[:, hf * 512:(hf + 1) * 512], in0=res[:, hf * 512:(hf + 1) * 512],
                                                               in1=tmp[:], op=ALU.add),
             reads=[ktmp, kres], writes=[kres])


def phase2_ffn(nc, S, st, io, n_main, shared):
    sb = lambda n, s, d: st.enter_context(nc.sbuf_tensor(n, s, d))
    ps = lambda n, s, d: st.enter_context(nc.psum_tensor(n, s, d))
    cc, kcc = shared["cc"], shared["kcc"]
    ident, kid = shared["ident"], shared["kid"]
    hscr, khscr = io["hscr"], io["khscr"]
    out = io["out"]

    wup = sb("wup", [128, 8, DFF * 2], BF16)
    wdn = sb("wdn", [128, NPAIR, D], BF16)
    kwup = [Key(f"wup{k}") for k in range(8)]
    kwdn = Key("wdn")
    grow = sb("grow2", [128, D], F32)
    kgrow = Key("grow2")
    fh = sb("fh", [128, NPAIR, 2, 2], F32)
    kfh = [Key(f"fh{i}") for i in range(NPAIR)]
    hts = [sb(f"ht{i}", [128, NBLK, D], F32) for i in range(2)]
    khts = [[Key(f"ht{i}_{b}") for b in range(NBLK)] for i in range(2)]
    hnTs = [sb(f"hnT{i}", [128, 8, W], BF16) for i in range(2)]
    khnT = [Key(f"hnT{i}") for i in range(2)]
    act = sb("actb", [128, NPAIR, W], BF16)
    kact = [Key(f"act{i}") for i in range(NPAIR)]
    scr = {
        "stat": Rot([sb(f"stat{i}", [128, 8], F32) for i in range(4)]),
        "xs": Rot([sb(f"xs{i}", [128, D], BF16) for i in range(2)]),
        "tmp512": Rot([sb(f"tmp512_{i}", [128, 512], F32) for i in range(2)]),
    }
    fbuf = Rot([sb(f"fbuf{i}", [128, 2, W + 2], F32) for i in range(4)])
    cg = Rot([sb(f"cg{i}", [128, W], F32) for i in range(4)])
    cu = Rot([sb(f"cu{i}", [128, W], F32) for i in range(4)])
    t1 = Rot([sb(f"t1_{i}", [128, W], F32) for i in range(3)])
    t2 = Rot([sb(f"t2_{i}", [128, W], F32) for i in range(3)])
    sg = Rot([sb(f"sg{i}", [128, W], F32) for i in range(3)])
    WQ = DFF // 4
    ptr = Rot([ps("ptr2", [128, 8, 128], BF16)], excl=True)
    pf = Rot([ps(f"pf{i}", [128, 2, W], F32) for i in range(3)], excl=True)
    pd = [[ps(f"pd{b}{h}", [128, 512], F32) for h in range(2)] for b in range(NBLK)]
    kpd = [[PKey(f"pd{b}{h}") for h in range(2)] for b in range(NBLK)]
    dsem = {n: S.new_dma_sem("p2_" + n) for n in ("wst0", "wst1", "wst2", "wst3", "wdn", "grow", "h0", "h1", "hw", "out0", "out1")}

    S.op("sp", lambda e: e.dma_start(out=grow[:], in_=io["post_ffn_g"].partition_broadcast(128)), writes=[kgrow], dma_sem=dsem["grow"])
    actf = act[:].rearrange("p i w -> p (i w)").bitcast(F32)
    kstg = [Key(f"stg{q}") for q in range(4)]
    for kc in range(8):
        for q in range(8):
            j = kc * 8 + q
            r_ = j % 4
            wt, kwt = actf[:, r_ * WQ:(r_ + 1) * WQ], kstg[r_]
            S.op("sp", lambda e, wt=wt, kc=kc, q=q: e.dma_start(out=wt, in_=io["w_up"][kc * 128:(kc + 1) * 128, q * WQ:(q + 1) * WQ]),
                 writes=[kwt], dma_sem=dsem[f"wst{r_}"])
            if j % 2 == 0:
                S.op("act", lambda e, wt=wt, kc=kc, q=q: e.activation(out=wup[:, kc, q * WQ:(q + 1) * WQ], in_=wt, func=AF.Copy,
                                                                       scale=cc[:, O_PFG + kc:O_PFG + kc + 1]),
                     reads=[kwt, kcc], writes=[kwup[kc]])
            else:
                S.op("dve", lambda e, wt=wt, kc=kc, q=q: e.tensor_scalar(out=wup[:, kc, q * WQ:(q + 1) * WQ], in0=wt,
                                                                          scalar1=cc[:, O_PFG + kc:O_PFG + kc + 1], scalar2=1.0, op0=ALU.mult, op1=ALU.mult),
                     reads=[kwt, kcc], writes=[kwup[kc]])
    S.op("pool", lambda e: e.memset(act[:, :, 0:1], 0.0), writes=kstg + kact)
    S.op("pool", lambda e: e.dma_start(out=wdn[:], in_=io["w_down"].rearrange("(i p) d -> p i d", p=128)), writes=[kwdn], dma_sem=dsem["wdn"])

    hw = hts[1]
    S.op("sp", lambda e: e.dma_start(out=hw[:, 0, :], in_=hscr[W - 128:W, :]), reads=[khscr[0]], writes=[khts[1][0]], dma_sem=dsem["hw"])
    _rms_transpose(S, hw[:, 0, :], khts[1][0], hnTs[1], khnT[1], 0, scr, ident, kid, ptr,
                   extra_scale=cc[:, O_HM:O_HM + 1], kextra=kcc)
    pfh, kpfh = pf.next()
    pfh_v = pfh[:].rearrange("p a w -> p (a w)")
    for ch in range(2 * NPAIR):
        i, hf = ch % NPAIR, ch // NPAIR
        col = (i * 2 + hf) * 2
        for kc in range(8):
            S.op("pe", lambda e, ch=ch, kc=kc, col=col: e.matmul(pfh_v[:, col:col + 2], lhsT=wup[:, kc, ch * 128:(ch + 1) * 128],
                                                                  rhs=hnTs[1][:, kc, 126:128], start=(kc == 0), stop=(kc == 7)),
                 reads=[kwup[kc], khnT[1]], writes=[kpfh])
    S.op("act", lambda e: e.activation(out=fh[:].rearrange("p i a b -> p (i a b)"), in_=pfh_v[:, 0:NPAIR * 4], func=AF.Copy),
         reads=[kpfh], writes=kfh)

    def load(t):
        b = t % 2
        S.op("sp", lambda e: e.dma_start(out=hts[b][:], in_=hscr[W * (1 + t):W * (2 + t), :].rearrange("(b p) d -> p b d", p=128)),
             reads=[khscr[1 + t]], writes=khts[b], dma_sem=dsem[f"h{b}"])

    def prologue(t):
        b = t % 2
        for blk in range(NBLK):
            _rms_transpose(S, hts[b][:, blk, :], khts[b][blk], hnTs[b], khnT[b], blk * 128, scr, ident, kid, ptr)

    state = {}

    def up_mm(t, i):
        b = t % 2
        p, kp = pf.next()
        state[(t, i)] = (p, kp)
        for hf in range(2):
            ch = hf * NPAIR + i
            for kc in range(8):
                S.op("pe", lambda e, p=p, hf=hf, ch=ch, kc=kc: e.matmul(p[:, hf, :], lhsT=wup[:, kc, ch * 128:(ch + 1) * 128],
                                                                         rhs=hnTs[b][:, kc, :], start=(kc == 0), stop=(kc == 7)),
                     reads=[kwup[kc], khnT[b]], writes=[kp])

    def elem(t, i):
        p, kp = state.pop((t, i))
        fb, kfb = fbuf.next()
        S.op("pool", lambda e: e.tensor_copy(out=fb[:, :, 0:2], in_=fh[:, i, :, :]), reads=[kfh[i]], writes=[kfb])
        S.op("act", lambda e: e.activation(out=fb[:, :, 2:W + 2], in_=p[:], func=AF.Copy), reads=[kp], writes=[kfb])
        yield
        S.op("pool", lambda e: e.tensor_copy(out=fh[:, i, :, :], in_=fb[:, :, W:W + 2]), reads=[kfb], writes=[kfh[i]])
        outs = []
        for hf, rot in ((0, cg), (1, cu)):
            ch = hf * NPAIR + i
            c, kc_ = rot.next()
            wof = O_FCW + ch * 3
            S.op("act", lambda e, c=c, hf=hf, wof=wof, ch=ch: e.activation(out=c[:], in_=fb[:, hf, 2:W + 2], func=AF.Identity,
                                                                          scale=cc[:, wof + 2:wof + 3], bias=cc[:, O_FCB + ch:O_FCB + ch + 1]),
                 reads=[kfb, kcc], writes=[kc_])
            outs.append((c, kc_, hf, wof))
        yield
        for c, kc_, hf, wof in outs:
            S.op("dve", lambda e, c=c, hf=hf, wof=wof: e.scalar_tensor_tensor(out=c[:], in0=fb[:, hf, 1:W + 1], scalar=cc[:, wof + 1:wof + 2],
                                                                             in1=c[:], op0=ALU.mult, op1=ALU.add),
                 reads=[kfb, kcc, kc_], writes=[kc_])
        yield
        for c, kc_, hf, wof in outs:
            S.op("dve", lambda e, c=c, hf=hf, wof=wof: e.scalar_tensor_tensor(out=c[:], in0=fb[:, hf, 0:W], scalar=cc[:, wof:wof + 1],
                                                                             in1=c[:], op0=ALU.mult, op1=ALU.add),
                 reads=[kfb, kcc, kc_], writes=[kc_])
        outs = [(c, kc_) for c, kc_, hf, wof in outs]
        yield
        (g_, kg), (u_, ku) = outs
        a1, ka1 = t1.next()
        a2, ka2 = t2.next()
        s_, ks = sg.next()
        S.op("pool", lambda e: e.tensor_tensor(out=a1[:], in0=g_[:], in1=g_[:], op=ALU.mult), reads=[kg], writes=[ka1])
        yield
        S.op("pool", lambda e: e.tensor_scalar(out=a1[:], in0=a1[:], scalar1=0.044715, scalar2=1.0, op0=ALU.mult, op1=ALU.add),
             reads=[ka1], writes=[ka1])
        S.op("pool", lambda e: e.tensor_tensor(out=a2[:], in0=a1[:], in1=g_[:], op=ALU.mult), reads=[ka1, kg], writes=[ka2])
        S.op("dve", lambda e: e.tensor_tensor(out=a1[:], in0=g_[:], in1=u_[:], op=ALU.mult), reads=[kg, ku, ka2], writes=[ka1])
        yield
        S.op("act", lambda e: e.activation(out=s_[:], in_=a2[:], func=AF.Sigmoid, scale=1.5957691216), reads=[ka2], writes=[ks])
        yield
        S.op("dve", lambda e: e.tensor_tensor(out=act[:, i, :], in0=a1[:], in1=s_[:], op=ALU.mult), reads=[ka1, ks], writes=[kact[i]])

    def down_mm(t, i):
        for blk in range(NBLK):
            for hf in range(2):
                S.op("pe", lambda e, blk=blk, hf=hf: e.matmul(pd[blk][hf][:], lhsT=act[:, i, blk * 128:(blk + 1) * 128],
                                                              rhs=wdn[:, i, hf * 512:(hf + 1) * 512], start=(i == 0), stop=(i == NPAIR - 1)),
                     reads=[kact[i], kwdn], writes=[kpd[blk][hf]])

    def epilogue(t):
        b = t % 2
        for blk in range(NBLK):
            _post_norm_residual(S, [(pd[blk][0], kpd[blk][0]), (pd[blk][1], kpd[blk][1])], hts[b][:, blk, :], khts[b][blk], grow, kgrow, scr)
        S.op("sp", lambda e: e.dma_start(out=out[W * t:W * (t + 1), :].rearrange("(b p) d -> p b d", p=128), in_=hts[b][:]),
             reads=khts[b], dma_sem=dsem[f"out{b}"])

    load(0)
    if n_main > 1:
        load(1)
    prologue(0)
    TSTEP = 2
    for t in range(n_main):
        active = []
        i_next, tick, pro_done = 0, 0, False
        while i_next < NPAIR or active:
            if i_next < NPAIR and tick % TSTEP == 0:
                up_mm(t, i_next)
                active.append((i_next, elem(t, i_next)))
                i_next += 1
                if i_next == 15 and t + 1 < n_main and not pro_done:
                    prologue(t + 1)
                    pro_done = True
            for item in list(active):
                i_, g_ = item
                try:
                    next(g_)
                except StopIteration:
                    active.remove(item)
                    down_mm(t, i_)
            tick += 1
        epilogue(t)
        if t + 2 < n_main:
            load(t + 2)
    S.final_wait("sp", [k for ks_ in khts for k in ks_])


def build(n_pre, n_main, mode="full"):
    nc = bass.Bass("TRN2", target_bir_lowering=False)
    TT = (n_pre + 1 + n_main) * W
    io = {}
    di = lambda n, s: nc.dram_tensor(n, s, F32, kind="ExternalInput").ap()
    io["cc"] = di("cc", [128, NCC])
    io["km"] = di("km", [128, NKM])
    io["post_ffn_g"] = di("post_ffn_g", [D])
    io["w_up"] = di("w_up", [D, 2 * DFF])
    io["w_down"] = di("w_down", [DFF, D])
    if mode == "ffn":
        io["hscr"] = di("hscr", [(1 + n_main) * W, D])
    else:
        io["xin"] = di("xin", [TT, D])
        io["post_mix_g"] = di("post_mix_g", [D])
        io["w_in"] = di("w_in", [D, INC])
        io["w_out"] = di("w_out", [D, D])
        io["w2"] = di("w2", [64, 512])
        io["a2"] = di("a2", [64, 512])
        io["g2"] = di("g2", [160, 512])
        io["hscr"] = nc.dram_tensor("hscr", [(1 + n_main) * W, D], F32, kind="Internal").ap()
    io["khscr"] = [Key(f"hscr{i}") for i in range(1 + n_main)]
    io["out"] = nc.dram_tensor("out", [n_main * W, D], F32, kind="ExternalOutput").ap()

    with contextlib.ExitStack() as sem_stack, contextlib.ExitStack() as st0:
        cc = st0.enter_context(nc.sbuf_tensor("cc_sb", [128, NCC], F32))
        ident = st0.enter_context(nc.sbuf_tensor("ident", [128, 128], BF16))

        def shared_loads(S):
            kcc, kid = Key("cc"), Key("ident")
            d0 = S.new_dma_sem("cc")
            d1 = S.new_dma_sem("ident")
            S.op("sp", lambda e: e.dma_start(out=cc[:], in_=io["cc"][:, :]), writes=[kcc], dma_sem=d0)
            S.op("pool", lambda e: e.dma_start(out=ident[:], in_=io["km"][:, M_ID:M_ID + 128]), writes=[kid], dma_sem=d1)
            return {"cc": cc, "kcc": kcc, "ident": ident, "kid": kid}

        if mode != "ffn":
            S1 = Sched(nc, sem_stack, "a")
            shared = shared_loads(S1)
            with contextlib.ExitStack() as st1:
                phase1_mixer(nc, S1, st1, io, n_pre, n_main, shared)
                S1.final_wait("sp", io["khscr"])
                S1.emit()
            S2 = Sched(nc, sem_stack, "b")
            shared = {"cc": cc, "kcc": Key("cc2"), "ident": ident, "kid": Key("ident2")}
            io["khscr"] = [Key(f"hscr2_{i}") for i in range(1 + n_main)]
        else:
            S2 = Sched(nc, sem_stack, "b")
            shared = shared_loads(S2)
        with contextlib.ExitStack() as st2:
            phase2_ffn(nc, S2, st2, io, n_main, shared)
            S2.emit()
    return nc


def phase1_mixer(nc, S, st, io, n_pre, n_main, shared):
    sb = lambda n, s, d: st.enter_context(nc.sbuf_tensor(n, s, d))
    ps = lambda n, s, d: st.enter_context(nc.psum_tensor(n, s, d))
    cc, kcc = shared["cc"], shared["kcc"]
    ident, kid = shared["ident"], shared["kid"]
    xin, hscr, khscr = io["xin"], io["hscr"], io["khscr"]
    n_tiles = n_pre + 1 + n_main
    C05 = 0.6065306597126334

    win = sb("win", [128, 8, INC], BF16)
    kwin = [Key(f"win{k}") for k in range(8)]
    wout = sb("wout", [128, 8, D], BF16)
    kwout = Key("wout")
    w2b = sb("w2b", [128, 512], BF16)
    a2b = sb("a2b", [128, 512], BF16)
    g2b0 = sb("g2b0", [128, 512], BF16)
    g2b1 = sb("g2b1", [128, 512], BF16)
    wg1 = sb("wg1", [128, 8, 128], BF16)
    kwg1 = Key("wg1")
    ksmallw = Key("smallw")
    msu4 = sb("msu4", [128, 512], BF16)
    msl = sb("msl", [128, 128], BF16)
    bones = sb("bones", [128, 128], BF16)
    fst = sb("fst", [128, 64], BF16)
    rst = sb("rst", [128, W], F32)
    kconst = Key("p1const")
    grow = sb("grow1", [128, D], F32)
    kgrow = Key("grow1")
    dc = sb("dc", [128, 20], F32)
    kdc = Key("dc")
    dsem = {n: S.new_dma_sem("p1_" + n) for n in ("wst0", "wst1", "wst2", "wst3", "const", "grow", "x0", "x1", "h0", "h1")}

    S.op("sp", lambda e: e.dma_start(out=grow[:], in_=io["post_mix_g"].partition_broadcast(128)), writes=[kgrow], dma_sem=dsem["grow"])
    S.op("sp", lambda e: e.dma_start(out=rst[:], in_=io["km"][:, M_RST:M_RST + W]), writes=[kconst], dma_sem=dsem["const"])
    uniq = [0]

    def pool_dma(fn, key):
        uniq[0] += 1
        S.op("pool", fn, writes=[key], dma_sem=S.new_dma_sem(f"p1u{uniq[0]}"))

    kmsl, kbones, kfst = Key("msl"), Key("bones"), Key("fst")
    kmsu4 = [Key(f"msu4_{q}") for q in range(4)]
    kw2b, ka2b, kg2b0, kg2b1 = Key("w2b"), Key("a2b"), Key("g2b0"), Key("g2b1")
    for dst, c0, n_, k_ in ((msl, M_SL, 128, kmsl), (bones, M_BO, 128, kbones), (fst, M_F, 64, kfst)):
        pool_dma(lambda e, dst=dst, c0=c0, n_=n_: e.dma_start(out=dst[:], in_=io["km"][:, c0:c0 + n_]), k_)
    for q, c0 in enumerate((M_SU, M_IU, M_SU, M_IU)):
        pool_dma(lambda e, q=q, c0=c0: e.dma_start(out=msu4[:, q * 128:(q + 1) * 128], in_=io["km"][:, c0:c0 + 128]), kmsu4[q])
    S.op("pool", lambda e: e.memset(w2b[:], 0.0), writes=[kw2b])
    S.op("pool", lambda e: e.memset(a2b[:], 0.0), writes=[ka2b])
    S.op("pool", lambda e: e.memset(g2b1[:], 0.0), writes=[kg2b1])
    S.op("pool", lambda e: e.memset(wg1[:], 0.0), writes=[kwg1])
    pool_dma(lambda e: e.dma_start(out=w2b[0:64, :], in_=io["w2"][:, :]), kw2b)
    pool_dma(lambda e: e.dma_start(out=a2b[64:128, :], in_=io["a2"][:, :]), ka2b)
    pool_dma(lambda e: e.dma_start(out=g2b0[:], in_=io["g2"][0:128, :]), kg2b0)
    pool_dma(lambda e: e.dma_start(out=g2b1[0:32, :], in_=io["g2"][128:160, :]), kg2b1)
    pool_dma(lambda e: e.dma_start(out=wout[:], in_=io["w_out"].rearrange("(k p) d -> p k d", p=128)), kwout)
    S.op("dve", lambda e: e.tensor_scalar(out=dc[:, 0:15], in0=cc[:, O_MU:O_MU + 15], scalar1=-1.0, scalar2=1.0, op0=ALU.mult, op1=ALU.add),
         reads=[kcc], writes=[kdc])
    S.op("dve", lambda e: e.tensor_scalar(out=dc[:, 15:19], in0=cc[:, O_KA:O_KA + 4], scalar1=-1.0, scalar2=1.0, op0=ALU.mult, op1=ALU.add),
         reads=[kcc], writes=[kdc])
    xts = [sb(f"xt{i}", [128, NBLK, D], F32) for i in range(2)]
    kxts = [[Key(f"xt{i}_{b}") for b in range(NBLK)] for i in range(2)]
    xnT = sb("xnT", [128, 8, W], BF16)
    kxnT = Key("xnT")
    scr = {
        "stat": Rot([sb(f"stat1_{i}", [128, 8], F32) for i in range(4)]),
        "xs": Rot([sb(f"xs1_{i}", [128, D], BF16) for i in range(2)]),
    }
    tf = Rot([sb(f"tf{i}", [128, W], F32) for i in range(9)])
    tb = Rot([sb(f"tb{i}", [128, W], BF16) for i in range(4)])
    qbuf = Rot([sb(f"qbuf{i}", [128, W + 1], F32) for i in range(2)])
    ubuf = Rot([sb(f"ubuf{i}", [128, W + 2], F32) for i in range(2)])
    qh = sb("qh", [128, 15], F32)
    kqh = [Key(f"qh{j}") for j in range(15)]
    uh = sb("uh", [128, 4, 2], F32)
    kuh = [Key(f"uh{c}") for c in range(4)]
    rkv = {n: Rot([sb(f"{n}c{i}", [128, W], F32) for i in range(2)]) for n in ("r", "k", "v")}
    wa = sb("wa", [128, W], F32); kwa = Key("wa")
    g0t = sb("g0t", [128, W], F32); kg0 = Key("g0t")
    g1t = sb("g1t", [128, W], F32); kg1 = Key("g1t")
    twad = sb("twad", [128, W], BF16); ktwad = Key("twad")
    sgds = [(sb(f"sgd0_{i}", [128, W], BF16), sb(f"sgd1_{i}", [128, W], BF16), Key(f"sgd{i}")) for i in range(2)]
    sig = sb("sig", [128, 4, W], F32); ksig = [Key(f"sig{c}") for c in range(4)]
    a4 = sb("a4", [128, 4, W], F32); ka4 = [Key(f"a4{c}") for c in range(4)]
    bonus = sb("bonus", [128, 4, W], F32); kbonus = [Key(f"bonus{c}") for c in range(4)]
    ARbd = sb("ARbd", [128, 4, NCH, 2, 128], BF16); kAbd = [Key(f"Abd{c}") for c in range(4)]; kRbd = [Key(f"Rbd{c}") for c in range(4)]
    Bbd = sb("Bbd", [128, 4, NCH, 128], BF16); kBbd = [Key(f"Bbd{c}") for c in range(4)]
    Kbd = sb("Kbd", [128, 4, NCH, 128], BF16); kKbd = [Key(f"Kbd{c}") for c in range(4)]
    Vbd = sb("Vbd", [128, 4, NCH, 128], BF16); kVbd = [Key(f"Vbd{c}") for c in range(4)]
    PC = sb("PC", [128, 4, NCH], F32); kPC = [Key(f"PC{c}") for c in range(4)]
    M1s = [(sb(f"M1_{i}", [128, 4, 512], BF16), Key(f"M1_{i}")) for i in range(2)]
    NT0s = [(sb(f"NT0_{i}", [128, 4, 128], BF16), Key(f"NT0_{i}")) for i in range(2)]
    M2s = [(sb(f"M2_{i}", [128, 4, 320], BF16), Key(f"M2_{i}")) for i in range(2)]
    NNs = [[(sb(f"NN{i}{j}", [128, 4, 256], BF16), Key(f"NN{i}{j}")) for j in range(2)] for i in range(2)]
    Tbs = [[(sb(f"Tb{i}{j}", [128, 4, 128], BF16), Key(f"Tb{i}{j}")) for j in range(2)] for i in range(2)]
    Zb = sb("Zb", [128, 4, 64], BF16); kZb = Key("Zb")
    Ub = sb("Ub", [128, 4, 64], BF16); kUb = Key("Ub")
    S32 = sb("S32", [128, 4, 64], F32); kS32 = Key("S32")
    Sbf = sb("Sbf", [128, 4, 64], BF16); kSbf = Key("Sbf")
    gst = Rot([sb(f"gst{i}", [128, 8, 4], F32) for i in range(2)])
    ynbd = sb("ynbd", [128, 4, 128], BF16); kynbd = Key("ynbd")
    ynf = sb("ynf", [128, 4, W], F32); _ka, _kb = Key("ynfA"), Key("ynfB"); kynf = [_ka, _ka, _kb, _kb]
    scr["tmp512"] = Rot([View(ynf[:, 0:2, :].rearrange("p c w -> p (c w)")), View(ynf[:, 2:4, :].rearrange("p c w -> p (c w)"))], keys=[_ka, _kb])
    yTc = [sb(f"yTc{i}", [128, 4, W], BF16) for i in range(2)]; kyTc = [[Key(f"yTc{i}_{c}") for c in range(4)] for i in range(2)]
    yTr = sb("yTr", [128, 4, W], BF16); kyTr = [Key(f"yTr{c}") for c in range(4)]

    WQ = INC // 4
    stg = [(sig, ksig), (a4, ka4), (bonus, kbonus), (ynf, kynf)]
    for kc in range(8):
        for q in range(4):
            j = kc * 4 + q
            buf, kbuf = stg[j % 4]
            wt = buf[:].rearrange("p c w -> p (c w)")[:, 0:WQ]
            S.op("sp", lambda e, wt=wt, kc=kc, q=q: e.dma_start(out=wt, in_=io["w_in"][kc * 128:(kc + 1) * 128, q * WQ:(q + 1) * WQ]),
                 writes=kbuf, dma_sem=dsem[f"wst{j % 4}"])
            if j % 2 == 0:
                S.op("act", lambda e, wt=wt, kc=kc, q=q: e.activation(out=win[:, kc, q * WQ:(q + 1) * WQ], in_=wt, func=AF.Copy,
                                                                       scale=cc[:, O_PMG + kc:O_PMG + kc + 1]),
                     reads=kbuf + [kcc], writes=[kwin[kc]])
            else:
                S.op("dve", lambda e, wt=wt, kc=kc, q=q: e.tensor_scalar(out=win[:, kc, q * WQ:(q + 1) * WQ], in0=wt,
                                                                          scalar1=cc[:, O_PMG + kc:O_PMG + kc + 1], scalar2=1.0, op0=ALU.mult, op1=ALU.mult),
                     reads=kbuf + [kcc], writes=[kwin[kc]])
    S.op("dve", lambda e: e.tensor_copy(out=wg1[:, :, 0:32], in_=win[:, :, INC - 32:INC]), reads=kwin + [kwg1], writes=[kwg1])

    ptr = Rot([ps("ptr1", [128, 8, 128], BF16)], excl=True)
    pp = [ps(f"pp{i}", [128, 2, W], F32) for i in range(2)]
    kpp = [PKey(f"pp{i}") for i in range(2)]
    psm = ps("psm", [128, 2, W], F32); kpsm = PKey("psm")
    PA = ps("PA", [128, 1024], F32); kPA = PKey("PA")
    PB = ps("PB", [128, 1024], F32); kPB0 = PKey("PB0"); kPB1 = PKey("PB1")

    for t_, k_ in ((qh, kqh), (uh, kuh)):
        S.op("pool", lambda e, t_=t_: e.memset(t_[:], 0.0), writes=k_)
    S.op("pool", lambda e: e.memset(S32[:], 0.0), writes=[kS32])
    S.op("pool", lambda e: e.memset(Sbf[:], 0.0), writes=[kSbf])
    S.op("pool", lambda e: e.memset(ARbd[:], 0.0), writes=kAbd + kRbd)
    S.op("pool", lambda e: e.memset(Bbd[:], 0.0), writes=kBbd)
    S.op("pool", lambda e: e.memset(Kbd[:], 0.0), writes=kKbd)
    S.op("pool", lambda e: e.memset(Vbd[:], 0.0), writes=kVbd)
    S.op("pool", lambda e: e.memset(ynbd[:], 0.0), writes=[kynbd])
    S.op("pool", lambda e: e.memset(g1t[:], 0.0), writes=[kg1])

    slot_i = [0]

    def proj(col0, ncols, wsrc=None, kw=None):
        j = slot_i[0] % 4
        slot_i[0] += 1
        t_, half, key = pp[j // 2], j % 2, kpp[j // 2]
        for kc in range(8):
            lhsT = win[:, kc, col0:col0 + ncols] if wsrc is None else wsrc[:, kc, :]
            S.op("pe", lambda e, kc=kc, lhsT=lhsT: e.matmul(t_[0:ncols, half, :], lhsT=lhsT, rhs=xnT[:, kc, :],
                                                            start=(kc == 0), stop=(kc == 7)),
                 reads=[(kwin if kw is None else kw)[kc], kxnT], writes=[key])
        return t_[0:ncols, half, :], key

    def shift_lerp(p_ap, kp, jj, dst_ap, kdst, np_=128):
        qb, kqb = qbuf.next()
        S.op("pool", lambda e: e.tensor_copy(out=qb[0:np_, 0:1], in_=qh[0:np_, jj:jj + 1]), reads=[kqh[jj]], writes=[kqb])
        S.op("act", lambda e: e.activation(out=qb[0:np_, 1:W + 1], in_=p_ap, func=AF.Copy), reads=[kp], writes=[kqb])
        S.op("pool", lambda e: e.tensor_copy(out=qh[0:np_, jj:jj + 1], in_=qb[0:np_, W:W + 1]), reads=[kqb], writes=[kqh[jj]])
        tmp, ktmp = tf.next()
        S.op("pool", lambda e: e.tensor_scalar(out=tmp[0:np_, :], in0=qb[0:np_, 1:W + 1], scalar1=dc[0:np_, jj:jj + 1], scalar2=1.0, op0=ALU.mult, op1=ALU.mult),
             reads=[kqb, kdc], writes=[ktmp])
        S.op("dve", lambda e: e.scalar_tensor_tensor(out=dst_ap, in0=qb[0:np_, 0:W], scalar=cc[0:np_, O_MU + jj:O_MU + jj + 1], in1=tmp[0:np_, :],
                                                     op0=ALU.mult, op1=ALU.add),
             reads=[kqb, kcc, ktmp], writes=[kdst])

    def bd_view(t4, c, h):
        return t4[h * 64:(h + 1) * 64, c, :, h * 64:(h + 1) * 64]

    def v3(t2, h):
        return t2[h * 64:(h + 1) * 64, :].rearrange("p (n t) -> p n t", n=NCH)

    def load_x(t):
        b = t % 2
        S.op("sp", lambda e: e.dma_start(out=xts[b][:], in_=xin[W * t:W * (t + 1), :].rearrange("(b p) d -> p b d", p=128)),
             writes=kxts[b], dma_sem=dsem[f"x{b}"])

    load_x(0)
    if n_tiles > 1:
        load_x(1)

    def tile_parts(t):
        full = t >= n_pre
        b = t % 2
        par = t % 2
        xt = xts[b]
        sgd0, sgd1, ksgd = sgds[par]
        yc, kyc = yTc[par], kyTc[par]

        def early():
            for blk in range(NBLK):
                _rms_transpose(S, xt[:, blk, :], kxts[b][blk], xnT, kxnT, blk * 128, scr, ident, kid, ptr)

            yield
            if full:
                def b1(c):
                    pB, kpB = proj(c * 128, 128)
                    pH, kpH = proj(1024 + c * 128, 128)
                    hAs, khAs = tf.next()
                    S.op("act", lambda e, hAs=hAs, pH=pH: e.activation(out=hAs[:], in_=pH, func=AF.Copy), reads=[kpH], writes=[khAs])
                    ub, kub = ubuf.next()
                    S.op("pool", lambda e, ub=ub, c=c: e.tensor_copy(out=ub[:, 0:2], in_=uh[:, c, :]), reads=[kuh[c]], writes=[kub])
                    S.op("dve", lambda e, ub=ub, pB=pB, hAs=hAs: e.tensor_tensor(out=ub[:, 2:W + 2], in0=pB, in1=hAs[:], op=ALU.mult),
                         reads=[kpB, khAs], writes=[kub])
                    S.op("pool", lambda e, ub=ub, c=c: e.tensor_copy(out=uh[:, c, :], in_=ub[:, W:W + 2]), reads=[kub], writes=[kuh[c]])
                    ta, kta = tf.next()
                    wof = O_CAW + c * 3
                    S.op("act", lambda e, ta=ta, ub=ub, wof=wof: e.activation(out=ta[:], in_=ub[:, 2:W + 2], func=AF.Copy, scale=cc[:, wof + 2:wof + 3]),
                         reads=[kub, kcc], writes=[kta])
                    S.op("dve", lambda e, ta=ta, ub=ub, wof=wof: e.scalar_tensor_tensor(out=ta[:], in0=ub[:, 1:W + 1], scalar=cc[:, wof + 1:wof + 2], in1=ta[:],
                                                                                      op0=ALU.mult, op1=ALU.add), reads=[kub, kcc, kta], writes=[kta])
                    S.op("dve", lambda e, ta=ta, ub=ub, wof=wof: e.scalar_tensor_tensor(out=ta[:], in0=ub[:, 0:W], scalar=cc[:, wof:wof + 1], in1=ta[:],
                                                                                      op0=ALU.mult, op1=ALU.add), reads=[kub, kcc, kta], writes=[kta])
                    pC, kpC = proj(512 + c * 128, 128)
                    S.op("dve", lambda e, ta=ta, pC=pC, c=c: e.tensor_tensor(out=yc[:, c, :], in0=pC, in1=ta[:], op=ALU.mult),
                         reads=[kpC, kta], writes=[kyc[c]])
                for c_ in range(4):
                    b1(c_)

            yield
            QC = 1536
            p_, kp_ = proj(QC + 12 * 128, 128)
            shift_lerp(p_, kp_, 12, wa[:], kwa)
            S.op("act", lambda e: e.activation(out=twad[0:64, :], in_=wa[0:64, :], func=AF.Tanh), reads=[kwa], writes=[ktwad])
            S.op("dve", lambda e: e.tensor_copy(out=twad[64:128, :], in_=wa[64:128, :]), reads=[kwa], writes=[ktwad])
            if full:
                p_, kp_ = proj(QC + 13 * 128, 128)
                shift_lerp(p_, kp_, 13, g0t[:], kg0)
                p_, kp_ = proj(None, 128, wsrc=wg1, kw=[kwg1] * 8)
                shift_lerp(p_[0:32, :], kp_, 14, g1t[0:32, :], kg1, np_=32)
                S.op("act", lambda e: e.activation(out=sgd0[:], in_=g0t[:], func=AF.Sigmoid), reads=[kg0], writes=[ksgd])
                S.op("act", lambda e: e.activation(out=sgd1[:], in_=g1t[:], func=AF.Sigmoid), reads=[kg1], writes=[ksgd])
            yield
            for c in range(4):
                S.op("pe", lambda e, c=c: e.matmul(psm[:, 0, :], lhsT=w2b[:, c * 128:(c + 1) * 128], rhs=twad[:], start=True, stop=True),
                     reads=[kw2b, ktwad], writes=[kpsm])
                S.op("pe", lambda e, c=c: e.matmul(psm[:, 1, :], lhsT=a2b[:, c * 128:(c + 1) * 128], rhs=twad[:], start=True, stop=True),
                     reads=[ka2b, ktwad], writes=[kpsm])
                S.op("act", lambda e, c=c: e.activation(out=sig[:, c, :], in_=psm[:, 0, :], func=AF.Sigmoid, bias=cc[:, O_W0 + c:O_W0 + c + 1]),
                     reads=[kpsm, kcc], writes=[ksig[c]])
                S.op("act", lambda e, c=c: e.activation(out=a4[:, c, :], in_=psm[:, 1, :], func=AF.Sigmoid, bias=cc[:, O_A0 + c:O_A0 + c + 1]),
                     reads=[kpsm, kcc], writes=[ka4[c]])

            yield

        def b4_all():
            def b4(c):
                kt_, kkt = rkv["k"].next()
                vt_, kvt = rkv["v"].next()
                p_, kp_ = proj(QC + (4 + c) * 128, 128)
                shift_lerp(p_, kp_, 4 + c, kt_[:], kkt)
                yield
                p_, kp_ = proj(QC + (8 + c) * 128, 128)
                shift_lerp(p_, kp_, 8 + c, vt_[:], kvt)
                yield
                if full:
                    rt_, krt = rkv["r"].next()
                    p_, kp_ = proj(QC + c * 128, 128)
                    shift_lerp(p_, kp_, c, rt_[:], krt)
                    yield
                kkr, kkkr = tf.next()
                S.op("pool", lambda e, kkr=kkr, kt_=kt_, c=c: e.tensor_scalar(out=kkr[:], in0=kt_[:], scalar1=cc[:, O_KK + c:O_KK + c + 1], scalar2=1.0, op0=ALU.mult, op1=ALU.mult),
                     reads=[kkt, kcc], writes=[kkkr])
                sq, ksq = tb.next()
                S.op("act", lambda e, sq=sq, kkr=kkr: e.activation(out=sq[:], in_=kkr[:], func=AF.Square), reads=[kkkr], writes=[ksq])
                S.op("pe", lambda e, sq=sq: e.matmul(psm[:, 0, :], lhsT=bones[:], rhs=sq[:], start=True, stop=True), reads=[kbones, ksq], writes=[kpsm])
                yield
                nrm, knrm = tf.next()
                S.op("act", lambda e, nrm=nrm: e.activation(out=nrm[:], in_=psm[:, 0, :], func=AF.Sqrt, bias=1e-24), reads=[kpsm], writes=[knrm])
                S.op("dve", lambda e, nrm=nrm: e.reciprocal(out=nrm[:], in_=nrm[:]), reads=[knrm], writes=[knrm])
                kk, kkk = tf.next()
                S.op("dve", lambda e, kk=kk, kkr=kkr, nrm=nrm: e.tensor_tensor(out=kk[:], in0=kkr[:], in1=nrm[:], op=ALU.mult), reads=[kkkr, knrm], writes=[kkk])
                cs, kcs = tf.next()
                S.op("dve", lambda e, cs=cs, c=c: e.tensor_tensor_scan(out=cs[:], data0=rst[:], data1=sig[:, c, :], initial=0.0, op0=ALU.mult, op1=ALU.add),
                     reads=[kconst, ksig[c]], writes=[kcs])
                yield
                E1, kE1 = tf.next()
                E2, kE2 = tf.next()
                E3, kE3 = tf.next()
                dd, kdd = tf.next()
                S.op("act", lambda e, E1=E1, cs=cs: e.activation(out=E1[:], in_=cs[:], func=AF.Exp, scale=-C05), reads=[kcs], writes=[kE1])
                S.op("act", lambda e, E2=E2, cs=cs: e.activation(out=E2[:], in_=cs[:], func=AF.Exp, scale=C05), reads=[kcs], writes=[kE2])
                S.op("pool", lambda e, dd=dd, cs=cs, c=c: e.tensor_tensor(out=dd[:], in0=cs[:], in1=sig[:, c, :], op=ALU.subtract), reads=[kcs, ksig[c]], writes=[kdd])
                S.op("act", lambda e, E3=E3, dd=dd: e.activation(out=E3[:], in_=dd[:], func=AF.Exp, scale=-C05), reads=[kdd], writes=[kE3])
                S.op("pool", lambda e, E1=E1, c=c: e.tensor_copy(out=PC[:, c, :], in_=E1[:].rearrange("p (n t) -> p n t", n=NCH)[:, :, CH - 1]),
                     reads=[kE1], writes=[kPC[c]])
                yield
                mm, kmm = tf.next()
                S.op("pool", lambda e, mm=mm, c=c: e.tensor_scalar(out=mm[:], in0=a4[:, c, :], scalar1=cc[:, O_KA + c:O_KA + c + 1], scalar2=dc[:, 15 + c:16 + c],
                                                                   op0=ALU.mult, op1=ALU.add), reads=[ka4[c], kcc, kdc], writes=[kmm])
                kp, kkp = tf.next()
                S.op("dve", lambda e, kp=kp, kt_=kt_, mm=mm: e.tensor_tensor(out=kp[:], in0=kt_[:], in1=mm[:], op=ALU.mult), reads=[kkt, kmm], writes=[kkp])
                akk, kakk = tf.next()
                S.op("pool", lambda e, akk=akk, kk=kk, c=c: e.tensor_tensor(out=akk[:], in0=a4[:, c, :], in1=kk[:], op=ALU.mult), reads=[ka4[c], kkk], writes=[kakk])
                yield
                for h in range(2):
                    S.op("dve", lambda e, h=h, kk=kk, E3=E3, c=c: e.scalar_tensor_tensor(out=ARbd[h * 64:(h + 1) * 64, c, :, 0, h * 64:(h + 1) * 64], in0=v3(kk, h), scalar=-1.0,
                                                                                       in1=v3(E3, h), op0=ALU.mult, op1=ALU.mult),
                         reads=[kkk, kE3], writes=[kAbd[c]])
                    S.op("dve", lambda e, h=h, akk=akk, E2=E2, c=c: e.tensor_tensor(out=bd_view(Bbd, c, h), in0=v3(akk, h), in1=v3(E2, h), op=ALU.mult),
                         reads=[kakk, kE2], writes=[kBbd[c]])
                    S.op("pool", lambda e, h=h, kp=kp, E2=E2, c=c: e.tensor_tensor(out=bd_view(Kbd, c, h), in0=v3(kp, h), in1=v3(E2, h), op=ALU.mult),
                         reads=[kkp, kE2], writes=[kKbd[c]])
                    S.op("act", lambda e, h=h, vt_=vt_, c=c: e.activation(out=bd_view(Vbd, c, h), in_=v3(vt_, h), func=AF.Copy), reads=[kvt], writes=[kVbd[c]])
                    if full:
                        S.op("pool", lambda e, h=h, rt_=rt_, E1=E1, c=c: e.tensor_tensor(out=ARbd[h * 64:(h + 1) * 64, c, :, 1, h * 64:(h + 1) * 64], in0=v3(rt_, h), in1=v3(E1, h),
                                                                                       op=ALU.mult), reads=[krt, kE1], writes=[kRbd[c]])
                yield
                if full:
                    rk, krk = tf.next()
                    S.op("pool", lambda e, rk=rk, rt_=rt_, kp=kp: e.tensor_tensor(out=rk[:], in0=rt_[:], in1=kp[:], op=ALU.mult), reads=[krt, kkp], writes=[krk])
                    rkb, krkb = tb.next()
                    S.op("dve", lambda e, rkb=rkb, rk=rk, c=c: e.tensor_scalar(out=rkb[:], in0=rk[:], scalar1=cc[:, O_RK + c:O_RK + c + 1], scalar2=1.0, op0=ALU.mult, op1=ALU.mult),
                         reads=[krk, kcc], writes=[krkb])
                    S.op("pe", lambda e, rkb=rkb: e.matmul(psm[:, 1, :], lhsT=bones[:], rhs=rkb[:], start=True, stop=True), reads=[kbones, krkb], writes=[kpsm])
                    S.op("dve", lambda e, vt_=vt_, c=c: e.tensor_tensor(out=bonus[:, c, :], in0=psm[:, 1, :], in1=vt_[:], op=ALU.mult), reads=[kpsm, kvt], writes=[kbonus[c]])

            for c_ in range(4):
                drain_g(b4(c_))
            if DEBUG_STOP < 5:
                return


        PA3 = PA[:].rearrange("p (a w) -> p a w", a=2)
        PAd = PA[:].rearrange("p (a w) -> p a w", a=4)
        PBt = PB[:, 0:512].rearrange("p (a w) -> p a w", a=4)
        PQ0 = PB[:, 512:768].rearrange("p (a w) -> p a w", a=4)
        PQ1 = PB[:, 768:1024].rearrange("p (a w) -> p a w", a=4)
        Tfinal = {}

        def gen_AD(n, s_):
            M1, kM1 = M1s[s_]
            NT0, kNT0 = NT0s[s_]
            M2, kM2 = M2s[s_]
            for pi in range(2):
                for i in range(2):
                    c = pi * 2 + i
                    rhsAR = ARbd[:, c, n, :, :].rearrange("p a w -> p (a w)")
                    S.op("pe", lambda e, i=i, c=c, rhsAR=rhsAR: e.matmul(PA3[:, i, 0:256], lhsT=Bbd[:, c, n, :], rhs=rhsAR, start=True, stop=True),
                         reads=[kBbd[c], kAbd[c], kRbd[c]], writes=[kPA])
                    S.op("pe", lambda e, i=i, c=c, rhsAR=rhsAR: e.matmul(PA3[:, i, 256:512], lhsT=Kbd[:, c, n, :], rhs=rhsAR, start=True, stop=True),
                         reads=[kKbd[c], kAbd[c], kRbd[c]], writes=[kPA])
                S.op("dve", lambda e, pi=pi: e.tensor_tensor(out=M1[:, 2 * pi:2 * pi + 2, :], in0=PA3, in1=msu4[:].unsqueeze(1).to_broadcast([128, 2, 512]), op=ALU.mult),
                     reads=[kPA] + kmsu4, writes=[kM1])
                yield
                for i in range(2):
                    c = pi * 2 + i
                    S.op("pe", lambda e, i=i, c=c: e.matmul(PA3[:, i, 0:128], lhsT=ARbd[:, c, n, 0, :], rhs=Bbd[:, c, n, :], start=True, stop=True),
                         reads=[kAbd[c], kBbd[c]], writes=[kPA])
                    S.op("pe", lambda e, i=i, c=c: e.matmul(PA3[:, i, 128:256], lhsT=Bbd[:, c, n, :], rhs=ident[:], start=True, stop=True),
                         reads=[kBbd[c], kid], writes=[kPA])
                    S.op("pe", lambda e, i=i, c=c: e.matmul(PA3[:, i, 256:384], lhsT=Kbd[:, c, n, :], rhs=ident[:], start=True, stop=True),
                         reads=[kKbd[c], kid], writes=[kPA])
                    S.op("pe", lambda e, i=i, c=c: e.matmul(PA3[:, i, 384:448], lhsT=Vbd[:, c, n, :], rhs=fst[:], start=True, stop=True),
                         reads=[kVbd[c], kfst], writes=[kPA])
                S.op("dve", lambda e, pi=pi: e.tensor_tensor(out=NT0[:, 2 * pi:2 * pi + 2, :], in0=PA3[:, :, 0:128], in1=msl[:].unsqueeze(1).to_broadcast([128, 2, 128]), op=ALU.mult),
                     reads=[kPA, kmsl], writes=[kNT0])
                S.op("act", lambda e, pi=pi: e.activation(out=M2[:, 2 * pi:2 * pi + 2, :], in_=PA3[:, :, 128:448], func=AF.Copy), reads=[kPA], writes=[kM2])
                yield
            Tcur, kTcur = Tbs[s_][0]
            S.op("pool", lambda e, Tcur=Tcur: e.tensor_tensor(out=Tcur[:], in0=M1[:, :, 0:128], in1=ident[:].unsqueeze(1).to_broadcast([128, 4, 128]), op=ALU.add),
                 reads=[kM1, kid], writes=[kTcur])
            Nprev = lambda c: M1[:, c, 0:128]
            NTprev = lambda c: NT0[:, c, :]
            kprev = [kM1, kNT0]
            for j in range(1, 6):
                NNj, kNNj = NNs[s_][j % 2]
                for c in range(4):
                    if j < 5:
                        S.op("pe", lambda e, c=c, Nprev=Nprev, NTprev=NTprev: e.matmul(PAd[:, c, 0:128], lhsT=NTprev(c), rhs=Nprev(c), start=True, stop=True),
                             reads=kprev, writes=[kPA])
                    S.op("pe", lambda e, c=c, Nprev=Nprev, NTprev=NTprev: e.matmul(PAd[:, c, 128:256], lhsT=Nprev(c), rhs=NTprev(c), start=True, stop=True),
                         reads=kprev, writes=[kPA])
                if j < 5:
                    S.op("act", lambda e, NNj=NNj: e.activation(out=NNj[:], in_=PAd, func=AF.Copy), reads=[kPA], writes=[kNNj])
                else:
                    S.op("act", lambda e, NNj=NNj: e.activation(out=NNj[:, :, 128:256], in_=PAd[:, :, 128:256], func=AF.Copy), reads=[kPA], writes=[kNNj])
                yield
                for c in range(4):
                    S.op("pe", lambda e, c=c, NNj=NNj, Tcur=Tcur: e.matmul(PBt[:, c, :], lhsT=NNj[:, c, 128:256], rhs=Tcur[:, c, :], start=True, stop=True),
                         reads=[kNNj, kTcur], writes=[kPB0])
                Tnew, kTnew = Tbs[s_][j % 2]
                S.op("dve", lambda e, Tnew=Tnew, Tcur=Tcur: e.tensor_tensor(out=Tnew[:], in0=PBt, in1=Tcur[:], op=ALU.add), reads=[kPB0, kTcur], writes=[kTnew])
                Tcur, kTcur = Tnew, kTnew
                Nprev = (lambda NNj: (lambda c: NNj[:, c, 0:128]))(NNj)
                NTprev = (lambda NNj: (lambda c: NNj[:, c, 128:256]))(NNj)
                kprev = [kNNj]
                yield
            Tfinal[n] = (Tcur, kTcur)

        def gen_SQ(n, s_):
            M1, kM1 = M1s[s_]
            M2, kM2 = M2s[s_]
            Tcur, kTcur = Tfinal[n]
            for c in range(4):
                S.op("pe", lambda e, c=c: e.matmul(PQ0[:, c, :], lhsT=ARbd[:, c, n, 0, :], rhs=Sbf[:, c, :], start=True, stop=False),
                     reads=[kAbd[c], kSbf], writes=[kPB1])
                S.op("pe", lambda e, c=c: e.matmul(PQ0[:, c, :], lhsT=M1[:, c, 256:384], rhs=M2[:, c, 256:320], start=False, stop=True),
                     reads=[kM1, kM2], writes=[kPB1])
            S.op("act", lambda e: e.activation(out=Zb[:], in_=PQ0, func=AF.Copy), reads=[kPB1], writes=[kZb])
            yield
            for c in range(4):
                S.op("pe", lambda e, c=c: e.matmul(PQ1[:, c, :], lhsT=Tcur[:, c, :], rhs=Zb[:, c, :], start=True, stop=True),
                     reads=[kTcur, kZb], writes=[kPB1])
            S.op("act", lambda e: e.activation(out=Ub[:], in_=PQ1, func=AF.Copy), reads=[kPB1], writes=[kUb])
            yield
            for c in range(4):
                S.op("pe", lambda e, c=c: e.matmul(PQ0[:, c, :], lhsT=M2[:, c, 0:128], rhs=Ub[:, c, :], start=True, stop=False),
                     reads=[kM2, kUb], writes=[kPB1])
                S.op("pe", lambda e, c=c: e.matmul(PQ0[:, c, :], lhsT=M2[:, c, 128:256], rhs=M2[:, c, 256:320], start=False, stop=True),
                     reads=[kM2], writes=[kPB1])
            if full:
                for c in range(4):
                    S.op("pe", lambda e, c=c: e.matmul(PQ1[:, c, :], lhsT=ARbd[:, c, n, 1, :], rhs=Sbf[:, c, :], start=True, stop=False),
                         reads=[kRbd[c], kSbf], writes=[kPB1])
                    S.op("pe", lambda e, c=c: e.matmul(PQ1[:, c, :], lhsT=M1[:, c, 128:256], rhs=Ub[:, c, :], start=False, stop=False),
                         reads=[kM1, kUb], writes=[kPB1])
                    S.op("pe", lambda e, c=c: e.matmul(PQ1[:, c, :], lhsT=M1[:, c, 384:512], rhs=M2[:, c, 256:320], start=False, stop=True),
                         reads=[kM1, kM2], writes=[kPB1])
            pcb = PC[:, :, n:n + 1].to_broadcast([128, 4, 64])
            tS_, ktS = tf.next()
            tmpS = tS_[:].rearrange("p (c v) -> p c v", c=4)
            S.op("dve", lambda e: e.tensor_tensor(out=tmpS, in0=PQ0, in1=S32[:], op=ALU.add), reads=[kPB1, kS32], writes=[ktS])
            S.op("dve", lambda e: e.tensor_tensor(out=Sbf[:], in0=tmpS, in1=pcb, op=ALU.mult), reads=[ktS] + kPC, writes=[kSbf])
            S.op("pool", lambda e: e.tensor_tensor(out=S32[:], in0=tmpS, in1=pcb, op=ALU.mult), reads=[ktS] + kPC, writes=[kS32])
            yield
            if full:
                g_, kg_ = gst.next()
                ys_, kysq = tf.next()
                ysq = ys_[:].rearrange("p (c v) -> p c v", c=4)
                yc_, kycen = tf.next()
                ycen = yc_[:].rearrange("p (c v) -> p c v", c=4)
                S.op("dve", lambda e: e.tensor_reduce(out=g_[:, 0, :], in_=PQ1, axis=AX.X, op=ALU.add), reads=[kPB1], writes=[kg_])
                S.op("act", lambda e: e.activation(out=ysq, in_=PQ1, func=AF.Square), reads=[kPB1], writes=[kysq])
                S.op("dve", lambda e: e.tensor_reduce(out=g_[:, 1, :], in_=ysq, axis=AX.X, op=ALU.add), reads=[kysq], writes=[kg_])
                S.op("dve", lambda e: e.tensor_scalar(out=g_[:, 2, :], in0=g_[:, 0, :], scalar1=1.0 / 64, scalar2=1.0, op0=ALU.mult, op1=ALU.mult), reads=[kg_], writes=[kg_])
                S.op("dve", lambda e: e.tensor_tensor(out=g_[:, 3, :], in0=g_[:, 2, :], in1=g_[:, 2, :], op=ALU.mult), reads=[kg_], writes=[kg_])
                S.op("dve", lambda e: e.scalar_tensor_tensor(out=g_[:, 4, :], in0=g_[:, 1, :], scalar=1.0 / 64, in1=g_[:, 3, :], op0=ALU.mult, op1=ALU.subtract),
                     reads=[kg_], writes=[kg_])
                S.op("act", lambda e: e.activation(out=g_[:, 5, :], in_=g_[:, 4, :], func=AF.Sqrt, bias=GN_EPS), reads=[kg_], writes=[kg_])
                S.op("dve", lambda e: e.reciprocal(out=g_[:, 6, :], in_=g_[:, 5, :]), reads=[kg_], writes=[kg_])
                S.op("dve", lambda e: e.tensor_tensor(out=ycen, in0=PQ1, in1=g_[:, 2, :].unsqueeze(2).to_broadcast([128, 4, 64]), op=ALU.subtract),
                     reads=[kPB1, kg_], writes=[kycen])
                yield
                for h in range(2):
                    hs = slice(h * 64, (h + 1) * 64)
                    S.op("dve", lambda e, hs=hs: e.tensor_tensor(out=ynbd[hs, :, hs], in0=ycen[hs, :, :], in1=g_[hs, 6, :].unsqueeze(2).to_broadcast([64, 4, 64]),
                                                                  op=ALU.mult), reads=[kycen, kg_], writes=[kynbd])
                for c in range(4):
                    S.op("pe", lambda e, c=c: e.matmul(PQ0[:, c, :], lhsT=ynbd[:, c, :], rhs=fst[:], start=True, stop=True), reads=[kynbd, kfst], writes=[kPB1])
                S.op("act", lambda e: e.activation(out=ynf[:, :, n * CH:(n + 1) * CH], in_=PQ0, func=AF.Copy), reads=[kPB1], writes=kynf)
                yield

        def drain(g):
            for _ in g:
                pass

        def interleave(ga, gb, ra=2):
            a_live, b_live = ga is not None, gb is not None
            while a_live or b_live:
                for _ in range(ra):
                    if a_live:
                        try:
                            next(ga)
                        except StopIteration:
                            a_live = False
                if b_live:
                    try:
                        next(gb)
                    except StopIteration:
                        b_live = False

        def interleave_g(ga, gb, ra=2):
            a_live, b_live = ga is not None, gb is not None
            while a_live or b_live:
                for _ in range(ra):
                    if a_live:
                        try:
                            next(ga)
                        except StopIteration:
                            a_live = False
                if b_live:
                    try:
                        next(gb)
                    except StopIteration:
                        b_live = False
                yield

        def c_stage():
            yield from gen_AD(0, 0)
            for n_ in range(NCH):
                gd = gen_AD(n_ + 1, (n_ + 1) % 2) if n_ + 1 < NCH else None
                yield from interleave_g(gd, gen_SQ(n_, n_ % 2))

        def de():
            if not full:
                if t + 2 < n_tiles:
                    load_x(t + 2)
                return
            for c in range(4):
                S.op("pe", lambda e, c=c: e.matmul(psm[:, 0, :], lhsT=g2b0[:, c * 128:(c + 1) * 128], rhs=sgd0[:], start=True, stop=False), reads=[kg2b0, ksgd], writes=[kpsm])
                S.op("pe", lambda e, c=c: e.matmul(psm[:, 0, :], lhsT=g2b1[:, c * 128:(c + 1) * 128], rhs=sgd1[:], start=False, stop=True), reads=[kg2b1, ksgd], writes=[kpsm])
                y1, ky1 = tf.next()
                S.op("dve", lambda e, c=c, y1=y1: e.scalar_tensor_tensor(out=y1[:], in0=ynf[:, c, :], scalar=cc[:, O_LW + c:O_LW + c + 1], in1=bonus[:, c, :], op0=ALU.mult, op1=ALU.add),
                     reads=[kynf[c], kcc, kbonus[c]], writes=[ky1])
                S.op("dve", lambda e, c=c, y1=y1: e.scalar_tensor_tensor(out=yTr[:, c, :], in0=y1[:], scalar=cc[:, O_LB + c:O_LB + c + 1], in1=psm[:, 0, :], op0=ALU.add, op1=ALU.mult),
                     reads=[ky1, kcc, kpsm], writes=[kyTr[c]])
            if DEBUG_STOP < 9:
                return
            for blk in range(NBLK):
                for hf in range(2):
                    pflat = pp[hf][:].rearrange("p a w -> p (a w)")
                    for e_ in range(8):
                        ysrc = yc[:, e_, blk * 128:(blk + 1) * 128] if e_ < 4 else yTr[:, e_ - 4, blk * 128:(blk + 1) * 128]
                        ykey = kyc[e_] if e_ < 4 else kyTr[e_ - 4]
                        S.op("pe", lambda e, e_=e_, hf=hf, pflat=pflat, ysrc=ysrc: e.matmul(pflat, lhsT=ysrc, rhs=wout[:, e_, hf * 512:(hf + 1) * 512],
                                                                                            start=(e_ == 0), stop=(e_ == 7)), reads=[ykey, kwout], writes=[kpp[hf]])
                class _V:
                    def __init__(self, ap): self.ap = ap
                    def __getitem__(self, k): return self.ap
                _post_norm_residual(S, [(_V(pp[0][:].rearrange("p a w -> p (a w)")), kpp[0]), (_V(pp[1][:].rearrange("p a w -> p (a w)")), kpp[1])],
                                    xt[:, blk, :], kxts[b][blk], grow, kgrow, scr)
            ht_i = t - n_pre
            S.op("sp", lambda e, xt=xt, ht_i=ht_i: e.dma_start(out=hscr[W * ht_i:W * (ht_i + 1), :].rearrange("(b p) d -> p b d", p=128), in_=xt[:]),
                 reads=kxts[b], writes=[khscr[ht_i]], dma_sem=dsem[f"h{b}"])
            if t + 2 < n_tiles:
                load_x(t + 2)

        return early, b4_all, c_stage, de

    def drain_g(g):
        for _ in g:
            pass

    def interleave2(ga, gb, ra, rb):
        a_live, b_live = ga is not None, gb is not None
        while a_live or b_live:
            for _ in range(ra):
                if a_live:
                    try:
                        next(ga)
                    except StopIteration:
                        a_live = False
            for _ in range(rb):
                if b_live:
                    try:
                        next(gb)
                    except StopIteration:
                        b_live = False

    parts = [tile_parts(t_i) for t_i in range(n_tiles)]
    drain_g(parts[0][0]())
    parts[0][1]()
    for t_i in range(n_tiles):
        early_next = parts[t_i + 1][0]() if t_i + 1 < n_tiles else None
        interleave2(parts[t_i][2](), early_next, 2, 1)
        parts[t_i][3]()
        if t_i + 1 < n_tiles:
            parts[t_i + 1][1]()


def _host_consts():
    km = np.zeros((128, NKM), np.float32)
    km[:, M_ID:M_ID + 128] = np.eye(128, dtype=np.float32)
    idx = np.arange(128)
    same = (idx[:, None] // 64) == (idx[None, :] // 64)
    s, t = idx[:, None] % 64, idx[None, :] % 64
    km[:, M_SU:M_SU + 128] = (same & (s < t)).astype(np.float32)
    km[:, M_IU:M_IU + 128] = (same & (s <= t)).astype(np.float32)
    km[:, M_SL:M_SL + 128] = (same & (s > t)).astype(np.float32)
    km[:, M_BO:M_BO + 128] = same.astype(np.float32)
    km[:, M_F:M_F + 64] = (idx[:, None] % 64 == np.arange(64)[None, :]).astype(np.float32)
    rst = np.ones((128, 256), np.float32)
    rst[:, ::64] = 0.0
    km[:, M_RST:M_RST + 256] = rst
    return km


def _pack_cc(inp, hmask):
    cc = np.zeros((128, NCC), np.float32)
    col = lambda v, n: np.ascontiguousarray(np.asarray(v, np.float32).reshape(n, 128).T)
    cc[:, O_PMG:O_PMG + 8] = col(inp["pre_mix_g"][0], 8)
    cc[:, O_PFG:O_PFG + 8] = col(inp["pre_ffn_g"][0], 8)
    caw = np.asarray(inp["conv_a_w"][0], np.float32)
    cc[:, O_CAW:O_CAW + 12] = caw.T.reshape(4, 128, 3).transpose(1, 0, 2).reshape(128, 12)
    mu = np.zeros(1920, np.float32)
    mu[:1824] = np.asarray(inp["shift_mu"][0], np.float32)
    cc[:, O_MU:O_MU + 15] = col(mu, 15)
    for off, name in ((O_W0, "w0"), (O_A0, "a0"), (O_KK, "k_k"), (O_KA, "k_a"), (O_LW, "lnx_w"), (O_LB, "lnx_b")):
        cc[:, off:off + 4] = col(inp[name][0], 4)
    cc[:, O_RK:O_RK + 4] = col(np.asarray(inp["r_k"][0], np.float32).reshape(512), 4)
    fcw = np.asarray(inp["ffn_conv_w"][0], np.float32)
    cc[:, O_FCW:O_FCW + 132] = fcw.T.reshape(44, 128, 3).transpose(1, 0, 2).reshape(128, 132)
    cc[:, O_FCB:O_FCB + 44] = col(inp["ffn_conv_b"][0], 44)
    cc[:, O_HM] = hmask
    return cc


_NC_CACHE = {}


def kernel(**inputs):
    n_pre, n_main = 15, 16
    x = np.asarray(inputs["x"], np.float32)
    B, T, _ = x.shape
    half = T // 2
    if "full" not in _NC_CACHE:
        _NC_CACHE["full"] = build(n_pre, n_main, "full")
    nc = _NC_CACHE["full"]
    km = _host_consts()
    f = lambda n: np.ascontiguousarray(np.asarray(inputs[n], np.float32)[0])
    in_maps = []
    for c in range(8):
        b, h = c // 2, c % 2
        xin = np.zeros((T, D), np.float32)
        if h == 0:
            xin[half:] = x[b, :half]
        else:
            xin[:] = x[b]
        in_maps.append({
            "xin": xin, "cc": _pack_cc(inputs, float(h)), "km": km,
            "post_mix_g": f("post_mix_g"), "post_ffn_g": f("post_ffn_g"),
            "w_in": f("w_in"), "w_out": f("w_out"), "w_up": f("w_up"), "w_down": f("w_down"),
            "w2": f("w2"), "a2": f("a2"), "g2": f("g2"),
        })
    res = run_bass_kernel_spmd(nc, in_maps, core_ids=list(range(8)))
    out = np.zeros((B, T, D), np.float32)
    for c in range(8):
        b, h = c // 2, c % 2
        out[b, h * half:(h + 1) * half] = res.results[c]["out"]
    return out
```

```python
import contextlib
import numpy as np
import concourse.bass as bass
import concourse.mybir as mybir
from concourse.bass_utils import run_bass_kernel_spmd

F32 = mybir.dt.float32
BF16 = mybir.dt.bfloat16
AF = mybir.ActivationFunctionType
ALU = mybir.AluOpType
AX = mybir.AxisListType

D = 1024
W = 256
NBLK = 2
CH = 64
NCH = W // CH
INC = 3360
DFF = 2816
NPAIR = 22
QC = 1536
RMS_EPS = 1e-6
GN_EPS = 64 * 1e-5
EPOCH = 12000
DEBUG_SUB = 99
DEBUG_STOP = 99

O_PMG, O_PFG, O_CAW, O_MU, O_W0, O_A0, O_KK, O_KA, O_RK, O_LW, O_LB, O_FCW, O_FCB, O_HM = (
    0, 8, 16, 28, 43, 47, 51, 55, 59, 63, 67, 71, 203, 247)
NCC = 248
M_ID, M_SU, M_IU, M_SL, M_BO, M_F, M_RST = 0, 128, 256, 384, 512, 640, 704
NKM = 704 + 256


class Key:
    __slots__ = ("name", "writer", "readers", "excl")

    def __init__(self, name, excl=False):
        self.name = name
        self.writer = None
        self.readers = []
        self.excl = excl


def PKey(name):
    return Key(name, excl=True)


class Sched:
    ENGS = ("pe", "act", "dve", "pool", "sp")

    def __init__(self, nc, sem_stack, prefix):
        self.nc = nc
        self.sem_stack = sem_stack
        self.prefix = prefix
        self.ops = {e: [] for e in self.ENGS}
        self.count = {e: 0 for e in self.ENGS}
        self.sems = {}
        self.waited = {e: {} for e in self.ENGS}
        self.dma_counts = {}
        self.last_tok = {e: None for e in self.ENGS}

    def _eng_sem(self, eng, idx):
        sid = f"{self.prefix}s_{eng}_{idx // EPOCH}"
        self.sems.setdefault(sid, None)
        return sid, (idx % EPOCH) + 1

    def new_dma_sem(self, name):
        sid = f"{self.prefix}d_{name}"
        assert sid not in self.sems, sid
        self.sems[sid] = None
        self.dma_counts[sid] = 0
        return sid

    def _need_waits(self, eng, tokens):
        w = self.waited[eng]
        best = {}
        for t in tokens:
            if t is None:
                continue
            sid, val, _ = t
            if w.get(sid, 0) >= val:
                continue
            if best.get(sid, 0) < val:
                best[sid] = val
        for sid, val in best.items():
            w[sid] = val
        return list(best.items())

    def op(self, eng, fn, reads=(), writes=(), dma_sem=None):
        toks = []
        raw = set()
        for k in reads:
            toks.append(k.writer)
            if k.writer is not None:
                raw.add(k.writer)
            if k.excl:
                toks.extend(r for r in k.readers if r[2] != eng)
        for k in writes:
            toks.append(k.writer)
            toks.extend(k.readers)
        if eng == "pe":
            toks = [t for t in toks if t is not None and t[2] != "pe"]
        waits = self._need_waits(eng, toks)
        if dma_sem is None:
            idx = self.count[eng]
            self.count[eng] += 1
            sid, val = self._eng_sem(eng, idx)
            tok = (sid, val, eng)
            inc = (sid, 1)
            self.last_tok[eng] = tok
        else:
            self.dma_counts[dma_sem] += 16
            tok = (dma_sem, self.dma_counts[dma_sem], "dma")
            inc = (dma_sem, 16)
        self.ops[eng].append((fn, waits, inc))
        for k in reads:
            k.readers.append(tok)
        for k in writes:
            k.writer = tok
            k.readers = []
        return tok

    def barrier(self, extra_keys=()):
        toks = [t for t in self.last_tok.values() if t is not None]
        for k in extra_keys:
            toks.append(k.writer)
            toks.extend(k.readers)
        for eng in self.ENGS:
            waits = self._need_waits(eng, [t for t in toks if t is not None and t[2] != eng])
            if waits:
                self.ops[eng].append((None, waits, None))

    def final_wait(self, eng, keys):
        toks = []
        for k in keys:
            toks.append(k.writer)
            toks.extend(k.readers)
        waits = self._need_waits(eng, toks)
        self.ops[eng].append((None, waits, None))

    def emit(self):
        nc = self.nc
        with contextlib.ExitStack() as st:
            handles = {sid: self.sem_stack.enter_context(nc.semaphore(sid)) for sid in self.sems}
            block = st.enter_context(nc.Block())

            def run(engobj, lst):
                for fn, waits, inc in lst:
                    for sid, val in waits:
                        engobj.wait_ge(handles[sid], val)
                    if fn is not None:
                        fn(engobj).then_inc(handles[inc[0]], inc[1])

            @block.tensor
            def _(e):
                run(e, self.ops["pe"])

            @block.scalar
            def _(e):
                run(e, self.ops["act"])

            @block.vector
            def _(e):
                run(e, self.ops["dve"])

            @block.gpsimd
            def _(e):
                run(e, self.ops["pool"])

            @block.sync
            def _(e):
                run(e, self.ops["sp"])


class View:
    def __init__(self, ap):
        self.ap = ap

    def __getitem__(self, k):
        return self.ap


class View3:
    def __init__(self, x):
        self.x = x

    def __getitem__(self, k):
        return self.x[k[0], k[1], 128:256]


class Rot:
    def __init__(self, tiles, excl=False, keys=None):
        self.tiles = tiles
        self.keys = keys if keys is not None else [Key(f"rot{i}", excl) for i in range(len(tiles))]
        self.i = 0

    def next(self):
        j = self.i % len(self.tiles)
        self.i += 1
        return self.tiles[j], self.keys[j]


def _rms_transpose(S, src, ksrc, dstT, kdst, tcol, scr, ident, kid, ptr, extra_scale=None, kextra=None):
    st, kst = scr["stat"].next()
    xs, kxs = scr["xs"].next()
    pt, kpt = ptr.next()
    S.op("act", lambda e: e.activation(out=xs[:], in_=src, func=AF.Square, accum_out=st[:, 0:1]),
         reads=[ksrc], writes=[kxs, kst])
    S.op("act", lambda e: e.activation(out=st[:, 1:2], in_=st[:, 0:1], func=AF.Sqrt, scale=1.0 / D, bias=RMS_EPS),
         reads=[kst], writes=[kst])
    S.op("dve", lambda e: e.reciprocal(out=st[:, 2:3], in_=st[:, 1:2]), reads=[kst], writes=[kst])
    rs = st[:, 2:3]
    if extra_scale is not None:
        S.op("dve", lambda e: e.tensor_tensor(out=st[:, 3:4], in0=st[:, 2:3], in1=extra_scale, op=ALU.mult),
             reads=[kst, kextra], writes=[kst])
        rs = st[:, 3:4]
    S.op("pool", lambda e: e.tensor_scalar(out=xs[:], in0=src, scalar1=rs, scalar2=1.0, op0=ALU.mult, op1=ALU.mult),
         reads=[ksrc, kst], writes=[kxs])
    for kc in range(8):
        S.op("pe", lambda e, kc=kc: e.transpose(out=pt[:, kc, :], in_=xs[:, kc * 128:(kc + 1) * 128], identity=ident[:]),
             reads=[kxs, kid], writes=[kpt])
    S.op("act", lambda e: e.activation(out=dstT[:, :, tcol:tcol + 128], in_=pt[:], func=AF.Copy),
         reads=[kpt], writes=[kdst])


def _post_norm_residual(S, pd_pairs, res, kres, grow, kgrow, scr):
    st, kst = scr["stat"].next()
    tmps = [scr["tmp512"].next() for _ in range(2)]
    for hf, (pd, kpd) in enumerate(pd_pairs):
        junk, kjunk = tmps[hf]
        S.op("act", lambda e, pd=pd, hf=hf, junk=junk: e.activation(out=junk[:], in_=pd[:], func=AF.Square,
                                                                  accum_out=st[:, hf:hf + 1]),
             reads=[kpd], writes=[kjunk, kst])
    S.op("dve", lambda e: e.tensor_tensor(out=st[:, 2:3], in0=st[:, 0:1], in1=st[:, 1:2], op=ALU.add), reads=[kst], writes=[kst])
    S.op("act", lambda e: e.activation(out=st[:, 3:4], in_=st[:, 2:3], func=AF.Sqrt, scale=1.0 / D, bias=RMS_EPS),
         reads=[kst], writes=[kst])
    S.op("dve", lambda e: e.reciprocal(out=st[:, 4:5], in_=st[:, 3:4]), reads=[kst], writes=[kst])
    for hf, (pd, kpd) in enumerate(pd_pairs):
        tmp, ktmp = tmps[hf]
        S.op("dve", lambda e, pd=pd, hf=hf, tmp=tmp: e.scalar_tensor_tensor(
            out=tmp[:], in0=pd[:], scalar=st[:, 4:5], in1=grow[:, hf * 512:(hf + 1) * 512], op0=ALU.mult, op1=ALU.mult),
            reads=[kpd, kst, kgrow], writes=[ktmp])
        S.op("pool", lambda e, hf=hf, tmp=tmp: e.tensor_tensor(out=res[:, hf * 512:(hf + 1) * 512], in0=res[:, hf * 512:(hf + 1) * 512],
                                                               in1=tmp[:], op=ALU.add),
             reads=[ktmp, kres], writes=[kres])


def phase2_ffn(nc, S, st, io, n_main, shared):
    sb = lambda n, s, d: st.enter_context(nc.sbuf_tensor(n, s, d))
    ps = lambda n, s, d: st.enter_context(nc.psum_tensor(n, s, d))
    cc, kcc = shared["cc"], shared["kcc"]
    ident, kid = shared["ident"], shared["kid"]
    hscr, khscr = io["hscr"], io["khscr"]
    out = io["out"]

    wup = sb("wup", [128, 8, DFF * 2], BF16)
    wdn = sb("wdn", [128, NPAIR, D], BF16)
    kwup = [Key(f"wup{k}") for k in range(8)]
    kwdn = Key("wdn")
    grow = sb("grow2", [128, D], F32)
    kgrow = Key("grow2")
    fh = sb("fh", [128, NPAIR, 2, 2], F32)
    kfh = [Key(f"fh{i}") for i in range(NPAIR)]
    hts = [sb(f"ht{i}", [128, NBLK, D], F32) for i in range(2)]
    khts = [[Key(f"ht{i}_{b}") for b in range(NBLK)] for i in range(2)]
    hnTs = [sb(f"hnT{i}", [128, 8, W], BF16) for i in range(2)]
    khnT = [Key(f"hnT{i}") for i in range(2)]
    act = sb("actb", [128, NPAIR, W], BF16)
    kact = [Key(f"act{i}") for i in range(NPAIR)]
    scr = {
        "stat": Rot([sb(f"stat{i}", [128, 8], F32) for i in range(4)]),
        "xs": Rot([sb(f"xs{i}", [128, D], BF16) for i in range(2)]),
        "tmp512": Rot([sb(f"tmp512_{i}", [128, 512], F32) for i in range(2)]),
    }
    fbuf = Rot([sb(f"fbuf{i}", [128, 2, W + 2], F32) for i in range(4)])
    cg = Rot([sb(f"cg{i}", [128, W], F32) for i in range(4)])
    cu = Rot([sb(f"cu{i}", [128, W], F32) for i in range(4)])
    t1 = Rot([sb(f"t1_{i}", [128, W], F32) for i in range(3)])
    t2 = Rot([sb(f"t2_{i}", [128, W], F32) for i in range(3)])
    sg = Rot([sb(f"sg{i}", [128, W], F32) for i in range(3)])
    WQ = DFF // 4
    ptr = Rot([ps("ptr2", [128, 8, 128], BF16)], excl=True)
    pf = Rot([ps(f"pf{i}", [128, 2, W], F32) for i in range(3)], excl=True)
    pd = [[ps(f"pd{b}{h}", [128, 512], F32) for h in range(2)] for b in range(NBLK)]
    kpd = [[PKey(f"pd{b}{h}") for h in range(2)] for b in range(NBLK)]
    dsem = {n: S.new_dma_sem("p2_" + n) for n in ("wst0", "wst1", "wst2", "wst3", "wdn", "grow", "h0", "h1", "hw", "out0", "out1")}

    S.op("sp", lambda e: e.dma_start(out=grow[:], in_=io["post_ffn_g"].partition_broadcast(128)), writes=[kgrow], dma_sem=dsem["grow"])
    actf = act[:].rearrange("p i w -> p (i w)").bitcast(F32)
    kstg = [Key(f"stg{q}") for q in range(4)]
    for kc in range(8):
        for q in range(8):
            j = kc * 8 + q
            r_ = j % 4
            wt, kwt = actf[:, r_ * WQ:(r_ + 1) * WQ], kstg[r_]
            S.op("sp", lambda e, wt=wt, kc=kc, q=q: e.dma_start(out=wt, in_=io["w_up"][kc * 128:(kc + 1) * 128, q * WQ:(q + 1) * WQ]),
                 writes=[kwt], dma_sem=dsem[f"wst{r_}"])
            if j % 2 == 0:
                S.op("act", lambda e, wt=wt, kc=kc, q=q: e.activation(out=wup[:, kc, q * WQ:(q + 1) * WQ], in_=wt, func=AF.Copy,
                                                                       scale=cc[:, O_PFG + kc:O_PFG + kc + 1]),
                     reads=[kwt, kcc], writes=[kwup[kc]])
            else:
                S.op("dve", lambda e, wt=wt, kc=kc, q=q: e.tensor_scalar(out=wup[:, kc, q * WQ:(q + 1) * WQ], in0=wt,
                                                                          scalar1=cc[:, O_PFG + kc:O_PFG + kc + 1], scalar2=1.0, op0=ALU.mult, op1=ALU.mult),
                     reads=[kwt, kcc], writes=[kwup[kc]])
    S.op("pool", lambda e: e.memset(act[:, :, 0:1], 0.0), writes=kstg + kact)
    S.op("pool", lambda e: e.dma_start(out=wdn[:], in_=io["w_down"].rearrange("(i p) d -> p i d", p=128)), writes=[kwdn], dma_sem=dsem["wdn"])

    hw = hts[1]
    S.op("sp", lambda e: e.dma_start(out=hw[:, 0, :], in_=hscr[W - 128:W, :]), reads=[khscr[0]], writes=[khts[1][0]], dma_sem=dsem["hw"])
    _rms_transpose(S, hw[:, 0, :], khts[1][0], hnTs[1], khnT[1], 0, scr, ident, kid, ptr,
                   extra_scale=cc[:, O_HM:O_HM + 1], kextra=kcc)
    pfh, kpfh = pf.next()
    pfh_v = pfh[:].rearrange("p a w -> p (a w)")
    for ch in range(2 * NPAIR):
        i, hf = ch % NPAIR, ch // NPAIR
        col = (i * 2 + hf) * 2
        for kc in range(8):
            S.op("pe", lambda e, ch=ch, kc=kc, col=col: e.matmul(pfh_v[:, col:col + 2], lhsT=wup[:, kc, ch * 128:(ch + 1) * 128],
                                                                  rhs=hnTs[1][:, kc, 126:128], start=(kc == 0), stop=(kc == 7)),
                 reads=[kwup[kc], khnT[1]], writes=[kpfh])
    S.op("act", lambda e: e.activation(out=fh[:].rearrange("p i a b -> p (i a b)"), in_=pfh_v[:, 0:NPAIR * 4], func=AF.Copy),
         reads=[kpfh], writes=kfh)

    def load(t):
        b = t % 2
        S.op("sp", lambda e: e.dma_start(out=hts[b][:], in_=hscr[W * (1 + t):W * (2 + t), :].rearrange("(b p) d -> p b d", p=128)),
             reads=[khscr[1 + t]], writes=khts[b], dma_sem=dsem[f"h{b}"])

    def prologue(t):
        b = t % 2
        for blk in range(NBLK):
            _rms_transpose(S, hts[b][:, blk, :], khts[b][blk], hnTs[b], khnT[b], blk * 128, scr, ident, kid, ptr)

    state = {}

    def up_mm(t, i):
        b = t % 2
        p, kp = pf.next()
        state[(t, i)] = (p, kp)
        for hf in range(2):
            ch = hf * NPAIR + i
            for kc in range(8):
                S.op("pe", lambda e, p=p, hf=hf, ch=ch, kc=kc: e.matmul(p[:, hf, :], lhsT=wup[:, kc, ch * 128:(ch + 1) * 128],
                                                                         rhs=hnTs[b][:, kc, :], start=(kc == 0), stop=(kc == 7)),
                     reads=[kwup[kc], khnT[b]], writes=[kp])

    def elem(t, i):
        p, kp = state.pop((t, i))
        fb, kfb = fbuf.next()
        S.op("pool", lambda e: e.tensor_copy(out=fb[:, :, 0:2], in_=fh[:, i, :, :]), reads=[kfh[i]], writes=[kfb])
        S.op("act", lambda e: e.activation(out=fb[:, :, 2:W + 2], in_=p[:], func=AF.Copy), reads=[kp], writes=[kfb])
        yield
        S.op("pool", lambda e: e.tensor_copy(out=fh[:, i, :, :], in_=fb[:, :, W:W + 2]), reads=[kfb], writes=[kfh[i]])
        outs = []
        for hf, rot in ((0, cg), (1, cu)):
            ch = hf * NPAIR + i
            c, kc_ = rot.next()
            wof = O_FCW + ch * 3
            S.op("act", lambda e, c=c, hf=hf, wof=wof, ch=ch: e.activation(out=c[:], in_=fb[:, hf, 2:W + 2], func=AF.Identity,
                                                                          scale=cc[:, wof + 2:wof + 3], bias=cc[:, O_FCB + ch:O_FCB + ch + 1]),
                 reads=[kfb, kcc], writes=[kc_])
            outs.append((c, kc_, hf, wof))
        yield
        for c, kc_, hf, wof in outs:
            S.op("dve", lambda e, c=c, hf=hf, wof=wof: e.scalar_tensor_tensor(out=c[:], in0=fb[:, hf, 1:W + 1], scalar=cc[:, wof + 1:wof + 2],
                                                                             in1=c[:], op0=ALU.mult, op1=ALU.add),
                 reads=[kfb, kcc, kc_], writes=[kc_])
        yield
        for c, kc_, hf, wof in outs:
            S.op("dve", lambda e, c=c, hf=hf, wof=wof: e.scalar_tensor_tensor(out=c[:], in0=fb[:, hf, 0:W], scalar=cc[:, wof:wof + 1],
                                                                             in1=c[:], op0=ALU.mult, op1=ALU.add),
                 reads=[kfb, kcc, kc_], writes=[kc_])
        outs = [(c, kc_) for c, kc_, hf, wof in outs]
        yield
        (g_, kg), (u_, ku) = outs
        a1, ka1 = t1.next()
        a2, ka2 = t2.next()
        s_, ks = sg.next()
        S.op("pool", lambda e: e.tensor_tensor(out=a1[:], in0=g_[:], in1=g_[:], op=ALU.mult), reads=[kg], writes=[ka1])
        yield
        S.op("pool", lambda e: e.tensor_scalar(out=a1[:], in0=a1[:], scalar1=0.044715, scalar2=1.0, op0=ALU.mult, op1=ALU.add),
             reads=[ka1], writes=[ka1])
        S.op("pool", lambda e: e.tensor_tensor(out=a2[:], in0=a1[:], in1=g_[:], op=ALU.mult), reads=[ka1, kg], writes=[ka2])
        S.op("dve", lambda e: e.tensor_tensor(out=a1[:], in0=g_[:], in1=u_[:], op=ALU.mult), reads=[kg, ku, ka2], writes=[ka1])
        yield
        S.op("act", lambda e: e.activation(out=s_[:], in_=a2[:], func=AF.Sigmoid, scale=1.5957691216), reads=[ka2], writes=[ks])
        yield
        S.op("dve", lambda e: e.tensor_tensor(out=act[:, i, :], in0=a1[:], in1=s_[:], op=ALU.mult), reads=[ka1, ks], writes=[kact[i]])

    def down_mm(t, i):
        for blk in range(NBLK):
            for hf in range(2):
                S.op("pe", lambda e, blk=blk, hf=hf: e.matmul(pd[blk][hf][:], lhsT=act[:, i, blk * 128:(blk + 1) * 128],
                                                              rhs=wdn[:, i, hf * 512:(hf + 1) * 512], start=(i == 0), stop=(i == NPAIR - 1)),
                     reads=[kact[i], kwdn], writes=[kpd[blk][hf]])

    def epilogue(t):
        b = t % 2
        for blk in range(NBLK):
            _post_norm_residual(S, [(pd[blk][0], kpd[blk][0]), (pd[blk][1], kpd[blk][1])], hts[b][:, blk, :], khts[b][blk], grow, kgrow, scr)
        S.op("sp", lambda e: e.dma_start(out=out[W * t:W * (t + 1), :].rearrange("(b p) d -> p b d", p=128), in_=hts[b][:]),
             reads=khts[b], dma_sem=dsem[f"out{b}"])

    load(0)
    if n_main > 1:
        load(1)
    prologue(0)
    TSTEP = 2
    for t in range(n_main):
        active = []
        i_next, tick, pro_done = 0, 0, False
        while i_next < NPAIR or active:
            if i_next < NPAIR and tick % TSTEP == 0:
                up_mm(t, i_next)
                active.append((i_next, elem(t, i_next)))
                i_next += 1
                if i_next == 15 and t + 1 < n_main and not pro_done:
                    prologue(t + 1)
                    pro_done = True
            for item in list(active):
                i_, g_ = item
                try:
                    next(g_)
                except StopIteration:
                    active.remove(item)
                    down_mm(t, i_)
            tick += 1
        epilogue(t)
        if t + 2 < n_main:
            load(t + 2)
    S.final_wait("sp", [k for ks_ in khts for k in ks_])


def build(n_pre, n_main, mode="full"):
    nc = bass.Bass("TRN2", target_bir_lowering=False)
    TT = (n_pre + 1 + n_main) * W
    io = {}
    di = lambda n, s: nc.dram_tensor(n, s, F32, kind="ExternalInput").ap()
    io["cc"] = di("cc", [128, NCC])
    io["km"] = di("km", [128, NKM])
    io["post_ffn_g"] = di("post_ffn_g", [D])
    io["w_up"] = di("w_up", [D, 2 * DFF])
    io["w_down"] = di("w_down", [DFF, D])
    if mode == "ffn":
        io["hscr"] = di("hscr", [(1 + n_main) * W, D])
    else:
        io["xin"] = di("xin", [TT, D])
        io["post_mix_g"] = di("post_mix_g", [D])
        io["w_in"] = di("w_in", [D, INC])
        io["w_out"] = di("w_out", [D, D])
        io["w2"] = di("w2", [64, 512])
        io["a2"] = di("a2", [64, 512])
        io["g2"] = di("g2", [160, 512])
        io["hscr"] = nc.dram_tensor("hscr", [(1 + n_main) * W, D], F32, kind="Internal").ap()
    io["khscr"] = [Key(f"hscr{i}") for i in range(1 + n_main)]
    io["out"] = nc.dram_tensor("out", [n_main * W, D], F32, kind="ExternalOutput").ap()

    with contextlib.ExitStack() as sem_stack, contextlib.ExitStack() as st0:
        cc = st0.enter_context(nc.sbuf_tensor("cc_sb", [128, NCC], F32))
        ident = st0.enter_context(nc.sbuf_tensor("ident", [128, 128], BF16))

        def shared_loads(S):
            kcc, kid = Key("cc"), Key("ident")
            d0 = S.new_dma_sem("cc")
            d1 = S.new_dma_sem("ident")
            S.op("sp", lambda e: e.dma_start(out=cc[:], in_=io["cc"][:, :]), writes=[kcc], dma_sem=d0)
            S.op("pool", lambda e: e.dma_start(out=ident[:], in_=io["km"][:, M_ID:M_ID + 128]), writes=[kid], dma_sem=d1)
            return {"cc": cc, "kcc": kcc, "ident": ident, "kid": kid}

        if mode != "ffn":
            S1 = Sched(nc, sem_stack, "a")
            shared = shared_loads(S1)
            with contextlib.ExitStack() as st1:
                phase1_mixer(nc, S1, st1, io, n_pre, n_main, shared)
                S1.final_wait("sp", io["khscr"])
                S1.emit()
            S2 = Sched(nc, sem_stack, "b")
            shared = {"cc": cc, "kcc": Key("cc2"), "ident": ident, "kid": Key("ident2")}
            io["khscr"] = [Key(f"hscr2_{i}") for i in range(1 + n_main)]
        else:
            S2 = Sched(nc, sem_stack, "b")
            shared = shared_loads(S2)
        with contextlib.ExitStack() as st2:
            phase2_ffn(nc, S2, st2, io, n_main, shared)
            S2.emit()
    return nc


def phase1_mixer(nc, S, st, io, n_pre, n_main, shared):
    sb = lambda n, s, d: st.enter_context(nc.sbuf_tensor(n, s, d))
    ps = lambda n, s, d: st.enter_context(nc.psum_tensor(n, s, d))
    cc, kcc = shared["cc"], shared["kcc"]
    ident, kid = shared["ident"], shared["kid"]
    xin, hscr, khscr = io["xin"], io["hscr"], io["khscr"]
    n_tiles = n_pre + 1 + n_main
    C05 = 0.6065306597126334

    win = sb("win", [128, 8, INC], BF16)
    kwin = [Key(f"win{k}") for k in range(8)]
    wout = sb("wout", [128, 8, D], BF16)
    kwout = Key("wout")
    w2b = sb("w2b", [128, 512], BF16)
    a2b = sb("a2b", [128, 512], BF16)
    g2b0 = sb("g2b0", [128, 512], BF16)
    g2b1 = sb("g2b1", [128, 512], BF16)
    wg1 = sb("wg1", [128, 8, 128], BF16)
    kwg1 = Key("wg1")
    ksmallw = Key("smallw")
    msu4 = sb("msu4", [128, 512], BF16)
    msl = sb("msl", [128, 128], BF16)
    bones = sb("bones", [128, 128], BF16)
    fst = sb("fst", [128, 64], BF16)
    rst = sb("rst", [128, W], F32)
    kconst = Key("p1const")
    grow = sb("grow1", [128, D], F32)
    kgrow = Key("grow1")
    dc = sb("dc", [128, 20], F32)
    kdc = Key("dc")
    dsem = {n: S.new_dma_sem("p1_" + n) for n in ("wst0", "wst1", "wst2", "wst3", "const", "grow", "x0", "x1", "h0", "h1")}

    S.op("sp", lambda e: e.dma_start(out=grow[:], in_=io["post_mix_g"].partition_broadcast(128)), writes=[kgrow], dma_sem=dsem["grow"])
    S.op("sp", lambda e: e.dma_start(out=rst[:], in_=io["km"][:, M_RST:M_RST + W]), writes=[kconst], dma_sem=dsem["const"])
    uniq = [0]

    def pool_dma(fn, key):
        uniq[0] += 1
        S.op("pool", fn, writes=[key], dma_sem=S.new_dma_sem(f"p1u{uniq[0]}"))

    kmsl, kbones, kfst = Key("msl"), Key("bones"), Key("fst")
    kmsu4 = [Key(f"msu4_{q}") for q in range(4)]
    kw2b, ka2b, kg2b0, kg2b1 = Key("w2b"), Key("a2b"), Key("g2b0"), Key("g2b1")
    for dst, c0, n_, k_ in ((msl, M_SL, 128, kmsl), (bones, M_BO, 128, kbones), (fst, M_F, 64, kfst)):
        pool_dma(lambda e, dst=dst, c0=c0, n_=n_: e.dma_start(out=dst[:], in_=io["km"][:, c0:c0 + n_]), k_)
    for q, c0 in enumerate((M_SU, M_IU, M_SU, M_IU)):
        pool_dma(lambda e, q=q, c0=c0: e.dma_start(out=msu4[:, q * 128:(q + 1) * 128], in_=io["km"][:, c0:c0 + 128]), kmsu4[q])
    S.op("pool", lambda e: e.memset(w2b[:], 0.0), writes=[kw2b])
    S.op("pool", lambda e: e.memset(a2b[:], 0.0), writes=[ka2b])
    S.op("pool", lambda e: e.memset(g2b1[:], 0.0), writes=[kg2b1])
    S.op("pool", lambda e: e.memset(wg1[:], 0.0), writes=[kwg1])
    pool_dma(lambda e: e.dma_start(out=w2b[0:64, :], in_=io["w2"][:, :]), kw2b)
    pool_dma(lambda e: e.dma_start(out=a2b[64:128, :], in_=io["a2"][:, :]), ka2b)
    pool_dma(lambda e: e.dma_start(out=g2b0[:], in_=io["g2"][0:128, :]), kg2b0)
    pool_dma(lambda e: e.dma_start(out=g2b1[0:32, :], in_=io["g2"][128:160, :]), kg2b1)
    pool_dma(lambda e: e.dma_start(out=wout[:], in_=io["w_out"].rearrange("(k p) d -> p k d", p=128)), kwout)
    S.op("dve", lambda e: e.tensor_scalar(out=dc[:, 0:15], in0=cc[:, O_MU:O_MU + 15], scalar1=-1.0, scalar2=1.0, op0=ALU.mult, op1=ALU.add),
         reads=[kcc], writes=[kdc])
    S.op("dve", lambda e: e.tensor_scalar(out=dc[:, 15:19], in0=cc[:, O_KA:O_KA + 4], scalar1=-1.0, scalar2=1.0, op0=ALU.mult, op1=ALU.add),
         reads=[kcc], writes=[kdc])
    xts = [sb(f"xt{i}", [128, NBLK, D], F32) for i in range(2)]
    kxts = [[Key(f"xt{i}_{b}") for b in range(NBLK)] for i in range(2)]
    xnT = sb("xnT", [128, 8, W], BF16)
    kxnT = Key("xnT")
    scr = {
        "stat": Rot([sb(f"stat1_{i}", [128, 8], F32) for i in range(4)]),
        "xs": Rot([sb(f"xs1_{i}", [128, D], BF16) for i in range(2)]),
    }
    tf = Rot([sb(f"tf{i}", [128, W], F32) for i in range(9)])
    tb = Rot([sb(f"tb{i}", [128, W], BF16) for i in range(4)])
    qbuf = Rot([sb(f"qbuf{i}", [128, W + 1], F32) for i in range(2)])
    ubuf = Rot([sb(f"ubuf{i}", [128, W + 2], F32) for i in range(2)])
    qh = sb("qh", [128, 15], F32)
    kqh = [Key(f"qh{j}") for j in range(15)]
    uh = sb("uh", [128, 4, 2], F32)
    kuh = [Key(f"uh{c}") for c in range(4)]
    rkv = {n: Rot([sb(f"{n}c{i}", [128, W], F32) for i in range(2)]) for n in ("r", "k", "v")}
    wa = sb("wa", [128, W], F32); kwa = Key("wa")
    g0t = sb("g0t", [128, W], F32); kg0 = Key("g0t")
    g1t = sb("g1t", [128, W], F32); kg1 = Key("g1t")
    twad = sb("twad", [128, W], BF16); ktwad = Key("twad")
    sgds = [(sb(f"sgd0_{i}", [128, W], BF16), sb(f"sgd1_{i}", [128, W], BF16), Key(f"sgd{i}")) for i in range(2)]
    sig = sb("sig", [128, 4, W], F32); ksig = [Key(f"sig{c}") for c in range(4)]
    a4 = sb("a4", [128, 4, W], F32); ka4 = [Key(f"a4{c}") for c in range(4)]
    bonus = sb("bonus", [128, 4, W], F32); kbonus = [Key(f"bonus{c}") for c in range(4)]
    ARbd = sb("ARbd", [128, 4, NCH, 2, 128], BF16); kAbd = [Key(f"Abd{c}") for c in range(4)]; kRbd = [Key(f"Rbd{c}") for c in range(4)]
    Bbd = sb("Bbd", [128, 4, NCH, 128], BF16); kBbd = [Key(f"Bbd{c}") for c in range(4)]
    Kbd = sb("Kbd", [128, 4, NCH, 128], BF16); kKbd = [Key(f"Kbd{c}") for c in range(4)]
    Vbd = sb("Vbd", [128, 4, NCH, 128], BF16); kVbd = [Key(f"Vbd{c}") for c in range(4)]
    PC = sb("PC", [128, 4, NCH], F32); kPC = [Key(f"PC{c}") for c in range(4)]
    M1s = [(sb(f"M1_{i}", [128, 4, 512], BF16), Key(f"M1_{i}")) for i in range(2)]
    NT0s = [(sb(f"NT0_{i}", [128, 4, 128], BF16), Key(f"NT0_{i}")) for i in range(2)]
    M2s = [(sb(f"M2_{i}", [128, 4, 320], BF16), Key(f"M2_{i}")) for i in range(2)]
    NNs = [[(sb(f"NN{i}{j}", [128, 4, 256], BF16), Key(f"NN{i}{j}")) for j in range(2)] for i in range(2)]
    Tbs = [[(sb(f"Tb{i}{j}", [128, 4, 128], BF16), Key(f"Tb{i}{j}")) for j in range(2)] for i in range(2)]
    Zb = sb("Zb", [128, 4, 64], BF16); kZb = Key("Zb")
    Ub = sb("Ub", [128, 4, 64], BF16); kUb = Key("Ub")
    S32 = sb("S32", [128, 4, 64], F32); kS32 = Key("S32")
    Sbf = sb("Sbf", [128, 4, 64], BF16); kSbf = Key("Sbf")
    gst = Rot([sb(f"gst{i}", [128, 8, 4], F32) for i in range(2)])
    ynbd = sb("ynbd", [128, 4, 128], BF16); kynbd = Key("ynbd")
    ynf = sb("ynf", [128, 4, W], F32); _ka, _kb = Key("ynfA"), Key("ynfB"); kynf = [_ka, _ka, _kb, _kb]
    scr["tmp512"] = Rot([View(ynf[:, 0:2, :].rearrange("p c w -> p (c w)")), View(ynf[:, 2:4, :].rearrange("p c w -> p (c w)"))], keys=[_ka, _kb])
    yTc = [sb(f"yTc{i}", [128, 4, W], BF16) for i in range(2)]; kyTc = [[Key(f"yTc{i}_{c}") for c in range(4)] for i in range(2)]
    yTr = sb("yTr", [128, 4, W], BF16); kyTr = [Key(f"yTr{c}") for c in range(4)]

    WQ = INC // 4
    stg = [(sig, ksig), (a4, ka4), (bonus, kbonus), (ynf, kynf)]
    for kc in range(8):
        for q in range(4):
            j = kc * 4 + q
            buf, kbuf = stg[j % 4]
            wt = buf[:].rearrange("p c w -> p (c w)")[:, 0:WQ]
            S.op("sp", lambda e, wt=wt, kc=kc, q=q: e.dma_start(out=wt, in_=io["w_in"][kc * 128:(kc + 1) * 128, q * WQ:(q + 1) * WQ]),
                 writes=kbuf, dma_sem=dsem[f"wst{j % 4}"])
            if j % 2 == 0:
                S.op("act", lambda e, wt=wt, kc=kc, q=q: e.activation(out=win[:, kc, q * WQ:(q + 1) * WQ], in_=wt, func=AF.Copy,
                                                                       scale=cc[:, O_PMG + kc:O_PMG + kc + 1]),
                     reads=kbuf + [kcc], writes=[kwin[kc]])
            else:
                S.op("dve", lambda e, wt=wt, kc=kc, q=q: e.tensor_scalar(out=win[:, kc, q * WQ:(q + 1) * WQ], in0=wt,
                                                                          scalar1=cc[:, O_PMG + kc:O_PMG + kc + 1], scalar2=1.0, op0=ALU.mult, op1=ALU.mult),
                     reads=kbuf + [kcc], writes=[kwin[kc]])
    S.op("dve", lambda e: e.tensor_copy(out=wg1[:, :, 0:32], in_=win[:, :, INC - 32:INC]), reads=kwin + [kwg1], writes=[kwg1])

    ptr = Rot([ps("ptr1", [128, 8, 128], BF16)], excl=True)
    pp = [ps(f"pp{i}", [128, 2, W], F32) for i in range(2)]
    kpp = [PKey(f"pp{i}") for i in range(2)]
    psm = ps("psm", [128, 2, W], F32); kpsm = PKey("psm")
    PA = ps("PA", [128, 1024], F32); kPA = PKey("PA")
    PB = ps("PB", [128, 1024], F32); kPB0 = PKey("PB0"); kPB1 = PKey("PB1")

    for t_, k_ in ((qh, kqh), (uh, kuh)):
        S.op("pool", lambda e, t_=t_: e.memset(t_[:], 0.0), writes=k_)
    S.op("pool", lambda e: e.memset(S32[:], 0.0), writes=[kS32])
    S.op("pool", lambda e: e.memset(Sbf[:], 0.0), writes=[kSbf])
    S.op("pool", lambda e: e.memset(ARbd[:], 0.0), writes=kAbd + kRbd)
    S.op("pool", lambda e: e.memset(Bbd[:], 0.0), writes=kBbd)
    S.op("pool", lambda e: e.memset(Kbd[:], 0.0), writes=kKbd)
    S.op("pool", lambda e: e.memset(Vbd[:], 0.0), writes=kVbd)
    S.op("pool", lambda e: e.memset(ynbd[:], 0.0), writes=[kynbd])
    S.op("pool", lambda e: e.memset(g1t[:], 0.0), writes=[kg1])

    slot_i = [0]

    def proj(col0, ncols, wsrc=None, kw=None):
        j = slot_i[0] % 4
        slot_i[0] += 1
        t_, half, key = pp[j // 2], j % 2, kpp[j // 2]
        for kc in range(8):
            lhsT = win[:, kc, col0:col0 + ncols] if wsrc is None else wsrc[:, kc, :]
            S.op("pe", lambda e, kc=kc, lhsT=lhsT: e.matmul(t_[0:ncols, half, :], lhsT=lhsT, rhs=xnT[:, kc, :],
                                                            start=(kc == 0), stop=(kc == 7)),
                 reads=[(kwin if kw is None else kw)[kc], kxnT], writes=[key])
        return t_[0:ncols, half, :], key

    def shift_lerp(p_ap, kp, jj, dst_ap, kdst, np_=128):
        qb, kqb = qbuf.next()
        S.op("pool", lambda e: e.tensor_copy(out=qb[0:np_, 0:1], in_=qh[0:np_, jj:jj + 1]), reads=[kqh[jj]], writes=[kqb])
        S.op("act", lambda e: e.activation(out=qb[0:np_, 1:W + 1], in_=p_ap, func=AF.Copy), reads=[kp], writes=[kqb])
        S.op("pool", lambda e: e.tensor_copy(out=qh[0:np_, jj:jj + 1], in_=qb[0:np_, W:W + 1]), reads=[kqb], writes=[kqh[jj]])
        tmp, ktmp = tf.next()
        S.op("pool", lambda e: e.tensor_scalar(out=tmp[0:np_, :], in0=qb[0:np_, 1:W + 1], scalar1=dc[0:np_, jj:jj + 1], scalar2=1.0, op0=ALU.mult, op1=ALU.mult),
             reads=[kqb, kdc], writes=[ktmp])
        S.op("dve", lambda e: e.scalar_tensor_tensor(out=dst_ap, in0=qb[0:np_, 0:W], scalar=cc[0:np_, O_MU + jj:O_MU + jj + 1], in1=tmp[0:np_, :],
                                                     op0=ALU.mult, op1=ALU.add),
             reads=[kqb, kcc, ktmp], writes=[kdst])

    def bd_view(t4, c, h):
        return t4[h * 64:(h + 1) * 64, c, :, h * 64:(h + 1) * 64]

    def v3(t2, h):
        return t2[h * 64:(h + 1) * 64, :].rearrange("p (n t) -> p n t", n=NCH)

    def load_x(t):
        b = t % 2
        S.op("sp", lambda e: e.dma_start(out=xts[b][:], in_=xin[W * t:W * (t + 1), :].rearrange("(b p) d -> p b d", p=128)),
             writes=kxts[b], dma_sem=dsem[f"x{b}"])

    load_x(0)
    if n_tiles > 1:
        load_x(1)

    def tile_parts(t):
        full = t >= n_pre
        b = t % 2
        par = t % 2
        xt = xts[b]
        sgd0, sgd1, ksgd = sgds[par]
        yc, kyc = yTc[par], kyTc[par]

        def early():
            for blk in range(NBLK):
                _rms_transpose(S, xt[:, blk, :], kxts[b][blk], xnT, kxnT, blk * 128, scr, ident, kid, ptr)

            yield
            if full:
                def b1(c):
                    pB, kpB = proj(c * 128, 128)
                    pH, kpH = proj(1024 + c * 128, 128)
                    hAs, khAs = tf.next()
                    S.op("act", lambda e, hAs=hAs, pH=pH: e.activation(out=hAs[:], in_=pH, func=AF.Copy), reads=[kpH], writes=[khAs])
                    ub, kub = ubuf.next()
                    S.op("pool", lambda e, ub=ub, c=c: e.tensor_copy(out=ub[:, 0:2], in_=uh[:, c, :]), reads=[kuh[c]], writes=[kub])
                    S.op("dve", lambda e, ub=ub, pB=pB, hAs=hAs: e.tensor_tensor(out=ub[:, 2:W + 2], in0=pB, in1=hAs[:], op=ALU.mult),
                         reads=[kpB, khAs], writes=[kub])
                    S.op("pool", lambda e, ub=ub, c=c: e.tensor_copy(out=uh[:, c, :], in_=ub[:, W:W + 2]), reads=[kub], writes=[kuh[c]])
                    ta, kta = tf.next()
                    wof = O_CAW + c * 3
                    S.op("act", lambda e, ta=ta, ub=ub, wof=wof: e.activation(out=ta[:], in_=ub[:, 2:W + 2], func=AF.Copy, scale=cc[:, wof + 2:wof + 3]),
                         reads=[kub, kcc], writes=[kta])
                    S.op("dve", lambda e, ta=ta, ub=ub, wof=wof: e.scalar_tensor_tensor(out=ta[:], in0=ub[:, 1:W + 1], scalar=cc[:, wof + 1:wof + 2], in1=ta[:],
                                                                                      op0=ALU.mult, op1=ALU.add), reads=[kub, kcc, kta], writes=[kta])
                    S.op("dve", lambda e, ta=ta, ub=ub, wof=wof: e.scalar_tensor_tensor(out=ta[:], in0=ub[:, 0:W], scalar=cc[:, wof:wof + 1], in1=ta[:],
                                                                                      op0=ALU.mult, op1=ALU.add), reads=[kub, kcc, kta], writes=[kta])
                    pC, kpC = proj(512 + c * 128, 128)
                    S.op("dve", lambda e, ta=ta, pC=pC, c=c: e.tensor_tensor(out=yc[:, c, :], in0=pC, in1=ta[:], op=ALU.mult),
                         reads=[kpC, kta], writes=[kyc[c]])
                for c_ in range(4):
                    b1(c_)

            yield
            QC = 1536
            p_, kp_ = proj(QC + 12 * 128, 128)
            shift_lerp(p_, kp_, 12, wa[:], kwa)
            S.op("act", lambda e: e.activation(out=twad[0:64, :], in_=wa[0:64, :], func=AF.Tanh), reads=[kwa], writes=[ktwad])
            S.op("dve", lambda e: e.tensor_copy(out=twad[64:128, :], in_=wa[64:128, :]), reads=[kwa], writes=[ktwad])
            if full:
                p_, kp_ = proj(QC + 13 * 128, 128)
                shift_lerp(p_, kp_, 13, g0t[:], kg0)
                p_, kp_ = proj(None, 128, wsrc=wg1, kw=[kwg1] * 8)
                shift_lerp(p_[0:32, :], kp_, 14, g1t[0:32, :], kg1, np_=32)
                S.op("act", lambda e: e.activation(out=sgd0[:], in_=g0t[:], func=AF.Sigmoid), reads=[kg0], writes=[ksgd])
                S.op("act", lambda e: e.activation(out=sgd1[:], in_=g1t[:], func=AF.Sigmoid), reads=[kg1], writes=[ksgd])
            yield
            for c in range(4):
                S.op("pe", lambda e, c=c: e.matmul(psm[:, 0, :], lhsT=w2b[:, c * 128:(c + 1) * 128], rhs=twad[:], start=True, stop=True),
                     reads=[kw2b, ktwad], writes=[kpsm])
                S.op("pe", lambda e, c=c: e.matmul(psm[:, 1, :], lhsT=a2b[:, c * 128:(c + 1) * 128], rhs=twad[:], start=True, stop=True),
                     reads=[ka2b, ktwad], writes=[kpsm])
                S.op("act", lambda e, c=c: e.activation(out=sig[:, c, :], in_=psm[:, 0, :], func=AF.Sigmoid, bias=cc[:, O_W0 + c:O_W0 + c + 1]),
                     reads=[kpsm, kcc], writes=[ksig[c]])
                S.op("act", lambda e, c=c: e.activation(out=a4[:, c, :], in_=psm[:, 1, :], func=AF.Sigmoid, bias=cc[:, O_A0 + c:O_A0 + c + 1]),
                     reads=[kpsm, kcc], writes=[ka4[c]])

            yield

        def b4(c):
            kt_, kkt = rkv["k"].next()
            vt_, kvt = rkv["v"].next()
            p_, kp_ = proj(QC + (4 + c) * 128, 128)
            shift_lerp(p_, kp_, 4 + c, kt_[:], kkt)
            yield
            p_, kp_ = proj(QC + (8 + c) * 128, 128)
            shift_lerp(p_, kp_, 8 + c, vt_[:], kvt)
            yield
            if full:
                rt_, krt = rkv["r"].next()
                p_, kp_ = proj(QC + c * 128, 128)
                shift_lerp(p_, kp_, c, rt_[:], krt)
                yield
            kkr, kkkr = tf.next()
            S.op("pool", lambda e, kkr=kkr, kt_=kt_, c=c: e.tensor_scalar(out=kkr[:], in0=kt_[:], scalar1=cc[:, O_KK + c:O_KK + c + 1], scalar2=1.0, op0=ALU.mult, op1=ALU.mult),
                 reads=[kkt, kcc], writes=[kkkr])
            sq, ksq = tb.next()
            S.op("act", lambda e, sq=sq, kkr=kkr: e.activation(out=sq[:], in_=kkr[:], func=AF.Square), reads=[kkkr], writes=[ksq])
            S.op("pe", lambda e, sq=sq: e.matmul(psm[:, 0, :], lhsT=bones[:], rhs=sq[:], start=True, stop=True), reads=[kbones, ksq], writes=[kpsm])
            yield
            cs, kcs = tf.next()
            S.op("dve", lambda e, cs=cs, c=c: e.tensor_tensor_scan(out=cs[:], data0=rst[:], data1=sig[:, c, :], initial=0.0, op0=ALU.mult, op1=ALU.add),
                 reads=[kconst, ksig[c]], writes=[kcs])
            yield
            E1, kE1 = tf.next()
            E2, kE2 = tf.next()
            E3, kE3 = tf.next()
            dd, kdd = tf.next()
            S.op("act", lambda e, E1=E1, cs=cs: e.activation(out=E1[:], in_=cs[:], func=AF.Exp, scale=-C05), reads=[kcs], writes=[kE1])
            S.op("act", lambda e, E2=E2, cs=cs: e.activation(out=E2[:], in_=cs[:], func=AF.Exp, scale=C05), reads=[kcs], writes=[kE2])
            S.op("pool", lambda e, dd=dd, cs=cs, c=c: e.tensor_tensor(out=dd[:], in0=cs[:], in1=sig[:, c, :], op=ALU.subtract), reads=[kcs, ksig[c]], writes=[kdd])
            S.op("act", lambda e, E3=E3, dd=dd: e.activation(out=E3[:], in_=dd[:], func=AF.Exp, scale=-C05), reads=[kdd], writes=[kE3])
            S.op("pool", lambda e, E1=E1, c=c: e.tensor_copy(out=PC[:, c, :], in_=E1[:].rearrange("p (n t) -> p n t", n=NCH)[:, :, CH - 1]),
                 reads=[kE1], writes=[kPC[c]])
            yield
            mm, kmm = tf.next()
            S.op("pool", lambda e, mm=mm, c=c: e.tensor_scalar(out=mm[:], in0=a4[:, c, :], scalar1=cc[:, O_KA + c:O_KA + c + 1], scalar2=dc[:, 15 + c:16 + c],
                                                               op0=ALU.mult, op1=ALU.add), reads=[ka4[c], kcc, kdc], writes=[kmm])
            kp, kkp = tf.next()
            S.op("dve", lambda e, kp=kp, kt_=kt_, mm=mm: e.tensor_tensor(out=kp[:], in0=kt_[:], in1=mm[:], op=ALU.mult), reads=[kkt, kmm], writes=[kkp])
            nrm, knrm = tf.next()
            S.op("act", lambda e, nrm=nrm: e.activation(out=nrm[:], in_=psm[:, 0, :], func=AF.Sqrt, bias=1e-24), reads=[kpsm], writes=[knrm])
            S.op("dve", lambda e, nrm=nrm: e.reciprocal(out=nrm[:], in_=nrm[:]), reads=[knrm], writes=[knrm])
            kk, kkk = tf.next()
            S.op("dve", lambda e, kk=kk, kkr=kkr, nrm=nrm: e.tensor_tensor(out=kk[:], in0=kkr[:], in1=nrm[:], op=ALU.mult), reads=[kkkr, knrm], writes=[kkk])
            akk, kakk = tf.next()
            S.op("pool", lambda e, akk=akk, kk=kk, c=c: e.tensor_tensor(out=akk[:], in0=a4[:, c, :], in1=kk[:], op=ALU.mult), reads=[ka4[c], kkk], writes=[kakk])
            yield
            for h in range(2):
                S.op("dve", lambda e, h=h, kk=kk, E3=E3, c=c: e.scalar_tensor_tensor(out=ARbd[h * 64:(h + 1) * 64, c, :, 0, h * 64:(h + 1) * 64], in0=v3(kk, h), scalar=-1.0,
                                                                                   in1=v3(E3, h), op0=ALU.mult, op1=ALU.mult),
                     reads=[kkk, kE3], writes=[kAbd[c]])
                S.op("dve", lambda e, h=h, akk=akk, E2=E2, c=c: e.tensor_tensor(out=bd_view(Bbd, c, h), in0=v3(akk, h), in1=v3(E2, h), op=ALU.mult),
                     reads=[kakk, kE2], writes=[kBbd[c]])
                S.op("pool", lambda e, h=h, kp=kp, E2=E2, c=c: e.tensor_tensor(out=bd_view(Kbd, c, h), in0=v3(kp, h), in1=v3(E2, h), op=ALU.mult),
                     reads=[kkp, kE2], writes=[kKbd[c]])
                S.op("act", lambda e, h=h, vt_=vt_, c=c: e.activation(out=bd_view(Vbd, c, h), in_=v3(vt_, h), func=AF.Copy), reads=[kvt], writes=[kVbd[c]])
                if full:
                    S.op("pool", lambda e, h=h, rt_=rt_, E1=E1, c=c: e.tensor_tensor(out=ARbd[h * 64:(h + 1) * 64, c, :, 1, h * 64:(h + 1) * 64], in0=v3(rt_, h), in1=v3(E1, h),
                                                                                   op=ALU.mult), reads=[krt, kE1], writes=[kRbd[c]])
            yield
            if full:
                rk, krk = tf.next()
                S.op("pool", lambda e, rk=rk, rt_=rt_, kp=kp: e.tensor_tensor(out=rk[:], in0=rt_[:], in1=kp[:], op=ALU.mult), reads=[krt, kkp], writes=[krk])
                rkb, krkb = tb.next()
                S.op("dve", lambda e, rkb=rkb, rk=rk, c=c: e.tensor_scalar(out=rkb[:], in0=rk[:], scalar1=cc[:, O_RK + c:O_RK + c + 1], scalar2=1.0, op0=ALU.mult, op1=ALU.mult),
                     reads=[krk, kcc], writes=[krkb])
                S.op("pe", lambda e, rkb=rkb: e.matmul(psm[:, 1, :], lhsT=bones[:], rhs=rkb[:], start=True, stop=True), reads=[kbones, krkb], writes=[kpsm])
                S.op("dve", lambda e, vt_=vt_, c=c: e.tensor_tensor(out=bonus[:, c, :], in0=psm[:, 1, :], in1=vt_[:], op=ALU.mult), reads=[kpsm, kvt], writes=[kbonus[c]])

        b4gens = {}

        def b4_start(c):
            g = b4(c)
            for _ in range(3 if full else 2):
                next(g)
            b4gens[c] = g

        def b4_pre():
            b4_start(0)
            yield

        def b4_all():
            if 0 not in b4gens:
                b4_start(0)
            for c_ in range(4):
                if c_ + 1 < 4:
                    b4_start(c_ + 1)
                drain_g(b4gens.pop(c_))
            if DEBUG_STOP < 5:
                return


        PA3 = PA[:].rearrange("p (a w) -> p a w", a=2)
        PAd = PA[:].rearrange("p (a w) -> p a w", a=4)
        PBt = PB[:, 0:512].rearrange("p (a w) -> p a w", a=4)
        PQ0 = PB[:, 512:768].rearrange("p (a w) -> p a w", a=4)
        PQ1 = PB[:, 768:1024].rearrange("p (a w) -> p a w", a=4)
        Tfinal = {}

        def gen_AD(n, s_):
            M1, kM1 = M1s[s_]
            NT0, kNT0 = NT0s[s_]
            M2, kM2 = M2s[s_]
            for pi in range(2):
                for i in range(2):
                    c = pi * 2 + i
                    rhsAR = ARbd[:, c, n, :, :].rearrange("p a w -> p (a w)")
                    S.op("pe", lambda e, i=i, c=c, rhsAR=rhsAR: e.matmul(PA3[:, i, 0:256], lhsT=Bbd[:, c, n, :], rhs=rhsAR, start=True, stop=True),
                         reads=[kBbd[c], kAbd[c], kRbd[c]], writes=[kPA])
                    S.op("pe", lambda e, i=i, c=c, rhsAR=rhsAR: e.matmul(PA3[:, i, 256:512], lhsT=Kbd[:, c, n, :], rhs=rhsAR, start=True, stop=True),
                         reads=[kKbd[c], kAbd[c], kRbd[c]], writes=[kPA])
                S.op("dve", lambda e, pi=pi: e.tensor_tensor(out=M1[:, 2 * pi:2 * pi + 2, :], in0=PA3, in1=msu4[:].unsqueeze(1).to_broadcast([128, 2, 512]), op=ALU.mult),
                     reads=[kPA] + kmsu4, writes=[kM1])
                yield
                for i in range(2):
                    c = pi * 2 + i
                    S.op("pe", lambda e, i=i, c=c: e.matmul(PA3[:, i, 0:128], lhsT=ARbd[:, c, n, 0, :], rhs=Bbd[:, c, n, :], start=True, stop=True),
                         reads=[kAbd[c], kBbd[c]], writes=[kPA])
                    S.op("pe", lambda e, i=i, c=c: e.matmul(PA3[:, i, 128:256], lhsT=Bbd[:, c, n, :], rhs=ident[:], start=True, stop=True),
                         reads=[kBbd[c], kid], writes=[kPA])
                    S.op("pe", lambda e, i=i, c=c: e.matmul(PA3[:, i, 256:384], lhsT=Kbd[:, c, n, :], rhs=ident[:], start=True, stop=True),
                         reads=[kKbd[c], kid], writes=[kPA])
                    S.op("pe", lambda e, i=i, c=c: e.matmul(PA3[:, i, 384:448], lhsT=Vbd[:, c, n, :], rhs=fst[:], start=True, stop=True),
                         reads=[kVbd[c], kfst], writes=[kPA])
                S.op("dve", lambda e, pi=pi: e.tensor_tensor(out=NT0[:, 2 * pi:2 * pi + 2, :], in0=PA3[:, :, 0:128], in1=msl[:].unsqueeze(1).to_broadcast([128, 2, 128]), op=ALU.mult),
                     reads=[kPA, kmsl], writes=[kNT0])
                S.op("act", lambda e, pi=pi: e.activation(out=M2[:, 2 * pi:2 * pi + 2, :], in_=PA3[:, :, 128:448], func=AF.Copy), reads=[kPA], writes=[kM2])
                yield
            Tcur, kTcur = Tbs[s_][0]
            S.op("pool", lambda e, Tcur=Tcur: e.tensor_tensor(out=Tcur[:], in0=M1[:, :, 0:128], in1=ident[:].unsqueeze(1).to_broadcast([128, 4, 128]), op=ALU.add),
                 reads=[kM1, kid], writes=[kTcur])
            Nprev = lambda c: M1[:, c, 0:128]
            NTprev = lambda c: NT0[:, c, :]
            kprev = [kM1, kNT0]
            for j in range(1, 6):
                NNj, kNNj = NNs[s_][j % 2]
                for c in range(4):
                    if j < 5:
                        S.op("pe", lambda e, c=c, Nprev=Nprev, NTprev=NTprev: e.matmul(PAd[:, c, 0:128], lhsT=NTprev(c), rhs=Nprev(c), start=True, stop=True),
                             reads=kprev, writes=[kPA])
                    S.op("pe", lambda e, c=c, Nprev=Nprev, NTprev=NTprev: e.matmul(PAd[:, c, 128:256], lhsT=Nprev(c), rhs=NTprev(c), start=True, stop=True),
                         reads=kprev, writes=[kPA])
                if j < 5:
                    S.op("act", lambda e, NNj=NNj: e.activation(out=NNj[:], in_=PAd, func=AF.Copy), reads=[kPA], writes=[kNNj])
                else:
                    S.op("act", lambda e, NNj=NNj: e.activation(out=NNj[:, :, 128:256], in_=PAd[:, :, 128:256], func=AF.Copy), reads=[kPA], writes=[kNNj])
                yield
                for c in range(4):
                    S.op("pe", lambda e, c=c, NNj=NNj, Tcur=Tcur: e.matmul(PBt[:, c, :], lhsT=NNj[:, c, 128:256], rhs=Tcur[:, c, :], start=True, stop=True),
                         reads=[kNNj, kTcur], writes=[kPB0])
                Tnew, kTnew = Tbs[s_][j % 2]
                S.op("dve", lambda e, Tnew=Tnew, Tcur=Tcur: e.tensor_tensor(out=Tnew[:], in0=PBt, in1=Tcur[:], op=ALU.add), reads=[kPB0, kTcur], writes=[kTnew])
                Tcur, kTcur = Tnew, kTnew
                Nprev = (lambda NNj: (lambda c: NNj[:, c, 0:128]))(NNj)
                NTprev = (lambda NNj: (lambda c: NNj[:, c, 128:256]))(NNj)
                kprev = [kNNj]
                yield
            Tfinal[n] = (Tcur, kTcur)

        def gen_SQ(n, s_):
            M1, kM1 = M1s[s_]
            M2, kM2 = M2s[s_]
            Tcur, kTcur = Tfinal[n]
            for c in range(4):
                S.op("pe", lambda e, c=c: e.matmul(PQ0[:, c, :], lhsT=ARbd[:, c, n, 0, :], rhs=Sbf[:, c, :], start=True, stop=False),
                     reads=[kAbd[c], kSbf], writes=[kPB1])
                S.op("pe", lambda e, c=c: e.matmul(PQ0[:, c, :], lhsT=M1[:, c, 256:384], rhs=M2[:, c, 256:320], start=False, stop=True),
                     reads=[kM1, kM2], writes=[kPB1])
            S.op("act", lambda e: e.activation(out=Zb[:], in_=PQ0, func=AF.Copy), reads=[kPB1], writes=[kZb])
            yield
            for c in range(4):
                S.op("pe", lambda e, c=c: e.matmul(PQ1[:, c, :], lhsT=Tcur[:, c, :], rhs=Zb[:, c, :], start=True, stop=True),
                     reads=[kTcur, kZb], writes=[kPB1])
            S.op("act", lambda e: e.activation(out=Ub[:], in_=PQ1, func=AF.Copy), reads=[kPB1], writes=[kUb])
            yield
            for c in range(4):
                S.op("pe", lambda e, c=c: e.matmul(PQ0[:, c, :], lhsT=M2[:, c, 0:128], rhs=Ub[:, c, :], start=True, stop=False),
                     reads=[kM2, kUb], writes=[kPB1])
                S.op("pe", lambda e, c=c: e.matmul(PQ0[:, c, :], lhsT=M2[:, c, 128:256], rhs=M2[:, c, 256:320], start=False, stop=True),
                     reads=[kM2], writes=[kPB1])
            if full:
                for c in range(4):
                    S.op("pe", lambda e, c=c: e.matmul(PQ1[:, c, :], lhsT=ARbd[:, c, n, 1, :], rhs=Sbf[:, c, :], start=True, stop=False),
                         reads=[kRbd[c], kSbf], writes=[kPB1])
                    S.op("pe", lambda e, c=c: e.matmul(PQ1[:, c, :], lhsT=M1[:, c, 128:256], rhs=Ub[:, c, :], start=False, stop=False),
                         reads=[kM1, kUb], writes=[kPB1])
                    S.op("pe", lambda e, c=c: e.matmul(PQ1[:, c, :], lhsT=M1[:, c, 384:512], rhs=M2[:, c, 256:320], start=False, stop=True),
                         reads=[kM1, kM2], writes=[kPB1])
            pcb = PC[:, :, n:n + 1].to_broadcast([128, 4, 64])
            tS_, ktS = tf.next()
            tmpS = tS_[:].rearrange("p (c v) -> p c v", c=4)
            S.op("dve", lambda e: e.tensor_tensor(out=tmpS, in0=PQ0, in1=S32[:], op=ALU.add), reads=[kPB1, kS32], writes=[ktS])
            S.op("dve", lambda e: e.tensor_tensor(out=Sbf[:], in0=tmpS, in1=pcb, op=ALU.mult), reads=[ktS] + kPC, writes=[kSbf])
            S.op("pool", lambda e: e.tensor_tensor(out=S32[:], in0=tmpS, in1=pcb, op=ALU.mult), reads=[ktS] + kPC, writes=[kS32])
            yield
            if full:
                g_, kg_ = gst.next()
                ys_, kysq = tf.next()
                ysq = ys_[:].rearrange("p (c v) -> p c v", c=4)
                yc_, kycen = tf.next()
                ycen = yc_[:].rearrange("p (c v) -> p c v", c=4)
                S.op("dve", lambda e: e.tensor_reduce(out=g_[:, 0, :], in_=PQ1, axis=AX.X, op=ALU.add), reads=[kPB1], writes=[kg_])
                S.op("act", lambda e: e.activation(out=ysq, in_=PQ1, func=AF.Square), reads=[kPB1], writes=[kysq])
                S.op("dve", lambda e: e.tensor_reduce(out=g_[:, 1, :], in_=ysq, axis=AX.X, op=ALU.add), reads=[kysq], writes=[kg_])
                S.op("dve", lambda e: e.tensor_scalar(out=g_[:, 2, :], in0=g_[:, 0, :], scalar1=1.0 / 64, scalar2=1.0, op0=ALU.mult, op1=ALU.mult), reads=[kg_], writes=[kg_])
                S.op("dve", lambda e: e.tensor_tensor(out=g_[:, 3, :], in0=g_[:, 2, :], in1=g_[:, 2, :], op=ALU.mult), reads=[kg_], writes=[kg_])
                S.op("dve", lambda e: e.scalar_tensor_tensor(out=g_[:, 4, :], in0=g_[:, 1, :], scalar=1.0 / 64, in1=g_[:, 3, :], op0=ALU.mult, op1=ALU.subtract),
                     reads=[kg_], writes=[kg_])
                S.op("act", lambda e: e.activation(out=g_[:, 5, :], in_=g_[:, 4, :], func=AF.Sqrt, bias=GN_EPS), reads=[kg_], writes=[kg_])
                S.op("dve", lambda e: e.reciprocal(out=g_[:, 6, :], in_=g_[:, 5, :]), reads=[kg_], writes=[kg_])
                S.op("dve", lambda e: e.tensor_tensor(out=ycen, in0=PQ1, in1=g_[:, 2, :].unsqueeze(2).to_broadcast([128, 4, 64]), op=ALU.subtract),
                     reads=[kPB1, kg_], writes=[kycen])
                yield
                for h in range(2):
                    hs = slice(h * 64, (h + 1) * 64)
                    S.op("dve", lambda e, hs=hs: e.tensor_tensor(out=ynbd[hs, :, hs], in0=ycen[hs, :, :], in1=g_[hs, 6, :].unsqueeze(2).to_broadcast([64, 4, 64]),
                                                                  op=ALU.mult), reads=[kycen, kg_], writes=[kynbd])
                for c in range(4):
                    S.op("pe", lambda e, c=c: e.matmul(PQ0[:, c, :], lhsT=ynbd[:, c, :], rhs=fst[:], start=True, stop=True), reads=[kynbd, kfst], writes=[kPB1])
                S.op("act", lambda e: e.activation(out=ynf[:, :, n * CH:(n + 1) * CH], in_=PQ0, func=AF.Copy), reads=[kPB1], writes=kynf)
                yield

        def drain(g):
            for _ in g:
                pass

        def interleave(ga, gb, ra=2):
            a_live, b_live = ga is not None, gb is not None
            while a_live or b_live:
                for _ in range(ra):
                    if a_live:
                        try:
                            next(ga)
                        except StopIteration:
                            a_live = False
                if b_live:
                    try:
                        next(gb)
                    except StopIteration:
                        b_live = False

        def interleave_g(ga, gb, ra=2):
            a_live, b_live = ga is not None, gb is not None
            while a_live or b_live:
                for _ in range(ra):
                    if a_live:
                        try:
                            next(ga)
                        except StopIteration:
                            a_live = False
                if b_live:
                    try:
                        next(gb)
                    except StopIteration:
                        b_live = False
                yield

        def c_stage():
            yield from gen_AD(0, 0)
            for n_ in range(NCH):
                gd = gen_AD(n_ + 1, (n_ + 1) % 2) if n_ + 1 < NCH else None
                yield from interleave_g(gd, gen_SQ(n_, n_ % 2))

        def de():
            if not full:
                if t + 2 < n_tiles:
                    load_x(t + 2)
                return
            for c in range(4):
                S.op("pe", lambda e, c=c: e.matmul(psm[:, 0, :], lhsT=g2b0[:, c * 128:(c + 1) * 128], rhs=sgd0[:], start=True, stop=False), reads=[kg2b0, ksgd], writes=[kpsm])
                S.op("pe", lambda e, c=c: e.matmul(psm[:, 0, :], lhsT=g2b1[:, c * 128:(c + 1) * 128], rhs=sgd1[:], start=False, stop=True), reads=[kg2b1, ksgd], writes=[kpsm])
                y1, ky1 = tf.next()
                S.op("dve", lambda e, c=c, y1=y1: e.scalar_tensor_tensor(out=y1[:], in0=ynf[:, c, :], scalar=cc[:, O_LW + c:O_LW + c + 1], in1=bonus[:, c, :], op0=ALU.mult, op1=ALU.add),
                     reads=[kynf[c], kcc, kbonus[c]], writes=[ky1])
                S.op("dve", lambda e, c=c, y1=y1: e.scalar_tensor_tensor(out=yTr[:, c, :], in0=y1[:], scalar=cc[:, O_LB + c:O_LB + c + 1], in1=psm[:, 0, :], op0=ALU.add, op1=ALU.mult),
                     reads=[ky1, kcc, kpsm], writes=[kyTr[c]])
            if DEBUG_STOP < 9:
                return
            for blk in range(NBLK):
                for hf in range(2):
                    pflat = pp[hf][:].rearrange("p a w -> p (a w)")
                    for e_ in range(8):
                        ysrc = yc[:, e_, blk * 128:(blk + 1) * 128] if e_ < 4 else yTr[:, e_ - 4, blk * 128:(blk + 1) * 128]
                        ykey = kyc[e_] if e_ < 4 else kyTr[e_ - 4]
                        S.op("pe", lambda e, e_=e_, hf=hf, pflat=pflat, ysrc=ysrc: e.matmul(pflat, lhsT=ysrc, rhs=wout[:, e_, hf * 512:(hf + 1) * 512],
                                                                                            start=(e_ == 0), stop=(e_ == 7)), reads=[ykey, kwout], writes=[kpp[hf]])
                class _V:
                    def __init__(self, ap): self.ap = ap
                    def __getitem__(self, k): return self.ap
                _post_norm_residual(S, [(_V(pp[0][:].rearrange("p a w -> p (a w)")), kpp[0]), (_V(pp[1][:].rearrange("p a w -> p (a w)")), kpp[1])],
                                    xt[:, blk, :], kxts[b][blk], grow, kgrow, scr)
            ht_i = t - n_pre
            S.op("sp", lambda e, xt=xt, ht_i=ht_i: e.dma_start(out=hscr[W * ht_i:W * (ht_i + 1), :].rearrange("(b p) d -> p b d", p=128), in_=xt[:]),
                 reads=kxts[b], writes=[khscr[ht_i]], dma_sem=dsem[f"h{b}"])
            if t + 2 < n_tiles:
                load_x(t + 2)

        return early, b4_all, c_stage, de, b4_pre

    def drain_g(g):
        for _ in g:
            pass

    def chain_g(*gs):
        for g in gs:
            yield from g

    def interleave2(ga, gb, ra, rb):
        a_live, b_live = ga is not None, gb is not None
        while a_live or b_live:
            for _ in range(ra):
                if a_live:
                    try:
                        next(ga)
                    except StopIteration:
                        a_live = False
            for _ in range(rb):
                if b_live:
                    try:
                        next(gb)
                    except StopIteration:
                        b_live = False

    parts = [tile_parts(t_i) for t_i in range(n_tiles)]
    drain_g(parts[0][0]())
    parts[0][1]()
    for t_i in range(n_tiles):
        early_next = chain_g(parts[t_i + 1][0](), parts[t_i + 1][4]()) if t_i + 1 < n_tiles else None
        interleave2(parts[t_i][2](), early_next, 2, 1)
        parts[t_i][3]()
        if t_i + 1 < n_tiles:
            parts[t_i + 1][1]()


def _host_consts():
    km = np.zeros((128, NKM), np.float32)
    km[:, M_ID:M_ID + 128] = np.eye(128, dtype=np.float32)
    idx = np.arange(128)
    same = (idx[:, None] // 64) == (idx[None, :] // 64)
    s, t = idx[:, None] % 64, idx[None, :] % 64
    km[:, M_SU:M_SU + 128] = (same & (s < t)).astype(np.float32)
    km[:, M_IU:M_IU + 128] = (same & (s <= t)).astype(np.float32)
    km[:, M_SL:M_SL + 128] = (same & (s > t)).astype(np.float32)
    km[:, M_BO:M_BO + 128] = same.astype(np.float32)
    km[:, M_F:M_F + 64] = (idx[:, None] % 64 == np.arange(64)[None, :]).astype(np.float32)
    rst = np.ones((128, 256), np.float32)
    rst[:, ::64] = 0.0
    km[:, M_RST:M_RST + 256] = rst
    return km


def _pack_cc(inp, hmask):
    cc = np.zeros((128, NCC), np.float32)
    col = lambda v, n: np.ascontiguousarray(np.asarray(v, np.float32).reshape(n, 128).T)
    cc[:, O_PMG:O_PMG + 8] = col(inp["pre_mix_g"][0], 8)
    cc[:, O_PFG:O_PFG + 8] = col(inp["pre_ffn_g"][0], 8)
    caw = np.asarray(inp["conv_a_w"][0], np.float32)
    cc[:, O_CAW:O_CAW + 12] = caw.T.reshape(4, 128, 3).transpose(1, 0, 2).reshape(128, 12)
    mu = np.zeros(1920, np.float32)
    mu[:1824] = np.asarray(inp["shift_mu"][0], np.float32)
    cc[:, O_MU:O_MU + 15] = col(mu, 15)
    for off, name in ((O_W0, "w0"), (O_A0, "a0"), (O_KK, "k_k"), (O_KA, "k_a"), (O_LW, "lnx_w"), (O_LB, "lnx_b")):
        cc[:, off:off + 4] = col(inp[name][0], 4)
    cc[:, O_RK:O_RK + 4] = col(np.asarray(inp["r_k"][0], np.float32).reshape(512), 4)
    fcw = np.asarray(inp["ffn_conv_w"][0], np.float32)
    cc[:, O_FCW:O_FCW + 132] = fcw.T.reshape(44, 128, 3).transpose(1, 0, 2).reshape(128, 132)
    cc[:, O_FCB:O_FCB + 44] = col(inp["ffn_conv_b"][0], 44)
    cc[:, O_HM] = hmask
    return cc


_NC_CACHE = {}


def kernel(**inputs):
    n_pre, n_main = 15, 16
    x = np.asarray(inputs["x"], np.float32)
    B, T, _ = x.shape
    half = T // 2
    if "full" not in _NC_CACHE:
        _NC_CACHE["full"] = build(n_pre, n_main, "full")
    nc = _NC_CACHE["full"]
    km = _host_consts()
    f = lambda n: np.ascontiguousarray(np.asarray(inputs[n], np.float32)[0])
    in_maps = []
    for c in range(8):
        b, h = c // 2, c % 2
        xin = np.zeros((T, D), np.float32)
        if h == 0:
            xin[half:] = x[b, :half]
        else:
            xin[:] = x[b]
        in_maps.append({
            "xin": xin, "cc": _pack_cc(inputs, float(h)), "km": km,
            "post_mix_g": f("post_mix_g"), "post_ffn_g": f("post_ffn_g"),
            "w_in": f("w_in"), "w_out": f("w_out"), "w_up": f("w_up"), "w_down": f("w_down"),
            "w2": f("w2"), "a2": f("a2"), "g2": f("g2"),
        })
    res = run_bass_kernel_spmd(nc, in_maps, core_ids=list(range(8)))
    out = np.zeros((B, T, D), np.float32)
    for c in range(8):
        b, h = c // 2, c % 2
        out[b, h * half:(h + 1) * half] = res.results[c]["out"]
    return out
```

```python
import contextlib
import numpy as np
import concourse.bass as bass
import concourse.mybir as mybir
from concourse.bass_utils import run_bass_kernel_spmd

F32 = mybir.dt.float32
BF16 = mybir.dt.bfloat16
AF = mybir.ActivationFunctionType
ALU = mybir.AluOpType
AX = mybir.AxisListType

D = 1024
W = 256
NBLK = 2
CH = 64
NCH = W // CH
INC = 3360
DFF = 2816
NPAIR = 22
QC = 1536
RMS_EPS = 1e-6
GN_EPS = 64 * 1e-5
EPOCH = 12000
DEBUG_SUB = 99
EMBED_WAITS = True
DEBUG_STOP = 99

O_PMG, O_PFG, O_CAW, O_MU, O_W0, O_A0, O_KK, O_KA, O_RK, O_LW, O_LB, O_FCW, O_FCB, O_HM = (
    0, 8, 16, 28, 43, 47, 51, 55, 59, 63, 67, 71, 203, 247)
NCC = 248
M_ID, M_SU, M_IU, M_SL, M_BO, M_F, M_RST = 0, 128, 256, 384, 512, 640, 704
NKM = 704 + 256


class Key:
    __slots__ = ("name", "writer", "readers", "excl")

    def __init__(self, name, excl=False):
        self.name = name
        self.writer = None
        self.readers = []
        self.excl = excl


def PKey(name):
    return Key(name, excl=True)


class Sched:
    ENGS = ("pe", "act", "dve", "pool", "sp")

    def __init__(self, nc, sem_stack, prefix):
        self.nc = nc
        self.sem_stack = sem_stack
        self.prefix = prefix
        self.ops = {e: [] for e in self.ENGS}
        self.count = {e: 0 for e in self.ENGS}
        self.sems = {}
        self.waited = {e: {} for e in self.ENGS}
        self.dma_counts = {}
        self.last_tok = {e: None for e in self.ENGS}

    def _eng_sem(self, eng, idx):
        sid = f"{self.prefix}s_{eng}_{idx // EPOCH}"
        self.sems.setdefault(sid, None)
        return sid, (idx % EPOCH) + 1

    def new_dma_sem(self, name):
        sid = f"{self.prefix}d_{name}"
        assert sid not in self.sems, sid
        self.sems[sid] = None
        self.dma_counts[sid] = 0
        return sid

    def _need_waits(self, eng, tokens):
        w = self.waited[eng]
        best = {}
        for t in tokens:
            if t is None:
                continue
            sid, val, _ = t
            if w.get(sid, 0) >= val:
                continue
            if best.get(sid, 0) < val:
                best[sid] = val
        for sid, val in best.items():
            w[sid] = val
        return list(best.items())

    def op(self, eng, fn, reads=(), writes=(), dma_sem=None, multi=False):
        toks = []
        raw = set()
        for k in reads:
            toks.append(k.writer)
            if k.writer is not None:
                raw.add(k.writer)
            if k.excl:
                toks.extend(r for r in k.readers if r[2] != eng)
        for k in writes:
            toks.append(k.writer)
            toks.extend(k.readers)
        if eng == "pe":
            toks = [t for t in toks if t is not None and t[2] != "pe"]
        waits = self._need_waits(eng, toks)
        if dma_sem is None:
            idx = self.count[eng]
            self.count[eng] += 1
            sid, val = self._eng_sem(eng, idx)
            tok = (sid, val, eng)
            inc = (sid, 1)
            self.last_tok[eng] = tok
        else:
            self.dma_counts[dma_sem] += 16
            tok = (dma_sem, self.dma_counts[dma_sem], "dma")
            inc = (dma_sem, 16)
        embed = EMBED_WAITS and dma_sem is None and eng != "pe" and not multi
        self.ops[eng].append((fn, waits, inc, embed))
        for k in reads:
            k.readers.append(tok)
        for k in writes:
            k.writer = tok
            k.readers = []
        return tok

    def barrier(self, extra_keys=()):
        toks = [t for t in self.last_tok.values() if t is not None]
        for k in extra_keys:
            toks.append(k.writer)
            toks.extend(k.readers)
        for eng in self.ENGS:
            waits = self._need_waits(eng, [t for t in toks if t is not None and t[2] != eng])
            if waits:
                self.ops[eng].append((None, waits, None, False))

    def final_wait(self, eng, keys):
        toks = []
        for k in keys:
            toks.append(k.writer)
            toks.extend(k.readers)
        waits = self._need_waits(eng, toks)
        self.ops[eng].append((None, waits, None, False))

    def emit(self):
        nc = self.nc
        with contextlib.ExitStack() as st:
            handles = {sid: self.sem_stack.enter_context(nc.semaphore(sid)) for sid in self.sems}
            block = st.enter_context(nc.Block())

            def run(engobj, lst):
                for fn, waits, inc, embed in lst:
                    emb = waits[-1] if (embed and waits and fn is not None) else None
                    for sid, val in (waits[:-1] if emb is not None else waits):
                        engobj.wait_ge(handles[sid], val)
                    if fn is not None:
                        n0 = nc.n_instructions()
                        ins = fn(engobj)
                        if emb is not None:
                            assert nc.n_instructions() - n0 == 1, "embedded wait on a multi-instruction op"
                            ins._wait_ge(handles[emb[0]], emb[1])
                        ins.then_inc(handles[inc[0]], inc[1])

            @block.tensor
            def _(e):
                run(e, self.ops["pe"])

            @block.scalar
            def _(e):
                run(e, self.ops["act"])

            @block.vector
            def _(e):
                run(e, self.ops["dve"])

            @block.gpsimd
            def _(e):
                run(e, self.ops["pool"])

            @block.sync
            def _(e):
                run(e, self.ops["sp"])


class View:
    def __init__(self, ap):
        self.ap = ap

    def __getitem__(self, k):
        return self.ap


class View3:
    def __init__(self, x):
        self.x = x

    def __getitem__(self, k):
        return self.x[k[0], k[1], 128:256]


class Rot:
    def __init__(self, tiles, excl=False, keys=None):
        self.tiles = tiles
        self.keys = keys if keys is not None else [Key(f"rot{i}", excl) for i in range(len(tiles))]
        self.i = 0

    def next(self):
        j = self.i % len(self.tiles)
        self.i += 1
        return self.tiles[j], self.keys[j]


def _rms_transpose(S, src, ksrc, dstT, kdst, tcol, scr, ident, kid, ptr, extra_scale=None, kextra=None):
    st, kst = scr["stat"].next()
    xs, kxs = scr["xs"].next()
    pt, kpt = ptr.next()
    S.op("act", lambda e: e.activation(out=xs[:], in_=src, func=AF.Square, accum_out=st[:, 0:1]),
         reads=[ksrc], writes=[kxs, kst], multi=True)
    S.op("act", lambda e: e.activation(out=st[:, 1:2], in_=st[:, 0:1], func=AF.Sqrt, scale=1.0 / D, bias=RMS_EPS),
         reads=[kst], writes=[kst])
    S.op("dve", lambda e: e.reciprocal(out=st[:, 2:3], in_=st[:, 1:2]), reads=[kst], writes=[kst])
    rs = st[:, 2:3]
    if extra_scale is not None:
        S.op("dve", lambda e: e.tensor_tensor(out=st[:, 3:4], in0=st[:, 2:3], in1=extra_scale, op=ALU.mult),
             reads=[kst, kextra], writes=[kst])
        rs = st[:, 3:4]
    S.op("pool", lambda e: e.tensor_scalar(out=xs[:], in0=src, scalar1=rs, scalar2=1.0, op0=ALU.mult, op1=ALU.mult),
         reads=[ksrc, kst], writes=[kxs])
    for kc in range(8):
        S.op("pe", lambda e, kc=kc: e.transpose(out=pt[:, kc, :], in_=xs[:, kc * 128:(kc + 1) * 128], identity=ident[:]),
             reads=[kxs, kid], writes=[kpt])
    S.op("act", lambda e: e.activation(out=dstT[:, :, tcol:tcol + 128], in_=pt[:], func=AF.Copy),
         reads=[kpt], writes=[kdst])


def _post_norm_residual(S, pd_pairs, res, kres, grow, kgrow, scr):
    st, kst = scr["stat"].next()
    tmps = [scr["tmp512"].next() for _ in range(2)]
    for hf, (pd, kpd) in enumerate(pd_pairs):
        junk, kjunk = tmps[hf]
        S.op("act", lambda e, pd=pd, hf=hf, junk=junk: e.activation(out=junk[:], in_=pd[:], func=AF.Square,
                                                                  accum_out=st[:, hf:hf + 1]),
             reads=[kpd], writes=[kjunk, kst], multi=True)
    S.op("dve", lambda e: e.tensor_tensor(out=st[:, 2:3], in0=st[:, 0:1], in1=st[:, 1:2], op=ALU.add), reads=[kst], writes=[kst])
    S.op("act", lambda e: e.activation(out=st[:, 3:4], in_=st[:, 2:3], func=AF.Sqrt, scale=1.0 / D, bias=RMS_EPS),
         reads=[kst], writes=[kst])
    S.op("dve", lambda e: e.reciprocal(out=st[:, 4:5], in_=st[:, 3:4]), reads=[kst], writes=[kst])
    for hf, (pd, kpd) in enumerate(pd_pairs):
        tmp, ktmp = tmps[hf]
        S.op("dve", lambda e, pd=pd, hf=hf, tmp=tmp: e.scalar_tensor_tensor(
            out=tmp[:], in0=pd[:], scalar=st[:, 4:5], in1=grow[:, hf * 512:(hf + 1) * 512], op0=ALU.mult, op1=ALU.mult),
            reads=[kpd, kst, kgrow], writes=[ktmp])
        S.op("pool", lambda e, hf=hf, tmp=tmp: e.tensor_tensor(out=res[:, hf * 512:(hf + 1) * 512], in0=res[:, hf * 512:(hf + 1) * 512],
                                                               in1=tmp[:], op=ALU.add),
             reads=[ktmp, kres], writes=[kres])


def phase2_ffn(nc, S, st, io, n_main, shared):
    sb = lambda n, s, d: st.enter_context(nc.sbuf_tensor(n, s, d))
    ps = lambda n, s, d: st.enter_context(nc.psum_tensor(n, s, d))
    cc, kcc = shared["cc"], shared["kcc"]
    ident, kid = shared["ident"], shared["kid"]
    hscr, khscr = io["hscr"], io["khscr"]
    out = io["out"]

    wup = sb("wup", [128, 8, DFF * 2], BF16)
    wdn = sb("wdn", [128, NPAIR, D], BF16)
    kwup = [Key(f"wup{k}") for k in range(8)]
    kwdn = Key("wdn")
    grow = sb("grow2", [128, D], F32)
    kgrow = Key("grow2")
    fh = sb("fh", [128, NPAIR, 2, 2], F32)
    kfh = [Key(f"fh{i}") for i in range(NPAIR)]
    hts = [sb(f"ht{i}", [128, NBLK, D], F32) for i in range(2)]
    khts = [[Key(f"ht{i}_{b}") for b in range(NBLK)] for i in range(2)]
    hnTs = [sb(f"hnT{i}", [128, 8, W], BF16) for i in range(2)]
    khnT = [Key(f"hnT{i}") for i in range(2)]
    act = sb("actb", [128, NPAIR, W], BF16)
    kact = [Key(f"act{i}") for i in range(NPAIR)]
    scr = {
        "stat": Rot([sb(f"stat{i}", [128, 8], F32) for i in range(4)]),
        "xs": Rot([sb(f"xs{i}", [128, D], BF16) for i in range(2)]),
        "tmp512": Rot([sb(f"tmp512_{i}", [128, 512], F32) for i in range(2)]),
    }
    fbuf = Rot([sb(f"fbuf{i}", [128, 2, W + 2], F32) for i in range(4)])
    cg = Rot([sb(f"cg{i}", [128, W], F32) for i in range(4)])
    cu = Rot([sb(f"cu{i}", [128, W], F32) for i in range(4)])
    t1 = Rot([sb(f"t1_{i}", [128, W], F32) for i in range(3)])
    t2 = Rot([sb(f"t2_{i}", [128, W], F32) for i in range(3)])
    sg = Rot([sb(f"sg{i}", [128, W], F32) for i in range(3)])
    WQ = DFF // 4
    ptr = Rot([ps("ptr2", [128, 8, 128], BF16)], excl=True)
    pf = Rot([ps(f"pf{i}", [128, 2, W], F32) for i in range(3)], excl=True)
    pd = [[ps(f"pd{b}{h}", [128, 512], F32) for h in range(2)] for b in range(NBLK)]
    kpd = [[PKey(f"pd{b}{h}") for h in range(2)] for b in range(NBLK)]
    dsem = {n: S.new_dma_sem("p2_" + n) for n in ("wst0", "wst1", "wst2", "wst3", "wdn", "grow", "h0", "h1", "hw", "out0", "out1")}

    S.op("sp", lambda e: e.dma_start(out=grow[:], in_=io["post_ffn_g"].partition_broadcast(128)), writes=[kgrow], dma_sem=dsem["grow"])
    actf = act[:].rearrange("p i w -> p (i w)").bitcast(F32)
    kstg = [Key(f"stg{q}") for q in range(4)]
    for kc in range(8):
        for q in range(8):
            j = kc * 8 + q
            r_ = j % 4
            wt, kwt = actf[:, r_ * WQ:(r_ + 1) * WQ], kstg[r_]
            S.op("sp", lambda e, wt=wt, kc=kc, q=q: e.dma_start(out=wt, in_=io["w_up"][kc * 128:(kc + 1) * 128, q * WQ:(q + 1) * WQ]),
                 writes=[kwt], dma_sem=dsem[f"wst{r_}"])
            if j % 2 == 0:
                S.op("act", lambda e, wt=wt, kc=kc, q=q: e.activation(out=wup[:, kc, q * WQ:(q + 1) * WQ], in_=wt, func=AF.Copy,
                                                                       scale=cc[:, O_PFG + kc:O_PFG + kc + 1]),
                     reads=[kwt, kcc], writes=[kwup[kc]])
            else:
                S.op("dve", lambda e, wt=wt, kc=kc, q=q: e.tensor_scalar(out=wup[:, kc, q * WQ:(q + 1) * WQ], in0=wt,
                                                                          scalar1=cc[:, O_PFG + kc:O_PFG + kc + 1], scalar2=1.0, op0=ALU.mult, op1=ALU.mult),
                     reads=[kwt, kcc], writes=[kwup[kc]])
    S.op("pool", lambda e: e.memset(act[:, :, 0:1], 0.0), writes=kstg + kact)
    S.op("pool", lambda e: e.dma_start(out=wdn[:], in_=io["w_down"].rearrange("(i p) d -> p i d", p=128)), writes=[kwdn], dma_sem=dsem["wdn"])

    hw = hts[1]
    S.op("sp", lambda e: e.dma_start(out=hw[:, 0, :], in_=hscr[W - 128:W, :]), reads=[khscr[0]], writes=[khts[1][0]], dma_sem=dsem["hw"])
    _rms_transpose(S, hw[:, 0, :], khts[1][0], hnTs[1], khnT[1], 0, scr, ident, kid, ptr,
                   extra_scale=cc[:, O_HM:O_HM + 1], kextra=kcc)
    pfh, kpfh = pf.next()
    pfh_v = pfh[:].rearrange("p a w -> p (a w)")
    for ch in range(2 * NPAIR):
        i, hf = ch % NPAIR, ch // NPAIR
        col = (i * 2 + hf) * 2
        for kc in range(8):
            S.op("pe", lambda e, ch=ch, kc=kc, col=col: e.matmul(pfh_v[:, col:col + 2], lhsT=wup[:, kc, ch * 128:(ch + 1) * 128],
                                                                  rhs=hnTs[1][:, kc, 126:128], start=(kc == 0), stop=(kc == 7)),
                 reads=[kwup[kc], khnT[1]], writes=[kpfh])
    S.op("act", lambda e: e.activation(out=fh[:].rearrange("p i a b -> p (i a b)"), in_=pfh_v[:, 0:NPAIR * 4], func=AF.Copy),
         reads=[kpfh], writes=kfh)

    def load(t):
        b = t % 2
        S.op("sp", lambda e: e.dma_start(out=hts[b][:], in_=hscr[W * (1 + t):W * (2 + t), :].rearrange("(b p) d -> p b d", p=128)),
             reads=[khscr[1 + t]], writes=khts[b], dma_sem=dsem[f"h{b}"])

    def prologue(t):
        b = t % 2
        for blk in range(NBLK):
            _rms_transpose(S, hts[b][:, blk, :], khts[b][blk], hnTs[b], khnT[b], blk * 128, scr, ident, kid, ptr)

    state = {}

    def up_mm(t, i):
        b = t % 2
        p, kp = pf.next()
        state[(t, i)] = (p, kp)
        for hf in range(2):
            ch = hf * NPAIR + i
            for kc in range(8):
                S.op("pe", lambda e, p=p, hf=hf, ch=ch, kc=kc: e.matmul(p[:, hf, :], lhsT=wup[:, kc, ch * 128:(ch + 1) * 128],
                                                                         rhs=hnTs[b][:, kc, :], start=(kc == 0), stop=(kc == 7)),
                     reads=[kwup[kc], khnT[b]], writes=[kp])

    def elem(t, i):
        p, kp = state.pop((t, i))
        fb, kfb = fbuf.next()
        S.op("pool", lambda e: e.tensor_copy(out=fb[:, :, 0:2], in_=fh[:, i, :, :]), reads=[kfh[i]], writes=[kfb])
        S.op("act", lambda e: e.activation(out=fb[:, :, 2:W + 2], in_=p[:], func=AF.Copy), reads=[kp], writes=[kfb])
        yield
        S.op("pool", lambda e: e.tensor_copy(out=fh[:, i, :, :], in_=fb[:, :, W:W + 2]), reads=[kfb], writes=[kfh[i]])
        outs = []
        for hf, rot in ((0, cg), (1, cu)):
            ch = hf * NPAIR + i
            c, kc_ = rot.next()
            wof = O_FCW + ch * 3
            S.op("act", lambda e, c=c, hf=hf, wof=wof, ch=ch: e.activation(out=c[:], in_=fb[:, hf, 2:W + 2], func=AF.Identity,
                                                                          scale=cc[:, wof + 2:wof + 3], bias=cc[:, O_FCB + ch:O_FCB + ch + 1]),
                 reads=[kfb, kcc], writes=[kc_])
            outs.append((c, kc_, hf, wof))
        yield
        for c, kc_, hf, wof in outs:
            S.op("dve", lambda e, c=c, hf=hf, wof=wof: e.scalar_tensor_tensor(out=c[:], in0=fb[:, hf, 1:W + 1], scalar=cc[:, wof + 1:wof + 2],
                                                                             in1=c[:], op0=ALU.mult, op1=ALU.add),
                 reads=[kfb, kcc, kc_], writes=[kc_])
        yield
        for c, kc_, hf, wof in outs:
            S.op("dve", lambda e, c=c, hf=hf, wof=wof: e.scalar_tensor_tensor(out=c[:], in0=fb[:, hf, 0:W], scalar=cc[:, wof:wof + 1],
                                                                             in1=c[:], op0=ALU.mult, op1=ALU.add),
                 reads=[kfb, kcc, kc_], writes=[kc_])
        outs = [(c, kc_) for c, kc_, hf, wof in outs]
        yield
        (g_, kg), (u_, ku) = outs
        a1, ka1 = t1.next()
        a2, ka2 = t2.next()
        s_, ks = sg.next()
        S.op("pool", lambda e: e.tensor_tensor(out=a1[:], in0=g_[:], in1=g_[:], op=ALU.mult), reads=[kg], writes=[ka1])
        yield
        S.op("pool", lambda e: e.tensor_scalar(out=a1[:], in0=a1[:], scalar1=0.044715, scalar2=1.0, op0=ALU.mult, op1=ALU.add),
             reads=[ka1], writes=[ka1])
        S.op("pool", lambda e: e.tensor_tensor(out=a2[:], in0=a1[:], in1=g_[:], op=ALU.mult), reads=[ka1, kg], writes=[ka2])
        S.op("dve", lambda e: e.tensor_tensor(out=a1[:], in0=g_[:], in1=u_[:], op=ALU.mult), reads=[kg, ku, ka2], writes=[ka1])
        yield
        S.op("act", lambda e: e.activation(out=s_[:], in_=a2[:], func=AF.Sigmoid, scale=1.5957691216), reads=[ka2], writes=[ks])
        yield
        S.op("dve", lambda e: e.tensor_tensor(out=act[:, i, :], in0=a1[:], in1=s_[:], op=ALU.mult), reads=[ka1, ks], writes=[kact[i]])

    def down_mm(t, i):
        for blk in range(NBLK):
            for hf in range(2):
                S.op("pe", lambda e, blk=blk, hf=hf: e.matmul(pd[blk][hf][:], lhsT=act[:, i, blk * 128:(blk + 1) * 128],
                                                              rhs=wdn[:, i, hf * 512:(hf + 1) * 512], start=(i == 0), stop=(i == NPAIR - 1)),
                     reads=[kact[i], kwdn], writes=[kpd[blk][hf]])

    def epilogue(t):
        b = t % 2
        for blk in range(NBLK):
            _post_norm_residual(S, [(pd[blk][0], kpd[blk][0]), (pd[blk][1], kpd[blk][1])], hts[b][:, blk, :], khts[b][blk], grow, kgrow, scr)
        S.op("sp", lambda e: e.dma_start(out=out[W * t:W * (t + 1), :].rearrange("(b p) d -> p b d", p=128), in_=hts[b][:]),
             reads=khts[b], dma_sem=dsem[f"out{b}"])

    load(0)
    if n_main > 1:
        load(1)
    prologue(0)
    TSTEP = 2
    for t in range(n_main):
        active = []
        i_next, tick, pro_done = 0, 0, False
        while i_next < NPAIR or active:
            if i_next < NPAIR and tick % TSTEP == 0:
                up_mm(t, i_next)
                active.append((i_next, elem(t, i_next)))
                i_next += 1
                if i_next == 15 and t + 1 < n_main and not pro_done:
                    prologue(t + 1)
                    pro_done = True
            for item in list(active):
                i_, g_ = item
                try:
                    next(g_)
                except StopIteration:
                    active.remove(item)
                    down_mm(t, i_)
            tick += 1
        epilogue(t)
        if t + 2 < n_main:
            load(t + 2)
    S.final_wait("sp", [k for ks_ in khts for k in ks_])


def build(n_pre, n_main, mode="full"):
    nc = bass.Bass("TRN2", target_bir_lowering=False)
    TT = (n_pre + 1 + n_main) * W
    io = {}
    di = lambda n, s: nc.dram_tensor(n, s, F32, kind="ExternalInput").ap()
    io["cc"] = di("cc", [128, NCC])
    io["km"] = di("km", [128, NKM])
    io["post_ffn_g"] = di("post_ffn_g", [D])
    io["w_up"] = di("w_up", [D, 2 * DFF])
    io["w_down"] = di("w_down", [DFF, D])
    if mode == "ffn":
        io["hscr"] = di("hscr", [(1 + n_main) * W, D])
    else:
        io["xin"] = di("xin", [TT, D])
        io["post_mix_g"] = di("post_mix_g", [D])
        io["w_in"] = di("w_in", [D, INC])
        io["w_out"] = di("w_out", [D, D])
        io["w2"] = di("w2", [64, 512])
        io["a2"] = di("a2", [64, 512])
        io["g2"] = di("g2", [160, 512])
        io["hscr"] = nc.dram_tensor("hscr", [(1 + n_main) * W, D], F32, kind="Internal").ap()
    io["khscr"] = [Key(f"hscr{i}") for i in range(1 + n_main)]
    io["out"] = nc.dram_tensor("out", [n_main * W, D], F32, kind="ExternalOutput").ap()

    with contextlib.ExitStack() as sem_stack, contextlib.ExitStack() as st0:
        cc = st0.enter_context(nc.sbuf_tensor("cc_sb", [128, NCC], F32))
        ident = st0.enter_context(nc.sbuf_tensor("ident", [128, 128], BF16))

        def shared_loads(S):
            kcc, kid = Key("cc"), Key("ident")
            d0 = S.new_dma_sem("cc")
            d1 = S.new_dma_sem("ident")
            S.op("sp", lambda e: e.dma_start(out=cc[:], in_=io["cc"][:, :]), writes=[kcc], dma_sem=d0)
            S.op("pool", lambda e: e.dma_start(out=ident[:], in_=io["km"][:, M_ID:M_ID + 128]), writes=[kid], dma_sem=d1)
            return {"cc": cc, "kcc": kcc, "ident": ident, "kid": kid}

        if mode != "ffn":
            S1 = Sched(nc, sem_stack, "a")
            shared = shared_loads(S1)
            with contextlib.ExitStack() as st1:
                phase1_mixer(nc, S1, st1, io, n_pre, n_main, shared)
                S1.final_wait("sp", io["khscr"])
                S1.emit()
            S2 = Sched(nc, sem_stack, "b")
            shared = {"cc": cc, "kcc": Key("cc2"), "ident": ident, "kid": Key("ident2")}
            io["khscr"] = [Key(f"hscr2_{i}") for i in range(1 + n_main)]
        else:
            S2 = Sched(nc, sem_stack, "b")
            shared = shared_loads(S2)
        with contextlib.ExitStack() as st2:
            phase2_ffn(nc, S2, st2, io, n_main, shared)
            S2.emit()
    return nc


def phase1_mixer(nc, S, st, io, n_pre, n_main, shared):
    sb = lambda n, s, d: st.enter_context(nc.sbuf_tensor(n, s, d))
    ps = lambda n, s, d: st.enter_context(nc.psum_tensor(n, s, d))
    cc, kcc = shared["cc"], shared["kcc"]
    ident, kid = shared["ident"], shared["kid"]
    xin, hscr, khscr = io["xin"], io["hscr"], io["khscr"]
    n_tiles = n_pre + 1 + n_main
    C05 = 0.6065306597126334

    win = sb("win", [128, 8, INC], BF16)
    kwin = [Key(f"win{k}") for k in range(8)]
    wout = sb("wout", [128, 8, D], BF16)
    kwout = Key("wout")
    w2b = sb("w2b", [128, 512], BF16)
    a2b = sb("a2b", [128, 512], BF16)
    g2b0 = sb("g2b0", [128, 512], BF16)
    g2b1 = sb("g2b1", [128, 512], BF16)
    wg1 = sb("wg1", [128, 8, 128], BF16)
    kwg1 = Key("wg1")
    ksmallw = Key("smallw")
    msu4 = sb("msu4", [128, 512], BF16)
    msl = sb("msl", [128, 128], BF16)
    bones = sb("bones", [128, 128], BF16)
    fst = sb("fst", [128, 64], BF16)
    rst = sb("rst", [128, W], F32)
    kconst = Key("p1const")
    grow = sb("grow1", [128, D], F32)
    kgrow = Key("grow1")
    dc = sb("dc", [128, 20], F32)
    kdc = Key("dc")
    dsem = {n: S.new_dma_sem("p1_" + n) for n in ("wst0", "wst1", "wst2", "wst3", "const", "grow", "x0", "x1", "h0", "h1")}

    S.op("sp", lambda e: e.dma_start(out=grow[:], in_=io["post_mix_g"].partition_broadcast(128)), writes=[kgrow], dma_sem=dsem["grow"])
    S.op("sp", lambda e: e.dma_start(out=rst[:], in_=io["km"][:, M_RST:M_RST + W]), writes=[kconst], dma_sem=dsem["const"])
    uniq = [0]

    def pool_dma(fn, key):
        uniq[0] += 1
        S.op("pool", fn, writes=[key], dma_sem=S.new_dma_sem(f"p1u{uniq[0]}"))

    kmsl, kbones, kfst = Key("msl"), Key("bones"), Key("fst")
    kmsu4 = [Key(f"msu4_{q}") for q in range(4)]
    kw2b, ka2b, kg2b0, kg2b1 = Key("w2b"), Key("a2b"), Key("g2b0"), Key("g2b1")
    for dst, c0, n_, k_ in ((msl, M_SL, 128, kmsl), (bones, M_BO, 128, kbones), (fst, M_F, 64, kfst)):
        pool_dma(lambda e, dst=dst, c0=c0, n_=n_: e.dma_start(out=dst[:], in_=io["km"][:, c0:c0 + n_]), k_)
    for q, c0 in enumerate((M_SU, M_IU, M_SU, M_IU)):
        pool_dma(lambda e, q=q, c0=c0: e.dma_start(out=msu4[:, q * 128:(q + 1) * 128], in_=io["km"][:, c0:c0 + 128]), kmsu4[q])
    S.op("pool", lambda e: e.memset(w2b[:], 0.0), writes=[kw2b])
    S.op("pool", lambda e: e.memset(a2b[:], 0.0), writes=[ka2b])
    S.op("pool", lambda e: e.memset(g2b1[:], 0.0), writes=[kg2b1])
    S.op("pool", lambda e: e.memset(wg1[:], 0.0), writes=[kwg1])
    pool_dma(lambda e: e.dma_start(out=w2b[0:64, :], in_=io["w2"][:, :]), kw2b)
    pool_dma(lambda e: e.dma_start(out=a2b[64:128, :], in_=io["a2"][:, :]), ka2b)
    pool_dma(lambda e: e.dma_start(out=g2b0[:], in_=io["g2"][0:128, :]), kg2b0)
    pool_dma(lambda e: e.dma_start(out=g2b1[0:32, :], in_=io["g2"][128:160, :]), kg2b1)
    pool_dma(lambda e: e.dma_start(out=wout[:], in_=io["w_out"].rearrange("(k p) d -> p k d", p=128)), kwout)
    S.op("dve", lambda e: e.tensor_scalar(out=dc[:, 0:15], in0=cc[:, O_MU:O_MU + 15], scalar1=-1.0, scalar2=1.0, op0=ALU.mult, op1=ALU.add),
         reads=[kcc], writes=[kdc])
    S.op("dve", lambda e: e.tensor_scalar(out=dc[:, 15:19], in0=cc[:, O_KA:O_KA + 4], scalar1=-1.0, scalar2=1.0, op0=ALU.mult, op1=ALU.add),
         reads=[kcc], writes=[kdc])
    xts = [sb(f"xt{i}", [128, NBLK, D], F32) for i in range(2)]
    kxts = [[Key(f"xt{i}_{b}") for b in range(NBLK)] for i in range(2)]
    xnT = sb("xnT", [128, 8, W], BF16)
    kxnT = Key("xnT")
    scr = {
        "stat": Rot([sb(f"stat1_{i}", [128, 8], F32) for i in range(4)]),
        "xs": Rot([sb(f"xs1_{i}", [128, D], BF16) for i in range(2)]),
    }
    tf = Rot([sb(f"tf{i}", [128, W], F32) for i in range(9)])
    tb = Rot([sb(f"tb{i}", [128, W], BF16) for i in range(4)])
    qbuf = Rot([sb(f"qbuf{i}", [128, W + 1], F32) for i in range(2)])
    ubuf = Rot([sb(f"ubuf{i}", [128, W + 2], F32) for i in range(2)])
    qh = sb("qh", [128, 15], F32)
    kqh = [Key(f"qh{j}") for j in range(15)]
    uh = sb("uh", [128, 4, 2], F32)
    kuh = [Key(f"uh{c}") for c in range(4)]
    rkv = {n: Rot([sb(f"{n}c{i}", [128, W], F32) for i in range(2)]) for n in ("r", "k", "v")}
    wa = sb("wa", [128, W], F32); kwa = Key("wa")
    g0t = sb("g0t", [128, W], F32); kg0 = Key("g0t")
    g1t = sb("g1t", [128, W], F32); kg1 = Key("g1t")
    twad = sb("twad", [128, W], BF16); ktwad = Key("twad")
    sgds = [(sb(f"sgd0_{i}", [128, W], BF16), sb(f"sgd1_{i}", [128, W], BF16), Key(f"sgd{i}")) for i in range(2)]
    sig = sb("sig", [128, 4, W], F32); ksig = [Key(f"sig{c}") for c in range(4)]
    a4 = sb("a4", [128, 4, W], F32); ka4 = [Key(f"a4{c}") for c in range(4)]
    bonus = sb("bonus", [128, 4, W], F32); kbonus = [Key(f"bonus{c}") for c in range(4)]
    ARbd = sb("ARbd", [128, 4, NCH, 2, 128], BF16); kAbd = [Key(f"Abd{c}") for c in range(4)]; kRbd = [Key(f"Rbd{c}") for c in range(4)]
    Bbd = sb("Bbd", [128, 4, NCH, 128], BF16); kBbd = [Key(f"Bbd{c}") for c in range(4)]
    Kbd = sb("Kbd", [128, 4, NCH, 128], BF16); kKbd = [Key(f"Kbd{c}") for c in range(4)]
    Vbd = sb("Vbd", [128, 4, NCH, 128], BF16); kVbd = [Key(f"Vbd{c}") for c in range(4)]
    PC = sb("PC", [128, 4, NCH], F32); kPC = [Key(f"PC{c}") for c in range(4)]
    M1s = [(sb(f"M1_{i}", [128, 4, 512], BF16), Key(f"M1_{i}")) for i in range(2)]
    NT0s = [(sb(f"NT0_{i}", [128, 4, 128], BF16), Key(f"NT0_{i}")) for i in range(2)]
    M2s = [(sb(f"M2_{i}", [128, 4, 320], BF16), Key(f"M2_{i}")) for i in range(2)]
    NNs = [[(sb(f"NN{i}{j}", [128, 4, 256], BF16), Key(f"NN{i}{j}")) for j in range(2)] for i in range(2)]
    Tbs = [[(sb(f"Tb{i}{j}", [128, 4, 128], BF16), Key(f"Tb{i}{j}")) for j in range(2)] for i in range(2)]
    Zb = sb("Zb", [128, 4, 64], BF16); kZb = Key("Zb")
    Ub = sb("Ub", [128, 4, 64], BF16); kUb = Key("Ub")
    S32 = sb("S32", [128, 4, 64], F32); kS32 = Key("S32")
    Sbf = sb("Sbf", [128, 4, 64], BF16); kSbf = Key("Sbf")
    gst = Rot([sb(f"gst{i}", [128, 8, 4], F32) for i in range(2)])
    ynbd = sb("ynbd", [128, 4, 128], BF16); kynbd = Key("ynbd")
    ynf = sb("ynf", [128, 4, W], F32); _ka, _kb = Key("ynfA"), Key("ynfB"); kynf = [_ka, _ka, _kb, _kb]
    scr["tmp512"] = Rot([View(ynf[:, 0:2, :].rearrange("p c w -> p (c w)")), View(ynf[:, 2:4, :].rearrange("p c w -> p (c w)"))], keys=[_ka, _kb])
    yTc = [sb(f"yTc{i}", [128, 4, W], BF16) for i in range(2)]; kyTc = [[Key(f"yTc{i}_{c}") for c in range(4)] for i in range(2)]
    yTr = sb("yTr", [128, 4, W], BF16); kyTr = [Key(f"yTr{c}") for c in range(4)]

    WQ = INC // 4
    stg = [(sig, ksig), (a4, ka4), (bonus, kbonus), (ynf, kynf)]
    for kc in range(8):
        for q in range(4):
            j = kc * 4 + q
            buf, kbuf = stg[j % 4]
            wt = buf[:].rearrange("p c w -> p (c w)")[:, 0:WQ]
            S.op("sp", lambda e, wt=wt, kc=kc, q=q: e.dma_start(out=wt, in_=io["w_in"][kc * 128:(kc + 1) * 128, q * WQ:(q + 1) * WQ]),
                 writes=kbuf, dma_sem=dsem[f"wst{j % 4}"])
            if j % 2 == 0:
                S.op("act", lambda e, wt=wt, kc=kc, q=q: e.activation(out=win[:, kc, q * WQ:(q + 1) * WQ], in_=wt, func=AF.Copy,
                                                                       scale=cc[:, O_PMG + kc:O_PMG + kc + 1]),
                     reads=kbuf + [kcc], writes=[kwin[kc]])
            else:
                S.op("dve", lambda e, wt=wt, kc=kc, q=q: e.tensor_scalar(out=win[:, kc, q * WQ:(q + 1) * WQ], in0=wt,
                                                                          scalar1=cc[:, O_PMG + kc:O_PMG + kc + 1], scalar2=1.0, op0=ALU.mult, op1=ALU.mult),
                     reads=kbuf + [kcc], writes=[kwin[kc]])
    S.op("dve", lambda e: e.tensor_copy(out=wg1[:, :, 0:32], in_=win[:, :, INC - 32:INC]), reads=kwin + [kwg1], writes=[kwg1])

    ptr = Rot([ps("ptr1", [128, 8, 128], BF16)], excl=True)
    pp = [ps(f"pp{i}", [128, 2, W], F32) for i in range(2)]
    kpp = [PKey(f"pp{i}") for i in range(2)]
    psm = ps("psm", [128, 2, W], F32); kpsm = PKey("psm")
    PA = ps("PA", [128, 1024], F32); kPA = PKey("PA")
    PB = ps("PB", [128, 1024], F32); kPB0 = PKey("PB0"); kPB1 = PKey("PB1")

    for t_, k_ in ((qh, kqh), (uh, kuh)):
        S.op("pool", lambda e, t_=t_: e.memset(t_[:], 0.0), writes=k_)
    S.op("pool", lambda e: e.memset(S32[:], 0.0), writes=[kS32])
    S.op("pool", lambda e: e.memset(Sbf[:], 0.0), writes=[kSbf])
    S.op("pool", lambda e: e.memset(ARbd[:], 0.0), writes=kAbd + kRbd)
    S.op("pool", lambda e: e.memset(Bbd[:], 0.0), writes=kBbd)
    S.op("pool", lambda e: e.memset(Kbd[:], 0.0), writes=kKbd)
    S.op("pool", lambda e: e.memset(Vbd[:], 0.0), writes=kVbd)
    S.op("pool", lambda e: e.memset(ynbd[:], 0.0), writes=[kynbd])
    S.op("pool", lambda e: e.memset(g1t[:], 0.0), writes=[kg1])

    slot_i = [0]

    def proj(col0, ncols, wsrc=None, kw=None):
        j = slot_i[0] % 4
        slot_i[0] += 1
        t_, half, key = pp[j // 2], j % 2, kpp[j // 2]
        for kc in range(8):
            lhsT = win[:, kc, col0:col0 + ncols] if wsrc is None else wsrc[:, kc, :]
            S.op("pe", lambda e, kc=kc, lhsT=lhsT: e.matmul(t_[0:ncols, half, :], lhsT=lhsT, rhs=xnT[:, kc, :],
                                                            start=(kc == 0), stop=(kc == 7)),
                 reads=[(kwin if kw is None else kw)[kc], kxnT], writes=[key])
        return t_[0:ncols, half, :], key

    def shift_lerp(p_ap, kp, jj, dst_ap, kdst, np_=128):
        qb, kqb = qbuf.next()
        S.op("pool", lambda e: e.tensor_copy(out=qb[0:np_, 0:1], in_=qh[0:np_, jj:jj + 1]), reads=[kqh[jj]], writes=[kqb])
        S.op("act", lambda e: e.activation(out=qb[0:np_, 1:W + 1], in_=p_ap, func=AF.Copy), reads=[kp], writes=[kqb])
        S.op("pool", lambda e: e.tensor_copy(out=qh[0:np_, jj:jj + 1], in_=qb[0:np_, W:W + 1]), reads=[kqb], writes=[kqh[jj]])
        tmp, ktmp = tf.next()
        S.op("pool", lambda e: e.tensor_scalar(out=tmp[0:np_, :], in0=qb[0:np_, 1:W + 1], scalar1=dc[0:np_, jj:jj + 1], scalar2=1.0, op0=ALU.mult, op1=ALU.mult),
             reads=[kqb, kdc], writes=[ktmp])
        S.op("dve", lambda e: e.scalar_tensor_tensor(out=dst_ap, in0=qb[0:np_, 0:W], scalar=cc[0:np_, O_MU + jj:O_MU + jj + 1], in1=tmp[0:np_, :],
                                                     op0=ALU.mult, op1=ALU.add),
             reads=[kqb, kcc, ktmp], writes=[kdst])

    def bd_view(t4, c, h):
        return t4[h * 64:(h + 1) * 64, c, :, h * 64:(h + 1) * 64]

    def v3(t2, h):
        return t2[h * 64:(h + 1) * 64, :].rearrange("p (n t) -> p n t", n=NCH)

    def load_x(t):
        b = t % 2
        S.op("sp", lambda e: e.dma_start(out=xts[b][:], in_=xin[W * t:W * (t + 1), :].rearrange("(b p) d -> p b d", p=128)),
             writes=kxts[b], dma_sem=dsem[f"x{b}"])

    load_x(0)
    if n_tiles > 1:
        load_x(1)

    def tile_parts(t):
        full = t >= n_pre
        b = t % 2
        par = t % 2
        xt = xts[b]
        sgd0, sgd1, ksgd = sgds[par]
        yc, kyc = yTc[par], kyTc[par]

        def early():
            for blk in range(NBLK):
                _rms_transpose(S, xt[:, blk, :], kxts[b][blk], xnT, kxnT, blk * 128, scr, ident, kid, ptr)

            yield
            if full:
                def b1(c):
                    pB, kpB = proj(c * 128, 128)
                    pH, kpH = proj(1024 + c * 128, 128)
                    hAs, khAs = tf.next()
                    S.op("act", lambda e, hAs=hAs, pH=pH: e.activation(out=hAs[:], in_=pH, func=AF.Copy), reads=[kpH], writes=[khAs])
                    ub, kub = ubuf.next()
                    S.op("pool", lambda e, ub=ub, c=c: e.tensor_copy(out=ub[:, 0:2], in_=uh[:, c, :]), reads=[kuh[c]], writes=[kub])
                    S.op("dve", lambda e, ub=ub, pB=pB, hAs=hAs: e.tensor_tensor(out=ub[:, 2:W + 2], in0=pB, in1=hAs[:], op=ALU.mult),
                         reads=[kpB, khAs], writes=[kub])
                    S.op("pool", lambda e, ub=ub, c=c: e.tensor_copy(out=uh[:, c, :], in_=ub[:, W:W + 2]), reads=[kub], writes=[kuh[c]])
                    ta, kta = tf.next()
                    wof = O_CAW + c * 3
                    S.op("act", lambda e, ta=ta, ub=ub, wof=wof: e.activation(out=ta[:], in_=ub[:, 2:W + 2], func=AF.Copy, scale=cc[:, wof + 2:wof + 3]),
                         reads=[kub, kcc], writes=[kta])
                    S.op("dve", lambda e, ta=ta, ub=ub, wof=wof: e.scalar_tensor_tensor(out=ta[:], in0=ub[:, 1:W + 1], scalar=cc[:, wof + 1:wof + 2], in1=ta[:],
                                                                                      op0=ALU.mult, op1=ALU.add), reads=[kub, kcc, kta], writes=[kta])
                    S.op("dve", lambda e, ta=ta, ub=ub, wof=wof: e.scalar_tensor_tensor(out=ta[:], in0=ub[:, 0:W], scalar=cc[:, wof:wof + 1], in1=ta[:],
                                                                                      op0=ALU.mult, op1=ALU.add), reads=[kub, kcc, kta], writes=[kta])
                    pC, kpC = proj(512 + c * 128, 128)
                    S.op("dve", lambda e, ta=ta, pC=pC, c=c: e.tensor_tensor(out=yc[:, c, :], in0=pC, in1=ta[:], op=ALU.mult),
                         reads=[kpC, kta], writes=[kyc[c]])
                for c_ in range(4):
                    b1(c_)

            yield
            QC = 1536
            p_, kp_ = proj(QC + 12 * 128, 128)
            shift_lerp(p_, kp_, 12, wa[:], kwa)
            S.op("act", lambda e: e.activation(out=twad[0:64, :], in_=wa[0:64, :], func=AF.Tanh), reads=[kwa], writes=[ktwad])
            S.op("dve", lambda e: e.tensor_copy(out=twad[64:128, :], in_=wa[64:128, :]), reads=[kwa], writes=[ktwad])
            if full:
                p_, kp_ = proj(QC + 13 * 128, 128)
                shift_lerp(p_, kp_, 13, g0t[:], kg0)
                p_, kp_ = proj(None, 128, wsrc=wg1, kw=[kwg1] * 8)
                shift_lerp(p_[0:32, :], kp_, 14, g1t[0:32, :], kg1, np_=32)
                S.op("act", lambda e: e.activation(out=sgd0[:], in_=g0t[:], func=AF.Sigmoid), reads=[kg0], writes=[ksgd])
                S.op("act", lambda e: e.activation(out=sgd1[:], in_=g1t[:], func=AF.Sigmoid), reads=[kg1], writes=[ksgd])
            yield
            for c in range(4):
                S.op("pe", lambda e, c=c: e.matmul(psm[:, 0, :], lhsT=w2b[:, c * 128:(c + 1) * 128], rhs=twad[:], start=True, stop=True),
                     reads=[kw2b, ktwad], writes=[kpsm])
                S.op("pe", lambda e, c=c: e.matmul(psm[:, 1, :], lhsT=a2b[:, c * 128:(c + 1) * 128], rhs=twad[:], start=True, stop=True),
                     reads=[ka2b, ktwad], writes=[kpsm])
                S.op("act", lambda e, c=c: e.activation(out=sig[:, c, :], in_=psm[:, 0, :], func=AF.Sigmoid, bias=cc[:, O_W0 + c:O_W0 + c + 1]),
                     reads=[kpsm, kcc], writes=[ksig[c]])
                S.op("act", lambda e, c=c: e.activation(out=a4[:, c, :], in_=psm[:, 1, :], func=AF.Sigmoid, bias=cc[:, O_A0 + c:O_A0 + c + 1]),
                     reads=[kpsm, kcc], writes=[ka4[c]])

            yield

        def b4(c):
            kt_, kkt = rkv["k"].next()
            vt_, kvt = rkv["v"].next()
            p_, kp_ = proj(QC + (4 + c) * 128, 128)
            shift_lerp(p_, kp_, 4 + c, kt_[:], kkt)
            yield
            p_, kp_ = proj(QC + (8 + c) * 128, 128)
            shift_lerp(p_, kp_, 8 + c, vt_[:], kvt)
            yield
            if full:
                rt_, krt = rkv["r"].next()
                p_, kp_ = proj(QC + c * 128, 128)
                shift_lerp(p_, kp_, c, rt_[:], krt)
                yield
            kkr, kkkr = tf.next()
            S.op("pool", lambda e, kkr=kkr, kt_=kt_, c=c: e.tensor_scalar(out=kkr[:], in0=kt_[:], scalar1=cc[:, O_KK + c:O_KK + c + 1], scalar2=1.0, op0=ALU.mult, op1=ALU.mult),
                 reads=[kkt, kcc], writes=[kkkr])
            sq, ksq = tb.next()
            S.op("act", lambda e, sq=sq, kkr=kkr: e.activation(out=sq[:], in_=kkr[:], func=AF.Square), reads=[kkkr], writes=[ksq])
            S.op("pe", lambda e, sq=sq: e.matmul(psm[:, 0, :], lhsT=bones[:], rhs=sq[:], start=True, stop=True), reads=[kbones, ksq], writes=[kpsm])
            yield
            cs, kcs = tf.next()
            S.op("dve", lambda e, cs=cs, c=c: e.tensor_tensor_scan(out=cs[:], data0=rst[:], data1=sig[:, c, :], initial=0.0, op0=ALU.mult, op1=ALU.add),
                 reads=[kconst, ksig[c]], writes=[kcs])
            yield
            E1, kE1 = tf.next()
            E2, kE2 = tf.next()
            E3, kE3 = tf.next()
            dd, kdd = tf.next()
            S.op("act", lambda e, E1=E1, cs=cs: e.activation(out=E1[:], in_=cs[:], func=AF.Exp, scale=-C05), reads=[kcs], writes=[kE1])
            S.op("act", lambda e, E2=E2, cs=cs: e.activation(out=E2[:], in_=cs[:], func=AF.Exp, scale=C05), reads=[kcs], writes=[kE2])
            S.op("pool", lambda e, dd=dd, cs=cs, c=c: e.tensor_tensor(out=dd[:], in0=cs[:], in1=sig[:, c, :], op=ALU.subtract), reads=[kcs, ksig[c]], writes=[kdd])
            S.op("act", lambda e, E3=E3, dd=dd: e.activation(out=E3[:], in_=dd[:], func=AF.Exp, scale=-C05), reads=[kdd], writes=[kE3])
            S.op("pool", lambda e, E1=E1, c=c: e.tensor_copy(out=PC[:, c, :], in_=E1[:].rearrange("p (n t) -> p n t", n=NCH)[:, :, CH - 1]),
                 reads=[kE1], writes=[kPC[c]])
            yield
            mm, kmm = tf.next()
            S.op("pool", lambda e, mm=mm, c=c: e.tensor_scalar(out=mm[:], in0=a4[:, c, :], scalar1=cc[:, O_KA + c:O_KA + c + 1], scalar2=dc[:, 15 + c:16 + c],
                                                               op0=ALU.mult, op1=ALU.add), reads=[ka4[c], kcc, kdc], writes=[kmm])
            kp, kkp = tf.next()
            S.op("dve", lambda e, kp=kp, kt_=kt_, mm=mm: e.tensor_tensor(out=kp[:], in0=kt_[:], in1=mm[:], op=ALU.mult), reads=[kkt, kmm], writes=[kkp])
            nrm, knrm = tf.next()
            S.op("act", lambda e, nrm=nrm: e.activation(out=nrm[:], in_=psm[:, 0, :], func=AF.Sqrt, bias=1e-24), reads=[kpsm], writes=[knrm])
            S.op("dve", lambda e, nrm=nrm: e.reciprocal(out=nrm[:], in_=nrm[:]), reads=[knrm], writes=[knrm])
            kk, kkk = tf.next()
            S.op("dve", lambda e, kk=kk, kkr=kkr, nrm=nrm: e.tensor_tensor(out=kk[:], in0=kkr[:], in1=nrm[:], op=ALU.mult), reads=[kkkr, knrm], writes=[kkk])
            akk, kakk = tf.next()
            S.op("pool", lambda e, akk=akk, kk=kk, c=c: e.tensor_tensor(out=akk[:], in0=a4[:, c, :], in1=kk[:], op=ALU.mult), reads=[ka4[c], kkk], writes=[kakk])
            yield
            for h in range(2):
                S.op("dve", lambda e, h=h, kk=kk, E3=E3, c=c: e.scalar_tensor_tensor(out=ARbd[h * 64:(h + 1) * 64, c, :, 0, h * 64:(h + 1) * 64], in0=v3(kk, h), scalar=-1.0,
                                                                                   in1=v3(E3, h), op0=ALU.mult, op1=ALU.mult),
                     reads=[kkk, kE3], writes=[kAbd[c]])
                S.op("dve", lambda e, h=h, akk=akk, E2=E2, c=c: e.tensor_tensor(out=bd_view(Bbd, c, h), in0=v3(akk, h), in1=v3(E2, h), op=ALU.mult),
                     reads=[kakk, kE2], writes=[kBbd[c]])
                S.op("pool", lambda e, h=h, kp=kp, E2=E2, c=c: e.tensor_tensor(out=bd_view(Kbd, c, h), in0=v3(kp, h), in1=v3(E2, h), op=ALU.mult),
                     reads=[kkp, kE2], writes=[kKbd[c]])
                S.op("act", lambda e, h=h, vt_=vt_, c=c: e.activation(out=bd_view(Vbd, c, h), in_=v3(vt_, h), func=AF.Copy), reads=[kvt], writes=[kVbd[c]])
                if full:
                    S.op("pool", lambda e, h=h, rt_=rt_, E1=E1, c=c: e.tensor_tensor(out=ARbd[h * 64:(h + 1) * 64, c, :, 1, h * 64:(h + 1) * 64], in0=v3(rt_, h), in1=v3(E1, h),
                                                                                   op=ALU.mult), reads=[krt, kE1], writes=[kRbd[c]])
            yield
            if full:
                rk, krk = tf.next()
                S.op("pool", lambda e, rk=rk, rt_=rt_, kp=kp: e.tensor_tensor(out=rk[:], in0=rt_[:], in1=kp[:], op=ALU.mult), reads=[krt, kkp], writes=[krk])
                rkb, krkb = tb.next()
                S.op("dve", lambda e, rkb=rkb, rk=rk, c=c: e.tensor_scalar(out=rkb[:], in0=rk[:], scalar1=cc[:, O_RK + c:O_RK + c + 1], scalar2=1.0, op0=ALU.mult, op1=ALU.mult),
                     reads=[krk, kcc], writes=[krkb])
                S.op("pe", lambda e, rkb=rkb: e.matmul(psm[:, 1, :], lhsT=bones[:], rhs=rkb[:], start=True, stop=True), reads=[kbones, krkb], writes=[kpsm])
                S.op("dve", lambda e, vt_=vt_, c=c: e.tensor_tensor(out=bonus[:, c, :], in0=psm[:, 1, :], in1=vt_[:], op=ALU.mult), reads=[kpsm, kvt], writes=[kbonus[c]])

        b4gens = {}

        def b4_start(c):
            g = b4(c)
            for _ in range(3 if full else 2):
                next(g)
            b4gens[c] = g

        def b4_pre():
            b4_start(0)
            yield

        def b4_all():
            if 0 not in b4gens:
                b4_start(0)
            for c_ in range(4):
                if c_ + 1 < 4:
                    b4_start(c_ + 1)
                drain_g(b4gens.pop(c_))
            if DEBUG_STOP < 5:
                return


        PA3 = PA[:].rearrange("p (a w) -> p a w", a=2)
        PAd = PA[:].rearrange("p (a w) -> p a w", a=4)
        PBt = PB[:, 0:512].rearrange("p (a w) -> p a w", a=4)
        PQ0 = PB[:, 512:768].rearrange("p (a w) -> p a w", a=4)
        PQ1 = PB[:, 768:1024].rearrange("p (a w) -> p a w", a=4)
        Tfinal = {}

        def gen_AD(n, s_):
            M1, kM1 = M1s[s_]
            NT0, kNT0 = NT0s[s_]
            M2, kM2 = M2s[s_]
            for pi in range(2):
                for i in range(2):
                    c = pi * 2 + i
                    rhsAR = ARbd[:, c, n, :, :].rearrange("p a w -> p (a w)")
                    S.op("pe", lambda e, i=i, c=c, rhsAR=rhsAR: e.matmul(PA3[:, i, 0:256], lhsT=Bbd[:, c, n, :], rhs=rhsAR, start=True, stop=True),
                         reads=[kBbd[c], kAbd[c], kRbd[c]], writes=[kPA])
                    S.op("pe", lambda e, i=i, c=c, rhsAR=rhsAR: e.matmul(PA3[:, i, 256:512], lhsT=Kbd[:, c, n, :], rhs=rhsAR, start=True, stop=True),
                         reads=[kKbd[c], kAbd[c], kRbd[c]], writes=[kPA])
                S.op("dve", lambda e, pi=pi: e.tensor_tensor(out=M1[:, 2 * pi:2 * pi + 2, :], in0=PA3, in1=msu4[:].unsqueeze(1).to_broadcast([128, 2, 512]), op=ALU.mult),
                     reads=[kPA] + kmsu4, writes=[kM1])
                yield
                for i in range(2):
                    c = pi * 2 + i
                    S.op("pe", lambda e, i=i, c=c: e.matmul(PA3[:, i, 0:128], lhsT=ARbd[:, c, n, 0, :], rhs=Bbd[:, c, n, :], start=True, stop=True),
                         reads=[kAbd[c], kBbd[c]], writes=[kPA])
                    S.op("pe", lambda e, i=i, c=c: e.matmul(PA3[:, i, 128:256], lhsT=Bbd[:, c, n, :], rhs=ident[:], start=True, stop=True),
                         reads=[kBbd[c], kid], writes=[kPA])
                    S.op("pe", lambda e, i=i, c=c: e.matmul(PA3[:, i, 256:384], lhsT=Kbd[:, c, n, :], rhs=ident[:], start=True, stop=True),
                         reads=[kKbd[c], kid], writes=[kPA])
                    S.op("pe", lambda e, i=i, c=c: e.matmul(PA3[:, i, 384:448], lhsT=Vbd[:, c, n, :], rhs=fst[:], start=True, stop=True),
                         reads=[kVbd[c], kfst], writes=[kPA])
                S.op("dve", lambda e, pi=pi: e.tensor_tensor(out=NT0[:, 2 * pi:2 * pi + 2, :], in0=PA3[:, :, 0:128], in1=msl[:].unsqueeze(1).to_broadcast([128, 2, 128]), op=ALU.mult),
                     reads=[kPA, kmsl], writes=[kNT0])
                S.op("act", lambda e, pi=pi: e.activation(out=M2[:, 2 * pi:2 * pi + 2, :], in_=PA3[:, :, 128:448], func=AF.Copy), reads=[kPA], writes=[kM2])
                yield
            Tcur, kTcur = Tbs[s_][0]
            S.op("pool", lambda e, Tcur=Tcur: e.tensor_tensor(out=Tcur[:], in0=M1[:, :, 0:128], in1=ident[:].unsqueeze(1).to_broadcast([128, 4, 128]), op=ALU.add),
                 reads=[kM1, kid], writes=[kTcur])
            Nprev = lambda c: M1[:, c, 0:128]
            NTprev = lambda c: NT0[:, c, :]
            kprev = [kM1, kNT0]
            for j in range(1, 6):
                NNj, kNNj = NNs[s_][j % 2]
                for c in range(4):
                    if j < 5:
                        S.op("pe", lambda e, c=c, Nprev=Nprev, NTprev=NTprev: e.matmul(PAd[:, c, 0:128], lhsT=NTprev(c), rhs=Nprev(c), start=True, stop=True),
                             reads=kprev, writes=[kPA])
                    S.op("pe", lambda e, c=c, Nprev=Nprev, NTprev=NTprev: e.matmul(PAd[:, c, 128:256], lhsT=Nprev(c), rhs=NTprev(c), start=True, stop=True),
                         reads=kprev, writes=[kPA])
                if j < 5:
                    S.op("act", lambda e, NNj=NNj: e.activation(out=NNj[:], in_=PAd, func=AF.Copy), reads=[kPA], writes=[kNNj])
                else:
                    S.op("act", lambda e, NNj=NNj: e.activation(out=NNj[:, :, 128:256], in_=PAd[:, :, 128:256], func=AF.Copy), reads=[kPA], writes=[kNNj])
                yield
                for c in range(4):
                    S.op("pe", lambda e, c=c, NNj=NNj, Tcur=Tcur: e.matmul(PBt[:, c, :], lhsT=NNj[:, c, 128:256], rhs=Tcur[:, c, :], start=True, stop=True),
                         reads=[kNNj, kTcur], writes=[kPB0])
                Tnew, kTnew = Tbs[s_][j % 2]
                S.op("dve", lambda e, Tnew=Tnew, Tcur=Tcur: e.tensor_tensor(out=Tnew[:], in0=PBt, in1=Tcur[:], op=ALU.add), reads=[kPB0, kTcur], writes=[kTnew])
                Tcur, kTcur = Tnew, kTnew
                Nprev = (lambda NNj: (lambda c: NNj[:, c, 0:128]))(NNj)
                NTprev = (lambda NNj: (lambda c: NNj[:, c, 128:256]))(NNj)
                kprev = [kNNj]
                yield
            Tfinal[n] = (Tcur, kTcur)

        def gen_SQ(n, s_):
            M1, kM1 = M1s[s_]
            M2, kM2 = M2s[s_]
            Tcur, kTcur = Tfinal[n]
            for c in range(4):
                S.op("pe", lambda e, c=c: e.matmul(PQ0[:, c, :], lhsT=ARbd[:, c, n, 0, :], rhs=Sbf[:, c, :], start=True, stop=False),
                     reads=[kAbd[c], kSbf], writes=[kPB1])
                S.op("pe", lambda e, c=c: e.matmul(PQ0[:, c, :], lhsT=M1[:, c, 256:384], rhs=M2[:, c, 256:320], start=False, stop=True),
                     reads=[kM1, kM2], writes=[kPB1])
            S.op("act", lambda e: e.activation(out=Zb[:], in_=PQ0, func=AF.Copy), reads=[kPB1], writes=[kZb])
            yield
            for c in range(4):
                S.op("pe", lambda e, c=c: e.matmul(PQ1[:, c, :], lhsT=Tcur[:, c, :], rhs=Zb[:, c, :], start=True, stop=True),
                     reads=[kTcur, kZb], writes=[kPB1])
            S.op("act", lambda e: e.activation(out=Ub[:], in_=PQ1, func=AF.Copy), reads=[kPB1], writes=[kUb])
            yield
            for c in range(4):
                S.op("pe", lambda e, c=c: e.matmul(PQ0[:, c, :], lhsT=M2[:, c, 0:128], rhs=Ub[:, c, :], start=True, stop=False),
                     reads=[kM2, kUb], writes=[kPB1])
                S.op("pe", lambda e, c=c: e.matmul(PQ0[:, c, :], lhsT=M2[:, c, 128:256], rhs=M2[:, c, 256:320], start=False, stop=True),
                     reads=[kM2], writes=[kPB1])
            if full:
                for c in range(4):
                    S.op("pe", lambda e, c=c: e.matmul(PQ1[:, c, :], lhsT=ARbd[:, c, n, 1, :], rhs=Sbf[:, c, :], start=True, stop=False),
                         reads=[kRbd[c], kSbf], writes=[kPB1])
                    S.op("pe", lambda e, c=c: e.matmul(PQ1[:, c, :], lhsT=M1[:, c, 128:256], rhs=Ub[:, c, :], start=False, stop=False),
                         reads=[kM1, kUb], writes=[kPB1])
                    S.op("pe", lambda e, c=c: e.matmul(PQ1[:, c, :], lhsT=M1[:, c, 384:512], rhs=M2[:, c, 256:320], start=False, stop=True),
                         reads=[kM1, kM2], writes=[kPB1])
            pcb = PC[:, :, n:n + 1].to_broadcast([128, 4, 64])
            tS_, ktS = tf.next()
            tmpS = tS_[:].rearrange("p (c v) -> p c v", c=4)
            S.op("dve", lambda e: e.tensor_tensor(out=tmpS, in0=PQ0, in1=S32[:], op=ALU.add), reads=[kPB1, kS32], writes=[ktS])
            S.op("dve", lambda e: e.tensor_tensor(out=Sbf[:], in0=tmpS, in1=pcb, op=ALU.mult), reads=[ktS] + kPC, writes=[kSbf])
            S.op("pool", lambda e: e.tensor_tensor(out=S32[:], in0=tmpS, in1=pcb, op=ALU.mult), reads=[ktS] + kPC, writes=[kS32])
            yield
            if full:
                g_, kg_ = gst.next()
                ys_, kysq = tf.next()
                ysq = ys_[:].rearrange("p (c v) -> p c v", c=4)
                yc_, kycen = tf.next()
                ycen = yc_[:].rearrange("p (c v) -> p c v", c=4)
                S.op("dve", lambda e: e.tensor_reduce(out=g_[:, 0, :], in_=PQ1, axis=AX.X, op=ALU.add), reads=[kPB1], writes=[kg_])
                S.op("act", lambda e: e.activation(out=ysq, in_=PQ1, func=AF.Square), reads=[kPB1], writes=[kysq])
                S.op("dve", lambda e: e.tensor_reduce(out=g_[:, 1, :], in_=ysq, axis=AX.X, op=ALU.add), reads=[kysq], writes=[kg_])
                S.op("dve", lambda e: e.tensor_scalar(out=g_[:, 2, :], in0=g_[:, 0, :], scalar1=1.0 / 64, scalar2=1.0, op0=ALU.mult, op1=ALU.mult), reads=[kg_], writes=[kg_])
                S.op("dve", lambda e: e.tensor_tensor(out=g_[:, 3, :], in0=g_[:, 2, :], in1=g_[:, 2, :], op=ALU.mult), reads=[kg_], writes=[kg_])
                S.op("dve", lambda e: e.scalar_tensor_tensor(out=g_[:, 4, :], in0=g_[:, 1, :], scalar=1.0 / 64, in1=g_[:, 3, :], op0=ALU.mult, op1=ALU.subtract),
                     reads=[kg_], writes=[kg_])
                S.op("act", lambda e: e.activation(out=g_[:, 5, :], in_=g_[:, 4, :], func=AF.Sqrt, bias=GN_EPS), reads=[kg_], writes=[kg_])
                S.op("dve", lambda e: e.reciprocal(out=g_[:, 6, :], in_=g_[:, 5, :]), reads=[kg_], writes=[kg_])
                S.op("dve", lambda e: e.tensor_tensor(out=ycen, in0=PQ1, in1=g_[:, 2, :].unsqueeze(2).to_broadcast([128, 4, 64]), op=ALU.subtract),
                     reads=[kPB1, kg_], writes=[kycen])
                yield
                for h in range(2):
                    hs = slice(h * 64, (h + 1) * 64)
                    S.op("dve", lambda e, hs=hs: e.tensor_tensor(out=ynbd[hs, :, hs], in0=ycen[hs, :, :], in1=g_[hs, 6, :].unsqueeze(2).to_broadcast([64, 4, 64]),
                                                                  op=ALU.mult), reads=[kycen, kg_], writes=[kynbd])
                for c in range(4):
                    S.op("pe", lambda e, c=c: e.matmul(PQ0[:, c, :], lhsT=ynbd[:, c, :], rhs=fst[:], start=True, stop=True), reads=[kynbd, kfst], writes=[kPB1])
                S.op("act", lambda e: e.activation(out=ynf[:, :, n * CH:(n + 1) * CH], in_=PQ0, func=AF.Copy), reads=[kPB1], writes=kynf)
                yield

        def drain(g):
            for _ in g:
                pass

        def interleave(ga, gb, ra=2):
            a_live, b_live = ga is not None, gb is not None
            while a_live or b_live:
                for _ in range(ra):
                    if a_live:
                        try:
                            next(ga)
                        except StopIteration:
                            a_live = False
                if b_live:
                    try:
                        next(gb)
                    except StopIteration:
                        b_live = False

        def interleave_g(ga, gb, ra=2):
            a_live, b_live = ga is not None, gb is not None
            while a_live or b_live:
                for _ in range(ra):
                    if a_live:
                        try:
                            next(ga)
                        except StopIteration:
                            a_live = False
                if b_live:
                    try:
                        next(gb)
                    except StopIteration:
                        b_live = False
                yield

        def c_stage():
            yield from gen_AD(0, 0)
            for n_ in range(NCH):
                gd = gen_AD(n_ + 1, (n_ + 1) % 2) if n_ + 1 < NCH else None
                yield from interleave_g(gd, gen_SQ(n_, n_ % 2))

        def de():
            if not full:
                if t + 2 < n_tiles:
                    load_x(t + 2)
                return
            for c in range(4):
                S.op("pe", lambda e, c=c: e.matmul(psm[:, 0, :], lhsT=g2b0[:, c * 128:(c + 1) * 128], rhs=sgd0[:], start=True, stop=False), reads=[kg2b0, ksgd], writes=[kpsm])
                S.op("pe", lambda e, c=c: e.matmul(psm[:, 0, :], lhsT=g2b1[:, c * 128:(c + 1) * 128], rhs=sgd1[:], start=False, stop=True), reads=[kg2b1, ksgd], writes=[kpsm])
                y1, ky1 = tf.next()
                S.op("dve", lambda e, c=c, y1=y1: e.scalar_tensor_tensor(out=y1[:], in0=ynf[:, c, :], scalar=cc[:, O_LW + c:O_LW + c + 1], in1=bonus[:, c, :], op0=ALU.mult, op1=ALU.add),
                     reads=[kynf[c], kcc, kbonus[c]], writes=[ky1])
                S.op("dve", lambda e, c=c, y1=y1: e.scalar_tensor_tensor(out=yTr[:, c, :], in0=y1[:], scalar=cc[:, O_LB + c:O_LB + c + 1], in1=psm[:, 0, :], op0=ALU.add, op1=ALU.mult),
                     reads=[ky1, kcc, kpsm], writes=[kyTr[c]])
            if DEBUG_STOP < 9:
                return
            for blk in range(NBLK):
                for hf in range(2):
                    pflat = pp[hf][:].rearrange("p a w -> p (a w)")
                    for e_ in range(8):
                        ysrc = yc[:, e_, blk * 128:(blk + 1) * 128] if e_ < 4 else yTr[:, e_ - 4, blk * 128:(blk + 1) * 128]
                        ykey = kyc[e_] if e_ < 4 else kyTr[e_ - 4]
                        S.op("pe", lambda e, e_=e_, hf=hf, pflat=pflat, ysrc=ysrc: e.matmul(pflat, lhsT=ysrc, rhs=wout[:, e_, hf * 512:(hf + 1) * 512],
                                                                                            start=(e_ == 0), stop=(e_ == 7)), reads=[ykey, kwout], writes=[kpp[hf]])
                class _V:
                    def __init__(self, ap): self.ap = ap
                    def __getitem__(self, k): return self.ap
                _post_norm_residual(S, [(_V(pp[0][:].rearrange("p a w -> p (a w)")), kpp[0]), (_V(pp[1][:].rearrange("p a w -> p (a w)")), kpp[1])],
                                    xt[:, blk, :], kxts[b][blk], grow, kgrow, scr)
            ht_i = t - n_pre
            S.op("sp", lambda e, xt=xt, ht_i=ht_i: e.dma_start(out=hscr[W * ht_i:W * (ht_i + 1), :].rearrange("(b p) d -> p b d", p=128), in_=xt[:]),
                 reads=kxts[b], writes=[khscr[ht_i]], dma_sem=dsem[f"h{b}"])
            if t + 2 < n_tiles:
                load_x(t + 2)

        return early, b4_all, c_stage, de, b4_pre

    def drain_g(g):
        for _ in g:
            pass

    def chain_g(*gs):
        for g in gs:
            yield from g

    def interleave2(ga, gb, ra, rb):
        a_live, b_live = ga is not None, gb is not None
        while a_live or b_live:
            for _ in range(ra):
                if a_live:
                    try:
                        next(ga)
                    except StopIteration:
                        a_live = False
            for _ in range(rb):
                if b_live:
                    try:
                        next(gb)
                    except StopIteration:
                        b_live = False

    parts = [tile_parts(t_i) for t_i in range(n_tiles)]
    drain_g(parts[0][0]())
    parts[0][1]()
    for t_i in range(n_tiles):
        early_next = chain_g(parts[t_i + 1][0](), parts[t_i + 1][4]()) if t_i + 1 < n_tiles else None
        interleave2(parts[t_i][2](), early_next, 2, 1)
        parts[t_i][3]()
        if t_i + 1 < n_tiles:
            parts[t_i + 1][1]()


def _host_consts():
    km = np.zeros((128, NKM), np.float32)
    km[:, M_ID:M_ID + 128] = np.eye(128, dtype=np.float32)
    idx = np.arange(128)
    same = (idx[:, None] // 64) == (idx[None, :] // 64)
    s, t = idx[:, None] % 64, idx[None, :] % 64
    km[:, M_SU:M_SU + 128] = (same & (s < t)).astype(np.float32)
    km[:, M_IU:M_IU + 128] = (same & (s <= t)).astype(np.float32)
    km[:, M_SL:M_SL + 128] = (same & (s > t)).astype(np.float32)
    km[:, M_BO:M_BO + 128] = same.astype(np.float32)
    km[:, M_F:M_F + 64] = (idx[:, None] % 64 == np.arange(64)[None, :]).astype(np.float32)
    rst = np.ones((128, 256), np.float32)
    rst[:, ::64] = 0.0
    km[:, M_RST:M_RST + 256] = rst
    return km


def _pack_cc(inp, hmask):
    cc = np.zeros((128, NCC), np.float32)
    col = lambda v, n: np.ascontiguousarray(np.asarray(v, np.float32).reshape(n, 128).T)
    cc[:, O_PMG:O_PMG + 8] = col(inp["pre_mix_g"][0], 8)
    cc[:, O_PFG:O_PFG + 8] = col(inp["pre_ffn_g"][0], 8)
    caw = np.asarray(inp["conv_a_w"][0], np.float32)
    cc[:, O_CAW:O_CAW + 12] = caw.T.reshape(4, 128, 3).transpose(1, 0, 2).reshape(128, 12)
    mu = np.zeros(1920, np.float32)
    mu[:1824] = np.asarray(inp["shift_mu"][0], np.float32)
    cc[:, O_MU:O_MU + 15] = col(mu, 15)
    for off, name in ((O_W0, "w0"), (O_A0, "a0"), (O_KK, "k_k"), (O_KA, "k_a"), (O_LW, "lnx_w"), (O_LB, "lnx_b")):
        cc[:, off:off + 4] = col(inp[name][0], 4)
    cc[:, O_RK:O_RK + 4] = col(np.asarray(inp["r_k"][0], np.float32).reshape(512), 4)
    fcw = np.asarray(inp["ffn_conv_w"][0], np.float32)
    cc[:, O_FCW:O_FCW + 132] = fcw.T.reshape(44, 128, 3).transpose(1, 0, 2).reshape(128, 132)
    cc[:, O_FCB:O_FCB + 44] = col(inp["ffn_conv_b"][0], 44)
    cc[:, O_HM] = hmask
    return cc


_NC_CACHE = {}


def kernel(**inputs):
    n_pre, n_main = 15, 16
    x = np.asarray(inputs["x"], np.float32)
    B, T, _ = x.shape
    half = T // 2
    if "full" not in _NC_CACHE:
        _NC_CACHE["full"] = build(n_pre, n_main, "full")
    nc = _NC_CACHE["full"]
    km = _host_consts()
    f = lambda n: np.ascontiguousarray(np.asarray(inputs[n], np.float32)[0])
    in_maps = []
    for c in range(8):
        b, h = c // 2, c % 2
        xin = np.zeros((T, D), np.float32)
        if h == 0:
            xin[half:] = x[b, :half]
        else:
            xin[:] = x[b]
        in_maps.append({
            "xin": xin, "cc": _pack_cc(inputs, float(h)), "km": km,
            "post_mix_g": f("post_mix_g"), "post_ffn_g": f("post_ffn_g"),
            "w_in": f("w_in"), "w_out": f("w_out"), "w_up": f("w_up"), "w_down": f("w_down"),
            "w2": f("w2"), "a2": f("a2"), "g2": f("g2"),
        })
    res = run_bass_kernel_spmd(nc, in_maps, core_ids=list(range(8)))
    out = np.zeros((B, T, D), np.float32)
    for c in range(8):
        b, h = c // 2, c % 2
        out[b, h * half:(h + 1) * half] = res.results[c]["out"]
    return out
```

```python
import contextlib
import numpy as np
import concourse.bass as bass
import concourse.mybir as mybir
from concourse.bass_utils import run_bass_kernel_spmd

F32 = mybir.dt.float32
BF16 = mybir.dt.bfloat16
AF = mybir.ActivationFunctionType
ALU = mybir.AluOpType
AX = mybir.AxisListType

D = 1024
W = 256
NBLK = 2
CH = 64
NCH = W // CH
INC = 3360
DFF = 2816
NPAIR = 22
QC = 1536
RMS_EPS = 1e-6
GN_EPS = 64 * 1e-5
EPOCH = 12000
DEBUG_SUB = 99
EMBED_WAITS = True
TRANSITIVE = True
DEBUG_STOP = 99

O_PMG, O_PFG, O_CAW, O_MU, O_W0, O_A0, O_KK, O_KA, O_RK, O_LW, O_LB, O_FCW, O_FCB, O_HM = (
    0, 8, 16, 28, 43, 47, 51, 55, 59, 63, 67, 71, 203, 247)
NCC = 248
M_ID, M_SU, M_IU, M_SL, M_BO, M_F, M_RST = 0, 128, 256, 384, 512, 640, 704
NKM = 704 + 256


class Key:
    __slots__ = ("name", "writer", "readers", "excl")

    def __init__(self, name, excl=False):
        self.name = name
        self.writer = None
        self.readers = []
        self.excl = excl


def PKey(name):
    return Key(name, excl=True)


class Sched:
    ENGS = ("pe", "act", "dve", "pool", "sp")

    def __init__(self, nc, sem_stack, prefix):
        self.nc = nc
        self.sem_stack = sem_stack
        self.prefix = prefix
        self.ops = {e: [] for e in self.ENGS}
        self.count = {e: 0 for e in self.ENGS}
        self.sems = {}
        self.waited = {e: {} for e in self.ENGS}
        self.dma_counts = {}
        self.last_tok = {e: None for e in self.ENGS}
        self.tok_order = {}
        self.n_tok = 0
        self.tok_know = {}

    def _eng_sem(self, eng, idx):
        sid = f"{self.prefix}s_{eng}_{idx // EPOCH}"
        self.sems.setdefault(sid, None)
        return sid, (idx % EPOCH) + 1

    def new_dma_sem(self, name):
        sid = f"{self.prefix}d_{name}"
        assert sid not in self.sems, sid
        self.sems[sid] = None
        self.dma_counts[sid] = 0
        return sid

    def _need_waits(self, eng, tokens):
        w = self.waited[eng]
        cand = {}
        for t in tokens:
            if t is None:
                continue
            sid, val, _ = t
            if w.get(sid, 0) >= val:
                continue
            if cand.get(sid, (0, None))[0] < val:
                cand[sid] = (val, t)
        out = []
        for sid, (val, t) in sorted(cand.items(), key=lambda kv: -self.tok_order.get(kv[1][1], 0)):
            if w.get(sid, 0) >= val:
                continue
            out.append((sid, val))
            w[sid] = val
            if TRANSITIVE:
                for s2, v2 in self.tok_know.get(t, {}).items():
                    if w.get(s2, 0) < v2:
                        w[s2] = v2
        return out

    def op(self, eng, fn, reads=(), writes=(), dma_sem=None, multi=False):
        toks = []
        raw = set()
        for k in reads:
            toks.append(k.writer)
            if k.writer is not None:
                raw.add(k.writer)
            if k.excl:
                toks.extend(r for r in k.readers if r[2] != eng)
        for k in writes:
            toks.append(k.writer)
            toks.extend(k.readers)
        if eng == "pe":
            toks = [t for t in toks if t is not None and t[2] != "pe"]
        waits = self._need_waits(eng, toks)
        if dma_sem is None:
            idx = self.count[eng]
            self.count[eng] += 1
            sid, val = self._eng_sem(eng, idx)
            tok = (sid, val, eng)
            inc = (sid, 1)
            self.last_tok[eng] = tok
        else:
            self.dma_counts[dma_sem] += 16
            tok = (dma_sem, self.dma_counts[dma_sem], "dma")
            inc = (dma_sem, 16)
        self.n_tok += 1
        self.tok_order[tok] = self.n_tok
        know = dict(self.waited[eng])
        if dma_sem is None and tok[1] > 1:
            know[tok[0]] = tok[1] - 1
        self.tok_know[tok] = know
        embed = EMBED_WAITS and dma_sem is None and eng != "pe" and not multi
        self.ops[eng].append((fn, waits, inc, embed))
        for k in reads:
            k.readers.append(tok)
        for k in writes:
            k.writer = tok
            k.readers = []
        return tok

    def barrier(self, extra_keys=()):
        toks = [t for t in self.last_tok.values() if t is not None]
        for k in extra_keys:
            toks.append(k.writer)
            toks.extend(k.readers)
        for eng in self.ENGS:
            waits = self._need_waits(eng, [t for t in toks if t is not None and t[2] != eng])
            if waits:
                self.ops[eng].append((None, waits, None, False))

    def final_wait(self, eng, keys):
        toks = []
        for k in keys:
            toks.append(k.writer)
            toks.extend(k.readers)
        waits = self._need_waits(eng, toks)
        self.ops[eng].append((None, waits, None, False))

    def emit(self):
        nc = self.nc
        with contextlib.ExitStack() as st:
            handles = {sid: self.sem_stack.enter_context(nc.semaphore(sid)) for sid in self.sems}
            block = st.enter_context(nc.Block())

            def run(engobj, lst):
                for fn, waits, inc, embed in lst:
                    emb = waits[-1] if (embed and waits and fn is not None) else None
                    for sid, val in (waits[:-1] if emb is not None else waits):
                        engobj.wait_ge(handles[sid], val)
                    if fn is not None:
                        n0 = nc.n_instructions()
                        ins = fn(engobj)
                        if emb is not None:
                            assert nc.n_instructions() - n0 == 1, "embedded wait on a multi-instruction op"
                            ins._wait_ge(handles[emb[0]], emb[1])
                        ins.then_inc(handles[inc[0]], inc[1])

            @block.tensor
            def _(e):
                run(e, self.ops["pe"])

            @block.scalar
            def _(e):
                run(e, self.ops["act"])

            @block.vector
            def _(e):
                run(e, self.ops["dve"])

            @block.gpsimd
            def _(e):
                run(e, self.ops["pool"])

            @block.sync
            def _(e):
                run(e, self.ops["sp"])


class View:
    def __init__(self, ap):
        self.ap = ap

    def __getitem__(self, k):
        return self.ap


class View3:
    def __init__(self, x):
        self.x = x

    def __getitem__(self, k):
        return self.x[k[0], k[1], 128:256]


class Rot:
    def __init__(self, tiles, excl=False, keys=None):
        self.tiles = tiles
        self.keys = keys if keys is not None else [Key(f"rot{i}", excl) for i in range(len(tiles))]
        self.i = 0

    def next(self):
        j = self.i % len(self.tiles)
        self.i += 1
        return self.tiles[j], self.keys[j]


def _rms_transpose(S, src, ksrc, dstT, kdst, tcol, scr, ident, kid, ptr, extra_scale=None, kextra=None):
    st, kst = scr["stat"].next()
    xs, kxs = scr["xs"].next()
    pt, kpt = ptr.next()
    S.op("act", lambda e: e.activation(out=xs[:], in_=src, func=AF.Square, accum_out=st[:, 0:1]),
         reads=[ksrc], writes=[kxs, kst], multi=True)
    S.op("act", lambda e: e.activation(out=st[:, 1:2], in_=st[:, 0:1], func=AF.Sqrt, scale=1.0 / D, bias=RMS_EPS),
         reads=[kst], writes=[kst])
    S.op("dve", lambda e: e.reciprocal(out=st[:, 2:3], in_=st[:, 1:2]), reads=[kst], writes=[kst])
    rs = st[:, 2:3]
    if extra_scale is not None:
        S.op("dve", lambda e: e.tensor_tensor(out=st[:, 3:4], in0=st[:, 2:3], in1=extra_scale, op=ALU.mult),
             reads=[kst, kextra], writes=[kst])
        rs = st[:, 3:4]
    S.op("pool", lambda e: e.tensor_scalar(out=xs[:], in0=src, scalar1=rs, scalar2=1.0, op0=ALU.mult, op1=ALU.mult),
         reads=[ksrc, kst], writes=[kxs])
    for kc in range(8):
        S.op("pe", lambda e, kc=kc: e.transpose(out=pt[:, kc, :], in_=xs[:, kc * 128:(kc + 1) * 128], identity=ident[:]),
             reads=[kxs, kid], writes=[kpt])
    S.op("act", lambda e: e.activation(out=dstT[:, :, tcol:tcol + 128], in_=pt[:], func=AF.Copy),
         reads=[kpt], writes=[kdst])


def _post_norm_residual(S, pd_pairs, res, kres, grow, kgrow, scr):
    st, kst = scr["stat"].next()
    tmps = [scr["tmp512"].next() for _ in range(2)]
    for hf, (pd, kpd) in enumerate(pd_pairs):
        junk, kjunk = tmps[hf]
        S.op("act", lambda e, pd=pd, hf=hf, junk=junk: e.activation(out=junk[:], in_=pd[:], func=AF.Square,
                                                                  accum_out=st[:, hf:hf + 1]),
             reads=[kpd], writes=[kjunk, kst], multi=True)
    S.op("dve", lambda e: e.tensor_tensor(out=st[:, 2:3], in0=st[:, 0:1], in1=st[:, 1:2], op=ALU.add), reads=[kst], writes=[kst])
    S.op("act", lambda e: e.activation(out=st[:, 3:4], in_=st[:, 2:3], func=AF.Sqrt, scale=1.0 / D, bias=RMS_EPS),
         reads=[kst], writes=[kst])
    S.op("dve", lambda e: e.reciprocal(out=st[:, 4:5], in_=st[:, 3:4]), reads=[kst], writes=[kst])
    for hf, (pd, kpd) in enumerate(pd_pairs):
        tmp, ktmp = tmps[hf]
        S.op("dve", lambda e, pd=pd, hf=hf, tmp=tmp: e.scalar_tensor_tensor(
            out=tmp[:], in0=pd[:], scalar=st[:, 4:5], in1=grow[:, hf * 512:(hf + 1) * 512], op0=ALU.mult, op1=ALU.mult),
            reads=[kpd, kst, kgrow], writes=[ktmp])
        S.op("pool", lambda e, hf=hf, tmp=tmp: e.tensor_tensor(out=res[:, hf * 512:(hf + 1) * 512], in0=res[:, hf * 512:(hf + 1) * 512],
                                                               in1=tmp[:], op=ALU.add),
             reads=[ktmp, kres], writes=[kres])


def phase2_ffn(nc, S, st, io, n_main, shared):
    sb = lambda n, s, d: st.enter_context(nc.sbuf_tensor(n, s, d))
    ps = lambda n, s, d: st.enter_context(nc.psum_tensor(n, s, d))
    cc, kcc = shared["cc"], shared["kcc"]
    ident, kid = shared["ident"], shared["kid"]
    hscr, khscr = io["hscr"], io["khscr"]
    out = io["out"]

    wup = sb("wup", [128, 8, DFF * 2], BF16)
    wdn = sb("wdn", [128, NPAIR, D], BF16)
    kwup = [Key(f"wup{k}") for k in range(8)]
    kwdn = Key("wdn")
    grow = sb("grow2", [128, D], F32)
    kgrow = Key("grow2")
    fh = sb("fh", [128, NPAIR, 2, 2], F32)
    kfh = [Key(f"fh{i}") for i in range(NPAIR)]
    hts = [sb(f"ht{i}", [128, NBLK, D], F32) for i in range(2)]
    khts = [[Key(f"ht{i}_{b}") for b in range(NBLK)] for i in range(2)]
    hnTs = [sb(f"hnT{i}", [128, 8, W], BF16) for i in range(2)]
    khnT = [Key(f"hnT{i}") for i in range(2)]
    act = sb("actb", [128, NPAIR, W], BF16)
    kact = [Key(f"act{i}") for i in range(NPAIR)]
    scr = {
        "stat": Rot([sb(f"stat{i}", [128, 8], F32) for i in range(4)]),
        "xs": Rot([sb(f"xs{i}", [128, D], BF16) for i in range(2)]),
        "tmp512": Rot([sb(f"tmp512_{i}", [128, 512], F32) for i in range(2)]),
    }
    fbuf = Rot([sb(f"fbuf{i}", [128, 2, W + 2], F32) for i in range(4)])
    cg = Rot([sb(f"cg{i}", [128, W], F32) for i in range(4)])
    cu = Rot([sb(f"cu{i}", [128, W], F32) for i in range(4)])
    t1 = Rot([sb(f"t1_{i}", [128, W], F32) for i in range(3)])
    t2 = Rot([sb(f"t2_{i}", [128, W], F32) for i in range(3)])
    sg = Rot([sb(f"sg{i}", [128, W], F32) for i in range(3)])
    WQ = DFF // 4
    ptr = Rot([ps("ptr2", [128, 8, 128], BF16)], excl=True)
    pf = Rot([ps(f"pf{i}", [128, 2, W], F32) for i in range(3)], excl=True)
    pd = [[ps(f"pd{b}{h}", [128, 512], F32) for h in range(2)] for b in range(NBLK)]
    kpd = [[PKey(f"pd{b}{h}") for h in range(2)] for b in range(NBLK)]
    dsem = {n: S.new_dma_sem("p2_" + n) for n in ("wst0", "wst1", "wst2", "wst3", "wdn", "grow", "h0", "h1", "hw", "out0", "out1")}

    S.op("sp", lambda e: e.dma_start(out=grow[:], in_=io["post_ffn_g"].partition_broadcast(128)), writes=[kgrow], dma_sem=dsem["grow"])
    actf = act[:].rearrange("p i w -> p (i w)").bitcast(F32)
    kstg = [Key(f"stg{q}") for q in range(4)]
    for kc in range(8):
        for q in range(8):
            j = kc * 8 + q
            r_ = j % 4
            wt, kwt = actf[:, r_ * WQ:(r_ + 1) * WQ], kstg[r_]
            S.op("sp", lambda e, wt=wt, kc=kc, q=q: e.dma_start(out=wt, in_=io["w_up"][kc * 128:(kc + 1) * 128, q * WQ:(q + 1) * WQ]),
                 writes=[kwt], dma_sem=dsem[f"wst{r_}"])
            if j % 2 == 0:
                S.op("act", lambda e, wt=wt, kc=kc, q=q: e.activation(out=wup[:, kc, q * WQ:(q + 1) * WQ], in_=wt, func=AF.Copy,
                                                                       scale=cc[:, O_PFG + kc:O_PFG + kc + 1]),
                     reads=[kwt, kcc], writes=[kwup[kc]])
            else:
                S.op("dve", lambda e, wt=wt, kc=kc, q=q: e.tensor_scalar(out=wup[:, kc, q * WQ:(q + 1) * WQ], in0=wt,
                                                                          scalar1=cc[:, O_PFG + kc:O_PFG + kc + 1], scalar2=1.0, op0=ALU.mult, op1=ALU.mult),
                     reads=[kwt, kcc], writes=[kwup[kc]])
    S.op("pool", lambda e: e.memset(act[:, :, 0:1], 0.0), writes=kstg + kact)
    S.op("pool", lambda e: e.dma_start(out=wdn[:], in_=io["w_down"].rearrange("(i p) d -> p i d", p=128)), writes=[kwdn], dma_sem=dsem["wdn"])

    hw = hts[1]
    S.op("sp", lambda e: e.dma_start(out=hw[:, 0, :], in_=hscr[W - 128:W, :]), reads=[khscr[0]], writes=[khts[1][0]], dma_sem=dsem["hw"])
    _rms_transpose(S, hw[:, 0, :], khts[1][0], hnTs[1], khnT[1], 0, scr, ident, kid, ptr,
                   extra_scale=cc[:, O_HM:O_HM + 1], kextra=kcc)
    pfh, kpfh = pf.next()
    pfh_v = pfh[:].rearrange("p a w -> p (a w)")
    for ch in range(2 * NPAIR):
        i, hf = ch % NPAIR, ch // NPAIR
        col = (i * 2 + hf) * 2
        for kc in range(8):
            S.op("pe", lambda e, ch=ch, kc=kc, col=col: e.matmul(pfh_v[:, col:col + 2], lhsT=wup[:, kc, ch * 128:(ch + 1) * 128],
                                                                  rhs=hnTs[1][:, kc, 126:128], start=(kc == 0), stop=(kc == 7)),
                 reads=[kwup[kc], khnT[1]], writes=[kpfh])
    S.op("act", lambda e: e.activation(out=fh[:].rearrange("p i a b -> p (i a b)"), in_=pfh_v[:, 0:NPAIR * 4], func=AF.Copy),
         reads=[kpfh], writes=kfh)

    def load(t):
        b = t % 2
        S.op("sp", lambda e: e.dma_start(out=hts[b][:], in_=hscr[W * (1 + t):W * (2 + t), :].rearrange("(b p) d -> p b d", p=128)),
             reads=[khscr[1 + t]], writes=khts[b], dma_sem=dsem[f"h{b}"])

    def prologue(t):
        b = t % 2
        for blk in range(NBLK):
            _rms_transpose(S, hts[b][:, blk, :], khts[b][blk], hnTs[b], khnT[b], blk * 128, scr, ident, kid, ptr)

    state = {}

    def up_mm(t, i):
        b = t % 2
        p, kp = pf.next()
        state[(t, i)] = (p, kp)
        for hf in range(2):
            ch = hf * NPAIR + i
            for kc in range(8):
                S.op("pe", lambda e, p=p, hf=hf, ch=ch, kc=kc: e.matmul(p[:, hf, :], lhsT=wup[:, kc, ch * 128:(ch + 1) * 128],
                                                                         rhs=hnTs[b][:, kc, :], start=(kc == 0), stop=(kc == 7)),
                     reads=[kwup[kc], khnT[b]], writes=[kp])

    def elem(t, i):
        p, kp = state.pop((t, i))
        fb, kfb = fbuf.next()
        S.op("pool", lambda e: e.tensor_copy(out=fb[:, :, 0:2], in_=fh[:, i, :, :]), reads=[kfh[i]], writes=[kfb])
        S.op("act", lambda e: e.activation(out=fb[:, :, 2:W + 2], in_=p[:], func=AF.Copy), reads=[kp], writes=[kfb])
        yield
        S.op("pool", lambda e: e.tensor_copy(out=fh[:, i, :, :], in_=fb[:, :, W:W + 2]), reads=[kfb], writes=[kfh[i]])
        outs = []
        for hf, rot in ((0, cg), (1, cu)):
            ch = hf * NPAIR + i
            c, kc_ = rot.next()
            wof = O_FCW + ch * 3
            S.op("act", lambda e, c=c, hf=hf, wof=wof, ch=ch: e.activation(out=c[:], in_=fb[:, hf, 2:W + 2], func=AF.Identity,
                                                                          scale=cc[:, wof + 2:wof + 3], bias=cc[:, O_FCB + ch:O_FCB + ch + 1]),
                 reads=[kfb, kcc], writes=[kc_])
            outs.append((c, kc_, hf, wof))
        yield
        for c, kc_, hf, wof in outs:
            S.op("dve", lambda e, c=c, hf=hf, wof=wof: e.scalar_tensor_tensor(out=c[:], in0=fb[:, hf, 1:W + 1], scalar=cc[:, wof + 1:wof + 2],
                                                                             in1=c[:], op0=ALU.mult, op1=ALU.add),
                 reads=[kfb, kcc, kc_], writes=[kc_])
        yield
        for c, kc_, hf, wof in outs:
            S.op("dve", lambda e, c=c, hf=hf, wof=wof: e.scalar_tensor_tensor(out=c[:], in0=fb[:, hf, 0:W], scalar=cc[:, wof:wof + 1],
                                                                             in1=c[:], op0=ALU.mult, op1=ALU.add),
                 reads=[kfb, kcc, kc_], writes=[kc_])
        outs = [(c, kc_) for c, kc_, hf, wof in outs]
        yield
        (g_, kg), (u_, ku) = outs
        a1, ka1 = t1.next()
        a2, ka2 = t2.next()
        s_, ks = sg.next()
        S.op("pool", lambda e: e.tensor_tensor(out=a1[:], in0=g_[:], in1=g_[:], op=ALU.mult), reads=[kg], writes=[ka1])
        yield
        S.op("pool", lambda e: e.tensor_scalar(out=a1[:], in0=a1[:], scalar1=0.044715, scalar2=1.0, op0=ALU.mult, op1=ALU.add),
             reads=[ka1], writes=[ka1])
        S.op("pool", lambda e: e.tensor_tensor(out=a2[:], in0=a1[:], in1=g_[:], op=ALU.mult), reads=[ka1, kg], writes=[ka2])
        S.op("dve", lambda e: e.tensor_tensor(out=a1[:], in0=g_[:], in1=u_[:], op=ALU.mult), reads=[kg, ku, ka2], writes=[ka1])
        yield
        S.op("act", lambda e: e.activation(out=s_[:], in_=a2[:], func=AF.Sigmoid, scale=1.5957691216), reads=[ka2], writes=[ks])
        yield
        S.op("dve", lambda e: e.tensor_tensor(out=act[:, i, :], in0=a1[:], in1=s_[:], op=ALU.mult), reads=[ka1, ks], writes=[kact[i]])

    def down_mm(t, i):
        for blk in range(NBLK):
            for hf in range(2):
                S.op("pe", lambda e, blk=blk, hf=hf: e.matmul(pd[blk][hf][:], lhsT=act[:, i, blk * 128:(blk + 1) * 128],
                                                              rhs=wdn[:, i, hf * 512:(hf + 1) * 512], start=(i == 0), stop=(i == NPAIR - 1)),
                     reads=[kact[i], kwdn], writes=[kpd[blk][hf]])

    def epilogue(t):
        b = t % 2
        for blk in range(NBLK):
            _post_norm_residual(S, [(pd[blk][0], kpd[blk][0]), (pd[blk][1], kpd[blk][1])], hts[b][:, blk, :], khts[b][blk], grow, kgrow, scr)
        S.op("sp", lambda e: e.dma_start(out=out[W * t:W * (t + 1), :].rearrange("(b p) d -> p b d", p=128), in_=hts[b][:]),
             reads=khts[b], dma_sem=dsem[f"out{b}"])

    load(0)
    if n_main > 1:
        load(1)
    prologue(0)
    TSTEP = 2
    for t in range(n_main):
        active = []
        i_next, tick, pro_done = 0, 0, False
        while i_next < NPAIR or active:
            if i_next < NPAIR and tick % TSTEP == 0:
                up_mm(t, i_next)
                active.append((i_next, elem(t, i_next)))
                i_next += 1
                if i_next == 15 and t + 1 < n_main and not pro_done:
                    prologue(t + 1)
                    pro_done = True
            for item in list(active):
                i_, g_ = item
                try:
                    next(g_)
                except StopIteration:
                    active.remove(item)
                    down_mm(t, i_)
            tick += 1
        epilogue(t)
        if t + 2 < n_main:
            load(t + 2)
    S.final_wait("sp", [k for ks_ in khts for k in ks_])


def build(n_pre, n_main, mode="full"):
    nc = bass.Bass("TRN2", target_bir_lowering=False)
    TT = (n_pre + 1 + n_main) * W
    io = {}
    di = lambda n, s: nc.dram_tensor(n, s, F32, kind="ExternalInput").ap()
    io["cc"] = di("cc", [128, NCC])
    io["km"] = di("km", [128, NKM])
    io["post_ffn_g"] = di("post_ffn_g", [D])
    io["w_up"] = di("w_up", [D, 2 * DFF])
    io["w_down"] = di("w_down", [DFF, D])
    if mode == "ffn":
        io["hscr"] = di("hscr", [(1 + n_main) * W, D])
    else:
        io["xin"] = di("xin", [TT, D])
        io["post_mix_g"] = di("post_mix_g", [D])
        io["w_in"] = di("w_in", [D, INC])
        io["w_out"] = di("w_out", [D, D])
        io["w2"] = di("w2", [64, 512])
        io["a2"] = di("a2", [64, 512])
        io["g2"] = di("g2", [160, 512])
        io["hscr"] = nc.dram_tensor("hscr", [(1 + n_main) * W, D], F32, kind="Internal").ap()
    io["khscr"] = [Key(f"hscr{i}") for i in range(1 + n_main)]
    io["out"] = nc.dram_tensor("out", [n_main * W, D], F32, kind="ExternalOutput").ap()

    with contextlib.ExitStack() as sem_stack, contextlib.ExitStack() as st0:
        cc = st0.enter_context(nc.sbuf_tensor("cc_sb", [128, NCC], F32))
        ident = st0.enter_context(nc.sbuf_tensor("ident", [128, 128], BF16))

        def shared_loads(S):
            kcc, kid = Key("cc"), Key("ident")
            d0 = S.new_dma_sem("cc")
            d1 = S.new_dma_sem("ident")
            S.op("sp", lambda e: e.dma_start(out=cc[:], in_=io["cc"][:, :]), writes=[kcc], dma_sem=d0)
            S.op("pool", lambda e: e.dma_start(out=ident[:], in_=io["km"][:, M_ID:M_ID + 128]), writes=[kid], dma_sem=d1)
            return {"cc": cc, "kcc": kcc, "ident": ident, "kid": kid}

        if mode != "ffn":
            S1 = Sched(nc, sem_stack, "a")
            shared = shared_loads(S1)
            with contextlib.ExitStack() as st1:
                phase1_mixer(nc, S1, st1, io, n_pre, n_main, shared)
                S1.final_wait("sp", io["khscr"])
                S1.emit()
            S2 = Sched(nc, sem_stack, "b")
            shared = {"cc": cc, "kcc": Key("cc2"), "ident": ident, "kid": Key("ident2")}
            io["khscr"] = [Key(f"hscr2_{i}") for i in range(1 + n_main)]
        else:
            S2 = Sched(nc, sem_stack, "b")
            shared = shared_loads(S2)
        with contextlib.ExitStack() as st2:
            phase2_ffn(nc, S2, st2, io, n_main, shared)
            S2.emit()
    return nc


def phase1_mixer(nc, S, st, io, n_pre, n_main, shared):
    sb = lambda n, s, d: st.enter_context(nc.sbuf_tensor(n, s, d))
    ps = lambda n, s, d: st.enter_context(nc.psum_tensor(n, s, d))
    cc, kcc = shared["cc"], shared["kcc"]
    ident, kid = shared["ident"], shared["kid"]
    xin, hscr, khscr = io["xin"], io["hscr"], io["khscr"]
    n_tiles = n_pre + 1 + n_main
    C05 = 0.6065306597126334

    win = sb("win", [128, 8, INC], BF16)
    kwin = [Key(f"win{k}") for k in range(8)]
    wout = sb("wout", [128, 8, D], BF16)
    kwout = Key("wout")
    w2b = sb("w2b", [128, 512], BF16)
    a2b = sb("a2b", [128, 512], BF16)
    g2b0 = sb("g2b0", [128, 512], BF16)
    g2b1 = sb("g2b1", [128, 512], BF16)
    wg1 = sb("wg1", [128, 8, 128], BF16)
    kwg1 = Key("wg1")
    ksmallw = Key("smallw")
    msu4 = sb("msu4", [128, 512], BF16)
    msl = sb("msl", [128, 128], BF16)
    bones = sb("bones", [128, 128], BF16)
    fst = sb("fst", [128, 64], BF16)
    rst = sb("rst", [128, W], F32)
    kconst = Key("p1const")
    grow = sb("grow1", [128, D], F32)
    kgrow = Key("grow1")
    dc = sb("dc", [128, 20], F32)
    kdc = Key("dc")
    dsem = {n: S.new_dma_sem("p1_" + n) for n in ("wst0", "wst1", "wst2", "wst3", "const", "grow", "x0", "x1", "h0", "h1")}

    S.op("sp", lambda e: e.dma_start(out=grow[:], in_=io["post_mix_g"].partition_broadcast(128)), writes=[kgrow], dma_sem=dsem["grow"])
    S.op("sp", lambda e: e.dma_start(out=rst[:], in_=io["km"][:, M_RST:M_RST + W]), writes=[kconst], dma_sem=dsem["const"])
    uniq = [0]

    def pool_dma(fn, key):
        uniq[0] += 1
        S.op("pool", fn, writes=[key], dma_sem=S.new_dma_sem(f"p1u{uniq[0]}"))

    kmsl, kbones, kfst = Key("msl"), Key("bones"), Key("fst")
    kmsu4 = [Key(f"msu4_{q}") for q in range(4)]
    kw2b, ka2b, kg2b0, kg2b1 = Key("w2b"), Key("a2b"), Key("g2b0"), Key("g2b1")
    for dst, c0, n_, k_ in ((msl, M_SL, 128, kmsl), (bones, M_BO, 128, kbones), (fst, M_F, 64, kfst)):
        pool_dma(lambda e, dst=dst, c0=c0, n_=n_: e.dma_start(out=dst[:], in_=io["km"][:, c0:c0 + n_]), k_)
    for q, c0 in enumerate((M_SU, M_IU, M_SU, M_IU)):
        pool_dma(lambda e, q=q, c0=c0: e.dma_start(out=msu4[:, q * 128:(q + 1) * 128], in_=io["km"][:, c0:c0 + 128]), kmsu4[q])
    S.op("pool", lambda e: e.memset(w2b[:], 0.0), writes=[kw2b])
    S.op("pool", lambda e: e.memset(a2b[:], 0.0), writes=[ka2b])
    S.op("pool", lambda e: e.memset(g2b1[:], 0.0), writes=[kg2b1])
    S.op("pool", lambda e: e.memset(wg1[:], 0.0), writes=[kwg1])
    pool_dma(lambda e: e.dma_start(out=w2b[0:64, :], in_=io["w2"][:, :]), kw2b)
    pool_dma(lambda e: e.dma_start(out=a2b[64:128, :], in_=io["a2"][:, :]), ka2b)
    pool_dma(lambda e: e.dma_start(out=g2b0[:], in_=io["g2"][0:128, :]), kg2b0)
    pool_dma(lambda e: e.dma_start(out=g2b1[0:32, :], in_=io["g2"][128:160, :]), kg2b1)
    pool_dma(lambda e: e.dma_start(out=wout[:], in_=io["w_out"].rearrange("(k p) d -> p k d", p=128)), kwout)
    S.op("dve", lambda e: e.tensor_scalar(out=dc[:, 0:15], in0=cc[:, O_MU:O_MU + 15], scalar1=-1.0, scalar2=1.0, op0=ALU.mult, op1=ALU.add),
         reads=[kcc], writes=[kdc])
    S.op("dve", lambda e: e.tensor_scalar(out=dc[:, 15:19], in0=cc[:, O_KA:O_KA + 4], scalar1=-1.0, scalar2=1.0, op0=ALU.mult, op1=ALU.add),
         reads=[kcc], writes=[kdc])
    xts = [sb(f"xt{i}", [128, NBLK, D], F32) for i in range(2)]
    kxts = [[Key(f"xt{i}_{b}") for b in range(NBLK)] for i in range(2)]
    xnT = sb("xnT", [128, 8, W], BF16)
    kxnT = Key("xnT")
    scr = {
        "stat": Rot([sb(f"stat1_{i}", [128, 8], F32) for i in range(4)]),
        "xs": Rot([sb(f"xs1_{i}", [128, D], BF16) for i in range(2)]),
    }
    tf = Rot([sb(f"tf{i}", [128, W], F32) for i in range(9)])
    tb = Rot([sb(f"tb{i}", [128, W], BF16) for i in range(4)])
    qbuf = Rot([sb(f"qbuf{i}", [128, W + 1], F32) for i in range(2)])
    ubuf = Rot([sb(f"ubuf{i}", [128, W + 2], F32) for i in range(2)])
    qh = sb("qh", [128, 15], F32)
    kqh = [Key(f"qh{j}") for j in range(15)]
    uh = sb("uh", [128, 4, 2], F32)
    kuh = [Key(f"uh{c}") for c in range(4)]
    rkv = {n: Rot([sb(f"{n}c{i}", [128, W], F32) for i in range(2)]) for n in ("r", "k", "v")}
    wa = sb("wa", [128, W], F32); kwa = Key("wa")
    g0t = sb("g0t", [128, W], F32); kg0 = Key("g0t")
    g1t = sb("g1t", [128, W], F32); kg1 = Key("g1t")
    twad = sb("twad", [128, W], BF16); ktwad = Key("twad")
    sgds = [(sb(f"sgd0_{i}", [128, W], BF16), sb(f"sgd1_{i}", [128, W], BF16), Key(f"sgd{i}")) for i in range(2)]
    sig = sb("sig", [128, 4, W], F32); ksig = [Key(f"sig{c}") for c in range(4)]
    a4 = sb("a4", [128, 4, W], F32); ka4 = [Key(f"a4{c}") for c in range(4)]
    bonus = sb("bonus", [128, 4, W], F32); kbonus = [Key(f"bonus{c}") for c in range(4)]
    ARbd = sb("ARbd", [128, 4, NCH, 2, 128], BF16); kAbd = [Key(f"Abd{c}") for c in range(4)]; kRbd = [Key(f"Rbd{c}") for c in range(4)]
    Bbd = sb("Bbd", [128, 4, NCH, 128], BF16); kBbd = [Key(f"Bbd{c}") for c in range(4)]
    Kbd = sb("Kbd", [128, 4, NCH, 128], BF16); kKbd = [Key(f"Kbd{c}") for c in range(4)]
    Vbd = sb("Vbd", [128, 4, NCH, 128], BF16); kVbd = [Key(f"Vbd{c}") for c in range(4)]
    PC = sb("PC", [128, 4, NCH], F32); kPC = [Key(f"PC{c}") for c in range(4)]
    M1s = [(sb(f"M1_{i}", [128, 4, 512], BF16), Key(f"M1_{i}")) for i in range(2)]
    NT0s = [(sb(f"NT0_{i}", [128, 4, 128], BF16), Key(f"NT0_{i}")) for i in range(2)]
    M2s = [(sb(f"M2_{i}", [128, 4, 320], BF16), Key(f"M2_{i}")) for i in range(2)]
    NNs = [[(sb(f"NN{i}{j}", [128, 4, 256], BF16), Key(f"NN{i}{j}")) for j in range(2)] for i in range(2)]
    Tbs = [[(sb(f"Tb{i}{j}", [128, 4, 128], BF16), Key(f"Tb{i}{j}")) for j in range(2)] for i in range(2)]
    Zb = sb("Zb", [128, 4, 64], BF16); kZb = Key("Zb")
    Ub = sb("Ub", [128, 4, 64], BF16); kUb = Key("Ub")
    S32 = sb("S32", [128, 4, 64], F32); kS32 = Key("S32")
    Sbf = sb("Sbf", [128, 4, 64], BF16); kSbf = Key("Sbf")
    gst = Rot([sb(f"gst{i}", [128, 8, 4], F32) for i in range(2)])
    ynbd = sb("ynbd", [128, 4, 128], BF16); kynbd = Key("ynbd")
    ynf = sb("ynf", [128, 4, W], F32); _ka, _kb = Key("ynfA"), Key("ynfB"); kynf = [_ka, _ka, _kb, _kb]
    scr["tmp512"] = Rot([View(ynf[:, 0:2, :].rearrange("p c w -> p (c w)")), View(ynf[:, 2:4, :].rearrange("p c w -> p (c w)"))], keys=[_ka, _kb])
    yTc = [sb(f"yTc{i}", [128, 4, W], BF16) for i in range(2)]; kyTc = [[Key(f"yTc{i}_{c}") for c in range(4)] for i in range(2)]
    yTr = sb("yTr", [128, 4, W], BF16); kyTr = [Key(f"yTr{c}") for c in range(4)]

    WQ = INC // 4
    stg = [(sig, ksig), (a4, ka4), (bonus, kbonus), (ynf, kynf)]
    for kc in range(8):
        for q in range(4):
            j = kc * 4 + q
            buf, kbuf = stg[j % 4]
            wt = buf[:].rearrange("p c w -> p (c w)")[:, 0:WQ]
            S.op("sp", lambda e, wt=wt, kc=kc, q=q: e.dma_start(out=wt, in_=io["w_in"][kc * 128:(kc + 1) * 128, q * WQ:(q + 1) * WQ]),
                 writes=kbuf, dma_sem=dsem[f"wst{j % 4}"])
            if j % 2 == 0:
                S.op("act", lambda e, wt=wt, kc=kc, q=q: e.activation(out=win[:, kc, q * WQ:(q + 1) * WQ], in_=wt, func=AF.Copy,
                                                                       scale=cc[:, O_PMG + kc:O_PMG + kc + 1]),
                     reads=kbuf + [kcc], writes=[kwin[kc]])
            else:
                S.op("dve", lambda e, wt=wt, kc=kc, q=q: e.tensor_scalar(out=win[:, kc, q * WQ:(q + 1) * WQ], in0=wt,
                                                                          scalar1=cc[:, O_PMG + kc:O_PMG + kc + 1], scalar2=1.0, op0=ALU.mult, op1=ALU.mult),
                     reads=kbuf + [kcc], writes=[kwin[kc]])
    S.op("dve", lambda e: e.tensor_copy(out=wg1[:, :, 0:32], in_=win[:, :, INC - 32:INC]), reads=kwin + [kwg1], writes=[kwg1])

    ptr = Rot([ps("ptr1", [128, 8, 128], BF16)], excl=True)
    pp = [ps(f"pp{i}", [128, 2, W], F32) for i in range(2)]
    kpp = [PKey(f"pp{i}") for i in range(2)]
    psm = ps("psm", [128, 2, W], F32); kpsm = PKey("psm")
    PA = ps("PA", [128, 1024], F32); kPA = PKey("PA")
    PB = ps("PB", [128, 1024], F32); kPB0 = PKey("PB0"); kPB1 = PKey("PB1")

    for t_, k_ in ((qh, kqh), (uh, kuh)):
        S.op("pool", lambda e, t_=t_: e.memset(t_[:], 0.0), writes=k_)
    S.op("pool", lambda e: e.memset(S32[:], 0.0), writes=[kS32])
    S.op("pool", lambda e: e.memset(Sbf[:], 0.0), writes=[kSbf])
    S.op("pool", lambda e: e.memset(ARbd[:], 0.0), writes=kAbd + kRbd)
    S.op("pool", lambda e: e.memset(Bbd[:], 0.0), writes=kBbd)
    S.op("pool", lambda e: e.memset(Kbd[:], 0.0), writes=kKbd)
    S.op("pool", lambda e: e.memset(Vbd[:], 0.0), writes=kVbd)
    S.op("pool", lambda e: e.memset(ynbd[:], 0.0), writes=[kynbd])
    S.op("pool", lambda e: e.memset(g1t[:], 0.0), writes=[kg1])

    slot_i = [0]

    def proj(col0, ncols, wsrc=None, kw=None):
        j = slot_i[0] % 4
        slot_i[0] += 1
        t_, half, key = pp[j // 2], j % 2, kpp[j // 2]
        for kc in range(8):
            lhsT = win[:, kc, col0:col0 + ncols] if wsrc is None else wsrc[:, kc, :]
            S.op("pe", lambda e, kc=kc, lhsT=lhsT: e.matmul(t_[0:ncols, half, :], lhsT=lhsT, rhs=xnT[:, kc, :],
                                                            start=(kc == 0), stop=(kc == 7)),
                 reads=[(kwin if kw is None else kw)[kc], kxnT], writes=[key])
        return t_[0:ncols, half, :], key

    def shift_lerp(p_ap, kp, jj, dst_ap, kdst, np_=128):
        qb, kqb = qbuf.next()
        S.op("pool", lambda e: e.tensor_copy(out=qb[0:np_, 0:1], in_=qh[0:np_, jj:jj + 1]), reads=[kqh[jj]], writes=[kqb])
        S.op("act", lambda e: e.activation(out=qb[0:np_, 1:W + 1], in_=p_ap, func=AF.Copy), reads=[kp], writes=[kqb])
        S.op("pool", lambda e: e.tensor_copy(out=qh[0:np_, jj:jj + 1], in_=qb[0:np_, W:W + 1]), reads=[kqb], writes=[kqh[jj]])
        tmp, ktmp = tf.next()
        S.op("pool", lambda e: e.tensor_scalar(out=tmp[0:np_, :], in0=qb[0:np_, 1:W + 1], scalar1=dc[0:np_, jj:jj + 1], scalar2=1.0, op0=ALU.mult, op1=ALU.mult),
             reads=[kqb, kdc], writes=[ktmp])
        S.op("dve", lambda e: e.scalar_tensor_tensor(out=dst_ap, in0=qb[0:np_, 0:W], scalar=cc[0:np_, O_MU + jj:O_MU + jj + 1], in1=tmp[0:np_, :],
                                                     op0=ALU.mult, op1=ALU.add),
             reads=[kqb, kcc, ktmp], writes=[kdst])

    def bd_view(t4, c, h):
        return t4[h * 64:(h + 1) * 64, c, :, h * 64:(h + 1) * 64]

    def v3(t2, h):
        return t2[h * 64:(h + 1) * 64, :].rearrange("p (n t) -> p n t", n=NCH)

    def load_x(t):
        b = t % 2
        S.op("sp", lambda e: e.dma_start(out=xts[b][:], in_=xin[W * t:W * (t + 1), :].rearrange("(b p) d -> p b d", p=128)),
             writes=kxts[b], dma_sem=dsem[f"x{b}"])

    load_x(0)
    if n_tiles > 1:
        load_x(1)

    def tile_parts(t):
        full = t >= n_pre
        b = t % 2
        par = t % 2
        xt = xts[b]
        sgd0, sgd1, ksgd = sgds[par]
        yc, kyc = yTc[par], kyTc[par]

        def early():
            for blk in range(NBLK):
                _rms_transpose(S, xt[:, blk, :], kxts[b][blk], xnT, kxnT, blk * 128, scr, ident, kid, ptr)

            yield
            if full:
                def b1(c):
                    pB, kpB = proj(c * 128, 128)
                    pH, kpH = proj(1024 + c * 128, 128)
                    hAs, khAs = tf.next()
                    S.op("act", lambda e, hAs=hAs, pH=pH: e.activation(out=hAs[:], in_=pH, func=AF.Copy), reads=[kpH], writes=[khAs])
                    ub, kub = ubuf.next()
                    S.op("pool", lambda e, ub=ub, c=c: e.tensor_copy(out=ub[:, 0:2], in_=uh[:, c, :]), reads=[kuh[c]], writes=[kub])
                    S.op("dve", lambda e, ub=ub, pB=pB, hAs=hAs: e.tensor_tensor(out=ub[:, 2:W + 2], in0=pB, in1=hAs[:], op=ALU.mult),
                         reads=[kpB, khAs], writes=[kub])
                    S.op("pool", lambda e, ub=ub, c=c: e.tensor_copy(out=uh[:, c, :], in_=ub[:, W:W + 2]), reads=[kub], writes=[kuh[c]])
                    ta, kta = tf.next()
                    wof = O_CAW + c * 3
                    S.op("act", lambda e, ta=ta, ub=ub, wof=wof: e.activation(out=ta[:], in_=ub[:, 2:W + 2], func=AF.Copy, scale=cc[:, wof + 2:wof + 3]),
                         reads=[kub, kcc], writes=[kta])
                    S.op("dve", lambda e, ta=ta, ub=ub, wof=wof: e.scalar_tensor_tensor(out=ta[:], in0=ub[:, 1:W + 1], scalar=cc[:, wof + 1:wof + 2], in1=ta[:],
                                                                                      op0=ALU.mult, op1=ALU.add), reads=[kub, kcc, kta], writes=[kta])
                    S.op("dve", lambda e, ta=ta, ub=ub, wof=wof: e.scalar_tensor_tensor(out=ta[:], in0=ub[:, 0:W], scalar=cc[:, wof:wof + 1], in1=ta[:],
                                                                                      op0=ALU.mult, op1=ALU.add), reads=[kub, kcc, kta], writes=[kta])
                    pC, kpC = proj(512 + c * 128, 128)
                    S.op("dve", lambda e, ta=ta, pC=pC, c=c: e.tensor_tensor(out=yc[:, c, :], in0=pC, in1=ta[:], op=ALU.mult),
                         reads=[kpC, kta], writes=[kyc[c]])
                for c_ in range(4):
                    b1(c_)

            yield
            QC = 1536
            p_, kp_ = proj(QC + 12 * 128, 128)
            shift_lerp(p_, kp_, 12, wa[:], kwa)
            S.op("act", lambda e: e.activation(out=twad[0:64, :], in_=wa[0:64, :], func=AF.Tanh), reads=[kwa], writes=[ktwad])
            S.op("dve", lambda e: e.tensor_copy(out=twad[64:128, :], in_=wa[64:128, :]), reads=[kwa], writes=[ktwad])
            if full:
                p_, kp_ = proj(QC + 13 * 128, 128)
                shift_lerp(p_, kp_, 13, g0t[:], kg0)
                p_, kp_ = proj(None, 128, wsrc=wg1, kw=[kwg1] * 8)
                shift_lerp(p_[0:32, :], kp_, 14, g1t[0:32, :], kg1, np_=32)
                S.op("act", lambda e: e.activation(out=sgd0[:], in_=g0t[:], func=AF.Sigmoid), reads=[kg0], writes=[ksgd])
                S.op("act", lambda e: e.activation(out=sgd1[:], in_=g1t[:], func=AF.Sigmoid), reads=[kg1], writes=[ksgd])
            yield
            for c in range(4):
                S.op("pe", lambda e, c=c: e.matmul(psm[:, 0, :], lhsT=w2b[:, c * 128:(c + 1) * 128], rhs=twad[:], start=True, stop=True),
                     reads=[kw2b, ktwad], writes=[kpsm])
                S.op("pe", lambda e, c=c: e.matmul(psm[:, 1, :], lhsT=a2b[:, c * 128:(c + 1) * 128], rhs=twad[:], start=True, stop=True),
                     reads=[ka2b, ktwad], writes=[kpsm])
                S.op("act", lambda e, c=c: e.activation(out=sig[:, c, :], in_=psm[:, 0, :], func=AF.Sigmoid, bias=cc[:, O_W0 + c:O_W0 + c + 1]),
                     reads=[kpsm, kcc], writes=[ksig[c]])
                S.op("act", lambda e, c=c: e.activation(out=a4[:, c, :], in_=psm[:, 1, :], func=AF.Sigmoid, bias=cc[:, O_A0 + c:O_A0 + c + 1]),
                     reads=[kpsm, kcc], writes=[ka4[c]])

            yield

        def b4(c):
            kt_, kkt = rkv["k"].next()
            vt_, kvt = rkv["v"].next()
            p_, kp_ = proj(QC + (4 + c) * 128, 128)
            shift_lerp(p_, kp_, 4 + c, kt_[:], kkt)
            yield
            p_, kp_ = proj(QC + (8 + c) * 128, 128)
            shift_lerp(p_, kp_, 8 + c, vt_[:], kvt)
            yield
            if full:
                rt_, krt = rkv["r"].next()
                p_, kp_ = proj(QC + c * 128, 128)
                shift_lerp(p_, kp_, c, rt_[:], krt)
                yield
            kkr, kkkr = tf.next()
            S.op("pool", lambda e, kkr=kkr, kt_=kt_, c=c: e.tensor_scalar(out=kkr[:], in0=kt_[:], scalar1=cc[:, O_KK + c:O_KK + c + 1], scalar2=1.0, op0=ALU.mult, op1=ALU.mult),
                 reads=[kkt, kcc], writes=[kkkr])
            sq, ksq = tb.next()
            S.op("act", lambda e, sq=sq, kkr=kkr: e.activation(out=sq[:], in_=kkr[:], func=AF.Square), reads=[kkkr], writes=[ksq])
            S.op("pe", lambda e, sq=sq: e.matmul(psm[:, 0, :], lhsT=bones[:], rhs=sq[:], start=True, stop=True), reads=[kbones, ksq], writes=[kpsm])
            yield
            cs, kcs = tf.next()
            S.op("dve", lambda e, cs=cs, c=c: e.tensor_tensor_scan(out=cs[:], data0=rst[:], data1=sig[:, c, :], initial=0.0, op0=ALU.mult, op1=ALU.add),
                 reads=[kconst, ksig[c]], writes=[kcs])
            yield
            E1, kE1 = tf.next()
            E2, kE2 = tf.next()
            E3, kE3 = tf.next()
            dd, kdd = tf.next()
            S.op("act", lambda e, E1=E1, cs=cs: e.activation(out=E1[:], in_=cs[:], func=AF.Exp, scale=-C05), reads=[kcs], writes=[kE1])
            S.op("act", lambda e, E2=E2, cs=cs: e.activation(out=E2[:], in_=cs[:], func=AF.Exp, scale=C05), reads=[kcs], writes=[kE2])
            S.op("pool", lambda e, dd=dd, cs=cs, c=c: e.tensor_tensor(out=dd[:], in0=cs[:], in1=sig[:, c, :], op=ALU.subtract), reads=[kcs, ksig[c]], writes=[kdd])
            S.op("act", lambda e, E3=E3, dd=dd: e.activation(out=E3[:], in_=dd[:], func=AF.Exp, scale=-C05), reads=[kdd], writes=[kE3])
            S.op("pool", lambda e, E1=E1, c=c: e.tensor_copy(out=PC[:, c, :], in_=E1[:].rearrange("p (n t) -> p n t", n=NCH)[:, :, CH - 1]),
                 reads=[kE1], writes=[kPC[c]])
            yield
            mm, kmm = tf.next()
            S.op("pool", lambda e, mm=mm, c=c: e.tensor_scalar(out=mm[:], in0=a4[:, c, :], scalar1=cc[:, O_KA + c:O_KA + c + 1], scalar2=dc[:, 15 + c:16 + c],
                                                               op0=ALU.mult, op1=ALU.add), reads=[ka4[c], kcc, kdc], writes=[kmm])
            kp, kkp = tf.next()
            S.op("dve", lambda e, kp=kp, kt_=kt_, mm=mm: e.tensor_tensor(out=kp[:], in0=kt_[:], in1=mm[:], op=ALU.mult), reads=[kkt, kmm], writes=[kkp])
            nrm, knrm = tf.next()
            S.op("act", lambda e, nrm=nrm: e.activation(out=nrm[:], in_=psm[:, 0, :], func=AF.Sqrt, bias=1e-24), reads=[kpsm], writes=[knrm])
            S.op("dve", lambda e, nrm=nrm: e.reciprocal(out=nrm[:], in_=nrm[:]), reads=[knrm], writes=[knrm])
            kk, kkk = tf.next()
            S.op("dve", lambda e, kk=kk, kkr=kkr, nrm=nrm: e.tensor_tensor(out=kk[:], in0=kkr[:], in1=nrm[:], op=ALU.mult), reads=[kkkr, knrm], writes=[kkk])
            akk, kakk = tf.next()
            S.op("pool", lambda e, akk=akk, kk=kk, c=c: e.tensor_tensor(out=akk[:], in0=a4[:, c, :], in1=kk[:], op=ALU.mult), reads=[ka4[c], kkk], writes=[kakk])
            yield
            for h in range(2):
                S.op("dve", lambda e, h=h, kk=kk, E3=E3, c=c: e.scalar_tensor_tensor(out=ARbd[h * 64:(h + 1) * 64, c, :, 0, h * 64:(h + 1) * 64], in0=v3(kk, h), scalar=-1.0,
                                                                                   in1=v3(E3, h), op0=ALU.mult, op1=ALU.mult),
                     reads=[kkk, kE3], writes=[kAbd[c]])
                S.op("dve", lambda e, h=h, akk=akk, E2=E2, c=c: e.tensor_tensor(out=bd_view(Bbd, c, h), in0=v3(akk, h), in1=v3(E2, h), op=ALU.mult),
                     reads=[kakk, kE2], writes=[kBbd[c]])
                S.op("pool", lambda e, h=h, kp=kp, E2=E2, c=c: e.tensor_tensor(out=bd_view(Kbd, c, h), in0=v3(kp, h), in1=v3(E2, h), op=ALU.mult),
                     reads=[kkp, kE2], writes=[kKbd[c]])
                S.op("act", lambda e, h=h, vt_=vt_, c=c: e.activation(out=bd_view(Vbd, c, h), in_=v3(vt_, h), func=AF.Copy), reads=[kvt], writes=[kVbd[c]])
                if full:
                    S.op("pool", lambda e, h=h, rt_=rt_, E1=E1, c=c: e.tensor_tensor(out=ARbd[h * 64:(h + 1) * 64, c, :, 1, h * 64:(h + 1) * 64], in0=v3(rt_, h), in1=v3(E1, h),
                                                                                   op=ALU.mult), reads=[krt, kE1], writes=[kRbd[c]])
            yield
            if full:
                rk, krk = tf.next()
                S.op("pool", lambda e, rk=rk, rt_=rt_, kp=kp: e.tensor_tensor(out=rk[:], in0=rt_[:], in1=kp[:], op=ALU.mult), reads=[krt, kkp], writes=[krk])
                rkb, krkb = tb.next()
                S.op("dve", lambda e, rkb=rkb, rk=rk, c=c: e.tensor_scalar(out=rkb[:], in0=rk[:], scalar1=cc[:, O_RK + c:O_RK + c + 1], scalar2=1.0, op0=ALU.mult, op1=ALU.mult),
                     reads=[krk, kcc], writes=[krkb])
                S.op("pe", lambda e, rkb=rkb: e.matmul(psm[:, 1, :], lhsT=bones[:], rhs=rkb[:], start=True, stop=True), reads=[kbones, krkb], writes=[kpsm])
                S.op("dve", lambda e, vt_=vt_, c=c: e.tensor_tensor(out=bonus[:, c, :], in0=psm[:, 1, :], in1=vt_[:], op=ALU.mult), reads=[kpsm, kvt], writes=[kbonus[c]])

        b4gens = {}

        def b4_start(c):
            g = b4(c)
            for _ in range(3 if full else 2):
                next(g)
            b4gens[c] = g

        def b4_pre():
            b4_start(0)
            yield

        def b4_all():
            if 0 not in b4gens:
                b4_start(0)
            for c_ in range(4):
                if c_ + 1 < 4:
                    b4_start(c_ + 1)
                drain_g(b4gens.pop(c_))
            if DEBUG_STOP < 5:
                return


        PA3 = PA[:].rearrange("p (a w) -> p a w", a=2)
        PAd = PA[:].rearrange("p (a w) -> p a w", a=4)
        PBt = PB[:, 0:512].rearrange("p (a w) -> p a w", a=4)
        PQ0 = PB[:, 512:768].rearrange("p (a w) -> p a w", a=4)
        PQ1 = PB[:, 768:1024].rearrange("p (a w) -> p a w", a=4)
        Tfinal = {}

        def gen_AD(n, s_):
            M1, kM1 = M1s[s_]
            NT0, kNT0 = NT0s[s_]
            M2, kM2 = M2s[s_]
            for pi in range(2):
                for i in range(2):
                    c = pi * 2 + i
                    rhsAR = ARbd[:, c, n, :, :].rearrange("p a w -> p (a w)")
                    S.op("pe", lambda e, i=i, c=c, rhsAR=rhsAR: e.matmul(PA3[:, i, 0:256], lhsT=Bbd[:, c, n, :], rhs=rhsAR, start=True, stop=True),
                         reads=[kBbd[c], kAbd[c], kRbd[c]], writes=[kPA])
                    S.op("pe", lambda e, i=i, c=c, rhsAR=rhsAR: e.matmul(PA3[:, i, 256:512], lhsT=Kbd[:, c, n, :], rhs=rhsAR, start=True, stop=True),
                         reads=[kKbd[c], kAbd[c], kRbd[c]], writes=[kPA])
                S.op("dve", lambda e, pi=pi: e.tensor_tensor(out=M1[:, 2 * pi:2 * pi + 2, :], in0=PA3, in1=msu4[:].unsqueeze(1).to_broadcast([128, 2, 512]), op=ALU.mult),
                     reads=[kPA] + kmsu4, writes=[kM1])
                yield
                for i in range(2):
                    c = pi * 2 + i
                    S.op("pe", lambda e, i=i, c=c: e.matmul(PA3[:, i, 0:128], lhsT=ARbd[:, c, n, 0, :], rhs=Bbd[:, c, n, :], start=True, stop=True),
                         reads=[kAbd[c], kBbd[c]], writes=[kPA])
                    S.op("pe", lambda e, i=i, c=c: e.matmul(PA3[:, i, 128:256], lhsT=Bbd[:, c, n, :], rhs=ident[:], start=True, stop=True),
                         reads=[kBbd[c], kid], writes=[kPA])
                    S.op("pe", lambda e, i=i, c=c: e.matmul(PA3[:, i, 256:384], lhsT=Kbd[:, c, n, :], rhs=ident[:], start=True, stop=True),
                         reads=[kKbd[c], kid], writes=[kPA])
                    S.op("pe", lambda e, i=i, c=c: e.matmul(PA3[:, i, 384:448], lhsT=Vbd[:, c, n, :], rhs=fst[:], start=True, stop=True),
                         reads=[kVbd[c], kfst], writes=[kPA])
                S.op("dve", lambda e, pi=pi: e.tensor_tensor(out=NT0[:, 2 * pi:2 * pi + 2, :], in0=PA3[:, :, 0:128], in1=msl[:].unsqueeze(1).to_broadcast([128, 2, 128]), op=ALU.mult),
                     reads=[kPA, kmsl], writes=[kNT0])
                S.op("act", lambda e, pi=pi: e.activation(out=M2[:, 2 * pi:2 * pi + 2, :], in_=PA3[:, :, 128:448], func=AF.Copy), reads=[kPA], writes=[kM2])
                yield
            Tcur, kTcur = Tbs[s_][0]
            S.op("pool", lambda e, Tcur=Tcur: e.tensor_tensor(out=Tcur[:], in0=M1[:, :, 0:128], in1=ident[:].unsqueeze(1).to_broadcast([128, 4, 128]), op=ALU.add),
                 reads=[kM1, kid], writes=[kTcur])
            Nprev = lambda c: M1[:, c, 0:128]
            NTprev = lambda c: NT0[:, c, :]
            kprev = [kM1, kNT0]
            for j in range(1, 6):
                NNj, kNNj = NNs[s_][j % 2]
                for c in range(4):
                    if j < 5:
                        S.op("pe", lambda e, c=c, Nprev=Nprev, NTprev=NTprev: e.matmul(PAd[:, c, 0:128], lhsT=NTprev(c), rhs=Nprev(c), start=True, stop=True),
                             reads=kprev, writes=[kPA])
                    S.op("pe", lambda e, c=c, Nprev=Nprev, NTprev=NTprev: e.matmul(PAd[:, c, 128:256], lhsT=Nprev(c), rhs=NTprev(c), start=True, stop=True),
                         reads=kprev, writes=[kPA])
                if j < 5:
                    S.op("act", lambda e, NNj=NNj: e.activation(out=NNj[:], in_=PAd, func=AF.Copy), reads=[kPA], writes=[kNNj])
                else:
                    S.op("act", lambda e, NNj=NNj: e.activation(out=NNj[:, :, 128:256], in_=PAd[:, :, 128:256], func=AF.Copy), reads=[kPA], writes=[kNNj])
                yield
                for c in range(4):
                    S.op("pe", lambda e, c=c, NNj=NNj, Tcur=Tcur: e.matmul(PBt[:, c, :], lhsT=NNj[:, c, 128:256], rhs=Tcur[:, c, :], start=True, stop=True),
                         reads=[kNNj, kTcur], writes=[kPB0])
                Tnew, kTnew = Tbs[s_][j % 2]
                S.op("dve", lambda e, Tnew=Tnew, Tcur=Tcur: e.tensor_tensor(out=Tnew[:], in0=PBt, in1=Tcur[:], op=ALU.add), reads=[kPB0, kTcur], writes=[kTnew])
                Tcur, kTcur = Tnew, kTnew
                Nprev = (lambda NNj: (lambda c: NNj[:, c, 0:128]))(NNj)
                NTprev = (lambda NNj: (lambda c: NNj[:, c, 128:256]))(NNj)
                kprev = [kNNj]
                yield
            Tfinal[n] = (Tcur, kTcur)

        def gen_SQ(n, s_):
            M1, kM1 = M1s[s_]
            M2, kM2 = M2s[s_]
            Tcur, kTcur = Tfinal[n]
            for c in range(4):
                S.op("pe", lambda e, c=c: e.matmul(PQ0[:, c, :], lhsT=ARbd[:, c, n, 0, :], rhs=Sbf[:, c, :], start=True, stop=False),
                     reads=[kAbd[c], kSbf], writes=[kPB1])
                S.op("pe", lambda e, c=c: e.matmul(PQ0[:, c, :], lhsT=M1[:, c, 256:384], rhs=M2[:, c, 256:320], start=False, stop=True),
                     reads=[kM1, kM2], writes=[kPB1])
            S.op("act", lambda e: e.activation(out=Zb[:], in_=PQ0, func=AF.Copy), reads=[kPB1], writes=[kZb])
            yield
            for c in range(4):
                S.op("pe", lambda e, c=c: e.matmul(PQ1[:, c, :], lhsT=Tcur[:, c, :], rhs=Zb[:, c, :], start=True, stop=True),
                     reads=[kTcur, kZb], writes=[kPB1])
            S.op("act", lambda e: e.activation(out=Ub[:], in_=PQ1, func=AF.Copy), reads=[kPB1], writes=[kUb])
            yield
            for c in range(4):
                S.op("pe", lambda e, c=c: e.matmul(PQ0[:, c, :], lhsT=M2[:, c, 0:128], rhs=Ub[:, c, :], start=True, stop=False),
                     reads=[kM2, kUb], writes=[kPB1])
                S.op("pe", lambda e, c=c: e.matmul(PQ0[:, c, :], lhsT=M2[:, c, 128:256], rhs=M2[:, c, 256:320], start=False, stop=True),
                     reads=[kM2], writes=[kPB1])
            if full:
                for c in range(4):
                    S.op("pe", lambda e, c=c: e.matmul(PQ1[:, c, :], lhsT=ARbd[:, c, n, 1, :], rhs=Sbf[:, c, :], start=True, stop=False),
                         reads=[kRbd[c], kSbf], writes=[kPB1])
                    S.op("pe", lambda e, c=c: e.matmul(PQ1[:, c, :], lhsT=M1[:, c, 128:256], rhs=Ub[:, c, :], start=False, stop=False),
                         reads=[kM1, kUb], writes=[kPB1])
                    S.op("pe", lambda e, c=c: e.matmul(PQ1[:, c, :], lhsT=M1[:, c, 384:512], rhs=M2[:, c, 256:320], start=False, stop=True),
                         reads=[kM1, kM2], writes=[kPB1])
            pcb = PC[:, :, n:n + 1].to_broadcast([128, 4, 64])
            tS_, ktS = tf.next()
            tmpS = tS_[:].rearrange("p (c v) -> p c v", c=4)
            S.op("dve", lambda e: e.tensor_tensor(out=tmpS, in0=PQ0, in1=S32[:], op=ALU.add), reads=[kPB1, kS32], writes=[ktS])
            S.op("dve", lambda e: e.tensor_tensor(out=Sbf[:], in0=tmpS, in1=pcb, op=ALU.mult), reads=[ktS] + kPC, writes=[kSbf])
            S.op("pool", lambda e: e.tensor_tensor(out=S32[:], in0=tmpS, in1=pcb, op=ALU.mult), reads=[ktS] + kPC, writes=[kS32])
            yield
            if full:
                g_, kg_ = gst.next()
                ys_, kysq = tf.next()
                ysq = ys_[:].rearrange("p (c v) -> p c v", c=4)
                yc_, kycen = tf.next()
                ycen = yc_[:].rearrange("p (c v) -> p c v", c=4)
                S.op("dve", lambda e: e.tensor_reduce(out=g_[:, 0, :], in_=PQ1, axis=AX.X, op=ALU.add), reads=[kPB1], writes=[kg_])
                S.op("act", lambda e: e.activation(out=ysq, in_=PQ1, func=AF.Square), reads=[kPB1], writes=[kysq])
                S.op("dve", lambda e: e.tensor_reduce(out=g_[:, 1, :], in_=ysq, axis=AX.X, op=ALU.add), reads=[kysq], writes=[kg_])
                S.op("dve", lambda e: e.tensor_scalar(out=g_[:, 2, :], in0=g_[:, 0, :], scalar1=1.0 / 64, scalar2=1.0, op0=ALU.mult, op1=ALU.mult), reads=[kg_], writes=[kg_])
                S.op("dve", lambda e: e.tensor_tensor(out=g_[:, 3, :], in0=g_[:, 2, :], in1=g_[:, 2, :], op=ALU.mult), reads=[kg_], writes=[kg_])
                S.op("dve", lambda e: e.scalar_tensor_tensor(out=g_[:, 4, :], in0=g_[:, 1, :], scalar=1.0 / 64, in1=g_[:, 3, :], op0=ALU.mult, op1=ALU.subtract),
                     reads=[kg_], writes=[kg_])
                S.op("act", lambda e: e.activation(out=g_[:, 5, :], in_=g_[:, 4, :], func=AF.Sqrt, bias=GN_EPS), reads=[kg_], writes=[kg_])
                S.op("dve", lambda e: e.reciprocal(out=g_[:, 6, :], in_=g_[:, 5, :]), reads=[kg_], writes=[kg_])
                S.op("dve", lambda e: e.tensor_tensor(out=ycen, in0=PQ1, in1=g_[:, 2, :].unsqueeze(2).to_broadcast([128, 4, 64]), op=ALU.subtract),
                     reads=[kPB1, kg_], writes=[kycen])
                yield
                for h in range(2):
                    hs = slice(h * 64, (h + 1) * 64)
                    S.op("dve", lambda e, hs=hs: e.tensor_tensor(out=ynbd[hs, :, hs], in0=ycen[hs, :, :], in1=g_[hs, 6, :].unsqueeze(2).to_broadcast([64, 4, 64]),
                                                                  op=ALU.mult), reads=[kycen, kg_], writes=[kynbd])
                for c in range(4):
                    S.op("pe", lambda e, c=c: e.matmul(PQ0[:, c, :], lhsT=ynbd[:, c, :], rhs=fst[:], start=True, stop=True), reads=[kynbd, kfst], writes=[kPB1])
                S.op("act", lambda e: e.activation(out=ynf[:, :, n * CH:(n + 1) * CH], in_=PQ0, func=AF.Copy), reads=[kPB1], writes=kynf)
                yield

        def drain(g):
            for _ in g:
                pass

        def interleave(ga, gb, ra=2):
            a_live, b_live = ga is not None, gb is not None
            while a_live or b_live:
                for _ in range(ra):
                    if a_live:
                        try:
                            next(ga)
                        except StopIteration:
                            a_live = False
                if b_live:
                    try:
                        next(gb)
                    except StopIteration:
                        b_live = False

        def interleave_g(ga, gb, ra=2):
            a_live, b_live = ga is not None, gb is not None
            while a_live or b_live:
                for _ in range(ra):
                    if a_live:
                        try:
                            next(ga)
                        except StopIteration:
                            a_live = False
                if b_live:
                    try:
                        next(gb)
                    except StopIteration:
                        b_live = False
                yield

        def c_stage():
            yield from gen_AD(0, 0)
            for n_ in range(NCH):
                gd = gen_AD(n_ + 1, (n_ + 1) % 2) if n_ + 1 < NCH else None
                yield from interleave_g(gd, gen_SQ(n_, n_ % 2))

        def de():
            if not full:
                if t + 2 < n_tiles:
                    load_x(t + 2)
                return
            for c in range(4):
                S.op("pe", lambda e, c=c: e.matmul(psm[:, 0, :], lhsT=g2b0[:, c * 128:(c + 1) * 128], rhs=sgd0[:], start=True, stop=False), reads=[kg2b0, ksgd], writes=[kpsm])
                S.op("pe", lambda e, c=c: e.matmul(psm[:, 0, :], lhsT=g2b1[:, c * 128:(c + 1) * 128], rhs=sgd1[:], start=False, stop=True), reads=[kg2b1, ksgd], writes=[kpsm])
                y1, ky1 = tf.next()
                S.op("dve", lambda e, c=c, y1=y1: e.scalar_tensor_tensor(out=y1[:], in0=ynf[:, c, :], scalar=cc[:, O_LW + c:O_LW + c + 1], in1=bonus[:, c, :], op0=ALU.mult, op1=ALU.add),
                     reads=[kynf[c], kcc, kbonus[c]], writes=[ky1])
                S.op("dve", lambda e, c=c, y1=y1: e.scalar_tensor_tensor(out=yTr[:, c, :], in0=y1[:], scalar=cc[:, O_LB + c:O_LB + c + 1], in1=psm[:, 0, :], op0=ALU.add, op1=ALU.mult),
                     reads=[ky1, kcc, kpsm], writes=[kyTr[c]])
            if DEBUG_STOP < 9:
                return
            for blk in range(NBLK):
                for hf in range(2):
                    pflat = pp[hf][:].rearrange("p a w -> p (a w)")
                    for e_ in range(8):
                        ysrc = yc[:, e_, blk * 128:(blk + 1) * 128] if e_ < 4 else yTr[:, e_ - 4, blk * 128:(blk + 1) * 128]
                        ykey = kyc[e_] if e_ < 4 else kyTr[e_ - 4]
                        S.op("pe", lambda e, e_=e_, hf=hf, pflat=pflat, ysrc=ysrc: e.matmul(pflat, lhsT=ysrc, rhs=wout[:, e_, hf * 512:(hf + 1) * 512],
                                                                                            start=(e_ == 0), stop=(e_ == 7)), reads=[ykey, kwout], writes=[kpp[hf]])
                class _V:
                    def __init__(self, ap): self.ap = ap
                    def __getitem__(self, k): return self.ap
                _post_norm_residual(S, [(_V(pp[0][:].rearrange("p a w -> p (a w)")), kpp[0]), (_V(pp[1][:].rearrange("p a w -> p (a w)")), kpp[1])],
                                    xt[:, blk, :], kxts[b][blk], grow, kgrow, scr)
            ht_i = t - n_pre
            S.op("sp", lambda e, xt=xt, ht_i=ht_i: e.dma_start(out=hscr[W * ht_i:W * (ht_i + 1), :].rearrange("(b p) d -> p b d", p=128), in_=xt[:]),
                 reads=kxts[b], writes=[khscr[ht_i]], dma_sem=dsem[f"h{b}"])
            if t + 2 < n_tiles:
                load_x(t + 2)

        return early, b4_all, c_stage, de, b4_pre

    def drain_g(g):
        for _ in g:
            pass

    def chain_g(*gs):
        for g in gs:
            yield from g

    def interleave2(ga, gb, ra, rb):
        a_live, b_live = ga is not None, gb is not None
        while a_live or b_live:
            for _ in range(ra):
                if a_live:
                    try:
                        next(ga)
                    except StopIteration:
                        a_live = False
            for _ in range(rb):
                if b_live:
                    try:
                        next(gb)
                    except StopIteration:
                        b_live = False

    parts = [tile_parts(t_i) for t_i in range(n_tiles)]
    drain_g(parts[0][0]())
    parts[0][1]()
    for t_i in range(n_tiles):
        early_next = chain_g(parts[t_i + 1][0](), parts[t_i + 1][4]()) if t_i + 1 < n_tiles else None
        interleave2(parts[t_i][2](), early_next, 2, 1)
        parts[t_i][3]()
        if t_i + 1 < n_tiles:
            parts[t_i + 1][1]()


def _host_consts():
    km = np.zeros((128, NKM), np.float32)
    km[:, M_ID:M_ID + 128] = np.eye(128, dtype=np.float32)
    idx = np.arange(128)
    same = (idx[:, None] // 64) == (idx[None, :] // 64)
    s, t = idx[:, None] % 64, idx[None, :] % 64
    km[:, M_SU:M_SU + 128] = (same & (s < t)).astype(np.float32)
    km[:, M_IU:M_IU + 128] = (same & (s <= t)).astype(np.float32)
    km[:, M_SL:M_SL + 128] = (same & (s > t)).astype(np.float32)
    km[:, M_BO:M_BO + 128] = same.astype(np.float32)
    km[:, M_F:M_F + 64] = (idx[:, None] % 64 == np.arange(64)[None, :]).astype(np.float32)
    rst = np.ones((128, 256), np.float32)
    rst[:, ::64] = 0.0
    km[:, M_RST:M_RST + 256] = rst
    return km


def _pack_cc(inp, hmask):
    cc = np.zeros((128, NCC), np.float32)
    col = lambda v, n: np.ascontiguousarray(np.asarray(v, np.float32).reshape(n, 128).T)
    cc[:, O_PMG:O_PMG + 8] = col(inp["pre_mix_g"][0], 8)
    cc[:, O_PFG:O_PFG + 8] = col(inp["pre_ffn_g"][0], 8)
    caw = np.asarray(inp["conv_a_w"][0], np.float32)
    cc[:, O_CAW:O_CAW + 12] = caw.T.reshape(4, 128, 3).transpose(1, 0, 2).reshape(128, 12)
    mu = np.zeros(1920, np.float32)
    mu[:1824] = np.asarray(inp["shift_mu"][0], np.float32)
    cc[:, O_MU:O_MU + 15] = col(mu, 15)
    for off, name in ((O_W0, "w0"), (O_A0, "a0"), (O_KK, "k_k"), (O_KA, "k_a"), (O_LW, "lnx_w"), (O_LB, "lnx_b")):
        cc[:, off:off + 4] = col(inp[name][0], 4)
    cc[:, O_RK:O_RK + 4] = col(np.asarray(inp["r_k"][0], np.float32).reshape(512), 4)
    fcw = np.asarray(inp["ffn_conv_w"][0], np.float32)
    cc[:, O_FCW:O_FCW + 132] = fcw.T.reshape(44, 128, 3).transpose(1, 0, 2).reshape(128, 132)
    cc[:, O_FCB:O_FCB + 44] = col(inp["ffn_conv_b"][0], 44)
    cc[:, O_HM] = hmask
    return cc


_NC_CACHE = {}


def kernel(**inputs):
    n_pre, n_main = 15, 16
    x = np.asarray(inputs["x"], np.float32)
    B, T, _ = x.shape
    half = T // 2
    if "full" not in _NC_CACHE:
        _NC_CACHE["full"] = build(n_pre, n_main, "full")
    nc = _NC_CACHE["full"]
    km = _host_consts()
    f = lambda n: np.ascontiguousarray(np.asarray(inputs[n], np.float32)[0])
    in_maps = []
    for c in range(8):
        b, h = c // 2, c % 2
        xin = np.zeros((T, D), np.float32)
        if h == 0:
            xin[half:] = x[b, :half]
        else:
            xin[:] = x[b]
        in_maps.append({
            "xin": xin, "cc": _pack_cc(inputs, float(h)), "km": km,
            "post_mix_g": f("post_mix_g"), "post_ffn_g": f("post_ffn_g"),
            "w_in": f("w_in"), "w_out": f("w_out"), "w_up": f("w_up"), "w_down": f("w_down"),
            "w2": f("w2"), "a2": f("a2"), "g2": f("g2"),
        })
    res = run_bass_kernel_spmd(nc, in_maps, core_ids=list(range(8)))
    out = np.zeros((B, T, D), np.float32)
    for c in range(8):
        b, h = c // 2, c % 2
        out[b, h * half:(h + 1) * half] = res.results[c]["out"]
    return out
```

```python
import contextlib
import numpy as np
import concourse.bass as bass
import concourse.mybir as mybir
from concourse.bass_utils import run_bass_kernel_spmd

F32 = mybir.dt.float32
BF16 = mybir.dt.bfloat16
AF = mybir.ActivationFunctionType
ALU = mybir.AluOpType
AX = mybir.AxisListType

D = 1024
W = 256
NBLK = 2
CH = 64
NCH = W // CH
INC = 3360
DFF = 2816
NPAIR = 22
QC = 1536
RMS_EPS = 1e-6
GN_EPS = 64 * 1e-5
EPOCH = 12000
DEBUG_SUB = 99
EMBED_WAITS = True
TRANSITIVE = True
DEBUG_STOP = 99

O_PMG, O_PFG, O_CAW, O_MU, O_W0, O_A0, O_KK, O_KA, O_RK, O_LW, O_LB, O_FCW, O_FCB, O_HM = (
    0, 8, 16, 28, 43, 47, 51, 55, 59, 63, 67, 71, 203, 247)
NCC = 248
M_ID, M_SU, M_IU, M_SL, M_BO, M_F, M_RST = 0, 128, 256, 384, 512, 640, 704
NKM = 704 + 256


class Key:
    __slots__ = ("name", "writer", "readers", "excl")

    def __init__(self, name, excl=False):
        self.name = name
        self.writer = None
        self.readers = []
        self.excl = excl


def PKey(name):
    return Key(name, excl=True)


class Sched:
    ENGS = ("pe", "act", "dve", "pool", "sp")

    def __init__(self, nc, sem_stack, prefix):
        self.nc = nc
        self.sem_stack = sem_stack
        self.prefix = prefix
        self.ops = {e: [] for e in self.ENGS}
        self.count = {e: 0 for e in self.ENGS}
        self.sems = {}
        self.waited = {e: {} for e in self.ENGS}
        self.dma_counts = {}
        self.last_tok = {e: None for e in self.ENGS}
        self.tok_order = {}
        self.n_tok = 0
        self.tok_know = {}

    def _eng_sem(self, eng, idx):
        sid = f"{self.prefix}s_{eng}_{idx // EPOCH}"
        self.sems.setdefault(sid, None)
        return sid, (idx % EPOCH) + 1

    def new_dma_sem(self, name):
        sid = f"{self.prefix}d_{name}"
        assert sid not in self.sems, sid
        self.sems[sid] = None
        self.dma_counts[sid] = 0
        return sid

    def _need_waits(self, eng, tokens):
        w = self.waited[eng]
        cand = {}
        for t in tokens:
            if t is None:
                continue
            sid, val, _ = t
            if w.get(sid, 0) >= val:
                continue
            if cand.get(sid, (0, None))[0] < val:
                cand[sid] = (val, t)
        out = []
        for sid, (val, t) in sorted(cand.items(), key=lambda kv: -self.tok_order.get(kv[1][1], 0)):
            if w.get(sid, 0) >= val:
                continue
            out.append((sid, val))
            w[sid] = val
            if TRANSITIVE:
                for s2, v2 in self.tok_know.get(t, {}).items():
                    if w.get(s2, 0) < v2:
                        w[s2] = v2
        return out

    def op(self, eng, fn, reads=(), writes=(), dma_sem=None, multi=False):
        toks = []
        raw = set()
        for k in reads:
            toks.append(k.writer)
            if k.writer is not None:
                raw.add(k.writer)
            if k.excl:
                toks.extend(r for r in k.readers if r[2] != eng)
        for k in writes:
            toks.append(k.writer)
            toks.extend(k.readers)
        if eng == "pe":
            toks = [t for t in toks if t is not None and t[2] != "pe"]
        waits = self._need_waits(eng, toks)
        if dma_sem is None:
            idx = self.count[eng]
            self.count[eng] += 1
            sid, val = self._eng_sem(eng, idx)
            tok = (sid, val, eng)
            inc = (sid, 1)
            self.last_tok[eng] = tok
        else:
            self.dma_counts[dma_sem] += 16
            tok = (dma_sem, self.dma_counts[dma_sem], "dma")
            inc = (dma_sem, 16)
        self.n_tok += 1
        self.tok_order[tok] = self.n_tok
        know = dict(self.waited[eng])
        if dma_sem is None and tok[1] > 1:
            know[tok[0]] = tok[1] - 1
        self.tok_know[tok] = know
        embed = EMBED_WAITS and dma_sem is None and eng != "pe" and not multi
        self.ops[eng].append((fn, waits, inc, embed))
        for k in reads:
            k.readers.append(tok)
        for k in writes:
            k.writer = tok
            k.readers = []
        return tok

    def barrier(self, extra_keys=()):
        toks = [t for t in self.last_tok.values() if t is not None]
        for k in extra_keys:
            toks.append(k.writer)
            toks.extend(k.readers)
        for eng in self.ENGS:
            waits = self._need_waits(eng, [t for t in toks if t is not None and t[2] != eng])
            if waits:
                self.ops[eng].append((None, waits, None, False))

    def final_wait(self, eng, keys):
        toks = []
        for k in keys:
            toks.append(k.writer)
            toks.extend(k.readers)
        waits = self._need_waits(eng, toks)
        self.ops[eng].append((None, waits, None, False))

    def emit(self):
        nc = self.nc
        with contextlib.ExitStack() as st:
            handles = {sid: self.sem_stack.enter_context(nc.semaphore(sid)) for sid in self.sems}
            block = st.enter_context(nc.Block())

            def run(engobj, lst):
                for fn, waits, inc, embed in lst:
                    emb = waits[-1] if (embed and waits and fn is not None) else None
                    for sid, val in (waits[:-1] if emb is not None else waits):
                        engobj.wait_ge(handles[sid], val)
                    if fn is not None:
                        n0 = nc.n_instructions()
                        ins = fn(engobj)
                        if emb is not None:
                            assert nc.n_instructions() - n0 == 1, "embedded wait on a multi-instruction op"
                            ins._wait_ge(handles[emb[0]], emb[1])
                        ins.then_inc(handles[inc[0]], inc[1])

            @block.tensor
            def _(e):
                run(e, self.ops["pe"])

            @block.scalar
            def _(e):
                run(e, self.ops["act"])

            @block.vector
            def _(e):
                run(e, self.ops["dve"])

            @block.gpsimd
            def _(e):
                run(e, self.ops["pool"])

            @block.sync
            def _(e):
                run(e, self.ops["sp"])


class View:
    def __init__(self, ap):
        self.ap = ap

    def __getitem__(self, k):
        return self.ap


class View3:
    def __init__(self, x):
        self.x = x

    def __getitem__(self, k):
        return self.x[k[0], k[1], 128:256]


class Rot:
    def __init__(self, tiles, excl=False, keys=None):
        self.tiles = tiles
        self.keys = keys if keys is not None else [Key(f"rot{i}", excl) for i in range(len(tiles))]
        self.i = 0

    def next(self):
        j = self.i % len(self.tiles)
        self.i += 1
        return self.tiles[j], self.keys[j]


def _rms_transpose(S, src, ksrc, dstT, kdst, tcol, scr, ident, kid, ptr, extra_scale=None, kextra=None):
    st, kst = scr["stat"].next()
    xs, kxs = scr["xs"].next()
    pt, kpt = ptr.next()
    S.op("act", lambda e: e.activation(out=xs[:], in_=src, func=AF.Square, accum_out=st[:, 0:1]),
         reads=[ksrc], writes=[kxs, kst], multi=True)
    S.op("act", lambda e: e.activation(out=st[:, 1:2], in_=st[:, 0:1], func=AF.Sqrt, scale=1.0 / D, bias=RMS_EPS),
         reads=[kst], writes=[kst])
    S.op("dve", lambda e: e.reciprocal(out=st[:, 2:3], in_=st[:, 1:2]), reads=[kst], writes=[kst])
    rs = st[:, 2:3]
    if extra_scale is not None:
        S.op("dve", lambda e: e.tensor_tensor(out=st[:, 3:4], in0=st[:, 2:3], in1=extra_scale, op=ALU.mult),
             reads=[kst, kextra], writes=[kst])
        rs = st[:, 3:4]
    S.op("pool", lambda e: e.tensor_scalar(out=xs[:], in0=src, scalar1=rs, scalar2=1.0, op0=ALU.mult, op1=ALU.mult),
         reads=[ksrc, kst], writes=[kxs])
    for kc in range(8):
        S.op("pe", lambda e, kc=kc: e.transpose(out=pt[:, kc, :], in_=xs[:, kc * 128:(kc + 1) * 128], identity=ident[:]),
             reads=[kxs, kid], writes=[kpt])
    S.op("act", lambda e: e.activation(out=dstT[:, :, tcol:tcol + 128], in_=pt[:], func=AF.Copy),
         reads=[kpt], writes=[kdst])


def _post_norm_residual(S, pd_pairs, res, kres, grow, kgrow, scr):
    st, kst = scr["stat"].next()
    tmps = [scr["tmp512"].next() for _ in range(2)]
    for hf, (pd, kpd) in enumerate(pd_pairs):
        junk, kjunk = tmps[hf]
        S.op("act", lambda e, pd=pd, hf=hf, junk=junk: e.activation(out=junk[:], in_=pd[:], func=AF.Square,
                                                                  accum_out=st[:, hf:hf + 1]),
             reads=[kpd], writes=[kjunk, kst], multi=True)
    S.op("dve", lambda e: e.tensor_tensor(out=st[:, 2:3], in0=st[:, 0:1], in1=st[:, 1:2], op=ALU.add), reads=[kst], writes=[kst])
    S.op("act", lambda e: e.activation(out=st[:, 3:4], in_=st[:, 2:3], func=AF.Sqrt, scale=1.0 / D, bias=RMS_EPS),
         reads=[kst], writes=[kst])
    S.op("dve", lambda e: e.reciprocal(out=st[:, 4:5], in_=st[:, 3:4]), reads=[kst], writes=[kst])
    for hf, (pd, kpd) in enumerate(pd_pairs):
        tmp, ktmp = tmps[hf]
        S.op("dve", lambda e, pd=pd, hf=hf, tmp=tmp: e.scalar_tensor_tensor(
            out=tmp[:], in0=pd[:], scalar=st[:, 4:5], in1=grow[:, hf * 512:(hf + 1) * 512], op0=ALU.mult, op1=ALU.mult),
            reads=[kpd, kst, kgrow], writes=[ktmp])
        S.op("pool", lambda e, hf=hf, tmp=tmp: e.tensor_tensor(out=res[:, hf * 512:(hf + 1) * 512], in0=res[:, hf * 512:(hf + 1) * 512],
                                                               in1=tmp[:], op=ALU.add),
             reads=[ktmp, kres], writes=[kres])


def phase2_ffn(nc, S, st, io, n_main, shared):
    sb = lambda n, s, d: st.enter_context(nc.sbuf_tensor(n, s, d))
    ps = lambda n, s, d: st.enter_context(nc.psum_tensor(n, s, d))
    cc, kcc = shared["cc"], shared["kcc"]
    ident, kid = shared["ident"], shared["kid"]
    hscr, khscr = io["hscr"], io["khscr"]
    out = io["out"]

    wup = sb("wup", [128, 8, DFF * 2], BF16)
    wdn = sb("wdn", [128, NPAIR, D], BF16)
    kwup = [Key(f"wup{k}") for k in range(8)]
    kwdn = Key("wdn")
    grow = sb("grow2", [128, D], F32)
    kgrow = Key("grow2")
    fh = sb("fh", [128, NPAIR, 2, 2], F32)
    kfh = [Key(f"fh{i}") for i in range(NPAIR)]
    hts = [sb(f"ht{i}", [128, NBLK, D], F32) for i in range(2)]
    khts = [[Key(f"ht{i}_{b}") for b in range(NBLK)] for i in range(2)]
    hnTs = [sb(f"hnT{i}", [128, 8, W], BF16) for i in range(2)]
    khnT = [Key(f"hnT{i}") for i in range(2)]
    act = sb("actb", [128, NPAIR, W], BF16)
    kact = [Key(f"act{i}") for i in range(NPAIR)]
    scr = {
        "stat": Rot([sb(f"stat{i}", [128, 8], F32) for i in range(4)]),
        "xs": Rot([sb(f"xs{i}", [128, D], BF16) for i in range(2)]),
        "tmp512": Rot([sb(f"tmp512_{i}", [128, 512], F32) for i in range(2)]),
    }
    fbuf = Rot([sb(f"fbuf{i}", [128, 2, W + 2], F32) for i in range(4)])
    cg = Rot([sb(f"cg{i}", [128, W], F32) for i in range(4)])
    cu = Rot([sb(f"cu{i}", [128, W], F32) for i in range(4)])
    t1 = Rot([sb(f"t1_{i}", [128, W], F32) for i in range(3)])
    t2 = Rot([sb(f"t2_{i}", [128, W], F32) for i in range(3)])
    sg = Rot([sb(f"sg{i}", [128, W], F32) for i in range(3)])
    WQ = DFF // 4
    ptr = Rot([ps("ptr2", [128, 8, 128], BF16)], excl=True)
    pf = Rot([ps(f"pf{i}", [128, 2, W], F32) for i in range(3)], excl=True)
    pd = [[ps(f"pd{b}{h}", [128, 512], F32) for h in range(2)] for b in range(NBLK)]
    kpd = [[PKey(f"pd{b}{h}") for h in range(2)] for b in range(NBLK)]
    dsem = {n: S.new_dma_sem("p2_" + n) for n in ("wst0", "wst1", "wst2", "wst3", "wdn", "grow", "h0", "h1", "hw", "out0", "out1")}

    S.op("sp", lambda e: e.dma_start(out=grow[:], in_=io["post_ffn_g"].partition_broadcast(128)), writes=[kgrow], dma_sem=dsem["grow"])
    actf = act[:].rearrange("p i w -> p (i w)").bitcast(F32)
    kstg = [Key(f"stg{q}") for q in range(4)]
    for kc in range(8):
        for q in range(8):
            j = kc * 8 + q
            r_ = j % 4
            wt, kwt = actf[:, r_ * WQ:(r_ + 1) * WQ], kstg[r_]
            S.op("sp", lambda e, wt=wt, kc=kc, q=q: e.dma_start(out=wt, in_=io["w_up"][kc * 128:(kc + 1) * 128, q * WQ:(q + 1) * WQ]),
                 writes=[kwt], dma_sem=dsem[f"wst{r_}"])
            if j % 2 == 0:
                S.op("act", lambda e, wt=wt, kc=kc, q=q: e.activation(out=wup[:, kc, q * WQ:(q + 1) * WQ], in_=wt, func=AF.Copy,
                                                                       scale=cc[:, O_PFG + kc:O_PFG + kc + 1]),
                     reads=[kwt, kcc], writes=[kwup[kc]])
            else:
                S.op("dve", lambda e, wt=wt, kc=kc, q=q: e.tensor_scalar(out=wup[:, kc, q * WQ:(q + 1) * WQ], in0=wt,
                                                                          scalar1=cc[:, O_PFG + kc:O_PFG + kc + 1], scalar2=1.0, op0=ALU.mult, op1=ALU.mult),
                     reads=[kwt, kcc], writes=[kwup[kc]])
    S.op("pool", lambda e: e.memset(act[:, :, 0:1], 0.0), writes=kstg + kact)
    S.op("pool", lambda e: e.dma_start(out=wdn[:], in_=io["w_down"].rearrange("(i p) d -> p i d", p=128)), writes=[kwdn], dma_sem=dsem["wdn"])

    hw = hts[1]
    S.op("sp", lambda e: e.dma_start(out=hw[:, 0, :], in_=hscr[W - 128:W, :]), reads=[khscr[0]], writes=[khts[1][0]], dma_sem=dsem["hw"])
    _rms_transpose(S, hw[:, 0, :], khts[1][0], hnTs[1], khnT[1], 0, scr, ident, kid, ptr,
                   extra_scale=cc[:, O_HM:O_HM + 1], kextra=kcc)
    pfh, kpfh = pf.next()
    pfh_v = pfh[:].rearrange("p a w -> p (a w)")
    for ch in range(2 * NPAIR):
        i, hf = ch % NPAIR, ch // NPAIR
        col = (i * 2 + hf) * 2
        for kc in range(8):
            S.op("pe", lambda e, ch=ch, kc=kc, col=col: e.matmul(pfh_v[:, col:col + 2], lhsT=wup[:, kc, ch * 128:(ch + 1) * 128],
                                                                  rhs=hnTs[1][:, kc, 126:128], start=(kc == 0), stop=(kc == 7)),
                 reads=[kwup[kc], khnT[1]], writes=[kpfh])
    S.op("act", lambda e: e.activation(out=fh[:].rearrange("p i a b -> p (i a b)"), in_=pfh_v[:, 0:NPAIR * 4], func=AF.Copy),
         reads=[kpfh], writes=kfh)

    def load(t):
        b = t % 2
        S.op("sp", lambda e: e.dma_start(out=hts[b][:], in_=hscr[W * (1 + t):W * (2 + t), :].rearrange("(b p) d -> p b d", p=128)),
             reads=[khscr[1 + t]], writes=khts[b], dma_sem=dsem[f"h{b}"])

    def prologue(t):
        b = t % 2
        for blk in range(NBLK):
            _rms_transpose(S, hts[b][:, blk, :], khts[b][blk], hnTs[b], khnT[b], blk * 128, scr, ident, kid, ptr)

    state = {}

    def up_mm(t, i):
        b = t % 2
        p, kp = pf.next()
        state[(t, i)] = (p, kp)
        for hf in range(2):
            ch = hf * NPAIR + i
            for kc in range(8):
                S.op("pe", lambda e, p=p, hf=hf, ch=ch, kc=kc: e.matmul(p[:, hf, :], lhsT=wup[:, kc, ch * 128:(ch + 1) * 128],
                                                                         rhs=hnTs[b][:, kc, :], start=(kc == 0), stop=(kc == 7)),
                     reads=[kwup[kc], khnT[b]], writes=[kp])

    def elem(t, i):
        p, kp = state.pop((t, i))
        fb, kfb = fbuf.next()
        S.op("pool", lambda e: e.tensor_copy(out=fb[:, :, 0:2], in_=fh[:, i, :, :]), reads=[kfh[i]], writes=[kfb])
        S.op("act", lambda e: e.activation(out=fb[:, :, 2:W + 2], in_=p[:], func=AF.Copy), reads=[kp], writes=[kfb])
        yield
        S.op("pool", lambda e: e.tensor_copy(out=fh[:, i, :, :], in_=fb[:, :, W:W + 2]), reads=[kfb], writes=[kfh[i]])
        outs = []
        for hf, rot in ((0, cg), (1, cu)):
            ch = hf * NPAIR + i
            c, kc_ = rot.next()
            wof = O_FCW + ch * 3
            S.op("act", lambda e, c=c, hf=hf, wof=wof, ch=ch: e.activation(out=c[:], in_=fb[:, hf, 2:W + 2], func=AF.Identity,
                                                                          scale=cc[:, wof + 2:wof + 3], bias=cc[:, O_FCB + ch:O_FCB + ch + 1]),
                 reads=[kfb, kcc], writes=[kc_])
            outs.append((c, kc_, hf, wof))
        yield
        for c, kc_, hf, wof in outs:
            S.op("dve", lambda e, c=c, hf=hf, wof=wof: e.scalar_tensor_tensor(out=c[:], in0=fb[:, hf, 1:W + 1], scalar=cc[:, wof + 1:wof + 2],
                                                                             in1=c[:], op0=ALU.mult, op1=ALU.add),
                 reads=[kfb, kcc, kc_], writes=[kc_])
        yield
        for c, kc_, hf, wof in outs:
            S.op("dve", lambda e, c=c, hf=hf, wof=wof: e.scalar_tensor_tensor(out=c[:], in0=fb[:, hf, 0:W], scalar=cc[:, wof:wof + 1],
                                                                             in1=c[:], op0=ALU.mult, op1=ALU.add),
                 reads=[kfb, kcc, kc_], writes=[kc_])
        outs = [(c, kc_) for c, kc_, hf, wof in outs]
        yield
        (g_, kg), (u_, ku) = outs
        a1, ka1 = t1.next()
        a2, ka2 = t2.next()
        s_, ks = sg.next()
        S.op("pool", lambda e: e.tensor_tensor(out=a1[:], in0=g_[:], in1=g_[:], op=ALU.mult), reads=[kg], writes=[ka1])
        yield
        S.op("pool", lambda e: e.tensor_scalar(out=a1[:], in0=a1[:], scalar1=0.044715, scalar2=1.0, op0=ALU.mult, op1=ALU.add),
             reads=[ka1], writes=[ka1])
        S.op("pool", lambda e: e.tensor_tensor(out=a2[:], in0=a1[:], in1=g_[:], op=ALU.mult), reads=[ka1, kg], writes=[ka2])
        S.op("dve", lambda e: e.tensor_tensor(out=a1[:], in0=g_[:], in1=u_[:], op=ALU.mult), reads=[kg, ku, ka2], writes=[ka1])
        yield
        S.op("act", lambda e: e.activation(out=s_[:], in_=a2[:], func=AF.Sigmoid, scale=1.5957691216), reads=[ka2], writes=[ks])
        yield
        S.op("dve", lambda e: e.tensor_tensor(out=act[:, i, :], in0=a1[:], in1=s_[:], op=ALU.mult), reads=[ka1, ks], writes=[kact[i]])

    def down_mm(t, i):
        for blk in range(NBLK):
            for hf in range(2):
                S.op("pe", lambda e, blk=blk, hf=hf: e.matmul(pd[blk][hf][:], lhsT=act[:, i, blk * 128:(blk + 1) * 128],
                                                              rhs=wdn[:, i, hf * 512:(hf + 1) * 512], start=(i == 0), stop=(i == NPAIR - 1)),
                     reads=[kact[i], kwdn], writes=[kpd[blk][hf]])

    def epilogue(t):
        b = t % 2
        for blk in range(NBLK):
            _post_norm_residual(S, [(pd[blk][0], kpd[blk][0]), (pd[blk][1], kpd[blk][1])], hts[b][:, blk, :], khts[b][blk], grow, kgrow, scr)
        S.op("sp", lambda e: e.dma_start(out=out[W * t:W * (t + 1), :].rearrange("(b p) d -> p b d", p=128), in_=hts[b][:]),
             reads=khts[b], dma_sem=dsem[f"out{b}"])

    load(0)
    if n_main > 1:
        load(1)
    prologue(0)
    TSTEP = 2
    for t in range(n_main):
        active = []
        i_next, tick, pro_done = 0, 0, False
        while i_next < NPAIR or active:
            if i_next < NPAIR and tick % TSTEP == 0:
                up_mm(t, i_next)
                active.append((i_next, elem(t, i_next)))
                i_next += 1
                if i_next == 15 and t + 1 < n_main and not pro_done:
                    prologue(t + 1)
                    pro_done = True
            for item in list(active):
                i_, g_ = item
                try:
                    next(g_)
                except StopIteration:
                    active.remove(item)
                    down_mm(t, i_)
            tick += 1
        epilogue(t)
        if t + 2 < n_main:
            load(t + 2)
    S.final_wait("sp", [k for ks_ in khts for k in ks_])


def build(n_pre, n_main, mode="full"):
    nc = bass.Bass("TRN2", target_bir_lowering=False)
    TT = (n_pre + 1 + n_main) * W
    io = {}
    di = lambda n, s: nc.dram_tensor(n, s, F32, kind="ExternalInput").ap()
    io["cc"] = di("cc", [128, NCC])
    io["km"] = di("km", [128, NKM])
    io["post_ffn_g"] = di("post_ffn_g", [D])
    io["w_up"] = di("w_up", [D, 2 * DFF])
    io["w_down"] = di("w_down", [DFF, D])
    if mode == "ffn":
        io["hscr"] = di("hscr", [(1 + n_main) * W, D])
    else:
        io["xin"] = di("xin", [TT, D])
        io["post_mix_g"] = di("post_mix_g", [D])
        io["w_in"] = di("w_in", [D, INC])
        io["w_out"] = di("w_out", [D, D])
        io["w2"] = di("w2", [64, 512])
        io["a2"] = di("a2", [64, 512])
        io["g2"] = di("g2", [160, 512])
        io["hscr"] = nc.dram_tensor("hscr", [(1 + n_main) * W, D], F32, kind="Internal").ap()
    io["khscr"] = [Key(f"hscr{i}") for i in range(1 + n_main)]
    io["out"] = nc.dram_tensor("out", [n_main * W, D], F32, kind="ExternalOutput").ap()

    with contextlib.ExitStack() as sem_stack, contextlib.ExitStack() as st0:
        cc = st0.enter_context(nc.sbuf_tensor("cc_sb", [128, NCC], F32))
        ident = st0.enter_context(nc.sbuf_tensor("ident", [128, 128], BF16))

        def shared_loads(S):
            kcc, kid = Key("cc"), Key("ident")
            d0 = S.new_dma_sem("cc")
            d1 = S.new_dma_sem("ident")
            S.op("sp", lambda e: e.dma_start(out=cc[:], in_=io["cc"][:, :]), writes=[kcc], dma_sem=d0)
            S.op("pool", lambda e: e.dma_start(out=ident[:], in_=io["km"][:, M_ID:M_ID + 128]), writes=[kid], dma_sem=d1)
            return {"cc": cc, "kcc": kcc, "ident": ident, "kid": kid}

        if mode != "ffn":
            S1 = Sched(nc, sem_stack, "a")
            shared = shared_loads(S1)
            with contextlib.ExitStack() as st1:
                phase1_mixer(nc, S1, st1, io, n_pre, n_main, shared)
                S1.final_wait("sp", io["khscr"])
                S1.emit()
            S2 = Sched(nc, sem_stack, "b")
            shared = {"cc": cc, "kcc": Key("cc2"), "ident": ident, "kid": Key("ident2")}
            io["khscr"] = [Key(f"hscr2_{i}") for i in range(1 + n_main)]
        else:
            S2 = Sched(nc, sem_stack, "b")
            shared = shared_loads(S2)
        with contextlib.ExitStack() as st2:
            phase2_ffn(nc, S2, st2, io, n_main, shared)
            S2.emit()
    return nc


def phase1_mixer(nc, S, st, io, n_pre, n_main, shared):
    sb = lambda n, s, d: st.enter_context(nc.sbuf_tensor(n, s, d))
    ps = lambda n, s, d: st.enter_context(nc.psum_tensor(n, s, d))
    cc, kcc = shared["cc"], shared["kcc"]
    ident, kid = shared["ident"], shared["kid"]
    xin, hscr, khscr = io["xin"], io["hscr"], io["khscr"]
    n_tiles = n_pre + 1 + n_main
    C05 = 0.6065306597126334

    win = sb("win", [128, 8, INC], BF16)
    kwin = [[Key(f"win{k}_{q}") for q in range(4)] for k in range(8)]
    wout = sb("wout", [128, 8, D], BF16)
    kwout = Key("wout")
    w2b = sb("w2b", [128, 512], BF16)
    a2b = sb("a2b", [128, 512], BF16)
    g2b0 = sb("g2b0", [128, 512], BF16)
    g2b1 = sb("g2b1", [128, 512], BF16)
    wg1 = sb("wg1", [128, 8, 128], BF16)
    kwg1 = Key("wg1")
    ksmallw = Key("smallw")
    msu4 = sb("msu4", [128, 512], BF16)
    msl = sb("msl", [128, 128], BF16)
    bones = sb("bones", [128, 128], BF16)
    fst = sb("fst", [128, 64], BF16)
    rst = sb("rst", [128, W], F32)
    kconst = Key("p1const")
    grow = sb("grow1", [128, D], F32)
    kgrow = Key("grow1")
    dc = sb("dc", [128, 20], F32)
    kdc = Key("dc")
    dsem = {n: S.new_dma_sem("p1_" + n) for n in ("wst0", "wst1", "wst2", "wst3", "const", "grow", "x0", "x1", "h0", "h1")}

    S.op("sp", lambda e: e.dma_start(out=grow[:], in_=io["post_mix_g"].partition_broadcast(128)), writes=[kgrow], dma_sem=dsem["grow"])
    S.op("sp", lambda e: e.dma_start(out=rst[:], in_=io["km"][:, M_RST:M_RST + W]), writes=[kconst], dma_sem=dsem["const"])
    uniq = [0]

    def pool_dma(fn, key):
        uniq[0] += 1
        S.op("pool", fn, writes=[key], dma_sem=S.new_dma_sem(f"p1u{uniq[0]}"))

    kmsl, kbones, kfst = Key("msl"), Key("bones"), Key("fst")
    kmsu4 = [Key(f"msu4_{q}") for q in range(4)]
    kw2b, ka2b, kg2b0, kg2b1 = Key("w2b"), Key("a2b"), Key("g2b0"), Key("g2b1")
    for dst, c0, n_, k_ in ((msl, M_SL, 128, kmsl), (bones, M_BO, 128, kbones), (fst, M_F, 64, kfst)):
        pool_dma(lambda e, dst=dst, c0=c0, n_=n_: e.dma_start(out=dst[:], in_=io["km"][:, c0:c0 + n_]), k_)
    for q, c0 in enumerate((M_SU, M_IU, M_SU, M_IU)):
        pool_dma(lambda e, q=q, c0=c0: e.dma_start(out=msu4[:, q * 128:(q + 1) * 128], in_=io["km"][:, c0:c0 + 128]), kmsu4[q])
    S.op("pool", lambda e: e.memset(w2b[:], 0.0), writes=[kw2b])
    S.op("pool", lambda e: e.memset(a2b[:], 0.0), writes=[ka2b])
    S.op("pool", lambda e: e.memset(g2b1[:], 0.0), writes=[kg2b1])
    S.op("pool", lambda e: e.memset(wg1[:], 0.0), writes=[kwg1])
    pool_dma(lambda e: e.dma_start(out=w2b[0:64, :], in_=io["w2"][:, :]), kw2b)
    pool_dma(lambda e: e.dma_start(out=a2b[64:128, :], in_=io["a2"][:, :]), ka2b)
    pool_dma(lambda e: e.dma_start(out=g2b0[:], in_=io["g2"][0:128, :]), kg2b0)
    pool_dma(lambda e: e.dma_start(out=g2b1[0:32, :], in_=io["g2"][128:160, :]), kg2b1)
    pool_dma(lambda e: e.dma_start(out=wout[:], in_=io["w_out"].rearrange("(k p) d -> p k d", p=128)), kwout)
    S.op("dve", lambda e: e.tensor_scalar(out=dc[:, 0:15], in0=cc[:, O_MU:O_MU + 15], scalar1=-1.0, scalar2=1.0, op0=ALU.mult, op1=ALU.add),
         reads=[kcc], writes=[kdc])
    S.op("dve", lambda e: e.tensor_scalar(out=dc[:, 15:19], in0=cc[:, O_KA:O_KA + 4], scalar1=-1.0, scalar2=1.0, op0=ALU.mult, op1=ALU.add),
         reads=[kcc], writes=[kdc])
    xts = [sb(f"xt{i}", [128, NBLK, D], F32) for i in range(2)]
    kxts = [[Key(f"xt{i}_{b}") for b in range(NBLK)] for i in range(2)]
    xnT = sb("xnT", [128, 8, W], BF16)
    kxnT = Key("xnT")
    scr = {
        "stat": Rot([sb(f"stat1_{i}", [128, 8], F32) for i in range(4)]),
        "xs": Rot([sb(f"xs1_{i}", [128, D], BF16) for i in range(2)]),
    }
    tf = Rot([sb(f"tf{i}", [128, W], F32) for i in range(9)])
    tb = Rot([sb(f"tb{i}", [128, W], BF16) for i in range(4)])
    qbuf = Rot([sb(f"qbuf{i}", [128, W + 1], F32) for i in range(2)])
    ubuf = Rot([sb(f"ubuf{i}", [128, W + 2], F32) for i in range(2)])
    qh = sb("qh", [128, 15], F32)
    kqh = [Key(f"qh{j}") for j in range(15)]
    uh = sb("uh", [128, 4, 2], F32)
    kuh = [Key(f"uh{c}") for c in range(4)]
    rkv = {n: Rot([sb(f"{n}c{i}", [128, W], F32) for i in range(2)]) for n in ("r", "k", "v")}
    wa = sb("wa", [128, W], F32); kwa = Key("wa")
    g0t = sb("g0t", [128, W], F32); kg0 = Key("g0t")
    g1t = sb("g1t", [128, W], F32); kg1 = Key("g1t")
    twad = sb("twad", [128, W], BF16); ktwad = Key("twad")
    sgds = [(sb(f"sgd0_{i}", [128, W], BF16), sb(f"sgd1_{i}", [128, W], BF16), Key(f"sgd{i}")) for i in range(2)]
    sig = sb("sig", [128, 4, W], F32); ksig = [Key(f"sig{c}") for c in range(4)]
    a4 = sb("a4", [128, 4, W], F32); ka4 = [Key(f"a4{c}") for c in range(4)]
    bonus = sb("bonus", [128, 4, W], F32); kbonus = [Key(f"bonus{c}") for c in range(4)]
    ARbd = sb("ARbd", [128, 4, NCH, 2, 128], BF16); kAbd = [Key(f"Abd{c}") for c in range(4)]; kRbd = [Key(f"Rbd{c}") for c in range(4)]
    Bbd = sb("Bbd", [128, 4, NCH, 128], BF16); kBbd = [Key(f"Bbd{c}") for c in range(4)]
    Kbd = sb("Kbd", [128, 4, NCH, 128], BF16); kKbd = [Key(f"Kbd{c}") for c in range(4)]
    Vbd = sb("Vbd", [128, 4, NCH, 128], BF16); kVbd = [Key(f"Vbd{c}") for c in range(4)]
    PC = sb("PC", [128, 4, NCH], F32); kPC = [Key(f"PC{c}") for c in range(4)]
    M1s = [(sb(f"M1_{i}", [128, 4, 512], BF16), Key(f"M1_{i}")) for i in range(2)]
    NT0s = [(sb(f"NT0_{i}", [128, 4, 128], BF16), Key(f"NT0_{i}")) for i in range(2)]
    M2s = [(sb(f"M2_{i}", [128, 4, 320], BF16), Key(f"M2_{i}")) for i in range(2)]
    NNs = [[(sb(f"NN{i}{j}", [128, 4, 256], BF16), Key(f"NN{i}{j}")) for j in range(2)] for i in range(2)]
    Tbs = [[(sb(f"Tb{i}{j}", [128, 4, 128], BF16), Key(f"Tb{i}{j}")) for j in range(2)] for i in range(2)]
    Zb = sb("Zb", [128, 4, 64], BF16); kZb = Key("Zb")
    Ub = sb("Ub", [128, 4, 64], BF16); kUb = Key("Ub")
    S32 = sb("S32", [128, 4, 64], F32); kS32 = Key("S32")
    Sbf = sb("Sbf", [128, 4, 64], BF16); kSbf = Key("Sbf")
    gst = Rot([sb(f"gst{i}", [128, 8, 4], F32) for i in range(2)])
    ynbd = sb("ynbd", [128, 4, 128], BF16); kynbd = Key("ynbd")
    ynf = sb("ynf", [128, 4, W], F32); _ka, _kb = Key("ynfA"), Key("ynfB"); kynf = [_ka, _ka, _kb, _kb]
    scr["tmp512"] = Rot([View(ynf[:, 0:2, :].rearrange("p c w -> p (c w)")), View(ynf[:, 2:4, :].rearrange("p c w -> p (c w)"))], keys=[_ka, _kb])
    yTc = [sb(f"yTc{i}", [128, 4, W], BF16) for i in range(2)]; kyTc = [[Key(f"yTc{i}_{c}") for c in range(4)] for i in range(2)]
    yTr = sb("yTr", [128, 4, W], BF16); kyTr = [Key(f"yTr{c}") for c in range(4)]

    WQ = INC // 4
    stg_a = [(sig, ksig), (a4, ka4), (bonus, kbonus), (ynf, kynf)]
    stg_b = [(bonus, kbonus), (ynf, kynf)]
    def stage_dma(q, kc, j):
        stg = stg_a if q >= 2 else stg_b
        buf, kbuf = stg[j % len(stg)]
        wt = buf[:].rearrange("p c w -> p (c w)")[:, 0:WQ]
        S.op("sp", lambda e: e.dma_start(out=wt, in_=io["w_in"][kc * 128:(kc + 1) * 128, q * WQ:(q + 1) * WQ]),
             writes=kbuf, dma_sem=dsem[f"wst{j % 4}"])

    def stage_cast(q, kc, j):
        stg = stg_a if q >= 2 else stg_b
        buf, kbuf = stg[j % len(stg)]
        wt = buf[:].rearrange("p c w -> p (c w)")[:, 0:WQ]
        if j % 2 == 0:
            S.op("act", lambda e: e.activation(out=win[:, kc, q * WQ:(q + 1) * WQ], in_=wt, func=AF.Copy,
                                               scale=cc[:, O_PMG + kc:O_PMG + kc + 1]),
                 reads=kbuf + [kcc], writes=[kwin[kc][q]])
        else:
            S.op("dve", lambda e: e.tensor_scalar(out=win[:, kc, q * WQ:(q + 1) * WQ], in0=wt,
                                                  scalar1=cc[:, O_PMG + kc:O_PMG + kc + 1], scalar2=1.0, op0=ALU.mult, op1=ALU.mult),
                 reads=kbuf + [kcc], writes=[kwin[kc][q]])

    pend_dma, pend_cast = [], []
    j = 0
    for q in (3, 2, 0, 1):
        for kc in range(8):
            if q >= 2:
                stage_dma(q, kc, j)
                stage_cast(q, kc, j)
            else:
                pend_dma.append((q, kc, j))
            j += 1

    def flush_stage(everything):
        while pend_cast:
            stage_cast(*pend_cast.pop(0))
        n = len(pend_dma) if everything else 2
        for _ in range(n):
            if pend_dma:
                p_ = pend_dma.pop(0)
                stage_dma(*p_)
                if everything:
                    stage_cast(*p_)
                else:
                    pend_cast.append(p_)

    if n_pre <= 1:
        flush_stage(True)
    S.op("dve", lambda e: e.tensor_copy(out=wg1[:, :, 0:32], in_=win[:, :, INC - 32:INC]), reads=[kwin[k_][3] for k_ in range(8)] + [kwg1], writes=[kwg1])

    ptr = Rot([ps("ptr1", [128, 8, 128], BF16)], excl=True)
    pp = [ps(f"pp{i}", [128, 2, W], F32) for i in range(2)]
    kpp = [PKey(f"pp{i}") for i in range(2)]
    psm = ps("psm", [128, 2, W], F32); kpsm = PKey("psm")
    PA = ps("PA", [128, 1024], F32); kPA = PKey("PA")
    PB = ps("PB", [128, 1024], F32); kPB0 = PKey("PB0"); kPB1 = PKey("PB1")

    for t_, k_ in ((qh, kqh), (uh, kuh)):
        S.op("pool", lambda e, t_=t_: e.memset(t_[:], 0.0), writes=k_)
    S.op("pool", lambda e: e.memset(S32[:], 0.0), writes=[kS32])
    S.op("pool", lambda e: e.memset(Sbf[:], 0.0), writes=[kSbf])
    S.op("pool", lambda e: e.memset(ARbd[:], 0.0), writes=kAbd + kRbd)
    S.op("pool", lambda e: e.memset(Bbd[:], 0.0), writes=kBbd)
    S.op("pool", lambda e: e.memset(Kbd[:], 0.0), writes=kKbd)
    S.op("pool", lambda e: e.memset(Vbd[:], 0.0), writes=kVbd)
    S.op("pool", lambda e: e.memset(ynbd[:], 0.0), writes=[kynbd])
    S.op("pool", lambda e: e.memset(g1t[:], 0.0), writes=[kg1])

    slot_i = [0]

    def proj(col0, ncols, wsrc=None, kw=None):
        j = slot_i[0] % 4
        slot_i[0] += 1
        t_, half, key = pp[j // 2], j % 2, kpp[j // 2]
        for kc in range(8):
            lhsT = win[:, kc, col0:col0 + ncols] if wsrc is None else wsrc[:, kc, :]
            if kw is None:
                wkeys = [kwin[kc][q_] for q_ in range(col0 // (INC // 4), (col0 + ncols - 1) // (INC // 4) + 1)]
            else:
                wkeys = [kw[kc]]
            S.op("pe", lambda e, kc=kc, lhsT=lhsT: e.matmul(t_[0:ncols, half, :], lhsT=lhsT, rhs=xnT[:, kc, :],
                                                            start=(kc == 0), stop=(kc == 7)),
                 reads=wkeys + [kxnT], writes=[key])
        return t_[0:ncols, half, :], key

    def shift_lerp(p_ap, kp, jj, dst_ap, kdst, np_=128):
        qb, kqb = qbuf.next()
        S.op("pool", lambda e: e.tensor_copy(out=qb[0:np_, 0:1], in_=qh[0:np_, jj:jj + 1]), reads=[kqh[jj]], writes=[kqb])
        S.op("act", lambda e: e.activation(out=qb[0:np_, 1:W + 1], in_=p_ap, func=AF.Copy), reads=[kp], writes=[kqb])
        S.op("pool", lambda e: e.tensor_copy(out=qh[0:np_, jj:jj + 1], in_=qb[0:np_, W:W + 1]), reads=[kqb], writes=[kqh[jj]])
        tmp, ktmp = tf.next()
        S.op("pool", lambda e: e.tensor_scalar(out=tmp[0:np_, :], in0=qb[0:np_, 1:W + 1], scalar1=dc[0:np_, jj:jj + 1], scalar2=1.0, op0=ALU.mult, op1=ALU.mult),
             reads=[kqb, kdc], writes=[ktmp])
        S.op("dve", lambda e: e.scalar_tensor_tensor(out=dst_ap, in0=qb[0:np_, 0:W], scalar=cc[0:np_, O_MU + jj:O_MU + jj + 1], in1=tmp[0:np_, :],
                                                     op0=ALU.mult, op1=ALU.add),
             reads=[kqb, kcc, ktmp], writes=[kdst])

    def bd_view(t4, c, h):
        return t4[h * 64:(h + 1) * 64, c, :, h * 64:(h + 1) * 64]

    def v3(t2, h):
        return t2[h * 64:(h + 1) * 64, :].rearrange("p (n t) -> p n t", n=NCH)

    def load_x(t):
        b = t % 2
        S.op("sp", lambda e: e.dma_start(out=xts[b][:], in_=xin[W * t:W * (t + 1), :].rearrange("(b p) d -> p b d", p=128)),
             writes=kxts[b], dma_sem=dsem[f"x{b}"])

    load_x(0)
    if n_tiles > 1:
        load_x(1)

    def tile_parts(t):
        full = t >= n_pre
        b = t % 2
        par = t % 2
        xt = xts[b]
        sgd0, sgd1, ksgd = sgds[par]
        yc, kyc = yTc[par], kyTc[par]

        def early():
            for blk in range(NBLK):
                _rms_transpose(S, xt[:, blk, :], kxts[b][blk], xnT, kxnT, blk * 128, scr, ident, kid, ptr)

            yield
            if full:
                def b1(c):
                    pB, kpB = proj(c * 128, 128)
                    pH, kpH = proj(1024 + c * 128, 128)
                    hAs, khAs = tf.next()
                    S.op("act", lambda e, hAs=hAs, pH=pH: e.activation(out=hAs[:], in_=pH, func=AF.Copy), reads=[kpH], writes=[khAs])
                    ub, kub = ubuf.next()
                    S.op("pool", lambda e, ub=ub, c=c: e.tensor_copy(out=ub[:, 0:2], in_=uh[:, c, :]), reads=[kuh[c]], writes=[kub])
                    S.op("dve", lambda e, ub=ub, pB=pB, hAs=hAs: e.tensor_tensor(out=ub[:, 2:W + 2], in0=pB, in1=hAs[:], op=ALU.mult),
                         reads=[kpB, khAs], writes=[kub])
                    S.op("pool", lambda e, ub=ub, c=c: e.tensor_copy(out=uh[:, c, :], in_=ub[:, W:W + 2]), reads=[kub], writes=[kuh[c]])
                    ta, kta = tf.next()
                    wof = O_CAW + c * 3
                    S.op("act", lambda e, ta=ta, ub=ub, wof=wof: e.activation(out=ta[:], in_=ub[:, 2:W + 2], func=AF.Copy, scale=cc[:, wof + 2:wof + 3]),
                         reads=[kub, kcc], writes=[kta])
                    S.op("dve", lambda e, ta=ta, ub=ub, wof=wof: e.scalar_tensor_tensor(out=ta[:], in0=ub[:, 1:W + 1], scalar=cc[:, wof + 1:wof + 2], in1=ta[:],
                                                                                      op0=ALU.mult, op1=ALU.add), reads=[kub, kcc, kta], writes=[kta])
                    S.op("dve", lambda e, ta=ta, ub=ub, wof=wof: e.scalar_tensor_tensor(out=ta[:], in0=ub[:, 0:W], scalar=cc[:, wof:wof + 1], in1=ta[:],
                                                                                      op0=ALU.mult, op1=ALU.add), reads=[kub, kcc, kta], writes=[kta])
                    pC, kpC = proj(512 + c * 128, 128)
                    S.op("dve", lambda e, ta=ta, pC=pC, c=c: e.tensor_tensor(out=yc[:, c, :], in0=pC, in1=ta[:], op=ALU.mult),
                         reads=[kpC, kta], writes=[kyc[c]])
                for c_ in range(4):
                    b1(c_)

            yield
            QC = 1536
            p_, kp_ = proj(QC + 12 * 128, 128)
            shift_lerp(p_, kp_, 12, wa[:], kwa)
            S.op("act", lambda e: e.activation(out=twad[0:64, :], in_=wa[0:64, :], func=AF.Tanh), reads=[kwa], writes=[ktwad])
            S.op("dve", lambda e: e.tensor_copy(out=twad[64:128, :], in_=wa[64:128, :]), reads=[kwa], writes=[ktwad])
            if full:
                p_, kp_ = proj(QC + 13 * 128, 128)
                shift_lerp(p_, kp_, 13, g0t[:], kg0)
                p_, kp_ = proj(None, 128, wsrc=wg1, kw=[kwg1] * 8)
                shift_lerp(p_[0:32, :], kp_, 14, g1t[0:32, :], kg1, np_=32)
                S.op("act", lambda e: e.activation(out=sgd0[:], in_=g0t[:], func=AF.Sigmoid), reads=[kg0], writes=[ksgd])
                S.op("act", lambda e: e.activation(out=sgd1[:], in_=g1t[:], func=AF.Sigmoid), reads=[kg1], writes=[ksgd])
            yield
            for c in range(4):
                S.op("pe", lambda e, c=c: e.matmul(psm[:, 0, :], lhsT=w2b[:, c * 128:(c + 1) * 128], rhs=twad[:], start=True, stop=True),
                     reads=[kw2b, ktwad], writes=[kpsm])
                S.op("pe", lambda e, c=c: e.matmul(psm[:, 1, :], lhsT=a2b[:, c * 128:(c + 1) * 128], rhs=twad[:], start=True, stop=True),
                     reads=[ka2b, ktwad], writes=[kpsm])
                S.op("act", lambda e, c=c: e.activation(out=sig[:, c, :], in_=psm[:, 0, :], func=AF.Sigmoid, bias=cc[:, O_W0 + c:O_W0 + c + 1]),
                     reads=[kpsm, kcc], writes=[ksig[c]])
                S.op("act", lambda e, c=c: e.activation(out=a4[:, c, :], in_=psm[:, 1, :], func=AF.Sigmoid, bias=cc[:, O_A0 + c:O_A0 + c + 1]),
                     reads=[kpsm, kcc], writes=[ka4[c]])

            yield

        def b4(c):
            kt_, kkt = rkv["k"].next()
            vt_, kvt = rkv["v"].next()
            p_, kp_ = proj(QC + (4 + c) * 128, 128)
            shift_lerp(p_, kp_, 4 + c, kt_[:], kkt)
            yield
            p_, kp_ = proj(QC + (8 + c) * 128, 128)
            shift_lerp(p_, kp_, 8 + c, vt_[:], kvt)
            yield
            if full:
                rt_, krt = rkv["r"].next()
                p_, kp_ = proj(QC + c * 128, 128)
                shift_lerp(p_, kp_, c, rt_[:], krt)
                yield
            kkr, kkkr = tf.next()
            S.op("pool", lambda e, kkr=kkr, kt_=kt_, c=c: e.tensor_scalar(out=kkr[:], in0=kt_[:], scalar1=cc[:, O_KK + c:O_KK + c + 1], scalar2=1.0, op0=ALU.mult, op1=ALU.mult),
                 reads=[kkt, kcc], writes=[kkkr])
            sq, ksq = tb.next()
            S.op("act", lambda e, sq=sq, kkr=kkr: e.activation(out=sq[:], in_=kkr[:], func=AF.Square), reads=[kkkr], writes=[ksq])
            S.op("pe", lambda e, sq=sq: e.matmul(psm[:, 0, :], lhsT=bones[:], rhs=sq[:], start=True, stop=True), reads=[kbones, ksq], writes=[kpsm])
            yield
            cs, kcs = tf.next()
            S.op("dve", lambda e, cs=cs, c=c: e.tensor_tensor_scan(out=cs[:], data0=rst[:], data1=sig[:, c, :], initial=0.0, op0=ALU.mult, op1=ALU.add),
                 reads=[kconst, ksig[c]], writes=[kcs])
            yield
            E1, kE1 = tf.next()
            E2, kE2 = tf.next()
            E3, kE3 = tf.next()
            dd, kdd = tf.next()
            S.op("act", lambda e, E1=E1, cs=cs: e.activation(out=E1[:], in_=cs[:], func=AF.Exp, scale=-C05), reads=[kcs], writes=[kE1])
            S.op("act", lambda e, E2=E2, cs=cs: e.activation(out=E2[:], in_=cs[:], func=AF.Exp, scale=C05), reads=[kcs], writes=[kE2])
            S.op("pool", lambda e, dd=dd, cs=cs, c=c: e.tensor_tensor(out=dd[:], in0=cs[:], in1=sig[:, c, :], op=ALU.subtract), reads=[kcs, ksig[c]], writes=[kdd])
            S.op("act", lambda e, E3=E3, dd=dd: e.activation(out=E3[:], in_=dd[:], func=AF.Exp, scale=-C05), reads=[kdd], writes=[kE3])
            S.op("pool", lambda e, E1=E1, c=c: e.tensor_copy(out=PC[:, c, :], in_=E1[:].rearrange("p (n t) -> p n t", n=NCH)[:, :, CH - 1]),
                 reads=[kE1], writes=[kPC[c]])
            yield
            mm, kmm = tf.next()
            S.op("pool", lambda e, mm=mm, c=c: e.tensor_scalar(out=mm[:], in0=a4[:, c, :], scalar1=cc[:, O_KA + c:O_KA + c + 1], scalar2=dc[:, 15 + c:16 + c],
                                                               op0=ALU.mult, op1=ALU.add), reads=[ka4[c], kcc, kdc], writes=[kmm])
            kp, kkp = tf.next()
            S.op("dve", lambda e, kp=kp, kt_=kt_, mm=mm: e.tensor_tensor(out=kp[:], in0=kt_[:], in1=mm[:], op=ALU.mult), reads=[kkt, kmm], writes=[kkp])
            nrm, knrm = tf.next()
            S.op("act", lambda e, nrm=nrm: e.activation(out=nrm[:], in_=psm[:, 0, :], func=AF.Sqrt, bias=1e-24), reads=[kpsm], writes=[knrm])
            S.op("dve", lambda e, nrm=nrm: e.reciprocal(out=nrm[:], in_=nrm[:]), reads=[knrm], writes=[knrm])
            kk, kkk = tf.next()
            S.op("dve", lambda e, kk=kk, kkr=kkr, nrm=nrm: e.tensor_tensor(out=kk[:], in0=kkr[:], in1=nrm[:], op=ALU.mult), reads=[kkkr, knrm], writes=[kkk])
            akk, kakk = tf.next()
            S.op("pool", lambda e, akk=akk, kk=kk, c=c: e.tensor_tensor(out=akk[:], in0=a4[:, c, :], in1=kk[:], op=ALU.mult), reads=[ka4[c], kkk], writes=[kakk])
            yield
            for h in range(2):
                S.op("dve", lambda e, h=h, kk=kk, E3=E3, c=c: e.scalar_tensor_tensor(out=ARbd[h * 64:(h + 1) * 64, c, :, 0, h * 64:(h + 1) * 64], in0=v3(kk, h), scalar=-1.0,
                                                                                   in1=v3(E3, h), op0=ALU.mult, op1=ALU.mult),
                     reads=[kkk, kE3], writes=[kAbd[c]])
                S.op("dve", lambda e, h=h, akk=akk, E2=E2, c=c: e.tensor_tensor(out=bd_view(Bbd, c, h), in0=v3(akk, h), in1=v3(E2, h), op=ALU.mult),
                     reads=[kakk, kE2], writes=[kBbd[c]])
                S.op("pool", lambda e, h=h, kp=kp, E2=E2, c=c: e.tensor_tensor(out=bd_view(Kbd, c, h), in0=v3(kp, h), in1=v3(E2, h), op=ALU.mult),
                     reads=[kkp, kE2], writes=[kKbd[c]])
                S.op("act", lambda e, h=h, vt_=vt_, c=c: e.activation(out=bd_view(Vbd, c, h), in_=v3(vt_, h), func=AF.Copy), reads=[kvt], writes=[kVbd[c]])
                if full:
                    S.op("pool", lambda e, h=h, rt_=rt_, E1=E1, c=c: e.tensor_tensor(out=ARbd[h * 64:(h + 1) * 64, c, :, 1, h * 64:(h + 1) * 64], in0=v3(rt_, h), in1=v3(E1, h),
                                                                                   op=ALU.mult), reads=[krt, kE1], writes=[kRbd[c]])
            yield
            if full:
                rk, krk = tf.next()
                S.op("pool", lambda e, rk=rk, rt_=rt_, kp=kp: e.tensor_tensor(out=rk[:], in0=rt_[:], in1=kp[:], op=ALU.mult), reads=[krt, kkp], writes=[krk])
                rkb, krkb = tb.next()
                S.op("dve", lambda e, rkb=rkb, rk=rk, c=c: e.tensor_scalar(out=rkb[:], in0=rk[:], scalar1=cc[:, O_RK + c:O_RK + c + 1], scalar2=1.0, op0=ALU.mult, op1=ALU.mult),
                     reads=[krk, kcc], writes=[krkb])
                S.op("pe", lambda e, rkb=rkb: e.matmul(psm[:, 1, :], lhsT=bones[:], rhs=rkb[:], start=True, stop=True), reads=[kbones, krkb], writes=[kpsm])
                S.op("dve", lambda e, vt_=vt_, c=c: e.tensor_tensor(out=bonus[:, c, :], in0=psm[:, 1, :], in1=vt_[:], op=ALU.mult), reads=[kpsm, kvt], writes=[kbonus[c]])

        b4gens = {}

        def b4_start(c):
            g = b4(c)
            for _ in range(3 if full else 2):
                next(g)
            b4gens[c] = g

        def b4_pre():
            b4_start(0)
            yield

        def b4_all():
            if 0 not in b4gens:
                b4_start(0)
            for c_ in range(4):
                if c_ + 1 < 4:
                    b4_start(c_ + 1)
                drain_g(b4gens.pop(c_))
            if DEBUG_STOP < 5:
                return


        PA3 = PA[:].rearrange("p (a w) -> p a w", a=2)
        PAd = PA[:].rearrange("p (a w) -> p a w", a=4)
        PBt = PB[:, 0:512].rearrange("p (a w) -> p a w", a=4)
        PQ0 = PB[:, 512:768].rearrange("p (a w) -> p a w", a=4)
        PQ1 = PB[:, 768:1024].rearrange("p (a w) -> p a w", a=4)
        Tfinal = {}

        def gen_AD(n, s_):
            M1, kM1 = M1s[s_]
            NT0, kNT0 = NT0s[s_]
            M2, kM2 = M2s[s_]
            for pi in range(2):
                for i in range(2):
                    c = pi * 2 + i
                    rhsAR = ARbd[:, c, n, :, :].rearrange("p a w -> p (a w)")
                    S.op("pe", lambda e, i=i, c=c, rhsAR=rhsAR: e.matmul(PA3[:, i, 0:256], lhsT=Bbd[:, c, n, :], rhs=rhsAR, start=True, stop=True),
                         reads=[kBbd[c], kAbd[c], kRbd[c]], writes=[kPA])
                    S.op("pe", lambda e, i=i, c=c, rhsAR=rhsAR: e.matmul(PA3[:, i, 256:512], lhsT=Kbd[:, c, n, :], rhs=rhsAR, start=True, stop=True),
                         reads=[kKbd[c], kAbd[c], kRbd[c]], writes=[kPA])
                S.op("dve", lambda e, pi=pi: e.tensor_tensor(out=M1[:, 2 * pi:2 * pi + 2, :], in0=PA3, in1=msu4[:].unsqueeze(1).to_broadcast([128, 2, 512]), op=ALU.mult),
                     reads=[kPA] + kmsu4, writes=[kM1])
                yield
                for i in range(2):
                    c = pi * 2 + i
                    S.op("pe", lambda e, i=i, c=c: e.matmul(PA3[:, i, 0:128], lhsT=ARbd[:, c, n, 0, :], rhs=Bbd[:, c, n, :], start=True, stop=True),
                         reads=[kAbd[c], kBbd[c]], writes=[kPA])
                    S.op("pe", lambda e, i=i, c=c: e.matmul(PA3[:, i, 128:256], lhsT=Bbd[:, c, n, :], rhs=ident[:], start=True, stop=True),
                         reads=[kBbd[c], kid], writes=[kPA])
                    S.op("pe", lambda e, i=i, c=c: e.matmul(PA3[:, i, 256:384], lhsT=Kbd[:, c, n, :], rhs=ident[:], start=True, stop=True),
                         reads=[kKbd[c], kid], writes=[kPA])
                    S.op("pe", lambda e, i=i, c=c: e.matmul(PA3[:, i, 384:448], lhsT=Vbd[:, c, n, :], rhs=fst[:], start=True, stop=True),
                         reads=[kVbd[c], kfst], writes=[kPA])
                S.op("dve", lambda e, pi=pi: e.tensor_tensor(out=NT0[:, 2 * pi:2 * pi + 2, :], in0=PA3[:, :, 0:128], in1=msl[:].unsqueeze(1).to_broadcast([128, 2, 128]), op=ALU.mult),
                     reads=[kPA, kmsl], writes=[kNT0])
                S.op("act", lambda e, pi=pi: e.activation(out=M2[:, 2 * pi:2 * pi + 2, :], in_=PA3[:, :, 128:448], func=AF.Copy), reads=[kPA], writes=[kM2])
                yield
            Tcur, kTcur = Tbs[s_][0]
            S.op("pool", lambda e, Tcur=Tcur: e.tensor_tensor(out=Tcur[:], in0=M1[:, :, 0:128], in1=ident[:].unsqueeze(1).to_broadcast([128, 4, 128]), op=ALU.add),
                 reads=[kM1, kid], writes=[kTcur])
            Nprev = lambda c: M1[:, c, 0:128]
            NTprev = lambda c: NT0[:, c, :]
            kprev = [kM1, kNT0]
            for j in range(1, 6):
                NNj, kNNj = NNs[s_][j % 2]
                for c in range(4):
                    if j < 5:
                        S.op("pe", lambda e, c=c, Nprev=Nprev, NTprev=NTprev: e.matmul(PAd[:, c, 0:128], lhsT=NTprev(c), rhs=Nprev(c), start=True, stop=True),
                             reads=kprev, writes=[kPA])
                    S.op("pe", lambda e, c=c, Nprev=Nprev, NTprev=NTprev: e.matmul(PAd[:, c, 128:256], lhsT=Nprev(c), rhs=NTprev(c), start=True, stop=True),
                         reads=kprev, writes=[kPA])
                if j < 5:
                    S.op("act", lambda e, NNj=NNj: e.activation(out=NNj[:], in_=PAd, func=AF.Copy), reads=[kPA], writes=[kNNj])
                else:
                    S.op("act", lambda e, NNj=NNj: e.activation(out=NNj[:, :, 128:256], in_=PAd[:, :, 128:256], func=AF.Copy), reads=[kPA], writes=[kNNj])
                yield
                for c in range(4):
                    S.op("pe", lambda e, c=c, NNj=NNj, Tcur=Tcur: e.matmul(PBt[:, c, :], lhsT=NNj[:, c, 128:256], rhs=Tcur[:, c, :], start=True, stop=True),
                         reads=[kNNj, kTcur], writes=[kPB0])
                Tnew, kTnew = Tbs[s_][j % 2]
                S.op("dve", lambda e, Tnew=Tnew, Tcur=Tcur: e.tensor_tensor(out=Tnew[:], in0=PBt, in1=Tcur[:], op=ALU.add), reads=[kPB0, kTcur], writes=[kTnew])
                Tcur, kTcur = Tnew, kTnew
                Nprev = (lambda NNj: (lambda c: NNj[:, c, 0:128]))(NNj)
                NTprev = (lambda NNj: (lambda c: NNj[:, c, 128:256]))(NNj)
                kprev = [kNNj]
                yield
            Tfinal[n] = (Tcur, kTcur)

        def gen_SQ(n, s_):
            M1, kM1 = M1s[s_]
            M2, kM2 = M2s[s_]
            Tcur, kTcur = Tfinal[n]
            for c in range(4):
                S.op("pe", lambda e, c=c: e.matmul(PQ0[:, c, :], lhsT=ARbd[:, c, n, 0, :], rhs=Sbf[:, c, :], start=True, stop=False),
                     reads=[kAbd[c], kSbf], writes=[kPB1])
                S.op("pe", lambda e, c=c: e.matmul(PQ0[:, c, :], lhsT=M1[:, c, 256:384], rhs=M2[:, c, 256:320], start=False, stop=True),
                     reads=[kM1, kM2], writes=[kPB1])
            S.op("act", lambda e: e.activation(out=Zb[:], in_=PQ0, func=AF.Copy), reads=[kPB1], writes=[kZb])
            yield
            for c in range(4):
                S.op("pe", lambda e, c=c: e.matmul(PQ1[:, c, :], lhsT=Tcur[:, c, :], rhs=Zb[:, c, :], start=True, stop=True),
                     reads=[kTcur, kZb], writes=[kPB1])
            S.op("act", lambda e: e.activation(out=Ub[:], in_=PQ1, func=AF.Copy), reads=[kPB1], writes=[kUb])
            yield
            for c in range(4):
                S.op("pe", lambda e, c=c: e.matmul(PQ0[:, c, :], lhsT=M2[:, c, 0:128], rhs=Ub[:, c, :], start=True, stop=False),
                     reads=[kM2, kUb], writes=[kPB1])
                S.op("pe", lambda e, c=c: e.matmul(PQ0[:, c, :], lhsT=M2[:, c, 128:256], rhs=M2[:, c, 256:320], start=False, stop=True),
                     reads=[kM2], writes=[kPB1])
            if full:
                for c in range(4):
                    S.op("pe", lambda e, c=c: e.matmul(PQ1[:, c, :], lhsT=ARbd[:, c, n, 1, :], rhs=Sbf[:, c, :], start=True, stop=False),
                         reads=[kRbd[c], kSbf], writes=[kPB1])
                    S.op("pe", lambda e, c=c: e.matmul(PQ1[:, c, :], lhsT=M1[:, c, 128:256], rhs=Ub[:, c, :], start=False, stop=False),
                         reads=[kM1, kUb], writes=[kPB1])
                    S.op("pe", lambda e, c=c: e.matmul(PQ1[:, c, :], lhsT=M1[:, c, 384:512], rhs=M2[:, c, 256:320], start=False, stop=True),
                         reads=[kM1, kM2], writes=[kPB1])
            pcb = PC[:, :, n:n + 1].to_broadcast([128, 4, 64])
            tS_, ktS = tf.next()
            tmpS = tS_[:].rearrange("p (c v) -> p c v", c=4)
            S.op("dve", lambda e: e.tensor_tensor(out=tmpS, in0=PQ0, in1=S32[:], op=ALU.add), reads=[kPB1, kS32], writes=[ktS])
            S.op("dve", lambda e: e.tensor_tensor(out=Sbf[:], in0=tmpS, in1=pcb, op=ALU.mult), reads=[ktS] + kPC, writes=[kSbf])
            S.op("pool", lambda e: e.tensor_tensor(out=S32[:], in0=tmpS, in1=pcb, op=ALU.mult), reads=[ktS] + kPC, writes=[kS32])
            yield
            if full:
                g_, kg_ = gst.next()
                ys_, kysq = tf.next()
                ysq = ys_[:].rearrange("p (c v) -> p c v", c=4)
                yc_, kycen = tf.next()
                ycen = yc_[:].rearrange("p (c v) -> p c v", c=4)
                S.op("dve", lambda e: e.tensor_reduce(out=g_[:, 0, :], in_=PQ1, axis=AX.X, op=ALU.add), reads=[kPB1], writes=[kg_])
                S.op("act", lambda e: e.activation(out=ysq, in_=PQ1, func=AF.Square), reads=[kPB1], writes=[kysq])
                S.op("dve", lambda e: e.tensor_reduce(out=g_[:, 1, :], in_=ysq, axis=AX.X, op=ALU.add), reads=[kysq], writes=[kg_])
                S.op("dve", lambda e: e.tensor_scalar(out=g_[:, 2, :], in0=g_[:, 0, :], scalar1=1.0 / 64, scalar2=1.0, op0=ALU.mult, op1=ALU.mult), reads=[kg_], writes=[kg_])
                S.op("dve", lambda e: e.tensor_tensor(out=g_[:, 3, :], in0=g_[:, 2, :], in1=g_[:, 2, :], op=ALU.mult), reads=[kg_], writes=[kg_])
                S.op("dve", lambda e: e.scalar_tensor_tensor(out=g_[:, 4, :], in0=g_[:, 1, :], scalar=1.0 / 64, in1=g_[:, 3, :], op0=ALU.mult, op1=ALU.subtract),
                     reads=[kg_], writes=[kg_])
                S.op("act", lambda e: e.activation(out=g_[:, 5, :], in_=g_[:, 4, :], func=AF.Sqrt, bias=GN_EPS), reads=[kg_], writes=[kg_])
                S.op("dve", lambda e: e.reciprocal(out=g_[:, 6, :], in_=g_[:, 5, :]), reads=[kg_], writes=[kg_])
                S.op("dve", lambda e: e.tensor_tensor(out=ycen, in0=PQ1, in1=g_[:, 2, :].unsqueeze(2).to_broadcast([128, 4, 64]), op=ALU.subtract),
                     reads=[kPB1, kg_], writes=[kycen])
                yield
                for h in range(2):
                    hs = slice(h * 64, (h + 1) * 64)
                    S.op("dve", lambda e, hs=hs: e.tensor_tensor(out=ynbd[hs, :, hs], in0=ycen[hs, :, :], in1=g_[hs, 6, :].unsqueeze(2).to_broadcast([64, 4, 64]),
                                                                  op=ALU.mult), reads=[kycen, kg_], writes=[kynbd])
                for c in range(4):
                    S.op("pe", lambda e, c=c: e.matmul(PQ0[:, c, :], lhsT=ynbd[:, c, :], rhs=fst[:], start=True, stop=True), reads=[kynbd, kfst], writes=[kPB1])
                S.op("act", lambda e: e.activation(out=ynf[:, :, n * CH:(n + 1) * CH], in_=PQ0, func=AF.Copy), reads=[kPB1], writes=kynf)
                yield

        def drain(g):
            for _ in g:
                pass

        def interleave(ga, gb, ra=2):
            a_live, b_live = ga is not None, gb is not None
            while a_live or b_live:
                for _ in range(ra):
                    if a_live:
                        try:
                            next(ga)
                        except StopIteration:
                            a_live = False
                if b_live:
                    try:
                        next(gb)
                    except StopIteration:
                        b_live = False

        def interleave_g(ga, gb, ra=2):
            a_live, b_live = ga is not None, gb is not None
            while a_live or b_live:
                for _ in range(ra):
                    if a_live:
                        try:
                            next(ga)
                        except StopIteration:
                            a_live = False
                if b_live:
                    try:
                        next(gb)
                    except StopIteration:
                        b_live = False
                yield

        def c_stage():
            yield from gen_AD(0, 0)
            for n_ in range(NCH):
                gd = gen_AD(n_ + 1, (n_ + 1) % 2) if n_ + 1 < NCH else None
                yield from interleave_g(gd, gen_SQ(n_, n_ % 2))

        def de():
            if not full:
                if t + 2 < n_tiles:
                    load_x(t + 2)
                return
            for c in range(4):
                S.op("pe", lambda e, c=c: e.matmul(psm[:, 0, :], lhsT=g2b0[:, c * 128:(c + 1) * 128], rhs=sgd0[:], start=True, stop=False), reads=[kg2b0, ksgd], writes=[kpsm])
                S.op("pe", lambda e, c=c: e.matmul(psm[:, 0, :], lhsT=g2b1[:, c * 128:(c + 1) * 128], rhs=sgd1[:], start=False, stop=True), reads=[kg2b1, ksgd], writes=[kpsm])
                y1, ky1 = tf.next()
                S.op("dve", lambda e, c=c, y1=y1: e.scalar_tensor_tensor(out=y1[:], in0=ynf[:, c, :], scalar=cc[:, O_LW + c:O_LW + c + 1], in1=bonus[:, c, :], op0=ALU.mult, op1=ALU.add),
                     reads=[kynf[c], kcc, kbonus[c]], writes=[ky1])
                S.op("dve", lambda e, c=c, y1=y1: e.scalar_tensor_tensor(out=yTr[:, c, :], in0=y1[:], scalar=cc[:, O_LB + c:O_LB + c + 1], in1=psm[:, 0, :], op0=ALU.add, op1=ALU.mult),
                     reads=[ky1, kcc, kpsm], writes=[kyTr[c]])
            if DEBUG_STOP < 9:
                return
            for blk in range(NBLK):
                for hf in range(2):
                    pflat = pp[hf][:].rearrange("p a w -> p (a w)")
                    for e_ in range(8):
                        ysrc = yc[:, e_, blk * 128:(blk + 1) * 128] if e_ < 4 else yTr[:, e_ - 4, blk * 128:(blk + 1) * 128]
                        ykey = kyc[e_] if e_ < 4 else kyTr[e_ - 4]
                        S.op("pe", lambda e, e_=e_, hf=hf, pflat=pflat, ysrc=ysrc: e.matmul(pflat, lhsT=ysrc, rhs=wout[:, e_, hf * 512:(hf + 1) * 512],
                                                                                            start=(e_ == 0), stop=(e_ == 7)), reads=[ykey, kwout], writes=[kpp[hf]])
                class _V:
                    def __init__(self, ap): self.ap = ap
                    def __getitem__(self, k): return self.ap
                _post_norm_residual(S, [(_V(pp[0][:].rearrange("p a w -> p (a w)")), kpp[0]), (_V(pp[1][:].rearrange("p a w -> p (a w)")), kpp[1])],
                                    xt[:, blk, :], kxts[b][blk], grow, kgrow, scr)
            ht_i = t - n_pre
            S.op("sp", lambda e, xt=xt, ht_i=ht_i: e.dma_start(out=hscr[W * ht_i:W * (ht_i + 1), :].rearrange("(b p) d -> p b d", p=128), in_=xt[:]),
                 reads=kxts[b], writes=[khscr[ht_i]], dma_sem=dsem[f"h{b}"])
            if t + 2 < n_tiles:
                load_x(t + 2)

        return early, b4_all, c_stage, de, b4_pre

    def drain_g(g):
        for _ in g:
            pass

    def chain_g(*gs):
        for g in gs:
            yield from g

    def interleave2(ga, gb, ra, rb):
        a_live, b_live = ga is not None, gb is not None
        while a_live or b_live:
            for _ in range(ra):
                if a_live:
                    try:
                        next(ga)
                    except StopIteration:
                        a_live = False
            for _ in range(rb):
                if b_live:
                    try:
                        next(gb)
                    except StopIteration:
                        b_live = False

    parts = [tile_parts(t_i) for t_i in range(n_tiles)]
    drain_g(parts[0][0]())
    parts[0][1]()
    for t_i in range(n_tiles):
        if pend_dma or pend_cast:
            flush_stage(t_i >= n_pre - 1)
        early_next = chain_g(parts[t_i + 1][0](), parts[t_i + 1][4]()) if t_i + 1 < n_tiles else None
        interleave2(parts[t_i][2](), early_next, 2, 1)
        parts[t_i][3]()
        if t_i + 1 < n_tiles:
            parts[t_i + 1][1]()


def _host_consts():
    km = np.zeros((128, NKM), np.float32)
    km[:, M_ID:M_ID + 128] = np.eye(128, dtype=np.float32)
    idx = np.arange(128)
    same = (idx[:, None] // 64) == (idx[None, :] // 64)
    s, t = idx[:, None] % 64, idx[None, :] % 64
    km[:, M_SU:M_SU + 128] = (same & (s < t)).astype(np.float32)
    km[:, M_IU:M_IU + 128] = (same & (s <= t)).astype(np.float32)
    km[:, M_SL:M_SL + 128] = (same & (s > t)).astype(np.float32)
    km[:, M_BO:M_BO + 128] = same.astype(np.float32)
    km[:, M_F:M_F + 64] = (idx[:, None] % 64 == np.arange(64)[None, :]).astype(np.float32)
    rst = np.ones((128, 256), np.float32)
    rst[:, ::64] = 0.0
    km[:, M_RST:M_RST + 256] = rst
    return km


def _pack_cc(inp, hmask):
    cc = np.zeros((128, NCC), np.float32)
    col = lambda v, n: np.ascontiguousarray(np.asarray(v, np.float32).reshape(n, 128).T)
    cc[:, O_PMG:O_PMG + 8] = col(inp["pre_mix_g"][0], 8)
    cc[:, O_PFG:O_PFG + 8] = col(inp["pre_ffn_g"][0], 8)
    caw = np.asarray(inp["conv_a_w"][0], np.float32)
    cc[:, O_CAW:O_CAW + 12] = caw.T.reshape(4, 128, 3).transpose(1, 0, 2).reshape(128, 12)
    mu = np.zeros(1920, np.float32)
    mu[:1824] = np.asarray(inp["shift_mu"][0], np.float32)
    cc[:, O_MU:O_MU + 15] = col(mu, 15)
    for off, name in ((O_W0, "w0"), (O_A0, "a0"), (O_KK, "k_k"), (O_KA, "k_a"), (O_LW, "lnx_w"), (O_LB, "lnx_b")):
        cc[:, off:off + 4] = col(inp[name][0], 4)
    cc[:, O_RK:O_RK + 4] = col(np.asarray(inp["r_k"][0], np.float32).reshape(512), 4)
    fcw = np.asarray(inp["ffn_conv_w"][0], np.float32)
    cc[:, O_FCW:O_FCW + 132] = fcw.T.reshape(44, 128, 3).transpose(1, 0, 2).reshape(128, 132)
    cc[:, O_FCB:O_FCB + 44] = col(inp["ffn_conv_b"][0], 44)
    cc[:, O_HM] = hmask
    return cc


_NC_CACHE = {}


def kernel(**inputs):
    n_pre, n_main = 15, 16
    x = np.asarray(inputs["x"], np.float32)
    B, T, _ = x.shape
    half = T // 2
    if "full" not in _NC_CACHE:
        _NC_CACHE["full"] = build(n_pre, n_main, "full")
    nc = _NC_CACHE["full"]
    km = _host_consts()
    f = lambda n: np.ascontiguousarray(np.asarray(inputs[n], np.float32)[0])
    in_maps = []
    for c in range(8):
        b, h = c // 2, c % 2
        xin = np.zeros((T, D), np.float32)
        if h == 0:
            xin[half:] = x[b, :half]
        else:
            xin[:] = x[b]
        in_maps.append({
            "xin": xin, "cc": _pack_cc(inputs, float(h)), "km": km,
            "post_mix_g": f("post_mix_g"), "post_ffn_g": f("post_ffn_g"),
            "w_in": f("w_in"), "w_out": f("w_out"), "w_up": f("w_up"), "w_down": f("w_down"),
            "w2": f("w2"), "a2": f("a2"), "g2": f("g2"),
        })
    res = run_bass_kernel_spmd(nc, in_maps, core_ids=list(range(8)))
    out = np.zeros((B, T, D), np.float32)
    for c in range(8):
        b, h = c // 2, c % 2
        out[b, h * half:(h + 1) * half] = res.results[c]["out"]
    return out
```

```python
import contextlib
import numpy as np
import concourse.bass as bass
import concourse.mybir as mybir
from concourse.bass_utils import run_bass_kernel_spmd

F32 = mybir.dt.float32
BF16 = mybir.dt.bfloat16
AF = mybir.ActivationFunctionType
ALU = mybir.AluOpType
AX = mybir.AxisListType

D = 1024
W = 256
NBLK = 2
CH = 64
NCH = W // CH
INC = 3360
DFF = 2816
NPAIR = 22
QC = 1536
RMS_EPS = 1e-6
GN_EPS = 64 * 1e-5
EPOCH = 12000
DEBUG_SUB = 99
EMBED_WAITS = True
TRANSITIVE = True
DEBUG_STOP = 99

O_PMG, O_PFG, O_CAW, O_MU, O_W0, O_A0, O_KK, O_KA, O_RK, O_LW, O_LB, O_FCW, O_FCB, O_HM = (
    0, 8, 16, 28, 43, 47, 51, 55, 59, 63, 67, 71, 203, 247)
NCC = 248
M_ID, M_SU, M_IU, M_SL, M_BO, M_F, M_RST = 0, 128, 256, 384, 512, 640, 704
NKM = 704 + 256


class Key:
    __slots__ = ("name", "writer", "readers", "excl")

    def __init__(self, name, excl=False):
        self.name = name
        self.writer = None
        self.readers = []
        self.excl = excl


def PKey(name):
    return Key(name, excl=True)


class Sched:
    ENGS = ("pe", "act", "dve", "pool", "sp")

    def __init__(self, nc, sem_stack, prefix):
        self.nc = nc
        self.sem_stack = sem_stack
        self.prefix = prefix
        self.ops = {e: [] for e in self.ENGS}
        self.count = {e: 0 for e in self.ENGS}
        self.sems = {}
        self.waited = {e: {} for e in self.ENGS}
        self.dma_counts = {}
        self.last_tok = {e: None for e in self.ENGS}
        self.tok_order = {}
        self.n_tok = 0
        self.tok_know = {}

    def _eng_sem(self, eng, idx):
        sid = f"{self.prefix}s_{eng}_{idx // EPOCH}"
        self.sems.setdefault(sid, None)
        return sid, (idx % EPOCH) + 1

    def new_dma_sem(self, name):
        sid = f"{self.prefix}d_{name}"
        assert sid not in self.sems, sid
        self.sems[sid] = None
        self.dma_counts[sid] = 0
        return sid

    def _need_waits(self, eng, tokens):
        w = self.waited[eng]
        cand = {}
        for t in tokens:
            if t is None:
                continue
            sid, val, _ = t
            if w.get(sid, 0) >= val:
                continue
            if cand.get(sid, (0, None))[0] < val:
                cand[sid] = (val, t)
        out = []
        for sid, (val, t) in sorted(cand.items(), key=lambda kv: -self.tok_order.get(kv[1][1], 0)):
            if w.get(sid, 0) >= val:
                continue
            out.append((sid, val))
            w[sid] = val
            if TRANSITIVE:
                for s2, v2 in self.tok_know.get(t, {}).items():
                    if w.get(s2, 0) < v2:
                        w[s2] = v2
        return out

    def op(self, eng, fn, reads=(), writes=(), dma_sem=None, multi=False):
        toks = []
        raw = set()
        for k in reads:
            toks.append(k.writer)
            if k.writer is not None:
                raw.add(k.writer)
            if k.excl:
                toks.extend(r for r in k.readers if r[2] != eng)
        for k in writes:
            toks.append(k.writer)
            toks.extend(k.readers)
        if eng == "pe":
            toks = [t for t in toks if t is not None and t[2] != "pe"]
        waits = self._need_waits(eng, toks)
        if dma_sem is None:
            idx = self.count[eng]
            self.count[eng] += 1
            sid, val = self._eng_sem(eng, idx)
            tok = (sid, val, eng)
            inc = (sid, 1)
            self.last_tok[eng] = tok
        else:
            self.dma_counts[dma_sem] += 16
            tok = (dma_sem, self.dma_counts[dma_sem], "dma")
            inc = (dma_sem, 16)
        self.n_tok += 1
        self.tok_order[tok] = self.n_tok
        know = dict(self.waited[eng])
        if dma_sem is None and tok[1] > 1:
            know[tok[0]] = tok[1] - 1
        self.tok_know[tok] = know
        embed = EMBED_WAITS and dma_sem is None and eng != "pe" and not multi
        self.ops[eng].append((fn, waits, inc, embed))
        for k in reads:
            k.readers.append(tok)
        for k in writes:
            k.writer = tok
            k.readers = []
        return tok

    def barrier(self, extra_keys=()):
        toks = [t for t in self.last_tok.values() if t is not None]
        for k in extra_keys:
            toks.append(k.writer)
            toks.extend(k.readers)
        for eng in self.ENGS:
            waits = self._need_waits(eng, [t for t in toks if t is not None and t[2] != eng])
            if waits:
                self.ops[eng].append((None, waits, None, False))

    def final_wait(self, eng, keys):
        toks = []
        for k in keys:
            toks.append(k.writer)
            toks.extend(k.readers)
        waits = self._need_waits(eng, toks)
        self.ops[eng].append((None, waits, None, False))

    def emit(self):
        nc = self.nc
        with contextlib.ExitStack() as st:
            handles = {sid: self.sem_stack.enter_context(nc.semaphore(sid)) for sid in self.sems}
            block = st.enter_context(nc.Block())

            def run(engobj, lst):
                for fn, waits, inc, embed in lst:
                    emb = waits[-1] if (embed and waits and fn is not None) else None
                    for sid, val in (waits[:-1] if emb is not None else waits):
                        engobj.wait_ge(handles[sid], val)
                    if fn is not None:
                        n0 = nc.n_instructions()
                        ins = fn(engobj)
                        if emb is not None:
                            assert nc.n_instructions() - n0 == 1, "embedded wait on a multi-instruction op"
                            ins._wait_ge(handles[emb[0]], emb[1])
                        ins.then_inc(handles[inc[0]], inc[1])

            @block.tensor
            def _(e):
                run(e, self.ops["pe"])

            @block.scalar
            def _(e):
                run(e, self.ops["act"])

            @block.vector
            def _(e):
                run(e, self.ops["dve"])

            @block.gpsimd
            def _(e):
                run(e, self.ops["pool"])

            @block.sync
            def _(e):
                run(e, self.ops["sp"])


class View:
    def __init__(self, ap):
        self.ap = ap

    def __getitem__(self, k):
        return self.ap


class View3:
    def __init__(self, x):
        self.x = x

    def __getitem__(self, k):
        return self.x[k[0], k[1], 128:256]


class Rot:
    def __init__(self, tiles, excl=False, keys=None):
        self.tiles = tiles
        self.keys = keys if keys is not None else [Key(f"rot{i}", excl) for i in range(len(tiles))]
        self.i = 0

    def next(self):
        j = self.i % len(self.tiles)
        self.i += 1
        return self.tiles[j], self.keys[j]


def _rms_transpose(S, src, ksrc, dstT, kdst, tcol, scr, ident, kid, ptr, extra_scale=None, kextra=None):
    st, kst = scr["stat"].next()
    xs, kxs = scr["xs"].next()
    pt, kpt = ptr.next()
    S.op("act", lambda e: e.activation(out=xs[:], in_=src, func=AF.Square, accum_out=st[:, 0:1]),
         reads=[ksrc], writes=[kxs, kst], multi=True)
    S.op("act", lambda e: e.activation(out=st[:, 1:2], in_=st[:, 0:1], func=AF.Sqrt, scale=1.0 / D, bias=RMS_EPS),
         reads=[kst], writes=[kst])
    S.op("dve", lambda e: e.reciprocal(out=st[:, 2:3], in_=st[:, 1:2]), reads=[kst], writes=[kst])
    rs = st[:, 2:3]
    if extra_scale is not None:
        S.op("dve", lambda e: e.tensor_tensor(out=st[:, 3:4], in0=st[:, 2:3], in1=extra_scale, op=ALU.mult),
             reads=[kst, kextra], writes=[kst])
        rs = st[:, 3:4]
    S.op("pool", lambda e: e.tensor_scalar(out=xs[:], in0=src, scalar1=rs, scalar2=1.0, op0=ALU.mult, op1=ALU.mult),
         reads=[ksrc, kst], writes=[kxs])
    for kc in range(8):
        S.op("pe", lambda e, kc=kc: e.transpose(out=pt[:, kc, :], in_=xs[:, kc * 128:(kc + 1) * 128], identity=ident[:]),
             reads=[kxs, kid], writes=[kpt])
    S.op("act", lambda e: e.activation(out=dstT[:, :, tcol:tcol + 128], in_=pt[:], func=AF.Copy),
         reads=[kpt], writes=[kdst])


def _post_norm_residual(S, pd_pairs, res, kres, grow, kgrow, scr):
    st, kst = scr["stat"].next()
    tmps = [scr["tmp512"].next() for _ in range(2)]
    for hf, (pd, kpd) in enumerate(pd_pairs):
        junk, kjunk = tmps[hf]
        S.op("act", lambda e, pd=pd, hf=hf, junk=junk: e.activation(out=junk[:], in_=pd[:], func=AF.Square,
                                                                  accum_out=st[:, hf:hf + 1]),
             reads=[kpd], writes=[kjunk, kst], multi=True)
    S.op("dve", lambda e: e.tensor_tensor(out=st[:, 2:3], in0=st[:, 0:1], in1=st[:, 1:2], op=ALU.add), reads=[kst], writes=[kst])
    S.op("act", lambda e: e.activation(out=st[:, 3:4], in_=st[:, 2:3], func=AF.Sqrt, scale=1.0 / D, bias=RMS_EPS),
         reads=[kst], writes=[kst])
    S.op("dve", lambda e: e.reciprocal(out=st[:, 4:5], in_=st[:, 3:4]), reads=[kst], writes=[kst])
    for hf, (pd, kpd) in enumerate(pd_pairs):
        tmp, ktmp = tmps[hf]
        S.op("dve", lambda e, pd=pd, hf=hf, tmp=tmp: e.scalar_tensor_tensor(
            out=tmp[:], in0=pd[:], scalar=st[:, 4:5], in1=grow[:, hf * 512:(hf + 1) * 512], op0=ALU.mult, op1=ALU.mult),
            reads=[kpd, kst, kgrow], writes=[ktmp])
        S.op("pool", lambda e, hf=hf, tmp=tmp: e.tensor_tensor(out=res[:, hf * 512:(hf + 1) * 512], in0=res[:, hf * 512:(hf + 1) * 512],
                                                               in1=tmp[:], op=ALU.add),
             reads=[ktmp, kres], writes=[kres])


def phase2_ffn(nc, S, st, io, n_main, shared):
    sb = lambda n, s, d: st.enter_context(nc.sbuf_tensor(n, s, d))
    ps = lambda n, s, d: st.enter_context(nc.psum_tensor(n, s, d))
    cc, kcc = shared["cc"], shared["kcc"]
    ident, kid = shared["ident"], shared["kid"]
    hscr, khscr = io["hscr"], io["khscr"]
    out = io["out"]

    wup = sb("wup", [128, 8, DFF * 2], BF16)
    wdn = sb("wdn", [128, NPAIR, D], BF16)
    kwup = [Key(f"wup{k}") for k in range(8)]
    kwdn = Key("wdn")
    grow = sb("grow2", [128, D], F32)
    kgrow = Key("grow2")
    fh = sb("fh", [128, NPAIR, 2, 2], F32)
    kfh = [Key(f"fh{i}") for i in range(NPAIR)]
    hts = [sb(f"ht{i}", [128, NBLK, D], F32) for i in range(2)]
    khts = [[Key(f"ht{i}_{b}") for b in range(NBLK)] for i in range(2)]
    hnTs = [sb(f"hnT{i}", [128, 8, W], BF16) for i in range(2)]
    khnT = [Key(f"hnT{i}") for i in range(2)]
    act = sb("actb", [128, NPAIR, W], BF16)
    kact = [Key(f"act{i}") for i in range(NPAIR)]
    scr = {
        "stat": Rot([sb(f"stat{i}", [128, 8], F32) for i in range(4)]),
        "xs": Rot([sb(f"xs{i}", [128, D], BF16) for i in range(2)]),
        "tmp512": Rot([sb(f"tmp512_{i}", [128, 512], F32) for i in range(2)]),
    }
    fbuf = Rot([sb(f"fbuf{i}", [128, 2, W + 2], F32) for i in range(4)])
    cg = Rot([sb(f"cg{i}", [128, W], F32) for i in range(4)])
    cu = Rot([sb(f"cu{i}", [128, W], F32) for i in range(4)])
    t1 = Rot([sb(f"t1_{i}", [128, W], F32) for i in range(3)])
    t2 = Rot([sb(f"t2_{i}", [128, W], F32) for i in range(3)])
    sg = Rot([sb(f"sg{i}", [128, W], F32) for i in range(3)])
    WQ = DFF // 4
    ptr = Rot([ps("ptr2", [128, 8, 128], BF16)], excl=True)
    pf = Rot([ps(f"pf{i}", [128, 2, W], F32) for i in range(3)], excl=True)
    pd = [[ps(f"pd{b}{h}", [128, 512], F32) for h in range(2)] for b in range(NBLK)]
    kpd = [[PKey(f"pd{b}{h}") for h in range(2)] for b in range(NBLK)]
    dsem = {n: S.new_dma_sem("p2_" + n) for n in ("wst0", "wst1", "wst2", "wst3", "wdn", "grow", "h0", "h1", "hw", "out0", "out1")}

    S.op("sp", lambda e: e.dma_start(out=grow[:], in_=io["post_ffn_g"].partition_broadcast(128)), writes=[kgrow], dma_sem=dsem["grow"])
    actf = act[:].rearrange("p i w -> p (i w)").bitcast(F32)
    kstg = [Key(f"stg{q}") for q in range(4)]
    for kc in range(8):
        for q in range(8):
            j = kc * 8 + q
            r_ = j % 4
            wt, kwt = actf[:, r_ * WQ:(r_ + 1) * WQ], kstg[r_]
            S.op("sp", lambda e, wt=wt, kc=kc, q=q: e.dma_start(out=wt, in_=io["w_up"][kc * 128:(kc + 1) * 128, q * WQ:(q + 1) * WQ]),
                 writes=[kwt], dma_sem=dsem[f"wst{r_}"])
            if j % 2 == 0:
                S.op("act", lambda e, wt=wt, kc=kc, q=q: e.activation(out=wup[:, kc, q * WQ:(q + 1) * WQ], in_=wt, func=AF.Copy,
                                                                       scale=cc[:, O_PFG + kc:O_PFG + kc + 1]),
                     reads=[kwt, kcc], writes=[kwup[kc]])
            else:
                S.op("dve", lambda e, wt=wt, kc=kc, q=q: e.tensor_scalar(out=wup[:, kc, q * WQ:(q + 1) * WQ], in0=wt,
                                                                          scalar1=cc[:, O_PFG + kc:O_PFG + kc + 1], scalar2=1.0, op0=ALU.mult, op1=ALU.mult),
                     reads=[kwt, kcc], writes=[kwup[kc]])
    S.op("pool", lambda e: e.memset(act[:, :, 0:1], 0.0), writes=kstg + kact)
    S.op("pool", lambda e: e.dma_start(out=wdn[:], in_=io["w_down"].rearrange("(i p) d -> p i d", p=128)), writes=[kwdn], dma_sem=dsem["wdn"])

    hw = hts[1]
    S.op("sp", lambda e: e.dma_start(out=hw[:, 0, :], in_=hscr[W - 128:W, :]), reads=[khscr[0]], writes=[khts[1][0]], dma_sem=dsem["hw"])
    _rms_transpose(S, hw[:, 0, :], khts[1][0], hnTs[1], khnT[1], 0, scr, ident, kid, ptr,
                   extra_scale=cc[:, O_HM:O_HM + 1], kextra=kcc)
    pfh, kpfh = pf.next()
    pfh_v = pfh[:].rearrange("p a w -> p (a w)")
    for ch in range(2 * NPAIR):
        i, hf = ch % NPAIR, ch // NPAIR
        col = (i * 2 + hf) * 2
        for kc in range(8):
            S.op("pe", lambda e, ch=ch, kc=kc, col=col: e.matmul(pfh_v[:, col:col + 2], lhsT=wup[:, kc, ch * 128:(ch + 1) * 128],
                                                                  rhs=hnTs[1][:, kc, 126:128], start=(kc == 0), stop=(kc == 7)),
                 reads=[kwup[kc], khnT[1]], writes=[kpfh])
    S.op("act", lambda e: e.activation(out=fh[:].rearrange("p i a b -> p (i a b)"), in_=pfh_v[:, 0:NPAIR * 4], func=AF.Copy),
         reads=[kpfh], writes=kfh)

    def load(t):
        b = t % 2
        S.op("sp", lambda e: e.dma_start(out=hts[b][:], in_=hscr[W * (1 + t):W * (2 + t), :].rearrange("(b p) d -> p b d", p=128)),
             reads=[khscr[1 + t]], writes=khts[b], dma_sem=dsem[f"h{b}"])

    def prologue(t):
        b = t % 2
        for blk in range(NBLK):
            _rms_transpose(S, hts[b][:, blk, :], khts[b][blk], hnTs[b], khnT[b], blk * 128, scr, ident, kid, ptr)

    state = {}

    def up_mm(t, i):
        b = t % 2
        p, kp = pf.next()
        state[(t, i)] = (p, kp)
        for hf in range(2):
            ch = hf * NPAIR + i
            for kc in range(8):
                S.op("pe", lambda e, p=p, hf=hf, ch=ch, kc=kc: e.matmul(p[:, hf, :], lhsT=wup[:, kc, ch * 128:(ch + 1) * 128],
                                                                         rhs=hnTs[b][:, kc, :], start=(kc == 0), stop=(kc == 7)),
                     reads=[kwup[kc], khnT[b]], writes=[kp])

    def elem(t, i):
        p, kp = state.pop((t, i))
        fb, kfb = fbuf.next()
        S.op("pool", lambda e: e.tensor_copy(out=fb[:, :, 0:2], in_=fh[:, i, :, :]), reads=[kfh[i]], writes=[kfb])
        S.op("act", lambda e: e.activation(out=fb[:, :, 2:W + 2], in_=p[:], func=AF.Copy), reads=[kp], writes=[kfb])
        yield
        S.op("pool", lambda e: e.tensor_copy(out=fh[:, i, :, :], in_=fb[:, :, W:W + 2]), reads=[kfb], writes=[kfh[i]])
        outs = []
        for hf, rot in ((0, cg), (1, cu)):
            ch = hf * NPAIR + i
            c, kc_ = rot.next()
            wof = O_FCW + ch * 3
            S.op("act", lambda e, c=c, hf=hf, wof=wof, ch=ch: e.activation(out=c[:], in_=fb[:, hf, 2:W + 2], func=AF.Identity,
                                                                          scale=cc[:, wof + 2:wof + 3], bias=cc[:, O_FCB + ch:O_FCB + ch + 1]),
                 reads=[kfb, kcc], writes=[kc_])
            outs.append((c, kc_, hf, wof))
        yield
        for c, kc_, hf, wof in outs:
            S.op("dve", lambda e, c=c, hf=hf, wof=wof: e.scalar_tensor_tensor(out=c[:], in0=fb[:, hf, 1:W + 1], scalar=cc[:, wof + 1:wof + 2],
                                                                             in1=c[:], op0=ALU.mult, op1=ALU.add),
                 reads=[kfb, kcc, kc_], writes=[kc_])
        yield
        for c, kc_, hf, wof in outs:
            S.op("dve", lambda e, c=c, hf=hf, wof=wof: e.scalar_tensor_tensor(out=c[:], in0=fb[:, hf, 0:W], scalar=cc[:, wof:wof + 1],
                                                                             in1=c[:], op0=ALU.mult, op1=ALU.add),
                 reads=[kfb, kcc, kc_], writes=[kc_])
        outs = [(c, kc_) for c, kc_, hf, wof in outs]
        yield
        (g_, kg), (u_, ku) = outs
        a1, ka1 = t1.next()
        a2, ka2 = t2.next()
        s_, ks = sg.next()
        S.op("pool", lambda e: e.tensor_tensor(out=a1[:], in0=g_[:], in1=g_[:], op=ALU.mult), reads=[kg], writes=[ka1])
        yield
        S.op("pool", lambda e: e.tensor_scalar(out=a1[:], in0=a1[:], scalar1=0.044715, scalar2=1.0, op0=ALU.mult, op1=ALU.add),
             reads=[ka1], writes=[ka1])
        S.op("pool", lambda e: e.tensor_tensor(out=a2[:], in0=a1[:], in1=g_[:], op=ALU.mult), reads=[ka1, kg], writes=[ka2])
        S.op("dve", lambda e: e.tensor_tensor(out=a1[:], in0=g_[:], in1=u_[:], op=ALU.mult), reads=[kg, ku, ka2], writes=[ka1])
        yield
        S.op("act", lambda e: e.activation(out=s_[:], in_=a2[:], func=AF.Sigmoid, scale=1.5957691216), reads=[ka2], writes=[ks])
        yield
        S.op("dve", lambda e: e.tensor_tensor(out=act[:, i, :], in0=a1[:], in1=s_[:], op=ALU.mult), reads=[ka1, ks], writes=[kact[i]])

    def down_mm(t, i):
        for blk in range(NBLK):
            for hf in range(2):
                S.op("pe", lambda e, blk=blk, hf=hf: e.matmul(pd[blk][hf][:], lhsT=act[:, i, blk * 128:(blk + 1) * 128],
                                                              rhs=wdn[:, i, hf * 512:(hf + 1) * 512], start=(i == 0), stop=(i == NPAIR - 1)),
                     reads=[kact[i], kwdn], writes=[kpd[blk][hf]])

    def epilogue(t):
        b = t % 2
        for blk in range(NBLK):
            _post_norm_residual(S, [(pd[blk][0], kpd[blk][0]), (pd[blk][1], kpd[blk][1])], hts[b][:, blk, :], khts[b][blk], grow, kgrow, scr)
        S.op("sp", lambda e: e.dma_start(out=out[W * t:W * (t + 1), :].rearrange("(b p) d -> p b d", p=128), in_=hts[b][:]),
             reads=khts[b], dma_sem=dsem[f"out{b}"])

    load(0)
    if n_main > 1:
        load(1)
    prologue(0)
    TSTEP = 2
    for t in range(n_main):
        active = []
        i_next, tick, pro_done = 0, 0, False
        while i_next < NPAIR or active:
            if i_next < NPAIR and tick % TSTEP == 0:
                up_mm(t, i_next)
                active.append((i_next, elem(t, i_next)))
                i_next += 1
                if i_next == 15 and t + 1 < n_main and not pro_done:
                    prologue(t + 1)
                    pro_done = True
            for item in list(active):
                i_, g_ = item
                try:
                    next(g_)
                except StopIteration:
                    active.remove(item)
                    down_mm(t, i_)
            tick += 1
        epilogue(t)
        if t + 2 < n_main:
            load(t + 2)
    S.final_wait("sp", [k for ks_ in khts for k in ks_])


def build(n_pre, n_main, mode="full"):
    nc = bass.Bass("TRN2", target_bir_lowering=False)
    TT = (n_pre + 1 + n_main) * W
    io = {}
    di = lambda n, s: nc.dram_tensor(n, s, F32, kind="ExternalInput").ap()
    io["cc"] = di("cc", [128, NCC])
    io["km"] = di("km", [128, NKM])
    io["post_ffn_g"] = di("post_ffn_g", [D])
    io["w_up"] = di("w_up", [D, 2 * DFF])
    io["w_down"] = di("w_down", [DFF, D])
    if mode == "ffn":
        io["hscr"] = di("hscr", [(1 + n_main) * W, D])
    else:
        io["xin"] = di("xin", [TT, D])
        io["post_mix_g"] = di("post_mix_g", [D])
        io["w_in"] = di("w_in", [D, INC])
        io["w_out"] = di("w_out", [D, D])
        io["w2"] = di("w2", [64, 512])
        io["a2"] = di("a2", [64, 512])
        io["g2"] = di("g2", [160, 512])
        io["hscr"] = nc.dram_tensor("hscr", [(1 + n_main) * W, D], F32, kind="Internal").ap()
    io["khscr"] = [Key(f"hscr{i}") for i in range(1 + n_main)]
    io["out"] = nc.dram_tensor("out", [n_main * W, D], F32, kind="ExternalOutput").ap()

    with contextlib.ExitStack() as sem_stack, contextlib.ExitStack() as st0:
        cc = st0.enter_context(nc.sbuf_tensor("cc_sb", [128, NCC], F32))
        ident = st0.enter_context(nc.sbuf_tensor("ident", [128, 128], BF16))

        def shared_loads(S):
            kcc, kid = Key("cc"), Key("ident")
            d0 = S.new_dma_sem("cc")
            d1 = S.new_dma_sem("ident")
            S.op("sp", lambda e: e.dma_start(out=cc[:], in_=io["cc"][:, :]), writes=[kcc], dma_sem=d0)
            S.op("pool", lambda e: e.dma_start(out=ident[:], in_=io["km"][:, M_ID:M_ID + 128]), writes=[kid], dma_sem=d1)
            return {"cc": cc, "kcc": kcc, "ident": ident, "kid": kid}

        if mode != "ffn":
            S1 = Sched(nc, sem_stack, "a")
            shared = shared_loads(S1)
            with contextlib.ExitStack() as st1:
                phase1_mixer(nc, S1, st1, io, n_pre, n_main, shared)
                S1.final_wait("sp", io["khscr"])
                S1.emit()
            S2 = Sched(nc, sem_stack, "b")
            shared = {"cc": cc, "kcc": Key("cc2"), "ident": ident, "kid": Key("ident2")}
            io["khscr"] = [Key(f"hscr2_{i}") for i in range(1 + n_main)]
        else:
            S2 = Sched(nc, sem_stack, "b")
            shared = shared_loads(S2)
        with contextlib.ExitStack() as st2:
            phase2_ffn(nc, S2, st2, io, n_main, shared)
            S2.emit()
    return nc


def phase1_mixer(nc, S, st, io, n_pre, n_main, shared):
    sb = lambda n, s, d: st.enter_context(nc.sbuf_tensor(n, s, d))
    ps = lambda n, s, d: st.enter_context(nc.psum_tensor(n, s, d))
    cc, kcc = shared["cc"], shared["kcc"]
    ident, kid = shared["ident"], shared["kid"]
    xin, hscr, khscr = io["xin"], io["hscr"], io["khscr"]
    n_tiles = n_pre + 1 + n_main
    C05 = 0.6065306597126334

    win = sb("win", [128, 8, INC], BF16)
    kwin = [[Key(f"win{k}_{q}") for q in range(4)] for k in range(8)]
    wout = sb("wout", [128, 8, D], BF16)
    kwout = Key("wout")
    w2b = sb("w2b", [128, 512], BF16)
    a2b = sb("a2b", [128, 512], BF16)
    g2b0 = sb("g2b0", [128, 512], BF16)
    g2b1 = sb("g2b1", [128, 512], BF16)
    wg1 = sb("wg1", [128, 8, 128], BF16)
    kwg1 = Key("wg1")
    ksmallw = Key("smallw")
    msu4 = sb("msu4", [128, 512], BF16)
    msl = sb("msl", [128, 128], BF16)
    bones = sb("bones", [128, 128], BF16)
    fst = sb("fst", [128, 64], BF16)
    rst = sb("rst", [128, W], F32)
    kconst = Key("p1const")
    grow = sb("grow1", [128, D], F32)
    kgrow = Key("grow1")
    dc = sb("dc", [128, 20], F32)
    kdc = Key("dc")
    dsem = {n: S.new_dma_sem("p1_" + n) for n in ("wst0", "wst1", "wst2", "wst3", "const", "grow", "x0", "x1", "h0", "h1")}

    S.op("sp", lambda e: e.dma_start(out=grow[:], in_=io["post_mix_g"].partition_broadcast(128)), writes=[kgrow], dma_sem=dsem["grow"])
    S.op("sp", lambda e: e.dma_start(out=rst[:], in_=io["km"][:, M_RST:M_RST + W]), writes=[kconst], dma_sem=dsem["const"])
    uniq = [0]

    def pool_dma(fn, key):
        uniq[0] += 1
        S.op("pool", fn, writes=[key], dma_sem=S.new_dma_sem(f"p1u{uniq[0]}"))

    kmsl, kbones, kfst = Key("msl"), Key("bones"), Key("fst")
    kmsu4 = [Key(f"msu4_{q}") for q in range(4)]
    kw2b, ka2b, kg2b0, kg2b1 = Key("w2b"), Key("a2b"), Key("g2b0"), Key("g2b1")
    for dst, c0, n_, k_ in ((msl, M_SL, 128, kmsl), (bones, M_BO, 128, kbones), (fst, M_F, 64, kfst)):
        pool_dma(lambda e, dst=dst, c0=c0, n_=n_: e.dma_start(out=dst[:], in_=io["km"][:, c0:c0 + n_]), k_)
    for q, c0 in enumerate((M_SU, M_IU, M_SU, M_IU)):
        pool_dma(lambda e, q=q, c0=c0: e.dma_start(out=msu4[:, q * 128:(q + 1) * 128], in_=io["km"][:, c0:c0 + 128]), kmsu4[q])
    S.op("pool", lambda e: e.memset(w2b[:], 0.0), writes=[kw2b])
    S.op("pool", lambda e: e.memset(a2b[:], 0.0), writes=[ka2b])
    S.op("pool", lambda e: e.memset(g2b1[:], 0.0), writes=[kg2b1])
    S.op("pool", lambda e: e.memset(wg1[:], 0.0), writes=[kwg1])
    pool_dma(lambda e: e.dma_start(out=w2b[0:64, :], in_=io["w2"][:, :]), kw2b)
    pool_dma(lambda e: e.dma_start(out=a2b[64:128, :], in_=io["a2"][:, :]), ka2b)
    pool_dma(lambda e: e.dma_start(out=g2b0[:], in_=io["g2"][0:128, :]), kg2b0)
    pool_dma(lambda e: e.dma_start(out=g2b1[0:32, :], in_=io["g2"][128:160, :]), kg2b1)
    pool_dma(lambda e: e.dma_start(out=wout[:], in_=io["w_out"].rearrange("(k p) d -> p k d", p=128)), kwout)
    S.op("dve", lambda e: e.tensor_scalar(out=dc[:, 0:15], in0=cc[:, O_MU:O_MU + 15], scalar1=-1.0, scalar2=1.0, op0=ALU.mult, op1=ALU.add),
         reads=[kcc], writes=[kdc])
    S.op("dve", lambda e: e.tensor_scalar(out=dc[:, 15:19], in0=cc[:, O_KA:O_KA + 4], scalar1=-1.0, scalar2=1.0, op0=ALU.mult, op1=ALU.add),
         reads=[kcc], writes=[kdc])
    xts = [sb(f"xt{i}", [128, NBLK, D], F32) for i in range(2)]
    kxts = [[Key(f"xt{i}_{b}") for b in range(NBLK)] for i in range(2)]
    xnT = sb("xnT", [128, 8, W], BF16)
    kxnT = Key("xnT")
    scr = {
        "stat": Rot([sb(f"stat1_{i}", [128, 8], F32) for i in range(4)]),
        "xs": Rot([sb(f"xs1_{i}", [128, D], BF16) for i in range(2)]),
    }
    tf = Rot([sb(f"tf{i}", [128, W], F32) for i in range(9)])
    tb = Rot([sb(f"tb{i}", [128, W], BF16) for i in range(4)])
    qbuf = Rot([sb(f"qbuf{i}", [128, W + 1], F32) for i in range(2)])
    ubuf = Rot([sb(f"ubuf{i}", [128, W + 2], F32) for i in range(2)])
    qh = sb("qh", [128, 15], F32)
    kqh = [Key(f"qh{j}") for j in range(15)]
    uh = sb("uh", [128, 4, 2], F32)
    kuh = [Key(f"uh{c}") for c in range(4)]
    rkv = {n: Rot([sb(f"{n}c{i}", [128, W], F32) for i in range(2)]) for n in ("r", "k", "v")}
    wa = sb("wa", [128, W], F32); kwa = Key("wa")
    g0t = sb("g0t", [128, W], F32); kg0 = Key("g0t")
    g1t = sb("g1t", [128, W], F32); kg1 = Key("g1t")
    twad = sb("twad", [128, W], BF16); ktwad = Key("twad")
    sgds = [(sb(f"sgd0_{i}", [128, W], BF16), sb(f"sgd1_{i}", [128, W], BF16), Key(f"sgd{i}")) for i in range(2)]
    sig = sb("sig", [128, 4, W], F32); ksig = [Key(f"sig{c}") for c in range(4)]
    a4 = sb("a4", [128, 4, W], F32); ka4 = [Key(f"a4{c}") for c in range(4)]
    bonus = sb("bonus", [128, 4, W], F32); kbonus = [Key(f"bonus{c}") for c in range(4)]
    ARbd = sb("ARbd", [128, 4, NCH, 2, 128], BF16); kAbd = [Key(f"Abd{c}") for c in range(4)]; kRbd = [Key(f"Rbd{c}") for c in range(4)]
    Bbd = sb("Bbd", [128, 4, NCH, 128], BF16); kBbd = [Key(f"Bbd{c}") for c in range(4)]
    Kbd = sb("Kbd", [128, 4, NCH, 128], BF16); kKbd = [Key(f"Kbd{c}") for c in range(4)]
    Vbd = sb("Vbd", [128, 4, NCH, 128], BF16); kVbd = [Key(f"Vbd{c}") for c in range(4)]
    PC = sb("PC", [128, 4, NCH], F32); kPC = [Key(f"PC{c}") for c in range(4)]
    M1s = [(sb(f"M1_{i}", [128, 4, 512], BF16), Key(f"M1_{i}")) for i in range(2)]
    NT0s = [(sb(f"NT0_{i}", [128, 4, 128], BF16), Key(f"NT0_{i}")) for i in range(2)]
    M2s = [(sb(f"M2_{i}", [128, 4, 320], BF16), Key(f"M2_{i}")) for i in range(2)]
    NNs = [[(sb(f"NN{i}{j}", [128, 4, 256], BF16), Key(f"NN{i}{j}")) for j in range(2)] for i in range(2)]
    Tbs = [[(sb(f"Tb{i}{j}", [128, 4, 128], BF16), Key(f"Tb{i}{j}")) for j in range(2)] for i in range(2)]
    Zb = sb("Zb", [128, 4, 64], BF16); kZb = Key("Zb")
    Ub = sb("Ub", [128, 4, 64], BF16); kUb = Key("Ub")
    S32 = sb("S32", [128, 4, 64], F32); kS32 = Key("S32")
    Sbf = sb("Sbf", [128, 4, 64], BF16); kSbf = Key("Sbf")
    gst = Rot([sb(f"gst{i}", [128, 8, 4], F32) for i in range(2)])
    ynbd = sb("ynbd", [128, 4, 128], BF16); kynbd = Key("ynbd")
    ynf = sb("ynf", [128, 4, W], F32); _ka, _kb = Key("ynfA"), Key("ynfB"); kynf = [_ka, _ka, _kb, _kb]
    scr["tmp512"] = Rot([View(ynf[:, 0:2, :].rearrange("p c w -> p (c w)")), View(ynf[:, 2:4, :].rearrange("p c w -> p (c w)"))], keys=[_ka, _kb])
    yTc = [sb(f"yTc{i}", [128, 4, W], BF16) for i in range(2)]; kyTc = [[Key(f"yTc{i}_{c}") for c in range(4)] for i in range(2)]
    yTr = sb("yTr", [128, 4, W], BF16); kyTr = [Key(f"yTr{c}") for c in range(4)]

    WQ = INC // 4
    stg_a = [(sig, ksig), (a4, ka4), (bonus, kbonus), (ynf, kynf)]
    stg_b = [(bonus, kbonus), (ynf, kynf)]
    def stage_dma(q, kc, j):
        stg = stg_a if q >= 2 else stg_b
        buf, kbuf = stg[j % len(stg)]
        wt = buf[:].rearrange("p c w -> p (c w)")[:, 0:WQ]
        S.op("sp", lambda e: e.dma_start(out=wt, in_=io["w_in"][kc * 128:(kc + 1) * 128, q * WQ:(q + 1) * WQ]),
             writes=kbuf, dma_sem=dsem[f"wst{j % 4}"])

    def stage_cast(q, kc, j):
        stg = stg_a if q >= 2 else stg_b
        buf, kbuf = stg[j % len(stg)]
        wt = buf[:].rearrange("p c w -> p (c w)")[:, 0:WQ]
        if j % 2 == 0:
            S.op("act", lambda e: e.activation(out=win[:, kc, q * WQ:(q + 1) * WQ], in_=wt, func=AF.Copy,
                                               scale=cc[:, O_PMG + kc:O_PMG + kc + 1]),
                 reads=kbuf + [kcc], writes=[kwin[kc][q]])
        else:
            S.op("dve", lambda e: e.tensor_scalar(out=win[:, kc, q * WQ:(q + 1) * WQ], in0=wt,
                                                  scalar1=cc[:, O_PMG + kc:O_PMG + kc + 1], scalar2=1.0, op0=ALU.mult, op1=ALU.mult),
                 reads=kbuf + [kcc], writes=[kwin[kc][q]])

    pend_dma, pend_cast = [], []
    j = 0
    for q in (3, 2, 0, 1):
        for kc in range(8):
            if q >= 2:
                stage_dma(q, kc, j)
                stage_cast(q, kc, j)
            else:
                pend_dma.append((q, kc, j))
            j += 1

    def flush_stage(everything):
        while pend_cast:
            stage_cast(*pend_cast.pop(0))
        n = len(pend_dma) if everything else 2
        for _ in range(n):
            if pend_dma:
                p_ = pend_dma.pop(0)
                stage_dma(*p_)
                if everything:
                    stage_cast(*p_)
                else:
                    pend_cast.append(p_)

    if n_pre <= 1:
        flush_stage(True)
    S.op("dve", lambda e: e.tensor_copy(out=wg1[:, :, 0:32], in_=win[:, :, INC - 32:INC]), reads=[kwin[k_][3] for k_ in range(8)] + [kwg1], writes=[kwg1])

    ptr = Rot([ps("ptr1", [128, 8, 128], BF16)], excl=True)
    pp = [ps(f"pp{i}", [128, 2, W], F32) for i in range(2)]
    kpp = [PKey(f"pp{i}") for i in range(2)]
    psm = ps("psm", [128, 2, W], F32); kpsm = PKey("psm")
    PA = ps("PA", [128, 1024], F32); kPA = PKey("PA")
    PB = ps("PB", [128, 1024], F32); kPB0 = PKey("PB0"); kPB1 = PKey("PB1")

    for t_, k_ in ((qh, kqh), (uh, kuh)):
        S.op("pool", lambda e, t_=t_: e.memset(t_[:], 0.0), writes=k_)
    S.op("pool", lambda e: e.memset(S32[:], 0.0), writes=[kS32])
    S.op("pool", lambda e: e.memset(Sbf[:], 0.0), writes=[kSbf])
    S.op("pool", lambda e: e.memset(ARbd[:], 0.0), writes=kAbd + kRbd)
    S.op("pool", lambda e: e.memset(Bbd[:], 0.0), writes=kBbd)
    S.op("pool", lambda e: e.memset(Kbd[:], 0.0), writes=kKbd)
    S.op("pool", lambda e: e.memset(Vbd[:], 0.0), writes=kVbd)
    S.op("pool", lambda e: e.memset(ynbd[:], 0.0), writes=[kynbd])
    S.op("pool", lambda e: e.memset(g1t[:], 0.0), writes=[kg1])

    slot_i = [0]

    def proj(col0, ncols, wsrc=None, kw=None):
        j = slot_i[0] % 4
        slot_i[0] += 1
        t_, half, key = pp[j // 2], j % 2, kpp[j // 2]
        for kc in range(8):
            lhsT = win[:, kc, col0:col0 + ncols] if wsrc is None else wsrc[:, kc, :]
            if kw is None:
                wkeys = [kwin[kc][q_] for q_ in range(col0 // (INC // 4), (col0 + ncols - 1) // (INC // 4) + 1)]
            else:
                wkeys = [kw[kc]]
            S.op("pe", lambda e, kc=kc, lhsT=lhsT: e.matmul(t_[0:ncols, half, :], lhsT=lhsT, rhs=xnT[:, kc, :],
                                                            start=(kc == 0), stop=(kc == 7)),
                 reads=wkeys + [kxnT], writes=[key])
        return t_[0:ncols, half, :], key

    def shift_lerp(p_ap, kp, jj, dst_ap, kdst, np_=128):
        qb, kqb = qbuf.next()
        S.op("pool", lambda e: e.tensor_copy(out=qb[0:np_, 0:1], in_=qh[0:np_, jj:jj + 1]), reads=[kqh[jj]], writes=[kqb])
        S.op("act", lambda e: e.activation(out=qb[0:np_, 1:W + 1], in_=p_ap, func=AF.Copy), reads=[kp], writes=[kqb])
        S.op("pool", lambda e: e.tensor_copy(out=qh[0:np_, jj:jj + 1], in_=qb[0:np_, W:W + 1]), reads=[kqb], writes=[kqh[jj]])
        tmp, ktmp = tf.next()
        S.op("pool", lambda e: e.tensor_scalar(out=tmp[0:np_, :], in0=qb[0:np_, 1:W + 1], scalar1=dc[0:np_, jj:jj + 1], scalar2=1.0, op0=ALU.mult, op1=ALU.mult),
             reads=[kqb, kdc], writes=[ktmp])
        S.op("dve", lambda e: e.scalar_tensor_tensor(out=dst_ap, in0=qb[0:np_, 0:W], scalar=cc[0:np_, O_MU + jj:O_MU + jj + 1], in1=tmp[0:np_, :],
                                                     op0=ALU.mult, op1=ALU.add),
             reads=[kqb, kcc, ktmp], writes=[kdst])

    def bd_view(t4, c, h):
        return t4[h * 64:(h + 1) * 64, c, :, h * 64:(h + 1) * 64]

    def v3(t2, h):
        return t2[h * 64:(h + 1) * 64, :].rearrange("p (n t) -> p n t", n=NCH)

    def load_x(t):
        b = t % 2
        S.op("sp", lambda e: e.dma_start(out=xts[b][:], in_=xin[W * t:W * (t + 1), :].rearrange("(b p) d -> p b d", p=128)),
             writes=kxts[b], dma_sem=dsem[f"x{b}"])

    load_x(0)
    if n_tiles > 1:
        load_x(1)

    def tile_parts(t):
        full = t >= n_pre
        b = t % 2
        par = t % 2
        xt = xts[b]
        sgd0, sgd1, ksgd = sgds[par]
        yc, kyc = yTc[par], kyTc[par]

        def early():
            for blk in range(NBLK):
                _rms_transpose(S, xt[:, blk, :], kxts[b][blk], xnT, kxnT, blk * 128, scr, ident, kid, ptr)

            yield
            if full:
                def b1(c):
                    pB, kpB = proj(c * 128, 128)
                    pH, kpH = proj(1024 + c * 128, 128)
                    hAs, khAs = tf.next()
                    S.op("act", lambda e, hAs=hAs, pH=pH: e.activation(out=hAs[:], in_=pH, func=AF.Copy), reads=[kpH], writes=[khAs])
                    ub, kub = ubuf.next()
                    S.op("pool", lambda e, ub=ub, c=c: e.tensor_copy(out=ub[:, 0:2], in_=uh[:, c, :]), reads=[kuh[c]], writes=[kub])
                    S.op("dve", lambda e, ub=ub, pB=pB, hAs=hAs: e.tensor_tensor(out=ub[:, 2:W + 2], in0=pB, in1=hAs[:], op=ALU.mult),
                         reads=[kpB, khAs], writes=[kub])
                    S.op("pool", lambda e, ub=ub, c=c: e.tensor_copy(out=uh[:, c, :], in_=ub[:, W:W + 2]), reads=[kub], writes=[kuh[c]])
                    ta, kta = tf.next()
                    wof = O_CAW + c * 3
                    S.op("act", lambda e, ta=ta, ub=ub, wof=wof: e.activation(out=ta[:], in_=ub[:, 2:W + 2], func=AF.Copy, scale=cc[:, wof + 2:wof + 3]),
                         reads=[kub, kcc], writes=[kta])
                    S.op("dve", lambda e, ta=ta, ub=ub, wof=wof: e.scalar_tensor_tensor(out=ta[:], in0=ub[:, 1:W + 1], scalar=cc[:, wof + 1:wof + 2], in1=ta[:],
                                                                                      op0=ALU.mult, op1=ALU.add), reads=[kub, kcc, kta], writes=[kta])
                    S.op("dve", lambda e, ta=ta, ub=ub, wof=wof: e.scalar_tensor_tensor(out=ta[:], in0=ub[:, 0:W], scalar=cc[:, wof:wof + 1], in1=ta[:],
                                                                                      op0=ALU.mult, op1=ALU.add), reads=[kub, kcc, kta], writes=[kta])
                    pC, kpC = proj(512 + c * 128, 128)
                    S.op("dve", lambda e, ta=ta, pC=pC, c=c: e.tensor_tensor(out=yc[:, c, :], in0=pC, in1=ta[:], op=ALU.mult),
                         reads=[kpC, kta], writes=[kyc[c]])
                for c_ in range(4):
                    b1(c_)

            yield
            QC = 1536
            p_, kp_ = proj(QC + 12 * 128, 128)
            shift_lerp(p_, kp_, 12, wa[:], kwa)
            S.op("act", lambda e: e.activation(out=twad[0:64, :], in_=wa[0:64, :], func=AF.Tanh), reads=[kwa], writes=[ktwad])
            S.op("dve", lambda e: e.tensor_copy(out=twad[64:128, :], in_=wa[64:128, :]), reads=[kwa], writes=[ktwad])
            if full:
                p_, kp_ = proj(QC + 13 * 128, 128)
                shift_lerp(p_, kp_, 13, g0t[:], kg0)
                p_, kp_ = proj(None, 128, wsrc=wg1, kw=[kwg1] * 8)
                shift_lerp(p_[0:32, :], kp_, 14, g1t[0:32, :], kg1, np_=32)
                S.op("act", lambda e: e.activation(out=sgd0[:], in_=g0t[:], func=AF.Sigmoid), reads=[kg0], writes=[ksgd])
                S.op("act", lambda e: e.activation(out=sgd1[:], in_=g1t[:], func=AF.Sigmoid), reads=[kg1], writes=[ksgd])
            yield
            for c in range(4):
                S.op("pe", lambda e, c=c: e.matmul(psm[:, 0, :], lhsT=w2b[:, c * 128:(c + 1) * 128], rhs=twad[:], start=True, stop=True),
                     reads=[kw2b, ktwad], writes=[kpsm])
                S.op("pe", lambda e, c=c: e.matmul(psm[:, 1, :], lhsT=a2b[:, c * 128:(c + 1) * 128], rhs=twad[:], start=True, stop=True),
                     reads=[ka2b, ktwad], writes=[kpsm])
                S.op("act", lambda e, c=c: e.activation(out=sig[:, c, :], in_=psm[:, 0, :], func=AF.Sigmoid, bias=cc[:, O_W0 + c:O_W0 + c + 1]),
                     reads=[kpsm, kcc], writes=[ksig[c]])
                S.op("act", lambda e, c=c: e.activation(out=a4[:, c, :], in_=psm[:, 1, :], func=AF.Sigmoid, bias=cc[:, O_A0 + c:O_A0 + c + 1]),
                     reads=[kpsm, kcc], writes=[ka4[c]])

            yield

        def b4(c):
            kt_, kkt = rkv["k"].next()
            vt_, kvt = rkv["v"].next()
            p_, kp_ = proj(QC + (4 + c) * 128, 128)
            shift_lerp(p_, kp_, 4 + c, kt_[:], kkt)
            yield
            p_, kp_ = proj(QC + (8 + c) * 128, 128)
            shift_lerp(p_, kp_, 8 + c, vt_[:], kvt)
            yield
            if full:
                rt_, krt = rkv["r"].next()
                p_, kp_ = proj(QC + c * 128, 128)
                shift_lerp(p_, kp_, c, rt_[:], krt)
                yield
            kkr, kkkr = tf.next()
            S.op("pool", lambda e, kkr=kkr, kt_=kt_, c=c: e.tensor_scalar(out=kkr[:], in0=kt_[:], scalar1=cc[:, O_KK + c:O_KK + c + 1], scalar2=1.0, op0=ALU.mult, op1=ALU.mult),
                 reads=[kkt, kcc], writes=[kkkr])
            sq, ksq = tb.next()
            S.op("act", lambda e, sq=sq, kkr=kkr: e.activation(out=sq[:], in_=kkr[:], func=AF.Square), reads=[kkkr], writes=[ksq])
            S.op("pe", lambda e, sq=sq: e.matmul(psm[:, 0, :], lhsT=bones[:], rhs=sq[:], start=True, stop=True), reads=[kbones, ksq], writes=[kpsm])
            yield
            cs, kcs = tf.next()
            S.op("dve", lambda e, cs=cs, c=c: e.tensor_tensor_scan(out=cs[:], data0=rst[:], data1=sig[:, c, :], initial=0.0, op0=ALU.mult, op1=ALU.add),
                 reads=[kconst, ksig[c]], writes=[kcs])
            yield
            E1, kE1 = tf.next()
            E2, kE2 = tf.next()
            E3, kE3 = tf.next()
            dd, kdd = tf.next()
            S.op("act", lambda e, E1=E1, cs=cs: e.activation(out=E1[:], in_=cs[:], func=AF.Exp, scale=-C05), reads=[kcs], writes=[kE1])
            S.op("act", lambda e, E2=E2, cs=cs: e.activation(out=E2[:], in_=cs[:], func=AF.Exp, scale=C05), reads=[kcs], writes=[kE2])
            S.op("pool", lambda e, dd=dd, cs=cs, c=c: e.tensor_tensor(out=dd[:], in0=cs[:], in1=sig[:, c, :], op=ALU.subtract), reads=[kcs, ksig[c]], writes=[kdd])
            S.op("act", lambda e, E3=E3, dd=dd: e.activation(out=E3[:], in_=dd[:], func=AF.Exp, scale=-C05), reads=[kdd], writes=[kE3])
            S.op("pool", lambda e, E1=E1, c=c: e.tensor_copy(out=PC[:, c, :], in_=E1[:].rearrange("p (n t) -> p n t", n=NCH)[:, :, CH - 1]),
                 reads=[kE1], writes=[kPC[c]])
            yield
            mm, kmm = tf.next()
            S.op("pool", lambda e, mm=mm, c=c: e.tensor_scalar(out=mm[:], in0=a4[:, c, :], scalar1=cc[:, O_KA + c:O_KA + c + 1], scalar2=dc[:, 15 + c:16 + c],
                                                               op0=ALU.mult, op1=ALU.add), reads=[ka4[c], kcc, kdc], writes=[kmm])
            kp, kkp = tf.next()
            S.op("dve", lambda e, kp=kp, kt_=kt_, mm=mm: e.tensor_tensor(out=kp[:], in0=kt_[:], in1=mm[:], op=ALU.mult), reads=[kkt, kmm], writes=[kkp])
            nrm, knrm = tf.next()
            S.op("act", lambda e, nrm=nrm: e.activation(out=nrm[:], in_=psm[:, 0, :], func=AF.Sqrt, bias=1e-24), reads=[kpsm], writes=[knrm])
            S.op("dve", lambda e, nrm=nrm: e.reciprocal(out=nrm[:], in_=nrm[:]), reads=[knrm], writes=[knrm])
            kk, kkk = tf.next()
            S.op("dve", lambda e, kk=kk, kkr=kkr, nrm=nrm: e.tensor_tensor(out=kk[:], in0=kkr[:], in1=nrm[:], op=ALU.mult), reads=[kkkr, knrm], writes=[kkk])
            akk, kakk = tf.next()
            S.op("pool", lambda e, akk=akk, kk=kk, c=c: e.tensor_tensor(out=akk[:], in0=a4[:, c, :], in1=kk[:], op=ALU.mult), reads=[ka4[c], kkk], writes=[kakk])
            yield
            for h in range(2):
                S.op("dve", lambda e, h=h, kk=kk, E3=E3, c=c: e.scalar_tensor_tensor(out=ARbd[h * 64:(h + 1) * 64, c, :, 0, h * 64:(h + 1) * 64], in0=v3(kk, h), scalar=-1.0,
                                                                                   in1=v3(E3, h), op0=ALU.mult, op1=ALU.mult),
                     reads=[kkk, kE3], writes=[kAbd[c]])
                S.op("dve", lambda e, h=h, akk=akk, E2=E2, c=c: e.tensor_tensor(out=bd_view(Bbd, c, h), in0=v3(akk, h), in1=v3(E2, h), op=ALU.mult),
                     reads=[kakk, kE2], writes=[kBbd[c]])
                S.op("pool", lambda e, h=h, kp=kp, E2=E2, c=c: e.tensor_tensor(out=bd_view(Kbd, c, h), in0=v3(kp, h), in1=v3(E2, h), op=ALU.mult),
                     reads=[kkp, kE2], writes=[kKbd[c]])
                S.op("act", lambda e, h=h, vt_=vt_, c=c: e.activation(out=bd_view(Vbd, c, h), in_=v3(vt_, h), func=AF.Copy), reads=[kvt], writes=[kVbd[c]])
                if full:
                    S.op("pool", lambda e, h=h, rt_=rt_, E1=E1, c=c: e.tensor_tensor(out=ARbd[h * 64:(h + 1) * 64, c, :, 1, h * 64:(h + 1) * 64], in0=v3(rt_, h), in1=v3(E1, h),
                                                                                   op=ALU.mult), reads=[krt, kE1], writes=[kRbd[c]])
            yield
            if full:
                rk, krk = tf.next()
                S.op("pool", lambda e, rk=rk, rt_=rt_, kp=kp: e.tensor_tensor(out=rk[:], in0=rt_[:], in1=kp[:], op=ALU.mult), reads=[krt, kkp], writes=[krk])
                rkb, krkb = tb.next()
                S.op("dve", lambda e, rkb=rkb, rk=rk, c=c: e.tensor_scalar(out=rkb[:], in0=rk[:], scalar1=cc[:, O_RK + c:O_RK + c + 1], scalar2=1.0, op0=ALU.mult, op1=ALU.mult),
                     reads=[krk, kcc], writes=[krkb])
                S.op("pe", lambda e, rkb=rkb: e.matmul(psm[:, 1, :], lhsT=bones[:], rhs=rkb[:], start=True, stop=True), reads=[kbones, krkb], writes=[kpsm])
                S.op("dve", lambda e, vt_=vt_, c=c: e.tensor_tensor(out=bonus[:, c, :], in0=psm[:, 1, :], in1=vt_[:], op=ALU.mult), reads=[kpsm, kvt], writes=[kbonus[c]])

        b4gens = {}

        def b4_start(c):
            g = b4(c)
            for _ in range(3 if full else 2):
                next(g)
            b4gens[c] = g

        def b4_pre():
            b4_start(0)
            yield

        def b4_all():
            if 0 not in b4gens:
                b4_start(0)
            for c_ in range(4):
                if c_ + 1 < 4:
                    b4_start(c_ + 1)
                drain_g(b4gens.pop(c_))
            if DEBUG_STOP < 5:
                return


        PA3 = PA[:].rearrange("p (a w) -> p a w", a=2)
        PAd = PA[:].rearrange("p (a w) -> p a w", a=4)
        PBt = PB[:, 0:512].rearrange("p (a w) -> p a w", a=4)
        PQ0 = PB[:, 512:768].rearrange("p (a w) -> p a w", a=4)
        PQ1 = PB[:, 768:1024].rearrange("p (a w) -> p a w", a=4)
        Tfinal = {}

        def gen_AD(n, s_):
            M1, kM1 = M1s[s_]
            NT0, kNT0 = NT0s[s_]
            M2, kM2 = M2s[s_]
            for pi in range(2):
                for i in range(2):
                    c = pi * 2 + i
                    rhsAR = ARbd[:, c, n, :, :].rearrange("p a w -> p (a w)")
                    S.op("pe", lambda e, i=i, c=c, rhsAR=rhsAR: e.matmul(PA3[:, i, 0:256], lhsT=Bbd[:, c, n, :], rhs=rhsAR, start=True, stop=True),
                         reads=[kBbd[c], kAbd[c], kRbd[c]], writes=[kPA])
                    S.op("pe", lambda e, i=i, c=c, rhsAR=rhsAR: e.matmul(PA3[:, i, 256:512], lhsT=Kbd[:, c, n, :], rhs=rhsAR, start=True, stop=True),
                         reads=[kKbd[c], kAbd[c], kRbd[c]], writes=[kPA])
                S.op("dve", lambda e, pi=pi: e.tensor_tensor(out=M1[:, 2 * pi:2 * pi + 2, :], in0=PA3, in1=msu4[:].unsqueeze(1).to_broadcast([128, 2, 512]), op=ALU.mult),
                     reads=[kPA] + kmsu4, writes=[kM1])
                yield
                for i in range(2):
                    c = pi * 2 + i
                    S.op("pe", lambda e, i=i, c=c: e.matmul(PA3[:, i, 0:128], lhsT=ARbd[:, c, n, 0, :], rhs=Bbd[:, c, n, :], start=True, stop=True),
                         reads=[kAbd[c], kBbd[c]], writes=[kPA])
                    S.op("pe", lambda e, i=i, c=c: e.matmul(PA3[:, i, 128:256], lhsT=Bbd[:, c, n, :], rhs=ident[:], start=True, stop=True),
                         reads=[kBbd[c], kid], writes=[kPA])
                    S.op("pe", lambda e, i=i, c=c: e.matmul(PA3[:, i, 256:384], lhsT=Kbd[:, c, n, :], rhs=ident[:], start=True, stop=True),
                         reads=[kKbd[c], kid], writes=[kPA])
                    S.op("pe", lambda e, i=i, c=c: e.matmul(PA3[:, i, 384:448], lhsT=Vbd[:, c, n, :], rhs=fst[:], start=True, stop=True),
                         reads=[kVbd[c], kfst], writes=[kPA])
                S.op("dve", lambda e, pi=pi: e.tensor_tensor(out=NT0[:, 2 * pi:2 * pi + 2, :], in0=PA3[:, :, 0:128], in1=msl[:].unsqueeze(1).to_broadcast([128, 2, 128]), op=ALU.mult),
                     reads=[kPA, kmsl], writes=[kNT0])
                S.op("act", lambda e, pi=pi: e.activation(out=M2[:, 2 * pi:2 * pi + 2, :], in_=PA3[:, :, 128:448], func=AF.Copy), reads=[kPA], writes=[kM2])
                yield
            Tcur, kTcur = Tbs[s_][0]
            S.op("pool", lambda e, Tcur=Tcur: e.tensor_tensor(out=Tcur[:], in0=M1[:, :, 0:128], in1=ident[:].unsqueeze(1).to_broadcast([128, 4, 128]), op=ALU.add),
                 reads=[kM1, kid], writes=[kTcur])
            Nprev = lambda c: M1[:, c, 0:128]
            NTprev = lambda c: NT0[:, c, :]
            kprev = [kM1, kNT0]
            for j in range(1, 6):
                NNj, kNNj = NNs[s_][j % 2]
                for c in range(4):
                    if j < 5:
                        S.op("pe", lambda e, c=c, Nprev=Nprev, NTprev=NTprev: e.matmul(PAd[:, c, 0:128], lhsT=NTprev(c), rhs=Nprev(c), start=True, stop=True),
                             reads=kprev, writes=[kPA])
                    S.op("pe", lambda e, c=c, Nprev=Nprev, NTprev=NTprev: e.matmul(PAd[:, c, 128:256], lhsT=Nprev(c), rhs=NTprev(c), start=True, stop=True),
                         reads=kprev, writes=[kPA])
                if j < 5:
                    S.op("act", lambda e, NNj=NNj: e.activation(out=NNj[:], in_=PAd, func=AF.Copy), reads=[kPA], writes=[kNNj])
                else:
                    S.op("act", lambda e, NNj=NNj: e.activation(out=NNj[:, :, 128:256], in_=PAd[:, :, 128:256], func=AF.Copy), reads=[kPA], writes=[kNNj])
                yield
                for c in range(4):
                    S.op("pe", lambda e, c=c, NNj=NNj, Tcur=Tcur: e.matmul(PBt[:, c, :], lhsT=NNj[:, c, 128:256], rhs=Tcur[:, c, :], start=True, stop=True),
                         reads=[kNNj, kTcur], writes=[kPB0])
                Tnew, kTnew = Tbs[s_][j % 2]
                S.op("dve", lambda e, Tnew=Tnew, Tcur=Tcur: e.tensor_tensor(out=Tnew[:], in0=PBt, in1=Tcur[:], op=ALU.add), reads=[kPB0, kTcur], writes=[kTnew])
                Tcur, kTcur = Tnew, kTnew
                Nprev = (lambda NNj: (lambda c: NNj[:, c, 0:128]))(NNj)
                NTprev = (lambda NNj: (lambda c: NNj[:, c, 128:256]))(NNj)
                kprev = [kNNj]
                yield
            Tfinal[n] = (Tcur, kTcur)

        def gen_SQ(n, s_):
            M1, kM1 = M1s[s_]
            M2, kM2 = M2s[s_]
            Tcur, kTcur = Tfinal[n]
            for c in range(4):
                S.op("pe", lambda e, c=c: e.matmul(PQ0[:, c, :], lhsT=ARbd[:, c, n, 0, :], rhs=Sbf[:, c, :], start=True, stop=False),
                     reads=[kAbd[c], kSbf], writes=[kPB1])
                S.op("pe", lambda e, c=c: e.matmul(PQ0[:, c, :], lhsT=M1[:, c, 256:384], rhs=M2[:, c, 256:320], start=False, stop=True),
                     reads=[kM1, kM2], writes=[kPB1])
            S.op("act", lambda e: e.activation(out=Zb[:], in_=PQ0, func=AF.Copy), reads=[kPB1], writes=[kZb])
            yield
            for c in range(4):
                S.op("pe", lambda e, c=c: e.matmul(PQ1[:, c, :], lhsT=Tcur[:, c, :], rhs=Zb[:, c, :], start=True, stop=True),
                     reads=[kTcur, kZb], writes=[kPB1])
            S.op("act", lambda e: e.activation(out=Ub[:], in_=PQ1, func=AF.Copy), reads=[kPB1], writes=[kUb])
            yield
            for c in range(4):
                S.op("pe", lambda e, c=c: e.matmul(PQ0[:, c, :], lhsT=M2[:, c, 0:128], rhs=Ub[:, c, :], start=True, stop=False),
                     reads=[kM2, kUb], writes=[kPB1])
                S.op("pe", lambda e, c=c: e.matmul(PQ0[:, c, :], lhsT=M2[:, c, 128:256], rhs=M2[:, c, 256:320], start=False, stop=True),
                     reads=[kM2], writes=[kPB1])
            if full:
                for c in range(4):
                    S.op("pe", lambda e, c=c: e.matmul(PQ1[:, c, :], lhsT=ARbd[:, c, n, 1, :], rhs=Sbf[:, c, :], start=True, stop=False),
                         reads=[kRbd[c], kSbf], writes=[kPB1])
                    S.op("pe", lambda e, c=c: e.matmul(PQ1[:, c, :], lhsT=M1[:, c, 128:256], rhs=Ub[:, c, :], start=False, stop=False),
                         reads=[kM1, kUb], writes=[kPB1])
                    S.op("pe", lambda e, c=c: e.matmul(PQ1[:, c, :], lhsT=M1[:, c, 384:512], rhs=M2[:, c, 256:320], start=False, stop=True),
                         reads=[kM1, kM2], writes=[kPB1])
            pcb = PC[:, :, n:n + 1].to_broadcast([128, 4, 64])
            tS_, ktS = tf.next()
            tmpS = tS_[:].rearrange("p (c v) -> p c v", c=4)
            S.op("dve", lambda e: e.tensor_tensor(out=tmpS, in0=PQ0, in1=S32[:], op=ALU.add), reads=[kPB1, kS32], writes=[ktS])
            S.op("dve", lambda e: e.tensor_tensor(out=Sbf[:], in0=tmpS, in1=pcb, op=ALU.mult), reads=[ktS] + kPC, writes=[kSbf])
            S.op("pool", lambda e: e.tensor_tensor(out=S32[:], in0=tmpS, in1=pcb, op=ALU.mult), reads=[ktS] + kPC, writes=[kS32])
            yield
            if full:
                g_, kg_ = gst.next()
                ys_, kysq = tf.next()
                ysq = ys_[:].rearrange("p (c v) -> p c v", c=4)
                yc_, kycen = tf.next()
                ycen = yc_[:].rearrange("p (c v) -> p c v", c=4)
                S.op("dve", lambda e: e.tensor_reduce(out=g_[:, 0, :], in_=PQ1, axis=AX.X, op=ALU.add), reads=[kPB1], writes=[kg_])
                S.op("act", lambda e: e.activation(out=ysq, in_=PQ1, func=AF.Square), reads=[kPB1], writes=[kysq])
                S.op("dve", lambda e: e.tensor_reduce(out=g_[:, 1, :], in_=ysq, axis=AX.X, op=ALU.add), reads=[kysq], writes=[kg_])
                S.op("dve", lambda e: e.tensor_scalar(out=g_[:, 2, :], in0=g_[:, 0, :], scalar1=1.0 / 64, scalar2=1.0, op0=ALU.mult, op1=ALU.mult), reads=[kg_], writes=[kg_])
                S.op("dve", lambda e: e.tensor_tensor(out=g_[:, 3, :], in0=g_[:, 2, :], in1=g_[:, 2, :], op=ALU.mult), reads=[kg_], writes=[kg_])
                S.op("dve", lambda e: e.scalar_tensor_tensor(out=g_[:, 4, :], in0=g_[:, 1, :], scalar=1.0 / 64, in1=g_[:, 3, :], op0=ALU.mult, op1=ALU.subtract),
                     reads=[kg_], writes=[kg_])
                S.op("act", lambda e: e.activation(out=g_[:, 5, :], in_=g_[:, 4, :], func=AF.Sqrt, bias=GN_EPS), reads=[kg_], writes=[kg_])
                S.op("dve", lambda e: e.reciprocal(out=g_[:, 6, :], in_=g_[:, 5, :]), reads=[kg_], writes=[kg_])
                S.op("dve", lambda e: e.tensor_tensor(out=ycen, in0=PQ1, in1=g_[:, 2, :].unsqueeze(2).to_broadcast([128, 4, 64]), op=ALU.subtract),
                     reads=[kPB1, kg_], writes=[kycen])
                yield
                for h in range(2):
                    hs = slice(h * 64, (h + 1) * 64)
                    S.op("dve", lambda e, hs=hs: e.tensor_tensor(out=ynbd[hs, :, hs], in0=ycen[hs, :, :], in1=g_[hs, 6, :].unsqueeze(2).to_broadcast([64, 4, 64]),
                                                                  op=ALU.mult), reads=[kycen, kg_], writes=[kynbd])
                for c in range(4):
                    S.op("pe", lambda e, c=c: e.matmul(PQ0[:, c, :], lhsT=ynbd[:, c, :], rhs=fst[:], start=True, stop=True), reads=[kynbd, kfst], writes=[kPB1])
                S.op("act", lambda e: e.activation(out=ynf[:, :, n * CH:(n + 1) * CH], in_=PQ0, func=AF.Copy), reads=[kPB1], writes=kynf)
                yield

        def drain(g):
            for _ in g:
                pass

        def interleave(ga, gb, ra=2):
            a_live, b_live = ga is not None, gb is not None
            while a_live or b_live:
                for _ in range(ra):
                    if a_live:
                        try:
                            next(ga)
                        except StopIteration:
                            a_live = False
                if b_live:
                    try:
                        next(gb)
                    except StopIteration:
                        b_live = False

        def interleave_g(ga, gb, ra=2):
            a_live, b_live = ga is not None, gb is not None
            while a_live or b_live:
                for _ in range(ra):
                    if a_live:
                        try:
                            next(ga)
                        except StopIteration:
                            a_live = False
                if b_live:
                    try:
                        next(gb)
                    except StopIteration:
                        b_live = False
                yield

        def c_stage():
            yield from gen_AD(0, 0)
            for n_ in range(NCH):
                gd = gen_AD(n_ + 1, (n_ + 1) % 2) if n_ + 1 < NCH else None
                yield from interleave_g(gd, gen_SQ(n_, n_ % 2), ra=3)

        def de():
            if not full:
                if t + 2 < n_tiles:
                    load_x(t + 2)
                return
            for c in range(4):
                S.op("pe", lambda e, c=c: e.matmul(psm[:, 0, :], lhsT=g2b0[:, c * 128:(c + 1) * 128], rhs=sgd0[:], start=True, stop=False), reads=[kg2b0, ksgd], writes=[kpsm])
                S.op("pe", lambda e, c=c: e.matmul(psm[:, 0, :], lhsT=g2b1[:, c * 128:(c + 1) * 128], rhs=sgd1[:], start=False, stop=True), reads=[kg2b1, ksgd], writes=[kpsm])
                y1, ky1 = tf.next()
                S.op("dve", lambda e, c=c, y1=y1: e.scalar_tensor_tensor(out=y1[:], in0=ynf[:, c, :], scalar=cc[:, O_LW + c:O_LW + c + 1], in1=bonus[:, c, :], op0=ALU.mult, op1=ALU.add),
                     reads=[kynf[c], kcc, kbonus[c]], writes=[ky1])
                S.op("dve", lambda e, c=c, y1=y1: e.scalar_tensor_tensor(out=yTr[:, c, :], in0=y1[:], scalar=cc[:, O_LB + c:O_LB + c + 1], in1=psm[:, 0, :], op0=ALU.add, op1=ALU.mult),
                     reads=[ky1, kcc, kpsm], writes=[kyTr[c]])
            if DEBUG_STOP < 9:
                return
            for blk in range(NBLK):
                for hf in range(2):
                    pflat = pp[hf][:].rearrange("p a w -> p (a w)")
                    for e_ in range(8):
                        ysrc = yc[:, e_, blk * 128:(blk + 1) * 128] if e_ < 4 else yTr[:, e_ - 4, blk * 128:(blk + 1) * 128]
                        ykey = kyc[e_] if e_ < 4 else kyTr[e_ - 4]
                        S.op("pe", lambda e, e_=e_, hf=hf, pflat=pflat, ysrc=ysrc: e.matmul(pflat, lhsT=ysrc, rhs=wout[:, e_, hf * 512:(hf + 1) * 512],
                                                                                            start=(e_ == 0), stop=(e_ == 7)), reads=[ykey, kwout], writes=[kpp[hf]])
                class _V:
                    def __init__(self, ap): self.ap = ap
                    def __getitem__(self, k): return self.ap
                _post_norm_residual(S, [(_V(pp[0][:].rearrange("p a w -> p (a w)")), kpp[0]), (_V(pp[1][:].rearrange("p a w -> p (a w)")), kpp[1])],
                                    xt[:, blk, :], kxts[b][blk], grow, kgrow, scr)
            ht_i = t - n_pre
            S.op("sp", lambda e, xt=xt, ht_i=ht_i: e.dma_start(out=hscr[W * ht_i:W * (ht_i + 1), :].rearrange("(b p) d -> p b d", p=128), in_=xt[:]),
                 reads=kxts[b], writes=[khscr[ht_i]], dma_sem=dsem[f"h{b}"])
            if t + 2 < n_tiles:
                load_x(t + 2)

        return early, b4_all, c_stage, de, b4_pre

    def drain_g(g):
        for _ in g:
            pass

    def chain_g(*gs):
        for g in gs:
            yield from g

    def interleave2(ga, gb, ra, rb):
        a_live, b_live = ga is not None, gb is not None
        while a_live or b_live:
            for _ in range(ra):
                if a_live:
                    try:
                        next(ga)
                    except StopIteration:
                        a_live = False
            for _ in range(rb):
                if b_live:
                    try:
                        next(gb)
                    except StopIteration:
                        b_live = False

    parts = [tile_parts(t_i) for t_i in range(n_tiles)]
    drain_g(parts[0][0]())
    parts[0][1]()
    for t_i in range(n_tiles):
        if pend_dma or pend_cast:
            flush_stage(t_i >= n_pre - 1)
        early_next = chain_g(parts[t_i + 1][0](), parts[t_i + 1][4]()) if t_i + 1 < n_tiles else None
        interleave2(parts[t_i][2](), early_next, 1, 1)
        parts[t_i][3]()
        if t_i + 1 < n_tiles:
            parts[t_i + 1][1]()


def _host_consts():
    km = np.zeros((128, NKM), np.float32)
    km[:, M_ID:M_ID + 128] = np.eye(128, dtype=np.float32)
    idx = np.arange(128)
    same = (idx[:, None] // 64) == (idx[None, :] // 64)
    s, t = idx[:, None] % 64, idx[None, :] % 64
    km[:, M_SU:M_SU + 128] = (same & (s < t)).astype(np.float32)
    km[:, M_IU:M_IU + 128] = (same & (s <= t)).astype(np.float32)
    km[:, M_SL:M_SL + 128] = (same & (s > t)).astype(np.float32)
    km[:, M_BO:M_BO + 128] = same.astype(np.float32)
    km[:, M_F:M_F + 64] = (idx[:, None] % 64 == np.arange(64)[None, :]).astype(np.float32)
    rst = np.ones((128, 256), np.float32)
    rst[:, ::64] = 0.0
    km[:, M_RST:M_RST + 256] = rst
    return km


def _pack_cc(inp, hmask):
    cc = np.zeros((128, NCC), np.float32)
    col = lambda v, n: np.ascontiguousarray(np.asarray(v, np.float32).reshape(n, 128).T)
    cc[:, O_PMG:O_PMG + 8] = col(inp["pre_mix_g"][0], 8)
    cc[:, O_PFG:O_PFG + 8] = col(inp["pre_ffn_g"][0], 8)
    caw = np.asarray(inp["conv_a_w"][0], np.float32)
    cc[:, O_CAW:O_CAW + 12] = caw.T.reshape(4, 128, 3).transpose(1, 0, 2).reshape(128, 12)
    mu = np.zeros(1920, np.float32)
    mu[:1824] = np.asarray(inp["shift_mu"][0], np.float32)
    cc[:, O_MU:O_MU + 15] = col(mu, 15)
    for off, name in ((O_W0, "w0"), (O_A0, "a0"), (O_KK, "k_k"), (O_KA, "k_a"), (O_LW, "lnx_w"), (O_LB, "lnx_b")):
        cc[:, off:off + 4] = col(inp[name][0], 4)
    cc[:, O_RK:O_RK + 4] = col(np.asarray(inp["r_k"][0], np.float32).reshape(512), 4)
    fcw = np.asarray(inp["ffn_conv_w"][0], np.float32)
    cc[:, O_FCW:O_FCW + 132] = fcw.T.reshape(44, 128, 3).transpose(1, 0, 2).reshape(128, 132)
    cc[:, O_FCB:O_FCB + 44] = col(inp["ffn_conv_b"][0], 44)
    cc[:, O_HM] = hmask
    return cc


_NC_CACHE = {}


def kernel(**inputs):
    n_pre, n_main = 15, 16
    x = np.asarray(inputs["x"], np.float32)
    B, T, _ = x.shape
    half = T // 2
    if "full" not in _NC_CACHE:
        _NC_CACHE["full"] = build(n_pre, n_main, "full")
    nc = _NC_CACHE["full"]
    km = _host_consts()
    f = lambda n: np.ascontiguousarray(np.asarray(inputs[n], np.float32)[0])
    in_maps = []
    for c in range(8):
        b, h = c // 2, c % 2
        xin = np.zeros((T, D), np.float32)
        if h == 0:
            xin[half:] = x[b, :half]
        else:
            xin[:] = x[b]
        in_maps.append({
            "xin": xin, "cc": _pack_cc(inputs, float(h)), "km": km,
            "post_mix_g": f("post_mix_g"), "post_ffn_g": f("post_ffn_g"),
            "w_in": f("w_in"), "w_out": f("w_out"), "w_up": f("w_up"), "w_down": f("w_down"),
            "w2": f("w2"), "a2": f("a2"), "g2": f("g2"),
        })
    res = run_bass_kernel_spmd(nc, in_maps, core_ids=list(range(8)))
    out = np.zeros((B, T, D), np.float32)
    for c in range(8):
        b, h = c // 2, c % 2
        out[b, h * half:(h + 1) * half] = res.results[c]["out"]
    return out
```
